# Optimizing a Trainium2 kernel written in Bass

```python
import math
import jax, jax.numpy as jnp
from jax import lax
import numpy as np

D_MODEL = 1024
BATCH = 8
SEQ = 4096
DEPTH = 1

NSA_HEADS = 8
NSA_HEAD_DIM = 64
NSA_KV_GROUPS = 2
NSA_WIDTH = NSA_HEADS * NSA_HEAD_DIM
NSA_KV_WIDTH = NSA_KV_GROUPS * NSA_HEAD_DIM
CMP_LEN = 32
CMP_STRIDE = 16
CMP_HIDDEN = 256
SLC_BLOCK = 64
SLC_TOPK = 16
WINDOW = 512
QUERY_BLOCK = 64
BIG = 1e9

SSD_HEADS = 8
SSD_HEAD_DIM = 64
SSD_WIDTH = SSD_HEADS * SSD_HEAD_DIM
SSD_GROUPS = 2
SSD_STATE = 128
SSD_CONV = 4
SSD_CHUNK = 128
DT_MIN = 0.001
DT_MAX = 0.1

MEM_LEN = 256
XA_HEADS = 4
XA_HEAD_DIM = 128
XA_WIDTH = XA_HEADS * XA_HEAD_DIM

MIX_WIDTH = NSA_WIDTH + SSD_WIDTH + XA_WIDTH
ROPE_THETA = 500000.0
ROPE_DIM = NSA_HEAD_DIM // 4
POS_OFFSET_MAX = 1024
EPS = 1e-6

kernel_name = "hymba_nsa_ssd_memory_layer"


def _in_proj_sizes():
    return [NSA_WIDTH,
            NSA_KV_WIDTH, NSA_KV_WIDTH,
            NSA_KV_WIDTH, NSA_KV_WIDTH,
            NSA_KV_WIDTH, NSA_KV_WIDTH,
            NSA_HEADS * 3,
            NSA_WIDTH,
            SSD_WIDTH,
            SSD_WIDTH + 2 * SSD_GROUPS * SSD_STATE,
            SSD_HEADS,
            XA_WIDTH,
            XA_WIDTH]


def _rmsnorm(x, g):
    xf = x.astype(jnp.float32)
    y = xf * lax.rsqrt(jnp.mean(xf * xf, axis=-1, keepdims=True) + EPS)
    return (y * g.astype(jnp.float32)).astype(x.dtype)


def _masked_softmax(s, mask):
    s = jnp.where(mask, s.astype(jnp.float32), -1e30)
    s = s - jnp.max(s, axis=-1, keepdims=True)
    p = jnp.exp(s) * mask
    return p / jnp.maximum(jnp.sum(p, axis=-1, keepdims=True), 1e-30)


def _rope_tables(pos):
    inv = ROPE_THETA ** (-jnp.arange(0, ROPE_DIM, 2, dtype=jnp.float32) / ROPE_DIM)
    ang = pos.astype(jnp.float32)[..., None] * inv
    return jnp.cos(ang), jnp.sin(ang)


def _apply_rope(x, cos, sin):
    half = ROPE_DIM // 2
    shp = cos.shape[:2] + (1,) * (x.ndim - 3) + (half,)
    c = cos.reshape(shp).astype(x.dtype)
    s = sin.reshape(shp).astype(x.dtype)
    x1, x2, rest = x[..., :half], x[..., half:ROPE_DIM], x[..., ROPE_DIM:]
    return jnp.concatenate([x1 * c - x2 * s, x2 * c + x1 * s, rest], axis=-1)


def _nsa_mixer(q, k_c, v_c, k_s, v_s, k_w, v_w, gate_logits, positions,
               pos_k, w1_k, w2_k, pos_v, w1_v, w2_v):
    B, S = q.shape[0], q.shape[1]
    G, E, Dh = NSA_KV_GROUPS, NSA_HEADS // NSA_KV_GROUPS, NSA_HEAD_DIM
    scale = Dh ** -0.5
    q = q.reshape(B, S, G, E, Dh)
    k_c, v_c, k_s, v_s, k_w, v_w = [a.reshape(B, S, G, Dh) for a in (k_c, v_c, k_s, v_s, k_w, v_w)]
    cos, sin = _rope_tables(positions)
    q = _apply_rope(q, cos, sin)
    k_s = _apply_rope(k_s, cos, sin)
    k_w = _apply_rope(k_w, cos, sin)
    t = jnp.arange(S)

    R = CMP_LEN // CMP_STRIDE
    n_cmp = S // CMP_STRIDE - (R - 1)

    def compress(a, pos_emb, w1, w2):
        chunks = a.reshape(B, S // CMP_STRIDE, CMP_STRIDE, G, Dh)
        blocks = jnp.concatenate([chunks[:, r:r + n_cmp] for r in range(R)], axis=2)
        blocks = blocks + pos_emb[:, None, :]
        flat = blocks.transpose(0, 1, 3, 2, 4).reshape(B, n_cmp, G, CMP_LEN * Dh)
        return jax.nn.silu(flat @ w1) @ w2

    kc = compress(k_c, pos_k, w1_k, w2_k)
    vc = compress(v_c, pos_v, w1_v, w2_v)
    cmp_end = jnp.arange(n_cmp) * CMP_STRIDE + CMP_LEN - 1
    cos_c, sin_c = _rope_tables(positions[:, cmp_end])
    kc = _apply_rope(kc, cos_c, sin_c)
    s_c = jnp.einsum('bsged,bngd->bgesn', q, kc) * scale
    p_c = _masked_softmax(s_c, cmp_end[None, :] <= t[:, None])
    o_c = jnp.einsum('bgesn,bngd->bsged', p_c.astype(vc.dtype), vc)

    n_slc = S // SLC_BLOCK
    topk = min(SLC_TOPK, n_slc)
    cmp_start = cmp_end - (CMP_LEN - 1)
    slc_start = jnp.arange(n_slc) * SLC_BLOCK
    overlap = ((cmp_start[:, None] < slc_start[None, :] + SLC_BLOCK)
               & (cmp_start[:, None] + CMP_LEN > slc_start[None, :])).astype(jnp.float32)
    imp = jnp.einsum('bgesn,nj->bgsj', p_c, overlap)
    cur = t // SLC_BLOCK
    j = jnp.arange(n_slc)
    forced = (j[None, :] == 0) | (j[None, :] == cur[:, None]) | (j[None, :] == cur[:, None] - 1)
    valid = j[None, :] <= cur[:, None]
    imp = jnp.where(forced, BIG, jnp.where(valid, imp, -BIG))
    _, sel = lax.top_k(imp, topk)

    k_blk = k_s.reshape(B, n_slc, SLC_BLOCK, G, Dh).transpose(0, 3, 1, 2, 4)
    v_blk = v_s.reshape(B, n_slc, SLC_BLOCK, G, Dh).transpose(0, 3, 1, 2, 4)
    k_pad = jnp.pad(k_w, ((0, 0), (WINDOW, 0), (0, 0), (0, 0)))
    v_pad = jnp.pad(v_w, ((0, 0), (WINDOW, 0), (0, 0), (0, 0)))
    b_ix = jnp.arange(B)[:, None, None, None]
    g_ix = jnp.arange(G)[None, :, None, None]
    in_blk = jnp.arange(SLC_BLOCK)
    span = WINDOW + QUERY_BLOCK
    w_off = jnp.arange(span)
    n_sel_keys = topk * SLC_BLOCK

    def query_block(i):
        s0 = i * QUERY_BLOCK
        tq = s0 + jnp.arange(QUERY_BLOCK)
        qb = lax.dynamic_slice_in_dim(q, s0, QUERY_BLOCK, axis=1)
        sb = lax.dynamic_slice_in_dim(sel, s0, QUERY_BLOCK, axis=2)
        ks = k_blk[b_ix, g_ix, sb].reshape(B, G, QUERY_BLOCK, n_sel_keys, Dh)
        vs = v_blk[b_ix, g_ix, sb].reshape(B, G, QUERY_BLOCK, n_sel_keys, Dh)
        kpos = (sb[..., None] * SLC_BLOCK + in_blk).reshape(B, G, QUERY_BLOCK, n_sel_keys)
        ps = _masked_softmax(jnp.einsum('bqged,bgqkd->bgeqk', qb, ks) * scale,
                             (kpos <= tq[:, None])[:, :, None])
        o_s = jnp.einsum('bgeqk,bgqkd->bqged', ps.astype(vs.dtype), vs)
        kw = lax.dynamic_slice_in_dim(k_pad, s0, span, axis=1)
        vw = lax.dynamic_slice_in_dim(v_pad, s0, span, axis=1)
        kpos_w = s0 - WINDOW + w_off
        mw = ((kpos_w[None, :] <= tq[:, None]) & (kpos_w[None, :] > tq[:, None] - WINDOW)
              & (kpos_w[None, :] >= 0))
        pw = _masked_softmax(jnp.einsum('bqged,bkgd->bgeqk', qb, kw) * scale, mw)
        o_w = jnp.einsum('bgeqk,bkgd->bqged', pw.astype(vw.dtype), vw)
        return o_s, o_w

    o_s, o_w = lax.map(query_block, jnp.arange(S // QUERY_BLOCK))
    o_s = jnp.moveaxis(o_s, 0, 1).reshape(B, S, G, E, Dh)
    o_w = jnp.moveaxis(o_w, 0, 1).reshape(B, S, G, E, Dh)

    gates = jax.nn.sigmoid(gate_logits.astype(jnp.float32)).astype(q.dtype).reshape(B, S, G, E, 3)
    o = gates[..., 0:1] * o_c + gates[..., 1:2] * o_s + gates[..., 2:3] * o_w
    return o.reshape(B, S, NSA_WIDTH)


def _ssd_mixer(z, xbc, dt, conv_w, conv_b, dt_bias, a_log, d_skip, g_norm):
    B, S = xbc.shape[0], xbc.shape[1]
    H, P, G, N, L = SSD_HEADS, SSD_HEAD_DIM, SSD_GROUPS, SSD_STATE, SSD_CHUNK
    E = H // G
    nc = S // L
    f32 = jnp.float32
    xp = jnp.pad(xbc, ((0, 0), (SSD_CONV - 1, 0), (0, 0)))
    acc = conv_b
    for k in range(SSD_CONV):
        acc = acc + xp[:, k:k + S] * conv_w[k]
    xbc = jax.nn.silu(acc)
    xs, bm, cm = jnp.split(xbc, [SSD_WIDTH, SSD_WIDTH + G * N], axis=-1)
    xs = xs.astype(f32).reshape(B, nc, L, G, E, P)
    bm = bm.astype(f32).reshape(B, nc, L, G, N)
    cm = cm.astype(f32).reshape(B, nc, L, G, N)
    dt = jax.nn.softplus(dt.astype(f32) + dt_bias.astype(f32)).reshape(B, nc, L, G, E)
    a = -jnp.exp(a_log.astype(f32)).reshape(G, E)
    a_cs = jnp.cumsum((dt * a).transpose(0, 1, 3, 4, 2), axis=-1)
    causal = jnp.tril(jnp.ones((L, L), dtype=bool))
    decay_in = jnp.exp(jnp.where(causal, a_cs[..., :, None] - a_cs[..., None, :], -jnp.inf))
    xdt = xs * dt[..., None]
    cb = jnp.einsum('bclgn,bcsgn->bcgls', cm, bm)
    y_diag = jnp.einsum('bcgels,bcsgep->bclgep', cb[:, :, :, None] * decay_in, xdt)
    decay_out = jnp.exp(a_cs[..., -1:] - a_cs)
    states = jnp.einsum('bclgn,bcgel,bclgep->bcgepn', bm, decay_out, xdt)
    chunk_decay = jnp.exp(a_cs[..., -1])

    def step(h, inp):
        st, dec = inp
        return h * dec[..., None, None] + st, h

    h0 = jnp.zeros((B, G, E, P, N), f32)
    _, prev = lax.scan(step, h0, (jnp.moveaxis(states, 1, 0), jnp.moveaxis(chunk_decay, 1, 0)))
    prev = jnp.moveaxis(prev, 0, 1)
    y_off = jnp.einsum('bclgn,bcgepn,bcgel->bclgep', cm, prev, jnp.exp(a_cs))
    y = y_diag + y_off + xs * d_skip.astype(f32).reshape(G, E, 1)
    y = y.reshape(B, S, SSD_WIDTH)
    return _rmsnorm(y * jax.nn.silu(z.astype(f32)), g_norm).astype(z.dtype)


def _memory_attention(q, mem, g_mem, w_mem_kv):
    B, S = q.shape[0], q.shape[1]
    M = mem.shape[1]
    q = q.reshape(B, S, XA_HEADS, XA_HEAD_DIM)
    kv = _rmsnorm(mem, g_mem) @ w_mem_kv
    k, v = jnp.split(kv, 2, axis=-1)
    k = k.reshape(B, M, XA_HEADS, XA_HEAD_DIM)
    v = v.reshape(B, M, XA_HEADS, XA_HEAD_DIM)
    s = jnp.einsum('bshd,bmhd->bhsm', q, k).astype(jnp.float32) * (XA_HEAD_DIM ** -0.5)
    p = jax.nn.softmax(s, axis=-1).astype(v.dtype)
    return jnp.einsum('bhsm,bmhd->bshd', p, v).reshape(B, S, XA_WIDTH)


def setup_inputs(seed: int = 0) -> dict:
    key = jax.random.key(seed)
    ks = jax.random.split(key, 24)
    f32 = jnp.float32

    def nrm(k, shape, scale):
        return jax.random.normal(k, shape, f32) * scale

    n_in = sum(_in_proj_sizes())
    conv_dim = SSD_WIDTH + 2 * SSD_GROUPS * SSD_STATE
    dt0 = jnp.exp(jax.random.uniform(ks[14], (DEPTH, SSD_HEADS), f32)
                  * (math.log(DT_MAX) - math.log(DT_MIN)) + math.log(DT_MIN))
    return {
        "x": nrm(ks[0], (BATCH, SEQ, D_MODEL), 1.0),
        "mem": nrm(ks[1], (BATCH, MEM_LEN, D_MODEL), 1.0),
        "positions": (jnp.arange(SEQ, dtype=jnp.int32)[None, :]
                      + jax.random.randint(ks[2], (BATCH, 1), 0, POS_OFFSET_MAX, dtype=jnp.int32)),
        "g_in": 1.0 + nrm(ks[3], (DEPTH, D_MODEL), 0.02),
        "w_in": nrm(ks[4], (DEPTH, D_MODEL, n_in), D_MODEL ** -0.5),
        "cmp_pos_k": nrm(ks[5], (DEPTH, CMP_LEN, NSA_HEAD_DIM), 0.02),
        "w_cmp1_k": nrm(ks[6], (DEPTH, CMP_LEN * NSA_HEAD_DIM, CMP_HIDDEN), (CMP_LEN * NSA_HEAD_DIM) ** -0.5),
        "w_cmp2_k": nrm(ks[7], (DEPTH, CMP_HIDDEN, NSA_HEAD_DIM), CMP_HIDDEN ** -0.5),
        "cmp_pos_v": nrm(ks[8], (DEPTH, CMP_LEN, NSA_HEAD_DIM), 0.02),
        "w_cmp1_v": nrm(ks[9], (DEPTH, CMP_LEN * NSA_HEAD_DIM, CMP_HIDDEN), (CMP_LEN * NSA_HEAD_DIM) ** -0.5),
        "w_cmp2_v": nrm(ks[10], (DEPTH, CMP_HIDDEN, NSA_HEAD_DIM), CMP_HIDDEN ** -0.5),
        "conv_w": nrm(ks[11], (DEPTH, SSD_CONV, conv_dim), SSD_CONV ** -0.5),
        "conv_b": nrm(ks[12], (DEPTH, conv_dim), 0.02),
        "dt_bias": dt0 + jnp.log(-jnp.expm1(-dt0)),
        "a_log": jnp.log(jax.random.uniform(ks[15], (DEPTH, SSD_HEADS), f32, 1.0, 16.0)),
        "d_skip": 1.0 + nrm(ks[16], (DEPTH, SSD_HEADS), 0.02),
        "g_ssd_norm": 1.0 + nrm(ks[17], (DEPTH, SSD_WIDTH), 0.02),
        "g_mem": 1.0 + nrm(ks[18], (DEPTH, D_MODEL), 0.02),
        "w_mem_kv": nrm(ks[19], (DEPTH, D_MODEL, 2 * XA_WIDTH), D_MODEL ** -0.5),
        "w_out": nrm(ks[20], (DEPTH, MIX_WIDTH, D_MODEL), MIX_WIDTH ** -0.5),
        "g_final": 1.0 + nrm(ks[21], (D_MODEL,), 0.02),
    }


def reference(x, mem, positions, g_in, w_in, cmp_pos_k, w_cmp1_k, w_cmp2_k, cmp_pos_v, w_cmp1_v,
              w_cmp2_v, conv_w, conv_b, dt_bias, a_log, d_skip, g_ssd_norm, g_mem, w_mem_kv, w_out,
              g_final):
    split_at = [int(v) for v in np.cumsum(_in_proj_sizes())[:-1]]
    h = x
    for l in range(DEPTH):
        xn = _rmsnorm(h, g_in[l])
        proj = xn @ w_in[l]
        (q_a, k_c, v_c, k_s, v_s, k_w, v_w, gate_logits, gate_a,
         z, xbc, dt, q_x, gate_x) = jnp.split(proj, split_at, axis=-1)
        o_a = _nsa_mixer(q_a, k_c, v_c, k_s, v_s, k_w, v_w, gate_logits, positions,
                         cmp_pos_k[l], w_cmp1_k[l], w_cmp2_k[l], cmp_pos_v[l], w_cmp1_v[l], w_cmp2_v[l])
        o_a = o_a * jax.nn.silu(gate_a)
        o_b = _ssd_mixer(z, xbc, dt, conv_w[l], conv_b[l], dt_bias[l], a_log[l], d_skip[l], g_ssd_norm[l])
        o_c = _memory_attention(q_x, mem, g_mem[l], w_mem_kv[l]) * jax.nn.silu(gate_x)
        h = h + jnp.concatenate([o_a, o_b, o_c], axis=-1) @ w_out[l]
    return _rmsnorm(h, g_final)
```

```python
import os
import math
import numpy as np
import ml_dtypes
from contextlib import ExitStack
import concourse.bass as bass
import concourse.mybir as mybir
from concourse.bass_utils import run_bass_kernel_spmd

F32 = mybir.dt.float32
BF16 = mybir.dt.bfloat16
I32 = mybir.dt.int32
AF = mybir.ActivationFunctionType
ALU = mybir.AluOpType
AX = mybir.AxisListType

S = 4096
D = 1024
NIN = 4384
NT = S // 128
NTT = S // 512
EPS = 1e-6
C_Q, C_KC, C_VC, C_KS, C_VS, C_KW, C_VW, C_GL, C_GA, C_Z, C_XBC, C_DT, C_QX, C_GX = (
    0, 512, 640, 768, 896, 1024, 1152, 1280, 1304, 1816, 2328, 3352, 3360, 3872)
P_Q, P_KC, P_KS, P_KW, P_XBC, P_QX, P_VC = 0, 512, 640, 768, 896, 1920, 2432
PT_ROWS = 2560
NEG = -30000.0

DEBUG = bool(int(os.environ.get("KDEBUG", "0")))
STAGES = os.environ.get("KSTAGES", "XPCBAF")
ASUB = int(os.environ.get("KASUB", "3"))


class Res:
    __slots__ = ("name", "w", "r")

    def __init__(self, name=""):
        self.name = name
        self.w = None
        self.r = []


class FW:
    def __init__(self, nc, n_dma_sems=24):
        self.nc = nc
        self.eng = {"pe": nc.tensor, "act": nc.scalar, "dve": nc.vector, "pool": nc.gpsimd, "sp": nc.sync}
        self.sem = {}
        self.cnt = {}
        self._ctx = []
        for e in ("pe", "act", "dve", "pool"):
            cm = nc.semaphore("sem_" + e)
            s = cm.__enter__()
            self._ctx.append(cm)
            self.sem[e] = s
            self.cnt[e] = 0
        self.dpool = {}
        for q, n in (("sp", n_dma_sems), ("pool", 12), ("act", 8)):
            lst = []
            for i in range(n):
                cm = nc.semaphore(f"dsem_{q}_{i}")
                s = cm.__enter__()
                self._ctx.append(cm)
                lst.append([s, 0])
            self.dpool[q] = [lst, 0]
        self.obs = {e: {} for e in self.eng}
        self.nwaits = 0
        self.nops = 0

    def close(self):
        for cm in reversed(self._ctx):
            cm.__exit__(None, None, None)

    def _need(self, e, tok, lst):
        if tok is None:
            return
        src, sem, val = tok
        key = id(sem)
        if self.obs[e].get(key, 0) >= val:
            return
        self.obs[e][key] = val
        lst[key] = (sem, max(val, lst.get(key, (None, 0))[1]))

    def _wait(self, e, tok):
        lst = {}
        self._need(e, tok, lst)
        for sem, val in lst.values():
            self.eng[e].wait_ge(sem, val)
            self.nwaits += 1

    def _deps(self, e, reads, writes):
        lst = {}
        for r in reads:
            if r.w is not None:
                if not (r.w[0] == e and e == "pe"):
                    self._need(e, r.w, lst)
        for w in writes:
            if w.w is not None and w.w[0] != e:
                self._need(e, w.w, lst)
            for t in w.r:
                if t[0] != e:
                    self._need(e, t, lst)
        return list(lst.values())

    def _update(self, tok, reads, writes):
        for r in reads:
            if tok[0].startswith("dma"):
                r.r = r.r + [tok]
            else:
                r.r = [t for t in r.r if t[0] != tok[0]] + [tok]
        for w in writes:
            w.w = tok
            w.r = []

    def op(self, e, fn, reads=(), writes=()):
        waits = self._deps(e, reads, writes)
        for sem, val in waits[:-1]:
            self.eng[e].wait_ge(sem, val)
            self.nwaits += 1
        ins = fn(self.eng[e])
        if waits:
            ins = ins._wait_ge(waits[-1][0], waits[-1][1])
        self.cnt[e] += 1
        ins.then_inc(self.sem[e], 1)
        tok = (e, self.sem[e], self.cnt[e])
        self._update(tok, reads, writes)
        self.nops += 1
        return tok

    def dma(self, q, out, in_, reads=(), writes=(), **kw):
        waits = self._deps(q, reads, writes)
        lst, idx = self.dpool[q]
        slot = lst[idx % len(lst)]
        self.dpool[q][1] = idx + 1
        sem, cur = slot
        if cur > 0:
            d = {}
            self._need(q, ("dma_" + q, sem, cur), d)
            waits += list(d.values())
        for sem_w, val in waits[:-1]:
            self.eng[q].wait_ge(sem_w, val)
            self.nwaits += 1
        ins = self.eng[q].dma_start(out=out, in_=in_, **kw)
        if waits:
            ins = ins._wait_ge(waits[-1][0], waits[-1][1])
        slot[1] = cur + 16
        ins.then_inc(sem, 16)
        tok = ("dma_" + q, sem, slot[1])
        self._update(tok, reads, writes)
        return tok

    def barrier(self):
        toks = []
        for e in ("pe", "act", "dve", "pool"):
            if self.cnt[e] > 0:
                toks.append((e, self.sem[e], self.cnt[e]))
        for q in self.dpool:
            for sem, cur in self.dpool[q][0]:
                if cur > 0:
                    toks.append(("dma_" + q, sem, cur))
        for e in ("pe", "act", "dve", "pool", "sp"):
            for t in toks:
                if t[0] != e:
                    self._wait(e, t)


class Ring:
    def __init__(self, bufs):
        self.bufs = [(b, Res()) for b in bufs]
        self.i = 0

    def next(self):
        b = self.bufs[self.i % len(self.bufs)]
        self.i += 1
        return b


def build_nc():
    nc = bass.Bass("TRN2", target_bir_lowering=False)
    fw = FW(nc)
    dbg_kind = "ExternalOutput" if DEBUG else "Internal"

    def din(name, shape, dt=F32):
        return nc.dram_tensor(name, list(shape), dt, kind="ExternalInput").ap()

    x = din("x", [S, D])
    mem = din("mem", [256, D])
    positions = din("positions", [1, S], I32)
    g_in = din("g_in", [D])
    w_in = din("w_in", [D, NIN])
    cmp_pos_k = din("cmp_pos_k", [32, 64])
    w_cmp1_k = din("w_cmp1_k", [2048, 256])
    w_cmp2_k = din("w_cmp2_k", [256, 64])
    cmp_pos_v = din("cmp_pos_v", [32, 64])
    w_cmp1_v = din("w_cmp1_v", [2048, 256])
    w_cmp2_v = din("w_cmp2_v", [256, 64])
    conv_w = din("conv_w", [4, 1024])
    conv_b = din("conv_b", [1024])
    dt_bias = din("dt_bias", [1, 8])
    a_log = din("a_log", [1, 8])
    d_skip = din("d_skip", [1, 8])
    g_ssd_norm = din("g_ssd_norm", [512])
    g_mem = din("g_mem", [D])
    w_mem_kv = din("w_mem_kv", [D, 1024])
    w_out = din("w_out", [1536, D])
    g_final = din("g_final", [1, D])
    c_ident = din("c_ident", [128, 128])
    c_ropeinv = din("c_ropeinv", [128, 1])
    c_psw = din("c_psw", [128, 128])
    c_triu = din("c_triu", [128, 128])
    c_trineg = din("c_trineg", [128, 128])
    c_hsel = din("c_hsel", [128, 32])
    c_ovx = din("c_ovx", [128, 130], BF16)
    c_w3 = din("c_w3", [128, 3072], BF16)
    c_wc = din("c_wc", [128, 896], BF16)
    c_ww = din("c_ww", [128, 1536], BF16)
    c_addm = din("c_addm", [S, 64])
    c_exg = din("c_exg", [24, 1536], BF16)
    c_ex = din("c_ex", [64, S], BF16)

    out = nc.dram_tensor("out", [S, D], F32, kind="ExternalOutput").ap()
    PT = nc.dram_tensor("PT", [PT_ROWS, S], BF16, kind=dbg_kind).ap()
    SG = nc.dram_tensor("SG", [1536, S], BF16, kind=dbg_kind).ap()
    GL = nc.dram_tensor("GL", [24, S], BF16, kind=dbg_kind).ap()
    VT = nc.dram_tensor("VT", [S, 392], F32, kind=dbg_kind).ap()
    CS = nc.dram_tensor("CS", [2, 128, 256], F32, kind=dbg_kind).ap()
    MT = nc.dram_tensor("MT", [1536, S], BF16, kind=dbg_kind).ap()

    R_out = Res("out")
    R_PT = [Res() for _ in range(PT_ROWS // 128)]
    R_SG = [Res() for _ in range(12)]
    R_GL = Res()
    R_VT = Res()
    R_CS = Res()
    R_MT = [Res() for _ in range(12)]

    with ExitStack() as top:
        top.enter_context(nc.allow_low_precision("bf16 matmul operands / bf16 staging by design"))

        def sb(name, shape, dt, stack=top):
            return stack.enter_context(nc.sbuf_tensor(name, list(shape), dt))

        ps = [top.enter_context(nc.psum_tensor(f"ps{i}", [128, 512], F32)) for i in range(8)]
        R_ps = [Res(f"ps{i}") for i in range(8)]
        ps_ring = {"i": 0}

        def next_ps(lo=0, hi=8):
            i = lo + ps_ring["i"] % (hi - lo)
            ps_ring["i"] += 1
            return ps[i], R_ps[i]

        ident_f = sb("ident_f", [128, 128], F32)
        ident_b = sb("ident_b", [128, 128], BF16)
        ones_f = sb("ones_f", [128, 128], F32)
        ones_b = sb("ones_b", [128, 128], BF16)
        psw_f = sb("psw_f", [128, 128], F32)
        R_const = Res("const")
        fw.dma("sp", ident_f[:], c_ident[:, :], writes=[R_const])
        fw.dma("sp", psw_f[:], c_psw[:, :], writes=[R_const])
        fw.op("dve", lambda e: e.tensor_copy(out=ident_b[:], in_=ident_f[:]), reads=[R_const], writes=[R_const])
        fw.op("dve", lambda e: e.memset(ones_f[:], 1.0), writes=[R_const])
        fw.op("dve", lambda e: e.memset(ones_b[:], 1.0), writes=[R_const])
        epsb = sb("epsb", [128, 1], F32)
        fw.op("dve", lambda e: e.memset(epsb[:], 1e-30), writes=[R_const])

        dt_all = sb("dt_all", [128, NT, 8], F32)
        R_dt = Res()
        xn_stack = ExitStack()
        xnT = sb("xnT", [128, 8, S], BF16, xn_stack)
        R_xn = [Res(f"xn{i}") for i in range(NTT)]

        def rms_transpose_phase(src, ntiles, g_vec, dstT, R_dst_of_tile, stack):
            g_sb = sb("g_sb_" + dstT.name, [128, 8], F32, stack)
            R_g = Res()
            fw.dma("sp", g_sb[:], g_vec.rearrange("(dk p) -> p dk", p=128), writes=[R_g], allow_slow_non_contiguous=True)
            xin = Ring([sb(f"xin{i}_" + dstT.name, [128, D], F32, stack) for i in range(min(4, ntiles))])
            xsc = Ring([sb(f"xsc{i}_" + dstT.name, [128, D], BF16, stack) for i in range(min(3, ntiles))])
            junk = sb("junk_" + dstT.name, [128, D], BF16, stack)
            R_junk = Res()
            st = Ring([sb(f"st{i}_" + dstT.name, [128, 4], F32, stack) for i in range(4)])
            epsd = sb("epsd_" + dstT.name, [128, 1], F32, stack)
            fw.op("dve", lambda e: e.memset(epsd[:], EPS), writes=[R_g])

            def stage_a(tt):
                xt, R_xt = xin.next()
                fw.dma("sp", xt[:], src[tt * 128:(tt + 1) * 128, :], writes=[R_xt])
                s4, R_s4 = st.next()
                fw.op("act", lambda e: e.activation(out=junk[:], in_=xt[:], func=AF.Square, accum_out=s4[:, 0:1]),
                      reads=[R_xt], writes=[R_junk, R_s4])
                fw.op("act", lambda e: e.activation(out=s4[:, 2:3], in_=s4[:, 0:1], func=AF.Ln, scale=1.0 / D, bias=epsd[:, 0:1]),
                      reads=[R_s4, R_g], writes=[R_s4])
                fw.op("act", lambda e: e.activation(out=s4[:, 3:4], in_=s4[:, 2:3], func=AF.Exp, scale=-0.5),
                      reads=[R_s4], writes=[R_s4])
                return (xt, R_xt, s4, R_s4)

            def stage_b(tt, xt, R_xt, s4, R_s4):
                xs, R_xs = xsc.next()
                fw.op("dve", lambda e: e.tensor_scalar(out=xs[:], in0=xt[:], scalar1=s4[:, 3:4], scalar2=None, op0=ALU.mult),
                      reads=[R_xt, R_s4], writes=[R_xs])
                pb, R_pb = next_ps()
                pbb = pb[:].bitcast(BF16)
                for dk in range(8):
                    fw.op("pe", lambda e: e.transpose(pbb[:, dk * 128:(dk + 1) * 128], xs[:, dk * 128:(dk + 1) * 128], ident_b[:]),
                          reads=[R_xs, R_const], writes=[R_pb])
                return (tt, pbb, R_pb)

            def stage_c(tt, pbb, R_pb):
                fw.op("dve", lambda e: e.tensor_tensor(
                    out=dstT[:, :, tt * 128:(tt + 1) * 128],
                    in0=pbb.rearrange("p (a b) -> p a b", a=8),
                    in1=g_sb[:].unsqueeze(2).to_broadcast([128, 8, 128]), op=ALU.mult),
                    reads=[R_pb, R_g], writes=[R_dst_of_tile(tt)])

            pa_, pb_ = [], []
            for tt in range(ntiles + 3):
                if tt < ntiles:
                    pa_.append((tt,) + stage_a(tt))
                if len(pb_) > 0 and tt >= 2:
                    stage_c(*pb_.pop(0))
                if len(pa_) > 0 and tt >= 1:
                    pb_.append(stage_b(*pa_.pop(0)))
            while pa_ or pb_:
                if pb_:
                    stage_c(*pb_.pop(0))
                if pa_:
                    pb_.append(stage_b(*pa_.pop(0)))

        if "X" in STAGES:
            rms_transpose_phase(x, NT, g_in, xnT, lambda tt: R_xn[tt // 4], xn_stack)

        if "P" in STAGES:
            with ExitStack() as ph:
                Ct = sb("Ct", [128, S], F32, ph)
                St = sb("St", [128, S], F32, ph)
                R_C = Res()
                def emit_rope_tables():
                    invp = sb("invp", [128, 1], F32, ph)
                    R_tmp = Res()
                    fw.dma("sp", invp[:], c_ropeinv[:, :], writes=[R_tmp])
                    CB = 512
                    posi = sb("posi", [128, CB], I32, ph)
                    ang = sb("ang", [128, CB], F32, ph)
                    kfi = sb("kfi", [128, CB], I32, ph)
                    kf = sb("kf", [128, CB], F32, ph)
                    rr = sb("rr", [128, CB], F32, ph)
                    rc = sb("rc", [128, CB], F32, ph)
                    C1 = 6.28125
                    C2 = 2 * math.pi - 6.28125
                    PI_LO = 3.141592
                    for cb in range(S // CB):
                        cs = slice(cb * CB, (cb + 1) * CB)
                        fw.dma("sp", posi[:], positions[:, cs].partition_broadcast(128), reads=[R_tmp], writes=[R_tmp])
                        fw.op("dve", lambda e: e.tensor_copy(out=ang[:], in_=posi[:]), reads=[R_tmp], writes=[R_tmp])
                        fw.op("dve", lambda e: e.tensor_scalar(out=ang[:], in0=ang[:], scalar1=invp[:, 0:1], scalar2=None, op0=ALU.mult),
                              reads=[R_tmp], writes=[R_tmp])
                        fw.op("dve", lambda e: e.tensor_scalar(out=kfi[:], in0=ang[:], scalar1=1.0 / (2 * math.pi), scalar2=None, op0=ALU.mult),
                              reads=[R_tmp], writes=[R_tmp])
                        fw.op("dve", lambda e: e.tensor_copy(out=kf[:], in_=kfi[:]), reads=[R_tmp], writes=[R_tmp])
                        fw.op("dve", lambda e: e.scalar_tensor_tensor(out=rr[:], in0=kf[:], scalar=-C1, in1=ang[:], op0=ALU.mult, op1=ALU.add),
                              reads=[R_tmp], writes=[R_tmp])
                        fw.op("dve", lambda e: e.scalar_tensor_tensor(out=rr[:], in0=kf[:], scalar=-C2, in1=rr[:], op0=ALU.mult, op1=ALU.add),
                              reads=[R_tmp], writes=[R_tmp])
                        fw.op("dve", lambda e: e.tensor_scalar(out=rc[:], in0=rr[:], scalar1=math.pi / 2, scalar2=-2 * math.pi,
                                                               op0=ALU.is_gt, op1=ALU.mult), reads=[R_tmp], writes=[R_tmp])
                        fw.op("dve", lambda e: e.scalar_tensor_tensor(out=rc[:], in0=rr[:], scalar=math.pi / 2, in1=rc[:], op0=ALU.add, op1=ALU.add),
                              reads=[R_tmp], writes=[R_tmp])
                        fw.op("dve", lambda e: e.tensor_scalar(out=rr[:], in0=rr[:], scalar1=-PI_LO, scalar2=PI_LO, op0=ALU.max, op1=ALU.min),
                              reads=[R_tmp], writes=[R_tmp])
                        fw.op("dve", lambda e: e.tensor_scalar(out=rc[:], in0=rc[:], scalar1=-PI_LO, scalar2=PI_LO, op0=ALU.max, op1=ALU.min),
                              reads=[R_tmp], writes=[R_tmp])
                        fw.op("act", lambda e: e.activation(out=St[:, cs], in_=rr[:], func=AF.Sin), reads=[R_tmp], writes=[R_C])
                        fw.op("act", lambda e: e.activation(out=Ct[:, cs], in_=rc[:], func=AF.Sin), reads=[R_tmp], writes=[R_C])
                    csc = sb("csc", [128, 2, 256], F32, ph)
                    R_csc = Res()
                    fw.op("dve", lambda e: e.tensor_copy(out=csc[:, 0, 0:255], in_=Ct[:, 31:S:16]), reads=[R_C], writes=[R_csc])
                    fw.op("dve", lambda e: e.tensor_copy(out=csc[:, 1, 0:255], in_=St[:, 31:S:16]), reads=[R_C], writes=[R_csc])
                    fw.dma("sp", CS[0, :, 0:255], csc[:, 0, 0:255], reads=[R_csc], writes=[R_CS])
                    fw.dma("sp", CS[1, :, 0:255], csc[:, 1, 0:255], reads=[R_csc], writes=[R_CS])


                wbf = Ring([sb(f"wbf{i}", [128, 8, 128], BF16, ph) for i in range(3)])
                OW = 2048
                otile = [(sb(f"otile{i}", [128, OW], BF16, ph), [Res() for _ in range(4)]) for i in range(3)]
                ot_i = {"i": 0}
                qf = Ring([sb(f"qf{i}", [128, 512], F32, ph) for i in range(2)])
                t1 = Ring([sb(f"t1_{i}", [128, 512], F32, ph) for i in range(2)])
                w_view = w_in.rearrange("(dk p) c -> p dk c", p=128)

                def load_w(c0, ncols):
                    wb, R_wb = wbf.next()
                    fw.dma("pool", wb[:, :, 0:ncols], w_view[:, :, c0:c0 + ncols], writes=[R_wb])
                    return wb, R_wb

                def proj_fm(wb, R_wb, ncols, T):
                    pb, R_pb = next_ps()
                    for dk in range(8):
                        fw.op("pe", lambda e: e.matmul(pb[0:ncols, :], lhsT=wb[:, dk, 0:ncols], rhs=xnT[:, dk, T * 512:(T + 1) * 512],
                                                       start=(dk == 0), stop=(dk == 7)),
                              reads=[R_wb, R_xn[T]], writes=[R_pb])
                    return pb, R_pb

                chunks = []
                for i in range(4):
                    chunks.append(("silu", C_GA + i * 128, 128, SG, i, R_SG))
                for i in range(4):
                    chunks.append(("silu", C_Z + i * 128, 128, SG, 4 + i, R_SG))
                for i in range(4):
                    chunks.append(("silu", C_GX + i * 128, 128, SG, 8 + i, R_SG))
                chunks.append(("copy", C_KC, 128, PT, P_KC // 128, R_PT))
                chunks.append(("copy", C_VC, 128, PT, P_VC // 128, R_PT))
                for i in range(8):
                    chunks.append(("copy", C_XBC + i * 128, 128, PT, P_XBC // 128 + i, R_PT))
                for i in range(4):
                    chunks.append(("copy", C_QX + i * 128, 128, PT, P_QX // 128 + i, R_PT))
                for i in range(4):
                    chunks.append(("rope", C_Q + i * 128, 128, PT, P_Q // 128 + i, R_PT))
                chunks.append(("rope", C_KS, 128, PT, P_KS // 128, R_PT))
                chunks.append(("rope", C_KW, 128, PT, P_KW // 128, R_PT))
                chunks.append(("gl", C_GL, 24, GL, 0, None))
                nxt = load_w(chunks[0][1], chunks[0][2])
                for ci, (kind, c0, ncols, dstT, drow, R_dst) in enumerate(chunks):
                    wb, R_wb = nxt
                    if ci + 1 < len(chunks):
                        nxt = load_w(chunks[ci + 1][1], chunks[ci + 1][2])
                    if ci == 12:
                        emit_rope_tables()
                    for half in range(2):
                        ot, R_ots = otile[ot_i["i"] % 3]
                        ot_i["i"] += 1
                        for T4 in range(4):
                            T = half * 4 + T4
                            osl = ot[:, T4 * 512:(T4 + 1) * 512]
                            R_o1 = R_ots[T4]
                            pb, R_pb = proj_fm(wb, R_wb, ncols, T)
                            if kind == "silu":
                                fw.op("act", lambda e: e.activation(out=osl, in_=pb[:], func=AF.Silu), reads=[R_pb], writes=[R_o1])
                            elif kind == "copy":
                                if T % 2 == 0:
                                    fw.op("dve", lambda e: e.tensor_copy(out=osl, in_=pb[:]), reads=[R_pb], writes=[R_o1])
                                else:
                                    fw.op("act", lambda e: e.activation(out=osl, in_=pb[:], func=AF.Copy), reads=[R_pb], writes=[R_o1])
                            elif kind == "rope":
                                q32, R_q32 = qf.next()
                                fw.op("act", lambda e: e.activation(out=q32[:], in_=pb[:], func=AF.Copy), reads=[R_pb], writes=[R_q32])
                                pb2, R_pb2 = next_ps()
                                fw.op("pe", lambda e: e.matmul(pb2[:, :], lhsT=psw_f[:], rhs=q32[:], start=True, stop=True),
                                      reads=[R_const, R_q32], writes=[R_pb2])
                                ta, R_ta = t1.next()
                                fw.op("dve", lambda e: e.tensor_tensor(out=ta[:], in0=pb2[:], in1=St[:, T * 512:(T + 1) * 512], op=ALU.mult),
                                      reads=[R_pb2, R_C], writes=[R_ta])
                                fw.op("pool", lambda e: e.tensor_tensor(out=q32[:], in0=q32[:], in1=Ct[:, T * 512:(T + 1) * 512], op=ALU.mult),
                                      reads=[R_q32, R_C], writes=[R_q32])
                                fw.op("dve", lambda e: e.tensor_tensor(out=osl, in0=ta[:], in1=q32[:], op=ALU.add),
                                      reads=[R_ta, R_q32], writes=[R_o1])
                            else:
                                ta, R_ta = t1.next()
                                fw.op("act", lambda e: e.activation(out=ta[0:24, :], in_=pb[0:24, :], func=AF.Exp, scale=-1.0), reads=[R_pb], writes=[R_ta])
                                fw.op("dve", lambda e: e.tensor_scalar(out=ta[0:24, :], in0=ta[0:24, :], scalar1=1.0, scalar2=None, op0=ALU.add),
                                      reads=[R_ta], writes=[R_ta])
                                fw.op("dve", lambda e: e.reciprocal(out=osl[0:24, :], in_=ta[0:24, :]), reads=[R_ta], writes=[R_o1])
                        if kind == "gl":
                            fw.dma("sp", GL[:, half * OW:(half + 1) * OW], ot[0:24, :], reads=R_ots, writes=[R_GL])
                        else:
                            fw.dma("sp", dstT[drow * 128:(drow + 1) * 128, half * OW:(half + 1) * OW], ot[:], reads=R_ots, writes=[R_dst[drow]])
                wtm = sb("wtm", [128, 8, 392], BF16, ph)
                R_wtm = Res()
                for j, c0 in enumerate((C_VC, C_VS, C_VW)):
                    fw.dma("pool", wtm[:, :, j * 128:(j + 1) * 128], w_view[:, :, c0:c0 + 128], writes=[R_wtm])
                fw.dma("pool", wtm[:, :, 384:392], w_view[:, :, C_DT:C_DT + 8], writes=[R_wtm])
                vt_o = Ring([sb(f"vt_o{i}", [128, 392], F32, ph) for i in range(3)])
                for tt in range(NT):
                    pb, R_pb = next_ps()
                    for dk in range(8):
                        fw.op("pe", lambda e: e.matmul(pb[:, 0:392], lhsT=xnT[:, dk, tt * 128:(tt + 1) * 128], rhs=wtm[:, dk, :],
                                                       start=(dk == 0), stop=(dk == 7)),
                              reads=[R_wtm, R_xn[tt // 4]], writes=[R_pb])
                    vo, R_vo = vt_o.next()
                    if tt % 2 == 0:
                        fw.op("dve", lambda e: e.tensor_copy(out=vo[:], in_=pb[:, 0:392]), reads=[R_pb], writes=[R_vo])
                    else:
                        fw.op("act", lambda e: e.activation(out=vo[:], in_=pb[:, 0:392], func=AF.Copy), reads=[R_pb], writes=[R_vo])
                    fw.op("dve", lambda e: e.tensor_copy(out=dt_all[:, tt, :], in_=pb[:, 384:392]), reads=[R_pb], writes=[R_dt])
                    fw.dma("sp", VT[tt * 128:(tt + 1) * 128, :], vo[:], reads=[R_vo], writes=[R_VT])
                fw.barrier()

        xn_stack.close()
        wo = sb("wo", [128, 12, D], BF16)
        R_wo = Res()
        for ck in range(12):
            fw.dma("pool", wo[:, ck, :], w_out[ck * 128:(ck + 1) * 128, :], writes=[R_wo])
        def build_phase_c(ph):
            memT = sb("memT", [128, 8, 256], BF16, ph)
            R_memT = Res()
            rms_transpose_phase(mem, 2, g_mem, memT, lambda tt: R_memT, ph)
            wkv_view = w_mem_kv.rearrange("(dk p) c -> p dk c", p=128)
            kT = sb("kT", [128, 4, 256], BF16, ph)
            vtok = sb("vtok", [128, 2, 512], BF16, ph)
            R_kv = Res()
            wst = Ring([sb(f"cwst{i}", [128, 8, 128], F32, ph) for i in range(2)])
            wv = sb("cwv", [128, 8, 512], BF16, ph)
            R_wv = Res()
            wkb = Ring([sb(f"cwkb{i}", [128, 8, 128], BF16, ph) for i in range(2)])
            for h in range(4):
                ws, R_ws = wst.next()
                fw.dma("sp", ws[:], wkv_view[:, :, h * 128:(h + 1) * 128], writes=[R_ws])
                wb, R_wb = wkb.next()
                fw.op("act", lambda e: e.activation(out=wb[:], in_=ws[:], func=AF.Copy), reads=[R_ws], writes=[R_wb])
                pb, R_pb = next_ps()
                for dk in range(8):
                    fw.op("pe", lambda e: e.matmul(pb[:, 0:256], lhsT=wb[:, dk, :], rhs=memT[:, dk, :], start=(dk == 0), stop=(dk == 7)),
                          reads=[R_wb, R_memT], writes=[R_pb])
                fw.op("dve", lambda e: e.tensor_copy(out=kT[:, h, :], in_=pb[:, 0:256]), reads=[R_pb], writes=[R_kv])
            for h in range(4):
                ws, R_ws = wst.next()
                fw.dma("sp", ws[:], wkv_view[:, :, 512 + h * 128:512 + (h + 1) * 128], writes=[R_ws])
                fw.op("dve", lambda e: e.tensor_copy(out=wv[:, :, h * 128:(h + 1) * 128], in_=ws[:]), reads=[R_ws], writes=[R_wv])
            for kc in range(2):
                pb, R_pb = next_ps()
                for dk in range(8):
                    fw.op("pe", lambda e: e.matmul(pb[:, :], lhsT=memT[:, dk, kc * 128:(kc + 1) * 128], rhs=wv[:, dk, :],
                                                   start=(dk == 0), stop=(dk == 7)), reads=[R_wv, R_memT], writes=[R_pb])
                fw.op("dve", lambda e: e.tensor_copy(out=vtok[:, kc, :], in_=pb[:, :]), reads=[R_pb], writes=[R_kv])
            qx = Ring([sb(f"cqx{i}", [128, 512], BF16, ph) for i in range(3)])
            sgx = Ring([sb(f"csgx{i}", [128, 512], BF16, ph) for i in range(3)])
            pT = Ring([sb(f"cpT{i}", [128, 512], BF16, ph) for i in range(4)])
            rden = Ring([sb(f"crden{i}", [128, 512], F32, ph) for i in range(2)])
            ot = Ring([sb(f"cot{i}", [128, 512], BF16, ph) for i in range(2)])
            xscale = 128.0 ** -0.5
            cjobs = []
            for T in range(NTT):
                for h in range(4):
                    def cscore(st, T=T, h=h):
                        ts = slice(T * 512, (T + 1) * 512)
                        q, R_q = qx.next()
                        fw.dma("sp", q[:], PT[P_QX + h * 128:P_QX + (h + 1) * 128, ts], reads=[R_PT[P_QX // 128 + h]], writes=[R_q])
                        sg, R_sg = sgx.next()
                        fw.dma("sp", sg[:], SG[1024 + h * 128:1024 + (h + 1) * 128, ts], reads=[R_SG[8 + h]], writes=[R_sg])
                        pts = []
                        for kc in range(2):
                            pa, R_pa = next_ps(0, 4)
                            fw.op("pe", lambda e: e.matmul(pa[:, :], lhsT=kT[:, h, kc * 128:(kc + 1) * 128], rhs=q[:], start=True, stop=True),
                                  reads=[R_kv, R_q], writes=[R_pa])
                            p, R_p = pT.next()
                            fw.op("act", lambda e: e.activation(out=p[:], in_=pa[:, :], func=AF.Exp, scale=xscale), reads=[R_pa], writes=[R_p])
                            pts.append((p, R_p))
                        st["pts"] = pts
                        st["sg"] = (sg, R_sg)

                    def cpv(st, T=T, h=h):
                        ts = slice(T * 512, (T + 1) * 512)
                        pts = st["pts"]
                        sg, R_sg = st["sg"]
                        po, R_po = next_ps(4, 6)
                        pd, R_pd = next_ps(6, 8)
                        for kc in range(2):
                            p, R_p = pts[kc]
                            fw.op("pe", lambda e: e.matmul(po[:, :], lhsT=vtok[:, kc, h * 128:(h + 1) * 128], rhs=p[:], start=(kc == 0), stop=(kc == 1)),
                                  reads=[R_kv, R_p], writes=[R_po])
                        for kc in range(2):
                            p, R_p = pts[kc]
                            fw.op("pe", lambda e: e.matmul(pd[:, :], lhsT=ones_b[:], rhs=p[:], start=(kc == 0), stop=(kc == 1)),
                                  reads=[R_const, R_p], writes=[R_pd])
                        rd, R_rd = rden.next()
                        fw.op("act", lambda e: e.activation(out=rd[:], in_=pd[:, :], func=AF.Ln), reads=[R_pd], writes=[R_rd])
                        fw.op("act", lambda e: e.activation(out=rd[:], in_=rd[:], func=AF.Exp, scale=-1.0), reads=[R_rd], writes=[R_rd])
                        fw.op("dve", lambda e: e.tensor_tensor(out=rd[:], in0=po[:, :], in1=rd[:], op=ALU.mult), reads=[R_po, R_rd], writes=[R_rd])
                        o, R_o = ot.next()
                        fw.op("dve", lambda e: e.tensor_tensor(out=o[:], in0=rd[:], in1=sg[:], op=ALU.mult), reads=[R_rd, R_sg], writes=[R_o])
                        fw.dma("pool", MT[1024 + h * 128:1024 + (h + 1) * 128, ts], o[:], reads=[R_o], writes=[R_MT[8 + h]])
                    stt = {}
                    cjobs.append((lambda f=cscore, st=stt: f(st), lambda f=cpv, st=stt: f(st)))

            cst = {"i": 0}
            n = len(cjobs)

            def cstep(k):
                for _ in range(k):
                    i = cst["i"]
                    if i > n:
                        return
                    if i < n:
                        cjobs[i][0]()
                    if i - 1 >= 0:
                        cjobs[i - 1][1]()
                    cst["i"] = i + 1
            return cstep

        if "C" in STAGES and "A" not in STAGES:
            with ExitStack() as ph:
                cstep = build_phase_c(ph)
                cstep(40)
                fw.barrier()

        if "B" in STAGES:
            with ExitStack() as ph:
                R_c = Res()
                cw = sb("b_cw", [128, 4, 8], F32, ph)
                cbias = sb("b_cb", [128, 8], F32, ph)
                for k in range(4):
                    fw.dma("sp", cw[:, k, :], conv_w[k].rearrange("(c p) -> p c", p=128), writes=[R_c], allow_slow_non_contiguous=True)
                fw.dma("sp", cbias[:], conv_b.rearrange("(c p) -> p c", p=128), writes=[R_c], allow_slow_non_contiguous=True)
                dtb = sb("b_dtb", [128, 8], F32, ph)
                alog = sb("b_alog", [128, 8], F32, ph)
                dskb = sb("b_dskb", [128, 8], F32, ph)
                hsel = sb("b_hsel", [128, 4, 8], F32, ph)
                gn = sb("b_gn", [128, 4], F32, ph)
                triu = sb("b_triu", [128, 128], F32, ph)
                trineg = sb("b_trineg", [128, 128], F32, ph)
                fw.dma("sp", dtb[:], dt_bias.partition_broadcast(128), writes=[R_c])
                fw.dma("sp", alog[:], a_log.partition_broadcast(128), writes=[R_c])
                fw.dma("sp", dskb[:], d_skip.partition_broadcast(128), writes=[R_c])
                fw.dma("sp", hsel[:], c_hsel.rearrange("p (a b) -> p a b", a=4), writes=[R_c])
                fw.dma("sp", gn[:], g_ssd_norm.rearrange("(c p) -> p c", p=128), writes=[R_c], allow_slow_non_contiguous=True)
                fw.dma("sp", triu[:], c_triu[:, :], writes=[R_c])
                fw.dma("sp", trineg[:], c_trineg[:, :], writes=[R_c])
                trineg_b = sb("b_trineg_b", [128, 128], BF16, ph)
                fw.op("dve", lambda e: e.tensor_copy(out=trineg_b[:], in_=trineg[:]), reads=[R_c], writes=[R_c])
                dsk = sb("b_dsk", [128, 4], F32, ph)
                hs2 = sb("b_hs2", [128, 4, 8], F32, ph)
                fw.op("dve", lambda e: e.tensor_tensor(out=hs2[:], in0=hsel[:], in1=dskb[:].unsqueeze(1).to_broadcast([128, 4, 8]), op=ALU.mult),
                      reads=[R_c], writes=[R_c])
                fw.op("dve", lambda e: e.tensor_reduce(out=dsk[:], in_=hs2[:], axis=AX.X, op=ALU.add), reads=[R_c], writes=[R_c])
                xact = sb("b_xact", [128, 8, S], BF16, ph)
                R_xact = [Res() for _ in range(8)]
                with ExitStack() as ph2:
                    xpad = Ring([sb(f"b_xpad{i}", [128, S + 4], BF16, ph2) for i in range(2)])
                    dg = sb("b_dg", [128, 32, 128], BF16, ph2)
                    R_dg = Res()
                    for c in range(8):
                        for k in range(4):
                            fw.op("dve", lambda e: e.tensor_scalar(out=dg[:, c * 4 + k, :], in0=ident_f[:], scalar1=cw[:, k, c:c + 1], scalar2=None, op0=ALU.mult),
                                  reads=[R_c, R_const], writes=[R_dg])
                    for i in range(2):
                        xp, R_xp = xpad.bufs[i]
                        fw.op("dve", lambda e: e.memset(xp[:, 0:4], 0.0), writes=[R_xp])
                    for c in range(8):
                        xp, R_xp = xpad.next()
                        fw.dma("sp", xp[:, 3:S + 3], PT[P_XBC + c * 128:P_XBC + (c + 1) * 128, :], reads=[R_PT[P_XBC // 128 + c]], writes=[R_xp])
                        for T in range(NTT):
                            pb, R_pb = next_ps()
                            for k in range(4):
                                fw.op("pe", lambda e: e.matmul(pb[:, :], lhsT=dg[:, c * 4 + k, :], rhs=xp[:, T * 512 + k:T * 512 + k + 512],
                                                               start=(k == 0), stop=(k == 3)), reads=[R_dg, R_xp], writes=[R_pb])
                            fw.op("act", lambda e: e.activation(out=xact[:, c, T * 512:(T + 1) * 512], in_=pb[:, :], func=AF.Silu, bias=cbias[:, c:c + 1]),
                                  reads=[R_pb, R_c], writes=[R_xact[c]])
                    fw.barrier()
                NCH = NT
                dtv = sb("b_dt", [128, NCH, 8], F32, ph)
                dtA = sb("b_dtA", [128, NCH, 8], F32, ph)
                acs = sb("b_acs", [128, NCH, 8], F32, ph)
                nacs = sb("b_nacs", [128, NCH, 8], F32, ph)
                tot = sb("b_tot", [128, NCH, 8], F32, ph)
                cdb = sb("b_cdb", [128, NCH, 8], F32, ph)
                w2 = sb("b_w2", [128, NCH, 8], F32, ph)
                aexp = sb("b_aexp", [128, 8], F32, ph)
                R_q = Res()
                fw.op("dve", lambda e: e.tensor_tensor(out=dtv[:], in0=dt_all[:], in1=dtb[:].unsqueeze(1).to_broadcast([128, NCH, 8]), op=ALU.add),
                      reads=[R_dt, R_c], writes=[R_q])
                fw.op("act", lambda e: e.activation(out=dtv[:], in_=dtv[:], func=AF.Exp), reads=[R_q], writes=[R_q])
                fw.op("dve", lambda e: e.tensor_scalar(out=dtv[:], in0=dtv[:], scalar1=1.0, scalar2=None, op0=ALU.add), reads=[R_q], writes=[R_q])
                fw.op("act", lambda e: e.activation(out=dtv[:], in_=dtv[:], func=AF.Ln), reads=[R_q], writes=[R_q])
                fw.op("act", lambda e: e.activation(out=aexp[:], in_=alog[:], func=AF.Exp), reads=[R_c], writes=[R_q])
                fw.op("dve", lambda e: e.scalar_tensor_tensor(out=dtA[:], in0=dtv[:], scalar=-1.0, in1=aexp[:].unsqueeze(1).to_broadcast([128, NCH, 8]),
                                                              op0=ALU.mult, op1=ALU.mult), reads=[R_q], writes=[R_q])
                dtA_hi = sb("b_dtA_hi", [128, NCH, 8], BF16, ph)
                dtA_lo = sb("b_dtA_lo", [128, NCH, 8], BF16, ph)
                dtA_hf = sb("b_dtA_hf", [128, NCH, 8], F32, ph)
                triu_b = sb("b_triu_b", [128, 128], BF16, ph)
                fw.op("dve", lambda e: e.tensor_copy(out=triu_b[:], in_=triu[:]), reads=[R_c], writes=[R_c])
                fw.op("dve", lambda e: e.tensor_copy(out=dtA_hi[:], in_=dtA[:]), reads=[R_q], writes=[R_q])
                fw.op("dve", lambda e: e.tensor_copy(out=dtA_hf[:], in_=dtA_hi[:]), reads=[R_q], writes=[R_q])
                fw.op("dve", lambda e: e.tensor_tensor(out=dtA_lo[:], in0=dtA[:], in1=dtA_hf[:], op=ALU.subtract), reads=[R_q], writes=[R_q])
                fw.op("dve", lambda e: e.tensor_copy(out=dtA_hf[:], in_=dtA_lo[:]), reads=[R_q], writes=[R_q])
                fw.op("dve", lambda e: e.tensor_tensor(out=dtA[:], in0=dtA_hf[:], in1=dtA_hi[:], op=ALU.add), reads=[R_q], writes=[R_q])
                dtA2 = dtA[:].rearrange("p c h -> p (c h)")
                pb, R_pb = next_ps()
                fw.op("pe", lambda e: e.matmul(pb[:, 0:256], lhsT=triu[:], rhs=dtA2, start=True, stop=True), reads=[R_q, R_c], writes=[R_pb])
                fw.op("dve", lambda e: e.tensor_copy(out=acs[:].rearrange("p c h -> p (c h)"), in_=pb[:, 0:256]), reads=[R_pb], writes=[R_q])
                pb, R_pb = next_ps()
                fw.op("pe", lambda e: e.matmul(pb[:, 0:256], lhsT=ones_f[:], rhs=dtA2, start=True, stop=True), reads=[R_q, R_const], writes=[R_pb])
                fw.op("dve", lambda e: e.tensor_copy(out=tot[:].rearrange("p c h -> p (c h)"), in_=pb[:, 0:256]), reads=[R_pb], writes=[R_q])
                fw.op("dve", lambda e: e.tensor_scalar(out=nacs[:], in0=acs[:], scalar1=-1.0, scalar2=None, op0=ALU.mult), reads=[R_q], writes=[R_q])
                fw.op("act", lambda e: e.activation(out=cdb[:], in_=tot[:], func=AF.Exp), reads=[R_q], writes=[R_q])
                fw.op("dve", lambda e: e.tensor_tensor(out=w2[:], in0=tot[:], in1=acs[:], op=ALU.subtract), reads=[R_q], writes=[R_q])
                fw.op("act", lambda e: e.activation(out=w2[:], in_=w2[:], func=AF.Exp), reads=[R_q], writes=[R_q])
                fw.op("dve", lambda e: e.tensor_tensor(out=w2[:], in0=w2[:], in1=dtv[:], op=ALU.mult), reads=[R_q], writes=[R_q])
                state = sb("b_state", [128, 512], F32, ph)
                state_bf = sb("b_state_bf", [128, 512], BF16, ph)
                R_state = Res()
                R_sbf = Res()
                fw.op("dve", lambda e: e.memset(state[:], 0.0), writes=[R_state])
                fw.op("dve", lambda e: e.memset(state_bf[:], 0.0), writes=[R_sbf])
                xbtok = Ring([sb(f"b_xbtok{i}", [128, 768], BF16, ph) for i in range(2)])
                xdt = Ring([sb(f"b_xdt{i}", [128, 512], BF16, ph) for i in range(2)])
                xdtd = Ring([sb(f"b_xdtd{i}", [128, 512], BF16, ph) for i in range(2)])
                eacs = Ring([sb(f"b_eacs{i}", [128, 8, 128], BF16, ph) for i in range(2)])
                decT = Ring([sb(f"b_decT{i}", [128, 8, 128], BF16, ph) for i in range(2)])
                Mh = Ring([sb(f"b_Mh{i}", [128, 8, 128], BF16, ph) for i in range(2)])
                cms = Ring([sb(f"b_cms{i}", [128, 8, 128], BF16, ph) for i in range(2)])
                yacc = Ring([sb(f"b_yacc{i}", [128, 4, 512], F32, ph) for i in range(2)])
                sgz = Ring([sb(f"b_sgz{i}", [128, 4, 512], BF16, ph) for i in range(2)])
                sq = sb("b_sq", [128, 4, 512], F32, ph)
                R_sq = Res()
                rstd = sb("b_rstd", [128, 512], F32, ph)
                R_rstd = Res()
                obt = Ring([sb(f"b_obt{i}", [128, 4, 512], BF16, ph) for i in range(2)])
                SG_v = SG.rearrange("(ck p) t -> p ck t", p=128)
                MT_vb = MT.rearrange("(ck p) t -> p ck t", p=128)
                prepd = {}
                ystate = {}

                def prep(c):
                    tk = slice(c * 128, (c + 1) * 128)
                    pt_, R_pt = ps[7], R_ps[7]
                    ptb = pt_[:].bitcast(BF16)
                    for j in range(6):
                        fw.op("pe", lambda e: e.transpose(ptb[:, j * 128:(j + 1) * 128], xact[:, j, tk], ident_b[:]),
                              reads=[R_xact[j], R_const], writes=[R_pt])
                    xb, R_xb = xbtok.next()
                    fw.op("act", lambda e: e.activation(out=xb[:], in_=ptb[:, 0:768], func=AF.Copy), reads=[R_pt], writes=[R_xb])
                    xd, R_xd = xdt.next()
                    xdd, R_xdd = xdtd.next()
                    fw.op("dve", lambda e: e.tensor_tensor(out=xd[:].rearrange("p (h q) -> p h q", h=8), in0=xb[:, 0:512].rearrange("p (h q) -> p h q", h=8),
                                                           in1=dtv[:, c, :].unsqueeze(2).to_broadcast([128, 8, 64]), op=ALU.mult),
                          reads=[R_xb, R_q], writes=[R_xd])
                    fw.op("dve", lambda e: e.tensor_tensor(out=xdd[:].rearrange("p (h q) -> p h q", h=8), in0=xb[:, 0:512].rearrange("p (h q) -> p h q", h=8),
                                                           in1=w2[:, c, :].unsqueeze(2).to_broadcast([128, 8, 64]), op=ALU.mult),
                          reads=[R_xb, R_q], writes=[R_xdd])
                    for g in range(2):
                        fw.op("pe", lambda e: e.matmul(ps[4][:, g * 128:(g + 1) * 128], lhsT=xact[:, 4 + g, tk], rhs=xact[:, 6 + g, tk], start=True, stop=True),
                              reads=[R_xact[4 + g], R_xact[6 + g]], writes=[R_ps[4]])
                    for h in range(8):
                        bk = h // 4
                        hs = slice((h % 4) * 128, (h % 4 + 1) * 128)
                        lbh = dtA_hi[:, c, h:h + 1].to_broadcast([128, 128])
                        lbl = dtA_lo[:, c, h:h + 1].to_broadcast([128, 128])
                        fw.op("pe", lambda e: e.matmul(ps[bk][:, hs], lhsT=lbh, rhs=triu_b[:], start=True, stop=False),
                              reads=[R_q, R_c], writes=[R_ps[bk]])
                        fw.op("pe", lambda e: e.matmul(ps[bk][:, hs], lhsT=lbl, rhs=triu_b[:], start=False, stop=True),
                              reads=[R_q, R_c], writes=[R_ps[bk]])
                        fw.op("pe", lambda e: e.matmul(ps[2 + bk][:, hs], lhsT=lbh, rhs=triu_b[:], start=True, stop=False),
                              reads=[R_q, R_c], writes=[R_ps[2 + bk]])
                        fw.op("pe", lambda e: e.matmul(ps[2 + bk][:, hs], lhsT=lbl, rhs=triu_b[:], start=False, stop=False),
                              reads=[R_q, R_c], writes=[R_ps[2 + bk]])
                        fw.op("pe", lambda e: e.matmul(ps[2 + bk][:, hs], lhsT=ident_b[:], rhs=trineg_b[:], start=False, stop=True),
                              reads=[R_const, R_c], writes=[R_ps[2 + bk]])
                    ea, R_ea = eacs.next()
                    for bk in range(2):
                        fw.op("act", lambda e: e.activation(out=ea[:, bk * 4:(bk + 1) * 4, :].rearrange("p a b -> p (a b)"), in_=ps[bk][:, :], func=AF.Exp),
                              reads=[R_ps[bk]], writes=[R_ea])
                    dc, R_dc = decT.next()
                    for h in range(8):
                        bk = h // 4
                        hs = slice((h % 4) * 128, (h % 4 + 1) * 128)
                        fw.op("act", lambda e: e.activation(out=dc[:, h, :], in_=ps[2 + bk][:, hs], func=AF.Exp, bias=nacs[:, c, h:h + 1]),
                              reads=[R_ps[2 + bk], R_q], writes=[R_dc])
                    mh, R_mh = Mh.next()
                    cm_, R_cm = cms.next()
                    for g in range(2):
                        fw.op("dve", lambda e: e.tensor_tensor(out=mh[:, g * 4:(g + 1) * 4, :], in0=dc[:, g * 4:(g + 1) * 4, :],
                                                               in1=ps[4][:, g * 128:(g + 1) * 128].unsqueeze(1).to_broadcast([128, 4, 128]), op=ALU.mult),
                              reads=[R_dc, R_ps[4]], writes=[R_mh])
                        fw.op("pool", lambda e: e.tensor_tensor(out=cm_[:, g * 4:(g + 1) * 4, :], in0=ea[:, g * 4:(g + 1) * 4, :],
                                                                in1=xact[:, 6 + g, tk].unsqueeze(1).to_broadcast([128, 4, 128]), op=ALU.mult),
                              reads=[R_ea, R_xact[6 + g]], writes=[R_cm])
                    prepd[c] = (xb, R_xb, xd, R_xd, xdd, R_xdd, mh, R_mh, cm_, R_cm)

                def fin(c):
                    T = c // 4
                    tk = slice(c * 128, (c + 1) * 128)
                    xb, R_xb, xd, R_xd, xdd, R_xdd, mh, R_mh, cm_, R_cm = prepd.pop(c)
                    if c % 4 == 0:
                        ystate["ya"] = yacc.next()
                        ystate["sg"] = sgz.next()
                        sgt, R_sgt = ystate["sg"]
                        fw.dma("sp", sgt[:], SG_v[:, 4:8, T * 512:(T + 1) * 512], reads=R_SG[4:8], writes=[R_sgt])
                    ya, R_ya = ystate["ya"]
                    sgt, R_sgt = ystate["sg"]
                    for h in range(8):
                        yo = ps[6][(h % 2) * 64:(h % 2 + 1) * 64, (h // 2) * 128:(h // 2 + 1) * 128]
                        fw.op("pe", lambda e: e.matmul(yo, lhsT=xd[:, h * 64:(h + 1) * 64], rhs=mh[:, h, :], start=True, stop=False),
                              reads=[R_xd, R_mh], writes=[R_ps[6]])
                        fw.op("pe", lambda e: e.matmul(yo, lhsT=state_bf[:, h * 64:(h + 1) * 64], rhs=cm_[:, h, :], start=False, stop=True),
                              reads=[R_sbf, R_cm], writes=[R_ps[6]])
                    for g in range(2):
                        fw.op("pe", lambda e: e.matmul(ps[5][:, g * 256:(g + 1) * 256], lhsT=xb[:, 512 + g * 128:512 + (g + 1) * 128],
                                                       rhs=xdd[:, g * 256:(g + 1) * 256], start=True, stop=True),
                              reads=[R_xb, R_xdd], writes=[R_ps[5]])
                    fw.op("dve", lambda e: e.tensor_tensor(out=state[:].rearrange("p (h q) -> p h q", h=8), in0=state[:].rearrange("p (h q) -> p h q", h=8),
                                                           in1=cdb[:, c, :].unsqueeze(2).to_broadcast([128, 8, 64]), op=ALU.mult),
                          reads=[R_state, R_q], writes=[R_state])
                    fw.op("dve", lambda e: e.tensor_tensor(out=state[:], in0=state[:], in1=ps[5][:, :], op=ALU.add),
                          reads=[R_state, R_ps[5]], writes=[R_state])
                    fw.op("act", lambda e: e.activation(out=state_bf[:], in_=state[:], func=AF.Copy), reads=[R_state], writes=[R_sbf])
                    for pr in range(4):
                        fw.op("dve", lambda e: e.scalar_tensor_tensor(out=ya[:, pr, (c % 4) * 128:(c % 4 + 1) * 128], in0=xact[:, pr, tk],
                                                                      scalar=dsk[:, pr:pr + 1], in1=ps[6][:, pr * 128:(pr + 1) * 128],
                                                                      op0=ALU.mult, op1=ALU.add),
                              reads=[R_xact[pr], R_c, R_ps[6]], writes=[R_ya])
                    if c % 4 == 3:
                        fw.op("dve", lambda e: e.tensor_tensor(out=ya[:], in0=ya[:], in1=sgt[:], op=ALU.mult), reads=[R_ya, R_sgt], writes=[R_ya])
                        fw.op("pool", lambda e: e.tensor_tensor(out=sq[:], in0=ya[:], in1=ya[:], op=ALU.mult), reads=[R_ya], writes=[R_sq])
                        for pr in range(4):
                            fw.op("pe", lambda e: e.matmul(ps[7][:, :], lhsT=ones_f[:], rhs=sq[:, pr, :], start=(pr == 0), stop=(pr == 3)),
                                  reads=[R_const, R_sq], writes=[R_ps[7]])
                        fw.op("dve", lambda e: e.tensor_scalar(out=rstd[:], in0=ps[7][:, :], scalar1=1.0 / 512, scalar2=EPS, op0=ALU.mult, op1=ALU.add),
                              reads=[R_ps[7]], writes=[R_rstd])
                        fw.op("act", lambda e: e.activation(out=rstd[:], in_=rstd[:], func=AF.Ln), reads=[R_rstd], writes=[R_rstd])
                        fw.op("act", lambda e: e.activation(out=rstd[:], in_=rstd[:], func=AF.Exp, scale=-0.5), reads=[R_rstd], writes=[R_rstd])
                        ob_, R_ob = obt.next()
                        for pr in range(4):
                            fw.op("dve", lambda e: e.scalar_tensor_tensor(out=ob_[:, pr, :], in0=ya[:, pr, :], scalar=gn[:, pr:pr + 1], in1=rstd[:],
                                                                          op0=ALU.mult, op1=ALU.mult),
                                  reads=[R_ya, R_c, R_rstd], writes=[R_ob])
                        fw.dma("sp", MT_vb[:, 4:8, T * 512:(T + 1) * 512], ob_[:], reads=[R_ob], writes=R_MT[4:8])

                prep(0)
                for c in range(NCH):
                    if c + 1 < NCH:
                        prep(c + 1)
                    fin(c)
                fw.barrier()


        if "A" in STAGES:
            with ExitStack() as ph:
                nscale = 64.0 ** -0.5
                R_ac = Res()
                kcT = sb("a_kcT", [64, 2, 256], BF16, ph)
                vcx = sb("a_vcx", [128, 2, 2, 128], BF16, ph)
                ovx = sb("a_ovx", [128, 2, 65], BF16, ph)
                W3 = sb("a_W3", [128, 3072], BF16, ph)
                Wc = sb("a_Wc", [128, 896], BF16, ph)
                Ww = sb("a_Ww", [128, 1536], BF16, ph)
                exg = sb("a_exg", [24, 1536], BF16, ph)
                GLs = sb("a_GLs", [24, S], BF16, ph)
                ksEx = sb("a_ksEx", [128, S], BF16, ph)
                R_ksEx = Res()
                fw.op("dve", lambda e: e.memset(kcT[:], 0.0), writes=[R_ac])
                fw.op("dve", lambda e: e.memset(vcx[:], 1.0), writes=[R_ac])
                fw.dma("sp", GLs[:], GL[:, :], reads=[R_GL], writes=[R_ac])
                phL = ExitStack()
                kv_ring = Ring([sb(f"a_kvsb{i}", [128, S], BF16, phL) for i in range(2)])
                w1b_ring = Ring([sb(f"a_w1b{i}", [128, 32, 256], BF16, phL) for i in range(2)])
                pre_cmp = []
                w1st = Ring([sb(f"a_w1st{i}", [64, 8, 256], F32, phL) for i in range(2)])
                for prow, w1 in ((P_KC, w_cmp1_k), (P_VC, w_cmp1_v)):
                    kv_sb, R_kvsb = kv_ring.next()
                    fw.dma("sp", kv_sb[:], PT[prow:prow + 128, :], reads=[R_PT[prow // 128]], writes=[R_kvsb])
                    w1v = w1.rearrange("(l d) h -> d l h", d=64)
                    w1b, R_w1b = w1b_ring.next()
                    for lq in range(4):
                        ws, R_ws = w1st.next()
                        fw.dma("sp", ws[:], w1v[:, lq * 8:(lq + 1) * 8, :], writes=[R_ws])
                        if lq % 2 == 0:
                            fw.op("dve", lambda e: e.tensor_copy(out=w1b[0:64, lq * 8:(lq + 1) * 8, :], in_=ws[:]), reads=[R_ws], writes=[R_w1b])
                        else:
                            fw.op("act", lambda e: e.activation(out=w1b[0:64, lq * 8:(lq + 1) * 8, :], in_=ws[:], func=AF.Copy), reads=[R_ws], writes=[R_w1b])
                    fw.dma("sp", w1b[64:128, :, :], w1b[0:64, :, :], reads=[R_w1b], writes=[R_w1b])
                    pre_cmp.append((kv_sb, R_kvsb, w1b, R_w1b))
                phC = ExitStack()
                cstep = build_phase_c(phC) if "C" in STAGES else (lambda k: None)
                with ExitStack() as ph2:
                    for dst, src in ((W3, c_w3), (Wc, c_wc), (Ww, c_ww)):
                        fw.dma("sp", dst[:], src[:, :], writes=[R_ac])
                    fw.dma("sp", ovx[:].rearrange("p a b -> p (a b)"), c_ovx[:, :], writes=[R_ac])
                    fw.dma("sp", exg[:], c_exg[:, :], writes=[R_ac])
                    fw.dma("sp", ksEx[64:128, :], c_ex[:, :], writes=[R_ksEx])
                    Cts = sb("a_Cts", [64, 256], F32, ph2)
                    Sts = sb("a_Sts", [64, 256], F32, ph2)
                    fw.dma("sp", Cts[:, 0:255], CS[0, 0:64, 0:255], reads=[R_CS], writes=[R_ac])
                    fw.dma("sp", Sts[:, 0:255], CS[1, 0:64, 0:255], reads=[R_CS], writes=[R_ac])
                    w2st = sb("a_w2st", [128, 2, 64], F32, ph2)
                    w2b = sb("a_w2b", [128, 2, 64], BF16, ph2)
                    posst = sb("a_posst", [32, 128], F32, ph2)
                    posb = sb("a_posb", [128, 32], BF16, ph2)
                    hT = sb("a_hT", [128, 2, 256], BF16, ph2)
                    hbias = sb("a_hbias", [128, 2], F32, ph2)
                    q32 = sb("a_q32", [64, 256], F32, ph2)
                    tq = sb("a_tq", [64, 256], F32, ph2)
                    R_m = Res()
                    fw.op("dve", lambda e: e.memset(hT[:], 0.0), writes=[R_m])
                    for which, (prow, w1, w2, pos) in enumerate(((P_KC, w_cmp1_k, w_cmp2_k, cmp_pos_k), (P_VC, w_cmp1_v, w_cmp2_v, cmp_pos_v))):
                        kv_sb, R_kvsb, w1b, R_w1b = pre_cmp[which]
                        fw.dma("sp", w2st[:], w2.rearrange("(c p) d -> p c d", p=128), reads=[R_m], writes=[R_m])
                        fw.op("dve", lambda e: e.tensor_copy(out=w2b[:], in_=w2st[:]), reads=[R_m], writes=[R_m])
                        for half in range(2):
                            fw.dma("sp", posst[0:32, half * 64:(half + 1) * 64], pos[:, :], reads=[R_m], writes=[R_m])
                        pbt, R_pbt = next_ps(0, 3)
                        fw.op("pe", lambda e: e.transpose(pbt[:, 0:32], posst[0:32, :], ident_f[0:32, 0:32]), reads=[R_m, R_const], writes=[R_pbt])
                        fw.op("dve", lambda e: e.tensor_copy(out=posb[:], in_=pbt[:, 0:32]), reads=[R_pbt], writes=[R_m])
                        for g in range(2):
                            gs = slice(g * 64, (g + 1) * 64)
                            for hc in range(2):
                                pb, R_pb = next_ps(0, 3)
                                pbb_, R_pbb = ps[7], R_ps[7]
                                for l in range(32):
                                    fw.op("pe", lambda e: e.matmul(pb[:, 0:255], lhsT=w1b[gs, l, hc * 128:(hc + 1) * 128],
                                                                   rhs=kv_sb[gs, l:l + 16 * 254 + 1:16], start=(l == 0), stop=(l == 31)),
                                          reads=[R_w1b, R_kvsb], writes=[R_pb])
                                for l in range(32):
                                    fw.op("pe", lambda e: e.matmul(pbb_[:, 0:1], lhsT=w1b[gs, l, hc * 128:(hc + 1) * 128],
                                                                   rhs=posb[gs, l:l + 1], start=(l == 0), stop=(l == 31)),
                                          reads=[R_w1b, R_m], writes=[R_pbb])
                                fw.op("dve", lambda e: e.tensor_copy(out=hbias[:, hc:hc + 1], in_=pbb_[:, 0:1]), reads=[R_pbb], writes=[R_m])
                                fw.op("act", lambda e: e.activation(out=hT[:, hc, 0:255], in_=pb[:, 0:255], func=AF.Silu, bias=hbias[:, hc:hc + 1]),
                                      reads=[R_pb, R_m], writes=[R_m])
                                cstep(4)
                            if which == 0:
                                pb, R_pb = next_ps(0, 3)
                                for hc in range(2):
                                    fw.op("pe", lambda e: e.matmul(pb[0:64, 0:256], lhsT=w2b[:, hc, :], rhs=hT[:, hc, :], start=(hc == 0), stop=(hc == 1)),
                                          reads=[R_m], writes=[R_pb])
                                fw.op("dve", lambda e: e.tensor_copy(out=q32[:], in_=pb[0:64, 0:256]), reads=[R_pb], writes=[R_m])
                                pb2, R_pb2 = next_ps(0, 3)
                                fw.op("pe", lambda e: e.matmul(pb2[0:64, 0:256], lhsT=psw_f[0:64, 0:64], rhs=q32[:], start=True, stop=True),
                                      reads=[R_m, R_const], writes=[R_pb2])
                                fw.op("dve", lambda e: e.tensor_tensor(out=tq[:, 0:255], in0=pb2[0:64, 0:255], in1=Sts[:, 0:255], op=ALU.mult),
                                      reads=[R_pb2, R_ac], writes=[R_m])
                                fw.op("dve", lambda e: e.tensor_tensor(out=q32[:, 0:255], in0=q32[:, 0:255], in1=Cts[:, 0:255], op=ALU.mult),
                                      reads=[R_m, R_ac], writes=[R_m])
                                fw.op("dve", lambda e: e.tensor_tensor(out=kcT[:, g, 0:255], in0=q32[:, 0:255], in1=tq[:, 0:255], op=ALU.add),
                                      reads=[R_m], writes=[R_ac])
                            else:
                                for nch in range(2):
                                    pb, R_pb = next_ps(0, 3)
                                    for hc in range(2):
                                        fw.op("pe", lambda e: e.matmul(pb[:, 0:64], lhsT=hT[:, hc, nch * 128:(nch + 1) * 128], rhs=w2b[:, hc, :],
                                                                       start=(hc == 0), stop=(hc == 1)), reads=[R_m], writes=[R_pb])
                                    fw.op("dve", lambda e: e.tensor_copy(out=vcx[:, g, nch, 0:64], in_=pb[:, 0:64]), reads=[R_pb], writes=[R_ac])
                    cstep(40)
                    fw.barrier()
                phC.close()
                phL.close()
                if DEBUG:
                    DBGK = nc.dram_tensor("DBGK", [64, 512], BF16, kind="ExternalOutput").ap()
                    DBGV = nc.dram_tensor("DBGV", [128, 512], BF16, kind="ExternalOutput").ap()
                    DBGS = nc.dram_tensor("DBGS", [128, 2 * S], BF16, kind="ExternalOutput").ap()
                    fw.dma("sp", DBGK[:, :], kcT[:].rearrange("p a b -> p (a b)"), reads=[R_ac], writes=[Res()])
                    fw.dma("sp", DBGV[:, :], vcx[:].rearrange("p a b c -> p (a b c)"), reads=[R_ac], writes=[Res()])
                qS = [sb(f"a_qS{e}", [128, S], BF16, ph) for e in range(4)]
                R_qS = [[Res() for _ in range(NTT)] for _ in range(4)]
                kwT = sb("a_kwT", [64, S], BF16, ph)
                R_kw = Res()
                vsx = sb("a_vsx", [128, NT, 128], BF16, ph)
                vwx = sb("a_vwx", [128, NT, 128], BF16, ph)
                R_v = Res()
                fw.op("dve", lambda e: e.memset(vsx[:], 1.0), writes=[R_v])
                fw.op("dve", lambda e: e.memset(vwx[:], 1.0), writes=[R_v])
                vst = sb("a_vst", [128, NT, 64], F32, ph)
                R_vst = Res()
                pr_ = Ring([sb(f"a_p{i}", [128, 512], BF16, ph) for i in range(6)])
                accs = [[sb(f"a_acc{par}_{e}", [64, 512], F32, ph) for e in range(4)] for par in range(2)]
                R_acc = [[Res() for _ in range(4)] for _ in range(2)]
                rdn = Ring([sb(f"a_rdn{i}", [64, 512], F32, ph) for i in range(3)])
                tmpo = Ring([sb(f"a_tmpo{i}", [64, 512], F32, ph) for i in range(2)])
                sga = Ring([sb(f"a_sga{i}", [64, 512], BF16, ph) for i in range(3)])
                oo = Ring([sb(f"a_oo{i}", [64, 512], BF16, ph) for i in range(2)])
                impacc = sb("a_impacc", [128, 4, 64], F32, ph)
                imptmp = sb("a_imptmp", [128, 4, 64], F32, ph)
                irec = sb("a_irec", [128, 4, 1], F32, ph)
                addm = sb("a_addm", [128, 4, 64], F32, ph)
                m8 = sb("a_m8", [128, 16], F32, ph)
                wk = sb("a_wk", [128, 64], F32, ph)
                nsel = sb("a_nsel", [128, 4, 64], BF16, ph)
                R_imp = Res()
                R_nsel = Res()
                R_addm = Res()
                nselT = sb("a_nselT", [64, 512], BF16, ph)
                R_nselT = Res()
                VT_v = VT.rearrange("(t p) f -> p t f", p=128)
                addm_v = c_addm.rearrange("(t p) j -> p t j", p=128)
                psI, R_psI = ps[6], R_ps[6]
                psG, R_psG = ps[7], R_ps[7]
                psTb = psI[:].bitcast(BF16)
                obank = {"i": 0}
                sbank = {"i": 0}

                def next_o():
                    i = 3 + obank["i"] % 3
                    obank["i"] += 1
                    return ps[i], R_ps[i]

                def next_s():
                    i = sbank["i"] % 3
                    sbank["i"] += 1
                    return ps[i], R_ps[i]

                def finish_branch(po, R_po, acc, R_a, h, br, first, ts):
                    rd, R_rd = rdn.next()
                    if br in (0, 1):
                        fw.op("act", lambda e: e.activation(out=rd[:], in_=po[64:128, :], func=AF.Ln, bias=epsb[0:64, 0:1]),
                              reads=[R_po, R_const], writes=[R_rd])
                        fw.op("act", lambda e: e.activation(out=rd[:], in_=rd[:], func=AF.Exp, scale=-1.0), reads=[R_rd], writes=[R_rd])
                    else:
                        fw.op("dve", lambda e: e.reciprocal(out=rd[:], in_=po[64:128, :]), reads=[R_po], writes=[R_rd])
                    fw.op("pe", lambda e: e.matmul(psG[0:64, :], lhsT=exg[:, (h * 3 + br) * 64:(h * 3 + br + 1) * 64], rhs=GLs[:, ts],
                                                   start=True, stop=True), reads=[R_ac], writes=[R_psG])
                    fw.op("dve", lambda e: e.tensor_tensor(out=rd[:], in0=rd[:], in1=psG[0:64, :], op=ALU.mult), reads=[R_rd, R_psG], writes=[R_rd])
                    if first:
                        fw.op("dve", lambda e: e.tensor_tensor(out=acc[:], in0=po[0:64, :], in1=rd[:], op=ALU.mult),
                              reads=[R_po, R_rd], writes=[R_a])
                    else:
                        tt_, R_tt = tmpo.next()
                        fw.op("dve", lambda e: e.tensor_tensor(out=tt_[:], in0=po[0:64, :], in1=rd[:], op=ALU.mult), reads=[R_po, R_rd], writes=[R_tt])
                        fw.op("pool", lambda e: e.tensor_tensor(out=acc[:], in0=acc[:], in1=tt_[:], op=ALU.add),
                              reads=[R_tt, R_a], writes=[R_a])

                def run_jobs(jobs, L=2):
                    n = len(jobs)
                    for i in range(n + L):
                        if i < n:
                            jobs[i][0]()
                        if i - L >= 0:
                            jobs[i - L][1]()

                def make_job(score_fn, pv_fn):
                    st = {}
                    return (lambda: score_fn(st), lambda: pv_fn(st))

                for g in range(2):
                    for e_ in range(4):
                        h = g * 4 + e_
                        fw.dma("sp", qS[e_][0:64, :], PT[P_Q + h * 64:P_Q + (h + 1) * 64, :], reads=[R_PT[(P_Q + h * 64) // 128]] + R_qS[e_], writes=R_qS[e_])
                    fw.dma("sp", ksEx[0:64, :], PT[P_KS + g * 64:P_KS + (g + 1) * 64, :], reads=[R_PT[P_KS // 128], R_ksEx], writes=[R_ksEx])
                    fw.dma("sp", kwT[:], PT[P_KW + g * 64:P_KW + (g + 1) * 64, :], reads=[R_PT[P_KW // 128], R_kw], writes=[R_kw])
                    for dst, c0 in ((vsx, 128 + g * 64), (vwx, 256 + g * 64)):
                        for q4 in range(4):
                            fw.dma("sp", vst[:, q4 * 8:(q4 + 1) * 8, :], VT_v[:, q4 * 8:(q4 + 1) * 8, c0:c0 + 64], reads=[R_VT, R_vst], writes=[R_vst])
                        fw.op("dve", lambda e: e.tensor_copy(out=dst[:, :, 0:64], in_=vst[:]), reads=[R_vst, R_v], writes=[R_v, R_vst])

                    def cmp_jobs(T):
                        ts = slice(T * 512, (T + 1) * 512)
                        nchs = [0] if T < 4 else [0, 1]
                        jobs = []
                        for e_ in range(4):
                            h = g * 4 + e_
                            hold = {}
                            for i, nch in enumerate(nchs):
                                def score(st, e_=e_, nch=nch):
                                    pa, R_pa = next_s()
                                    need_mask = not (nch == 0 and T >= 5)
                                    fw.op("pe", lambda e: e.matmul(pa[:, :], lhsT=kcT[:, g, nch * 128:(nch + 1) * 128], rhs=qS[e_][0:64, ts],
                                                                   start=True, stop=not need_mask), reads=[R_ac, R_qS[e_][T]], writes=[R_pa])
                                    if need_mask:
                                        sh = 512 * T - 2048 * nch
                                        fw.op("pe", lambda e: e.matmul(pa[:, :], lhsT=ident_b[:], rhs=W3[:, sh:sh + 512], start=False, stop=True),
                                              reads=[R_ac, R_const], writes=[R_pa])
                                    p, R_p = pr_.next()
                                    fw.op("act", lambda e: e.activation(out=p[:], in_=pa[:, :], func=AF.Exp, scale=nscale), reads=[R_pa], writes=[R_p])
                                    st["p"] = (p, R_p)

                                def pv(st, e_=e_, h=h, i=i, nch=nch, hold=hold):
                                    p, R_p = st["p"]
                                    if i == 0:
                                        hold["po"] = next_o()
                                    po, R_po = hold["po"]
                                    last = (i == len(nchs) - 1)
                                    fw.op("pe", lambda e: e.matmul(po[:, :], lhsT=vcx[:, g, nch, :], rhs=p[:], start=(i == 0), stop=last),
                                          reads=[R_ac, R_p], writes=[R_po])
                                    for sub in range(4):
                                        fw.op("pe", lambda e: e.matmul(psI[:, sub * 65:(sub + 1) * 65], lhsT=p[:, sub * 128:(sub + 1) * 128], rhs=ovx[:, nch, :],
                                                                       start=(i == 0 and sub == 0), stop=(last and sub == 3), skip_group_check=True),
                                              reads=[R_ac, R_p], writes=[R_psI])
                                    if last:
                                        finish_branch(po, R_po, accs[T % 2][e_], R_acc[T % 2][e_], h, 0, True, ts)
                                        pI = psI[:, 0:260].rearrange("p (s f) -> p s f", s=4)
                                        fw.op("dve", lambda e: e.tensor_scalar(out=irec[:], in0=pI[:, :, 64:65], scalar1=1e-30, scalar2=None, op0=ALU.max),
                                              reads=[R_psI], writes=[R_imp])
                                        fw.op("dve", lambda e: e.reciprocal(out=irec[:], in_=irec[:]), reads=[R_imp], writes=[R_imp])
                                        if e_ == 0:
                                            fw.op("dve", lambda e: e.tensor_tensor(out=impacc[:], in0=pI[:, :, 0:64], in1=irec[:].to_broadcast([128, 4, 64]), op=ALU.mult),
                                                  reads=[R_psI, R_imp], writes=[R_imp])
                                        else:
                                            fw.op("dve", lambda e: e.tensor_tensor(out=imptmp[:], in0=pI[:, :, 0:64], in1=irec[:].to_broadcast([128, 4, 64]), op=ALU.mult),
                                                  reads=[R_psI, R_imp], writes=[R_imp])
                                            fw.op("dve", lambda e: e.tensor_tensor(out=impacc[:], in0=impacc[:], in1=imptmp[:], op=ALU.add), reads=[R_imp], writes=[R_imp])
                                jobs.append(make_job(score, pv))
                        return jobs

                    def sel_dve(T):
                        fw.dma("sp", addm[:], addm_v[:, T * 4:(T + 1) * 4, :], reads=[R_addm], writes=[R_addm])
                        fw.op("dve", lambda e: e.tensor_tensor(out=impacc[:], in0=impacc[:], in1=addm[:], op=ALU.add), reads=[R_imp, R_addm], writes=[R_imp, R_addm])
                        for sub in range(4):
                            fw.op("dve", lambda e: e.max(out=m8[:, 0:8], in_=impacc[:, sub, :]), reads=[R_imp], writes=[R_imp])
                            fw.op("dve", lambda e: e.match_replace(out=wk[:], in_to_replace=m8[:, 0:8], in_values=impacc[:, sub, :], imm_value=-3.0e9),
                                  reads=[R_imp], writes=[R_imp])
                            fw.op("dve", lambda e: e.max(out=m8[:, 8:16], in_=wk[:]), reads=[R_imp], writes=[R_imp])
                            fw.op("dve", lambda e: e.tensor_scalar(out=nsel[:, sub, :], in0=impacc[:, sub, :], scalar1=m8[:, 15:16], scalar2=NEG,
                                                                   op0=ALU.is_lt, op1=ALU.mult), reads=[R_imp], writes=[R_nsel])

                    def sel_pe(T):
                        ts = slice(T * 512, (T + 1) * 512)
                        for sub in range(4):
                            fw.op("pe", lambda e: e.transpose(psTb[0:64, sub * 128:(sub + 1) * 128], nsel[:, sub, :], ident_b[:]),
                                  reads=[R_nsel, R_const], writes=[R_psI])
                        fw.op("dve", lambda e: e.tensor_copy(out=nselT[:], in_=psTb[0:64, 0:512]), reads=[R_psI], writes=[R_nselT])
                        for e_ in range(4):
                            fw.dma("sp", qS[e_][64:128, ts], nselT[:], reads=[R_nselT], writes=[R_qS[e_][T]])

                    def selwin_jobs(T):
                        ts = slice(T * 512, (T + 1) * 512)
                        jobs = []
                        for e_ in range(4):
                            h = g * 4 + e_
                            acc, R_a = accs[T % 2][e_], R_acc[T % 2][e_]
                            nk = 4 * T + 4
                            hold_s = {}
                            for k in range(nk):
                                def score(st, e_=e_, k=k):
                                    pa, R_pa = next_s()
                                    diag = k >= 4 * T
                                    i = k - 4 * T
                                    c0 = i * 128 if diag else 0
                                    tq = slice(T * 512 + c0, (T + 1) * 512)
                                    fw.op("pe", lambda e: e.matmul(pa[:, c0:512], lhsT=ksEx[:, k * 128:(k + 1) * 128], rhs=qS[e_][:, tq], start=True, stop=not diag),
                                          reads=[R_ksEx, R_qS[e_][T]], writes=[R_pa])
                                    if diag:
                                        fw.op("pe", lambda e: e.matmul(pa[:, c0:512], lhsT=ident_b[:], rhs=Wc[:, 384 - i * 128 + c0:384 - i * 128 + 512], start=False, stop=True),
                                              reads=[R_ac, R_const], writes=[R_pa])
                                    p, R_p = pr_.next()
                                    fw.op("act", lambda e: e.activation(out=p[:, c0:512], in_=pa[:, c0:512], func=AF.Exp, scale=nscale), reads=[R_pa], writes=[R_p])
                                    st["p"] = (p, R_p, c0)

                                def pv(st, e_=e_, h=h, k=k, nk=nk, hold=hold_s, acc=acc, R_a=R_a):
                                    p, R_p, c0 = st["p"]
                                    if k == 0:
                                        hold["po"] = next_o()
                                    po, R_po = hold["po"]
                                    fw.op("pe", lambda e: e.matmul(po[:, c0:512], lhsT=vsx[:, k, :], rhs=p[:, c0:512], start=(k == 0), stop=(k == nk - 1),
                                                                   skip_group_check=True),
                                          reads=[R_v, R_p], writes=[R_po])
                                    if k == nk - 1:
                                        finish_branch(po, R_po, acc, R_a, h, 1, False, ts)
                                jobs.append(make_job(score, pv))
                            ks_ = [k for k in range(4 * T - 4, 4 * T + 4) if k >= 0]
                            hold_w = {}
                            for j, k in enumerate(ks_):
                                def score(st, e_=e_, k=k):
                                    pa, R_pa = next_s()
                                    i = k - 4 * T
                                    c0 = max(0, i * 128)
                                    c1 = min(512, (i + 5) * 128)
                                    tq = slice(T * 512 + c0, T * 512 + c1)
                                    fw.op("pe", lambda e: e.matmul(pa[:, c0:c1], lhsT=kwT[:, k * 128:(k + 1) * 128], rhs=qS[e_][0:64, tq], start=True, stop=False),
                                          reads=[R_kw, R_qS[e_][T]], writes=[R_pa])
                                    fw.op("pe", lambda e: e.matmul(pa[:, c0:c1], lhsT=ident_b[:], rhs=Ww[:, 512 - i * 128 + c0:512 - i * 128 + c1], start=False, stop=True),
                                          reads=[R_ac, R_const], writes=[R_pa])
                                    p, R_p = pr_.next()
                                    fw.op("act", lambda e: e.activation(out=p[:, c0:c1], in_=pa[:, c0:c1], func=AF.Exp, scale=nscale), reads=[R_pa], writes=[R_p])
                                    st["p"] = (p, R_p, c0, c1)

                                def pv(st, e_=e_, h=h, j=j, k=k, nw=len(ks_), hold=hold_w, acc=acc, R_a=R_a):
                                    p, R_p, c0, c1 = st["p"]
                                    if j == 0:
                                        hold["po"] = next_o()
                                    po, R_po = hold["po"]
                                    fw.op("pe", lambda e: e.matmul(po[:, c0:c1], lhsT=vwx[:, k, :], rhs=p[:, c0:c1], start=(j == 0), stop=(j == nw - 1),
                                                                   skip_group_check=True),
                                          reads=[R_v, R_p], writes=[R_po])
                                    if j == nw - 1:
                                        finish_branch(po, R_po, acc, R_a, h, 2, False, ts)
                                        sg_, R_sg = sga.next()
                                        fw.dma("sp", sg_[:], SG[h * 64:(h + 1) * 64, ts], reads=[R_SG[h // 2]], writes=[R_sg])
                                        o_, R_o = oo.next()
                                        fw.op("pool", lambda e: e.tensor_tensor(out=o_[:], in0=acc[:], in1=sg_[:], op=ALU.mult),
                                              reads=[R_a, R_sg], writes=[R_o])
                                        fw.dma("pool", MT[h * 64:(h + 1) * 64, ts], o_[:], reads=[R_o], writes=[R_MT[h // 2]])
                                jobs.append(make_job(score, pv))
                        return jobs

                    run_jobs(cmp_jobs(0))
                    sel_dve(0)
                    sel_pe(0)
                    for T in range(NTT):
                        if T + 1 < NTT:
                            run_jobs(cmp_jobs(T + 1))
                            sel_dve(T + 1)
                        run_jobs(selwin_jobs(T))
                        if T + 1 < NTT:
                            sel_pe(T + 1)
                fw.barrier()

        active = []
        if "A" in STAGES:
            active += [0, 1, 2, 3]
        if "B" in STAGES:
            active += [4, 5, 6, 7]
        if "C" in STAGES:
            active += [8, 9, 10, 11]
        if "F" in STAGES:
            with ExitStack() as ph:
                gf = sb("gf", [128, D], F32, ph)
                R_gf = Res()
                fw.dma("sp", gf[:], g_final.partition_broadcast(128), writes=[R_gf])
                mt = Ring([sb(f"mt{i}", [128, 12, 512], BF16, ph) for i in range(2)])
                xin = Ring([sb(f"fxin{i}", [128, D], F32, ph) for i in range(2)])
                hb = Ring([sb(f"hb{i}", [128, D], F32, ph) for i in range(2)])
                ob = Ring([sb(f"ob{i}", [128, D], F32, ph) for i in range(2)])
                junk = sb("fjunk", [128, D], BF16, ph)
                R_junk = Res()
                st = Ring([sb(f"fst{i}", [128, 4], F32, ph) for i in range(2)])
                MT_v = MT.rearrange("(ck p) t -> p ck t", p=128)
                for T in range(NTT):
                    m, R_m = mt.next()
                    if active:
                        lo, hi = min(active), max(active) + 1
                        fw.dma("sp", m[:, lo:hi, :], MT_v[:, lo:hi, T * 512:(T + 1) * 512], reads=[R_MT[c] for c in active], writes=[R_m])
                    for sub in range(4):
                        tt = T * 4 + sub
                        xt, R_xt = xin.next()
                        fw.dma("sp", xt[:], x[tt * 128:(tt + 1) * 128, :], writes=[R_xt])
                        h, R_h = hb.next()
                        if active:
                            for half in range(2):
                                pb, R_pb = next_ps()
                                for i, ck in enumerate(active):
                                    fw.op("pe", lambda e: e.matmul(pb[:, :], lhsT=m[:, ck, sub * 128:(sub + 1) * 128],
                                                                   rhs=wo[:, ck, half * 512:(half + 1) * 512],
                                                                   start=(i == 0), stop=(i == len(active) - 1)),
                                          reads=[R_m, R_wo], writes=[R_pb])
                                fw.op("dve", lambda e: e.tensor_tensor(out=h[:, half * 512:(half + 1) * 512], in0=pb[:, :],
                                                                       in1=xt[:, half * 512:(half + 1) * 512], op=ALU.add),
                                      reads=[R_pb, R_xt], writes=[R_h])
                        else:
                            fw.op("dve", lambda e: e.tensor_copy(out=h[:], in_=xt[:]), reads=[R_xt], writes=[R_h])
                        s4, R_s4 = st.next()
                        fw.op("act", lambda e: e.activation(out=junk[:], in_=h[:], func=AF.Square, accum_out=s4[:, 0:1]),
                              reads=[R_h], writes=[R_junk, R_s4])
                        fw.op("dve", lambda e: e.tensor_scalar(out=s4[:, 1:2], in0=s4[:, 0:1], scalar1=1.0 / D, scalar2=EPS,
                                                               op0=ALU.mult, op1=ALU.add), reads=[R_s4], writes=[R_s4])
                        fw.op("act", lambda e: e.activation(out=s4[:, 2:3], in_=s4[:, 1:2], func=AF.Ln), reads=[R_s4], writes=[R_s4])
                        fw.op("act", lambda e: e.activation(out=s4[:, 3:4], in_=s4[:, 2:3], func=AF.Exp, scale=-0.5),
                              reads=[R_s4], writes=[R_s4])
                        o, R_o = ob.next()
                        fw.op("dve", lambda e: e.scalar_tensor_tensor(out=o[:], in0=h[:], scalar=s4[:, 3:4], in1=gf[:],
                                                                      op0=ALU.mult, op1=ALU.mult),
                              reads=[R_h, R_s4, R_gf], writes=[R_o])
                        fw.dma("pool", out[tt * 128:(tt + 1) * 128, :], o[:], reads=[R_o], writes=[R_out])
                fw.barrier()
        fw.barrier()
    print(f"[kernel] ops={fw.nops} waits={fw.nwaits} cnt={fw.cnt}")
    fw.close()
    return nc


def _host_consts():
    bf = ml_dtypes.bfloat16
    ident = np.eye(128, dtype=np.float32)
    inv = (np.float32(500000.0) ** (-(np.arange(0, 16, 2, dtype=np.float32)) / np.float32(16))).astype(np.float32)
    ropeinv = np.zeros((128, 1), np.float32)
    psw = np.zeros((128, 128), np.float32)
    for h in range(2):
        for i in range(8):
            ropeinv[h * 64 + i, 0] = inv[i]
            ropeinv[h * 64 + 8 + i, 0] = inv[i]
            psw[h * 64 + 8 + i, h * 64 + i] = -1.0
            psw[h * 64 + i, h * 64 + 8 + i] = 1.0
    ii = np.arange(128)
    triu = (ii[:, None] <= ii[None, :]).astype(np.float32)
    trineg = np.where(ii[:, None] <= ii[None, :], 0.0, NEG).astype(np.float32)
    hsel = np.zeros((128, 4, 8), np.float32)
    for p in range(128):
        for pr in range(4):
            hsel[p, pr, 2 * pr + p // 64] = 1.0
    n = np.arange(256)
    j = np.arange(64)
    cstart = 16 * n
    ov = ((cstart[:, None] < 64 * j[None, :] + 64) & (cstart[:, None] + 32 > 64 * j[None, :])).astype(np.float32)
    ov[255, :] = 0.0
    ovx = np.zeros((128, 2, 65), np.float32)
    for nch in range(2):
        ovx[:, nch, 0:64] = ov[nch * 128:(nch + 1) * 128]
        ovx[:, nch, 64] = 1.0
    col = np.arange(3072)
    w3 = np.where(col[None, :] >= 16 * ii[:, None] + 31, 0.0, NEG).astype(np.float32)
    col = np.arange(896)
    wc = np.where((col[None, :] - 384) >= ii[:, None], 0.0, NEG).astype(np.float32)
    col = np.arange(1536)
    u = col[None, :] - 512 - ii[:, None]
    ww = np.where((u >= 0) & (u < 512), 0.0, NEG).astype(np.float32)
    t = np.arange(S)
    cur = t // 64
    forced = (j[None, :] == 0) | (j[None, :] == cur[:, None]) | (j[None, :] == cur[:, None] - 1)
    valid = j[None, :] <= cur[:, None]
    addm = np.where(forced, 1e9, np.where(valid, 0.0, -1e9)).astype(np.float32)
    exg = np.zeros((24, 24, 64), np.float32)
    for r in range(24):
        exg[r, r, :] = 1.0
    ex = (np.arange(S)[None, :] // 64 == j[:, None]).astype(np.float32)
    return {"c_ident": ident, "c_ropeinv": ropeinv, "c_psw": psw, "c_triu": triu, "c_trineg": trineg,
            "c_hsel": hsel.reshape(128, 32), "c_ovx": ovx.reshape(128, 130).astype(bf), "c_w3": w3.astype(bf), "c_wc": wc.astype(bf),
            "c_ww": ww.astype(bf), "c_addm": addm, "c_exg": exg.reshape(24, 1536).astype(bf), "c_ex": ex.astype(bf)}


_NC_CACHE = {}


def kernel(**inputs):
    if "nc" not in _NC_CACHE:
        _NC_CACHE["nc"] = build_nc()
    nc = _NC_CACHE["nc"]
    consts = _host_consts()
    B = inputs["x"].shape[0]
    in_maps = []
    for b in range(B):
        m = {
            "x": np.ascontiguousarray(inputs["x"][b], dtype=np.float32),
            "mem": np.ascontiguousarray(inputs["mem"][b], dtype=np.float32),
            "positions": np.ascontiguousarray(inputs["positions"][b:b + 1], dtype=np.int32),
            "g_in": np.ascontiguousarray(inputs["g_in"][0], dtype=np.float32),
            "w_in": np.ascontiguousarray(inputs["w_in"][0], dtype=np.float32),
            "cmp_pos_k": np.ascontiguousarray(inputs["cmp_pos_k"][0], dtype=np.float32),
            "w_cmp1_k": np.ascontiguousarray(inputs["w_cmp1_k"][0], dtype=np.float32),
            "w_cmp2_k": np.ascontiguousarray(inputs["w_cmp2_k"][0], dtype=np.float32),
            "cmp_pos_v": np.ascontiguousarray(inputs["cmp_pos_v"][0], dtype=np.float32),
            "w_cmp1_v": np.ascontiguousarray(inputs["w_cmp1_v"][0], dtype=np.float32),
            "w_cmp2_v": np.ascontiguousarray(inputs["w_cmp2_v"][0], dtype=np.float32),
            "conv_w": np.ascontiguousarray(inputs["conv_w"][0], dtype=np.float32),
            "conv_b": np.ascontiguousarray(inputs["conv_b"][0], dtype=np.float32),
            "dt_bias": np.ascontiguousarray(inputs["dt_bias"][0:1], dtype=np.float32),
            "a_log": np.ascontiguousarray(inputs["a_log"][0:1], dtype=np.float32),
            "d_skip": np.ascontiguousarray(inputs["d_skip"][0:1], dtype=np.float32),
            "g_ssd_norm": np.ascontiguousarray(inputs["g_ssd_norm"][0], dtype=np.float32),
            "g_mem": np.ascontiguousarray(inputs["g_mem"][0], dtype=np.float32),
            "w_mem_kv": np.ascontiguousarray(inputs["w_mem_kv"][0], dtype=np.float32),
            "w_out": np.ascontiguousarray(inputs["w_out"][0], dtype=np.float32),
            "g_final": np.ascontiguousarray(np.asarray(inputs["g_final"]).reshape(1, D), dtype=np.float32),
        }
        m.update(consts)
        in_maps.append(m)
    if os.environ.get("KTRACE"):
        res = run_bass_kernel_spmd(nc, in_maps, core_ids=list(range(B)), trace=True)
        print("[kernel] exec_time_ns", res.exec_time_ns)
    else:
        res = run_bass_kernel_spmd(nc, in_maps, core_ids=list(range(B)))
    if DEBUG:
        _NC_CACHE["last"] = res
    return np.stack([np.asarray(r["out"], dtype=np.float32) for r in res.results], axis=0)
```

```python
import os
import math
import numpy as np
import ml_dtypes
from contextlib import ExitStack
import concourse.bass as bass
import concourse.mybir as mybir
from concourse.bass_utils import run_bass_kernel_spmd

F32 = mybir.dt.float32
BF16 = mybir.dt.bfloat16
I32 = mybir.dt.int32
AF = mybir.ActivationFunctionType
ALU = mybir.AluOpType
AX = mybir.AxisListType

S = 4096
D = 1024
NIN = 4384
NT = S // 128
NTT = S // 512
EPS = 1e-6
C_Q, C_KC, C_VC, C_KS, C_VS, C_KW, C_VW, C_GL, C_GA, C_Z, C_XBC, C_DT, C_QX, C_GX = (
    0, 512, 640, 768, 896, 1024, 1152, 1280, 1304, 1816, 2328, 3352, 3360, 3872)
P_Q, P_KC, P_KS, P_KW, P_XBC, P_QX, P_VC = 0, 512, 640, 768, 896, 1920, 2432
PT_ROWS = 2560
NEG = -30000.0

DEBUG = bool(int(os.environ.get("KDEBUG", "0")))
STAGES = os.environ.get("KSTAGES", "XPCBAF")
ASUB = int(os.environ.get("KASUB", "3"))


class Res:
    __slots__ = ("name", "w", "r")

    def __init__(self, name=""):
        self.name = name
        self.w = None
        self.r = []


class FW:
    def __init__(self, nc, n_dma_sems=24):
        self.nc = nc
        self.eng = {"pe": nc.tensor, "act": nc.scalar, "dve": nc.vector, "pool": nc.gpsimd, "sp": nc.sync}
        self.sem = {}
        self.cnt = {}
        self._ctx = []
        for e in ("pe", "act", "dve", "pool"):
            cm = nc.semaphore("sem_" + e)
            s = cm.__enter__()
            self._ctx.append(cm)
            self.sem[e] = s
            self.cnt[e] = 0
        self.dpool = {}
        for q, n in (("sp", n_dma_sems), ("pool", 12), ("act", 8)):
            lst = []
            for i in range(n):
                cm = nc.semaphore(f"dsem_{q}_{i}")
                s = cm.__enter__()
                self._ctx.append(cm)
                lst.append([s, 0])
            self.dpool[q] = [lst, 0]
        self.obs = {e: {} for e in self.eng}
        self.nwaits = 0
        self.nops = 0

    def close(self):
        for cm in reversed(self._ctx):
            cm.__exit__(None, None, None)

    def _need(self, e, tok, lst):
        if tok is None:
            return
        src, sem, val = tok
        key = id(sem)
        if self.obs[e].get(key, 0) >= val:
            return
        self.obs[e][key] = val
        lst[key] = (sem, max(val, lst.get(key, (None, 0))[1]))

    def _wait(self, e, tok):
        lst = {}
        self._need(e, tok, lst)
        for sem, val in lst.values():
            self.eng[e].wait_ge(sem, val)
            self.nwaits += 1

    def _deps(self, e, reads, writes):
        lst = {}
        for r in reads:
            if r.w is not None:
                if not (r.w[0] == e and e == "pe"):
                    self._need(e, r.w, lst)
        for w in writes:
            if w.w is not None and w.w[0] != e:
                self._need(e, w.w, lst)
            for t in w.r:
                if t[0] != e:
                    self._need(e, t, lst)
        return list(lst.values())

    def _update(self, tok, reads, writes):
        for r in reads:
            if tok[0].startswith("dma"):
                r.r = r.r + [tok]
            else:
                r.r = [t for t in r.r if t[0] != tok[0]] + [tok]
        for w in writes:
            w.w = tok
            w.r = []

    def op(self, e, fn, reads=(), writes=()):
        waits = self._deps(e, reads, writes)
        for sem, val in waits[:-1]:
            self.eng[e].wait_ge(sem, val)
            self.nwaits += 1
        ins = fn(self.eng[e])
        if waits:
            ins = ins._wait_ge(waits[-1][0], waits[-1][1])
        self.cnt[e] += 1
        ins.then_inc(self.sem[e], 1)
        tok = (e, self.sem[e], self.cnt[e])
        self._update(tok, reads, writes)
        self.nops += 1
        return tok

    def dma(self, q, out, in_, reads=(), writes=(), **kw):
        waits = self._deps(q, reads, writes)
        lst, idx = self.dpool[q]
        slot = lst[idx % len(lst)]
        self.dpool[q][1] = idx + 1
        sem, cur = slot
        if cur > 0:
            d = {}
            self._need(q, ("dma_" + q, sem, cur), d)
            waits += list(d.values())
        for sem_w, val in waits[:-1]:
            self.eng[q].wait_ge(sem_w, val)
            self.nwaits += 1
        ins = self.eng[q].dma_start(out=out, in_=in_, **kw)
        if waits:
            ins = ins._wait_ge(waits[-1][0], waits[-1][1])
        slot[1] = cur + 16
        ins.then_inc(sem, 16)
        tok = ("dma_" + q, sem, slot[1])
        self._update(tok, reads, writes)
        return tok

    def barrier(self):
        toks = []
        for e in ("pe", "act", "dve", "pool"):
            if self.cnt[e] > 0:
                toks.append((e, self.sem[e], self.cnt[e]))
        for q in self.dpool:
            for sem, cur in self.dpool[q][0]:
                if cur > 0:
                    toks.append(("dma_" + q, sem, cur))
        for e in ("pe", "act", "dve", "pool", "sp"):
            for t in toks:
                if t[0] != e:
                    self._wait(e, t)


class Ring:
    def __init__(self, bufs):
        self.bufs = [(b, Res()) for b in bufs]
        self.i = 0

    def next(self):
        b = self.bufs[self.i % len(self.bufs)]
        self.i += 1
        return b


def build_nc():
    nc = bass.Bass("TRN2", target_bir_lowering=False)
    fw = FW(nc)
    dbg_kind = "ExternalOutput" if DEBUG else "Internal"

    def din(name, shape, dt=F32):
        return nc.dram_tensor(name, list(shape), dt, kind="ExternalInput").ap()

    x = din("x", [S, D])
    mem = din("mem", [256, D])
    positions = din("positions", [1, S], I32)
    g_in = din("g_in", [D])
    w_in = din("w_in", [D, NIN])
    cmp_pos_k = din("cmp_pos_k", [32, 64])
    w_cmp1_k = din("w_cmp1_k", [2048, 256])
    w_cmp2_k = din("w_cmp2_k", [256, 64])
    cmp_pos_v = din("cmp_pos_v", [32, 64])
    w_cmp1_v = din("w_cmp1_v", [2048, 256])
    w_cmp2_v = din("w_cmp2_v", [256, 64])
    conv_w = din("conv_w", [4, 1024])
    conv_b = din("conv_b", [1024])
    dt_bias = din("dt_bias", [1, 8])
    a_log = din("a_log", [1, 8])
    d_skip = din("d_skip", [1, 8])
    g_ssd_norm = din("g_ssd_norm", [512])
    g_mem = din("g_mem", [D])
    w_mem_kv = din("w_mem_kv", [D, 1024])
    w_out = din("w_out", [1536, D])
    g_final = din("g_final", [1, D])
    c_ident = din("c_ident", [128, 128])
    c_ropeinv = din("c_ropeinv", [128, 1])
    c_psw = din("c_psw", [128, 128])
    c_triu = din("c_triu", [128, 128])
    c_trineg = din("c_trineg", [128, 128])
    c_hsel = din("c_hsel", [128, 32])
    c_ovx = din("c_ovx", [128, 130], BF16)
    c_w3 = din("c_w3", [128, 3072], BF16)
    c_wc = din("c_wc", [128, 896], BF16)
    c_ww = din("c_ww", [128, 1536], BF16)
    c_addm = din("c_addm", [S, 64])
    c_exg = din("c_exg", [24, 1536], BF16)
    c_ex = din("c_ex", [64, S], BF16)

    out = nc.dram_tensor("out", [S, D], F32, kind="ExternalOutput").ap()
    PT = nc.dram_tensor("PT", [PT_ROWS, S], BF16, kind=dbg_kind).ap()
    SG = nc.dram_tensor("SG", [1536, S], BF16, kind=dbg_kind).ap()
    GL = nc.dram_tensor("GL", [24, S], BF16, kind=dbg_kind).ap()
    VT = nc.dram_tensor("VT", [S, 392], F32, kind=dbg_kind).ap()
    CS = nc.dram_tensor("CS", [2, 128, 256], F32, kind=dbg_kind).ap()
    MT = nc.dram_tensor("MT", [1536, S], BF16, kind=dbg_kind).ap()

    R_out = Res("out")
    R_PT = [Res() for _ in range(PT_ROWS // 128)]
    R_SG = [Res() for _ in range(12)]
    R_GL = Res()
    R_VT = Res()
    R_CS = Res()
    R_MT = [Res() for _ in range(12)]

    with ExitStack() as top:
        top.enter_context(nc.allow_low_precision("bf16 matmul operands / bf16 staging by design"))

        def sb(name, shape, dt, stack=top):
            return stack.enter_context(nc.sbuf_tensor(name, list(shape), dt))

        ps = [top.enter_context(nc.psum_tensor(f"ps{i}", [128, 512], F32)) for i in range(8)]
        R_ps = [Res(f"ps{i}") for i in range(8)]
        ps_ring = {"i": 0}

        def next_ps(lo=0, hi=8):
            i = lo + ps_ring["i"] % (hi - lo)
            ps_ring["i"] += 1
            return ps[i], R_ps[i]

        ident_f = sb("ident_f", [128, 128], F32)
        ident_b = sb("ident_b", [128, 128], BF16)
        ones_f = sb("ones_f", [128, 128], F32)
        ones_b = sb("ones_b", [128, 128], BF16)
        psw_f = sb("psw_f", [128, 128], F32)
        R_const = Res("const")
        fw.dma("sp", ident_f[:], c_ident[:, :], writes=[R_const])
        fw.dma("sp", psw_f[:], c_psw[:, :], writes=[R_const])
        fw.op("dve", lambda e: e.tensor_copy(out=ident_b[:], in_=ident_f[:]), reads=[R_const], writes=[R_const])
        fw.op("dve", lambda e: e.memset(ones_f[:], 1.0), writes=[R_const])
        fw.op("dve", lambda e: e.memset(ones_b[:], 1.0), writes=[R_const])
        epsb = sb("epsb", [128, 1], F32)
        fw.op("dve", lambda e: e.memset(epsb[:], 1e-30), writes=[R_const])

        dt_all = sb("dt_all", [128, NT, 8], F32)
        R_dt = Res()
        xn_stack = ExitStack()
        xnT = sb("xnT", [128, 8, S], BF16, xn_stack)
        R_xn = [Res(f"xn{i}") for i in range(NTT)]

        def rms_transpose_phase(src, ntiles, g_vec, dstT, R_dst_of_tile, stack):
            g_sb = sb("g_sb_" + dstT.name, [128, 8], F32, stack)
            R_g = Res()
            fw.dma("sp", g_sb[:], g_vec.rearrange("(dk p) -> p dk", p=128), writes=[R_g], allow_slow_non_contiguous=True)
            xin = Ring([sb(f"xin{i}_" + dstT.name, [128, D], F32, stack) for i in range(min(4, ntiles))])
            xsc = Ring([sb(f"xsc{i}_" + dstT.name, [128, D], BF16, stack) for i in range(min(3, ntiles))])
            junk = sb("junk_" + dstT.name, [128, D], BF16, stack)
            R_junk = Res()
            st = Ring([sb(f"st{i}_" + dstT.name, [128, 4], F32, stack) for i in range(4)])
            epsd = sb("epsd_" + dstT.name, [128, 1], F32, stack)
            fw.op("dve", lambda e: e.memset(epsd[:], EPS), writes=[R_g])

            def stage_a(tt):
                xt, R_xt = xin.next()
                fw.dma("sp", xt[:], src[tt * 128:(tt + 1) * 128, :], writes=[R_xt])
                s4, R_s4 = st.next()
                fw.op("act", lambda e: e.activation(out=junk[:], in_=xt[:], func=AF.Square, accum_out=s4[:, 0:1]),
                      reads=[R_xt], writes=[R_junk, R_s4])
                fw.op("act", lambda e: e.activation(out=s4[:, 2:3], in_=s4[:, 0:1], func=AF.Ln, scale=1.0 / D, bias=epsd[:, 0:1]),
                      reads=[R_s4, R_g], writes=[R_s4])
                fw.op("act", lambda e: e.activation(out=s4[:, 3:4], in_=s4[:, 2:3], func=AF.Exp, scale=-0.5),
                      reads=[R_s4], writes=[R_s4])
                return (xt, R_xt, s4, R_s4)

            def stage_b(tt, xt, R_xt, s4, R_s4):
                xs, R_xs = xsc.next()
                fw.op("dve", lambda e: e.tensor_scalar(out=xs[:], in0=xt[:], scalar1=s4[:, 3:4], scalar2=None, op0=ALU.mult),
                      reads=[R_xt, R_s4], writes=[R_xs])
                pb, R_pb = next_ps()
                pbb = pb[:].bitcast(BF16)
                for dk in range(8):
                    fw.op("pe", lambda e: e.transpose(pbb[:, dk * 128:(dk + 1) * 128], xs[:, dk * 128:(dk + 1) * 128], ident_b[:]),
                          reads=[R_xs, R_const], writes=[R_pb])
                return (tt, pbb, R_pb)

            def stage_c(tt, pbb, R_pb):
                fw.op("dve", lambda e: e.tensor_tensor(
                    out=dstT[:, :, tt * 128:(tt + 1) * 128],
                    in0=pbb.rearrange("p (a b) -> p a b", a=8),
                    in1=g_sb[:].unsqueeze(2).to_broadcast([128, 8, 128]), op=ALU.mult),
                    reads=[R_pb, R_g], writes=[R_dst_of_tile(tt)])

            pa_, pb_ = [], []
            for tt in range(ntiles + 3):
                if tt < ntiles:
                    pa_.append((tt,) + stage_a(tt))
                if len(pb_) > 0 and tt >= 2:
                    stage_c(*pb_.pop(0))
                if len(pa_) > 0 and tt >= 1:
                    pb_.append(stage_b(*pa_.pop(0)))
            while pa_ or pb_:
                if pb_:
                    stage_c(*pb_.pop(0))
                if pa_:
                    pb_.append(stage_b(*pa_.pop(0)))

        if "X" in STAGES:
            rms_transpose_phase(x, NT, g_in, xnT, lambda tt: R_xn[tt // 4], xn_stack)

        if "P" in STAGES:
            with ExitStack() as ph:
                Ct = sb("Ct", [128, S], F32, ph)
                St = sb("St", [128, S], F32, ph)
                R_C = Res()
                def emit_rope_tables():
                    invp = sb("invp", [128, 1], F32, ph)
                    R_tmp = Res()
                    fw.dma("sp", invp[:], c_ropeinv[:, :], writes=[R_tmp])
                    CB = 512
                    posi = sb("posi", [128, CB], I32, ph)
                    ang = sb("ang", [128, CB], F32, ph)
                    kfi = sb("kfi", [128, CB], I32, ph)
                    kf = sb("kf", [128, CB], F32, ph)
                    rr = sb("rr", [128, CB], F32, ph)
                    rc = sb("rc", [128, CB], F32, ph)
                    C1 = 6.28125
                    C2 = 2 * math.pi - 6.28125
                    PI_LO = 3.141592
                    for cb in range(S // CB):
                        cs = slice(cb * CB, (cb + 1) * CB)
                        fw.dma("sp", posi[:], positions[:, cs].partition_broadcast(128), reads=[R_tmp], writes=[R_tmp])
                        fw.op("dve", lambda e: e.tensor_copy(out=ang[:], in_=posi[:]), reads=[R_tmp], writes=[R_tmp])
                        fw.op("dve", lambda e: e.tensor_scalar(out=ang[:], in0=ang[:], scalar1=invp[:, 0:1], scalar2=None, op0=ALU.mult),
                              reads=[R_tmp], writes=[R_tmp])
                        fw.op("dve", lambda e: e.tensor_scalar(out=kfi[:], in0=ang[:], scalar1=1.0 / (2 * math.pi), scalar2=None, op0=ALU.mult),
                              reads=[R_tmp], writes=[R_tmp])
                        fw.op("dve", lambda e: e.tensor_copy(out=kf[:], in_=kfi[:]), reads=[R_tmp], writes=[R_tmp])
                        fw.op("dve", lambda e: e.scalar_tensor_tensor(out=rr[:], in0=kf[:], scalar=-C1, in1=ang[:], op0=ALU.mult, op1=ALU.add),
                              reads=[R_tmp], writes=[R_tmp])
                        fw.op("dve", lambda e: e.scalar_tensor_tensor(out=rr[:], in0=kf[:], scalar=-C2, in1=rr[:], op0=ALU.mult, op1=ALU.add),
                              reads=[R_tmp], writes=[R_tmp])
                        fw.op("dve", lambda e: e.tensor_scalar(out=rc[:], in0=rr[:], scalar1=math.pi / 2, scalar2=-2 * math.pi,
                                                               op0=ALU.is_gt, op1=ALU.mult), reads=[R_tmp], writes=[R_tmp])
                        fw.op("dve", lambda e: e.scalar_tensor_tensor(out=rc[:], in0=rr[:], scalar=math.pi / 2, in1=rc[:], op0=ALU.add, op1=ALU.add),
                              reads=[R_tmp], writes=[R_tmp])
                        fw.op("dve", lambda e: e.tensor_scalar(out=rr[:], in0=rr[:], scalar1=-PI_LO, scalar2=PI_LO, op0=ALU.max, op1=ALU.min),
                              reads=[R_tmp], writes=[R_tmp])
                        fw.op("dve", lambda e: e.tensor_scalar(out=rc[:], in0=rc[:], scalar1=-PI_LO, scalar2=PI_LO, op0=ALU.max, op1=ALU.min),
                              reads=[R_tmp], writes=[R_tmp])
                        fw.op("act", lambda e: e.activation(out=St[:, cs], in_=rr[:], func=AF.Sin), reads=[R_tmp], writes=[R_C])
                        fw.op("act", lambda e: e.activation(out=Ct[:, cs], in_=rc[:], func=AF.Sin), reads=[R_tmp], writes=[R_C])
                    csc = sb("csc", [128, 2, 256], F32, ph)
                    R_csc = Res()
                    fw.op("dve", lambda e: e.tensor_copy(out=csc[:, 0, 0:255], in_=Ct[:, 31:S:16]), reads=[R_C], writes=[R_csc])
                    fw.op("dve", lambda e: e.tensor_copy(out=csc[:, 1, 0:255], in_=St[:, 31:S:16]), reads=[R_C], writes=[R_csc])
                    fw.dma("sp", CS[0, :, 0:255], csc[:, 0, 0:255], reads=[R_csc], writes=[R_CS])
                    fw.dma("sp", CS[1, :, 0:255], csc[:, 1, 0:255], reads=[R_csc], writes=[R_CS])


                wbf = Ring([sb(f"wbf{i}", [128, 8, 128], BF16, ph) for i in range(3)])
                OW = 2048
                otile = [(sb(f"otile{i}", [128, OW], BF16, ph), [Res() for _ in range(4)]) for i in range(3)]
                ot_i = {"i": 0}
                qf = Ring([sb(f"qf{i}", [128, 512], F32, ph) for i in range(2)])
                t1 = Ring([sb(f"t1_{i}", [128, 512], F32, ph) for i in range(2)])
                w_view = w_in.rearrange("(dk p) c -> p dk c", p=128)

                def load_w(c0, ncols):
                    wb, R_wb = wbf.next()
                    fw.dma("pool", wb[:, :, 0:ncols], w_view[:, :, c0:c0 + ncols], writes=[R_wb])
                    return wb, R_wb

                def proj_fm(wb, R_wb, ncols, T):
                    pb, R_pb = next_ps()
                    for dk in range(8):
                        fw.op("pe", lambda e: e.matmul(pb[0:ncols, :], lhsT=wb[:, dk, 0:ncols], rhs=xnT[:, dk, T * 512:(T + 1) * 512],
                                                       start=(dk == 0), stop=(dk == 7)),
                              reads=[R_wb, R_xn[T]], writes=[R_pb])
                    return pb, R_pb

                chunks = []
                for i in range(4):
                    chunks.append(("silu", C_GA + i * 128, 128, SG, i, R_SG))
                for i in range(4):
                    chunks.append(("silu", C_Z + i * 128, 128, SG, 4 + i, R_SG))
                for i in range(4):
                    chunks.append(("silu", C_GX + i * 128, 128, SG, 8 + i, R_SG))
                chunks.append(("copy", C_KC, 128, PT, P_KC // 128, R_PT))
                chunks.append(("copy", C_VC, 128, PT, P_VC // 128, R_PT))
                for i in range(8):
                    chunks.append(("copy", C_XBC + i * 128, 128, PT, P_XBC // 128 + i, R_PT))
                for i in range(4):
                    chunks.append(("copy", C_QX + i * 128, 128, PT, P_QX // 128 + i, R_PT))
                for i in range(4):
                    chunks.append(("rope", C_Q + i * 128, 128, PT, P_Q // 128 + i, R_PT))
                chunks.append(("rope", C_KS, 128, PT, P_KS // 128, R_PT))
                chunks.append(("rope", C_KW, 128, PT, P_KW // 128, R_PT))
                chunks.append(("gl", C_GL, 24, GL, 0, None))
                nxt = load_w(chunks[0][1], chunks[0][2])
                for ci, (kind, c0, ncols, dstT, drow, R_dst) in enumerate(chunks):
                    wb, R_wb = nxt
                    if ci + 1 < len(chunks):
                        nxt = load_w(chunks[ci + 1][1], chunks[ci + 1][2])
                    if ci == 12:
                        emit_rope_tables()
                    for half in range(2):
                        ot, R_ots = otile[ot_i["i"] % 3]
                        ot_i["i"] += 1
                        for T4 in range(4):
                            T = half * 4 + T4
                            osl = ot[:, T4 * 512:(T4 + 1) * 512]
                            R_o1 = R_ots[T4]
                            pb, R_pb = proj_fm(wb, R_wb, ncols, T)
                            if kind == "silu":
                                fw.op("act", lambda e: e.activation(out=osl, in_=pb[:], func=AF.Silu), reads=[R_pb], writes=[R_o1])
                            elif kind == "copy":
                                if T % 2 == 0:
                                    fw.op("dve", lambda e: e.tensor_copy(out=osl, in_=pb[:]), reads=[R_pb], writes=[R_o1])
                                else:
                                    fw.op("act", lambda e: e.activation(out=osl, in_=pb[:], func=AF.Copy), reads=[R_pb], writes=[R_o1])
                            elif kind == "rope":
                                q32, R_q32 = qf.next()
                                fw.op("act", lambda e: e.activation(out=q32[:], in_=pb[:], func=AF.Copy), reads=[R_pb], writes=[R_q32])
                                pb2, R_pb2 = next_ps()
                                fw.op("pe", lambda e: e.matmul(pb2[:, :], lhsT=psw_f[:], rhs=q32[:], start=True, stop=True),
                                      reads=[R_const, R_q32], writes=[R_pb2])
                                ta, R_ta = t1.next()
                                fw.op("dve", lambda e: e.tensor_tensor(out=ta[:], in0=pb2[:], in1=St[:, T * 512:(T + 1) * 512], op=ALU.mult),
                                      reads=[R_pb2, R_C], writes=[R_ta])
                                fw.op("pool", lambda e: e.tensor_tensor(out=q32[:], in0=q32[:], in1=Ct[:, T * 512:(T + 1) * 512], op=ALU.mult),
                                      reads=[R_q32, R_C], writes=[R_q32])
                                fw.op("dve", lambda e: e.tensor_tensor(out=osl, in0=ta[:], in1=q32[:], op=ALU.add),
                                      reads=[R_ta, R_q32], writes=[R_o1])
                            else:
                                ta, R_ta = t1.next()
                                fw.op("act", lambda e: e.activation(out=ta[0:24, :], in_=pb[0:24, :], func=AF.Exp, scale=-1.0), reads=[R_pb], writes=[R_ta])
                                fw.op("dve", lambda e: e.tensor_scalar(out=ta[0:24, :], in0=ta[0:24, :], scalar1=1.0, scalar2=None, op0=ALU.add),
                                      reads=[R_ta], writes=[R_ta])
                                fw.op("dve", lambda e: e.reciprocal(out=osl[0:24, :], in_=ta[0:24, :]), reads=[R_ta], writes=[R_o1])
                        if kind == "gl":
                            fw.dma("sp", GL[:, half * OW:(half + 1) * OW], ot[0:24, :], reads=R_ots, writes=[R_GL])
                        else:
                            fw.dma("sp", dstT[drow * 128:(drow + 1) * 128, half * OW:(half + 1) * OW], ot[:], reads=R_ots, writes=[R_dst[drow]])
                wtm = sb("wtm", [128, 8, 392], BF16, ph)
                R_wtm = Res()
                for j, c0 in enumerate((C_VC, C_VS, C_VW)):
                    fw.dma("pool", wtm[:, :, j * 128:(j + 1) * 128], w_view[:, :, c0:c0 + 128], writes=[R_wtm])
                fw.dma("pool", wtm[:, :, 384:392], w_view[:, :, C_DT:C_DT + 8], writes=[R_wtm])
                vt_o = Ring([sb(f"vt_o{i}", [128, 392], F32, ph) for i in range(3)])
                for tt in range(NT):
                    pb, R_pb = next_ps()
                    for dk in range(8):
                        fw.op("pe", lambda e: e.matmul(pb[:, 0:392], lhsT=xnT[:, dk, tt * 128:(tt + 1) * 128], rhs=wtm[:, dk, :],
                                                       start=(dk == 0), stop=(dk == 7)),
                              reads=[R_wtm, R_xn[tt // 4]], writes=[R_pb])
                    vo, R_vo = vt_o.next()
                    if tt % 2 == 0:
                        fw.op("dve", lambda e: e.tensor_copy(out=vo[:], in_=pb[:, 0:392]), reads=[R_pb], writes=[R_vo])
                    else:
                        fw.op("act", lambda e: e.activation(out=vo[:], in_=pb[:, 0:392], func=AF.Copy), reads=[R_pb], writes=[R_vo])
                    fw.op("dve", lambda e: e.tensor_copy(out=dt_all[:, tt, :], in_=pb[:, 384:392]), reads=[R_pb], writes=[R_dt])
                    fw.dma("sp", VT[tt * 128:(tt + 1) * 128, :], vo[:], reads=[R_vo], writes=[R_VT])
                fw.barrier()

        xn_stack.close()
        wo = sb("wo", [128, 12, D], BF16)
        R_wo = Res()
        for ck in range(12):
            fw.dma("pool", wo[:, ck, :], w_out[ck * 128:(ck + 1) * 128, :], writes=[R_wo])
        def build_phase_c(ph):
            memT = sb("memT", [128, 8, 256], BF16, ph)
            R_memT = Res()
            rms_transpose_phase(mem, 2, g_mem, memT, lambda tt: R_memT, ph)
            wkv_view = w_mem_kv.rearrange("(dk p) c -> p dk c", p=128)
            kT = sb("kT", [128, 4, 256], BF16, ph)
            vtok = sb("vtok", [128, 2, 512], BF16, ph)
            R_kv = Res()
            wst = Ring([sb(f"cwst{i}", [128, 8, 128], F32, ph) for i in range(2)])
            wv = sb("cwv", [128, 8, 512], BF16, ph)
            R_wv = Res()
            wkb = Ring([sb(f"cwkb{i}", [128, 8, 128], BF16, ph) for i in range(2)])
            for h in range(4):
                ws, R_ws = wst.next()
                fw.dma("sp", ws[:], wkv_view[:, :, h * 128:(h + 1) * 128], writes=[R_ws])
                wb, R_wb = wkb.next()
                fw.op("act", lambda e: e.activation(out=wb[:], in_=ws[:], func=AF.Copy), reads=[R_ws], writes=[R_wb])
                pb, R_pb = next_ps()
                for dk in range(8):
                    fw.op("pe", lambda e: e.matmul(pb[:, 0:256], lhsT=wb[:, dk, :], rhs=memT[:, dk, :], start=(dk == 0), stop=(dk == 7)),
                          reads=[R_wb, R_memT], writes=[R_pb])
                fw.op("dve", lambda e: e.tensor_copy(out=kT[:, h, :], in_=pb[:, 0:256]), reads=[R_pb], writes=[R_kv])
            for h in range(4):
                ws, R_ws = wst.next()
                fw.dma("sp", ws[:], wkv_view[:, :, 512 + h * 128:512 + (h + 1) * 128], writes=[R_ws])
                fw.op("dve", lambda e: e.tensor_copy(out=wv[:, :, h * 128:(h + 1) * 128], in_=ws[:]), reads=[R_ws], writes=[R_wv])
            for kc in range(2):
                pb, R_pb = next_ps()
                for dk in range(8):
                    fw.op("pe", lambda e: e.matmul(pb[:, :], lhsT=memT[:, dk, kc * 128:(kc + 1) * 128], rhs=wv[:, dk, :],
                                                   start=(dk == 0), stop=(dk == 7)), reads=[R_wv, R_memT], writes=[R_pb])
                fw.op("dve", lambda e: e.tensor_copy(out=vtok[:, kc, :], in_=pb[:, :]), reads=[R_pb], writes=[R_kv])
            qx = Ring([sb(f"cqx{i}", [128, 512], BF16, ph) for i in range(3)])
            sgx = Ring([sb(f"csgx{i}", [128, 512], BF16, ph) for i in range(3)])
            pT = Ring([sb(f"cpT{i}", [128, 512], BF16, ph) for i in range(4)])
            rden = Ring([sb(f"crden{i}", [128, 512], F32, ph) for i in range(2)])
            ot = Ring([sb(f"cot{i}", [128, 512], BF16, ph) for i in range(2)])
            xscale = 128.0 ** -0.5
            cjobs = []
            for T in range(NTT):
                for h in range(4):
                    def cscore(st, T=T, h=h):
                        ts = slice(T * 512, (T + 1) * 512)
                        q, R_q = qx.next()
                        fw.dma("sp", q[:], PT[P_QX + h * 128:P_QX + (h + 1) * 128, ts], reads=[R_PT[P_QX // 128 + h]], writes=[R_q])
                        sg, R_sg = sgx.next()
                        fw.dma("sp", sg[:], SG[1024 + h * 128:1024 + (h + 1) * 128, ts], reads=[R_SG[8 + h]], writes=[R_sg])
                        pts = []
                        for kc in range(2):
                            pa, R_pa = next_ps(0, 4)
                            fw.op("pe", lambda e: e.matmul(pa[:, :], lhsT=kT[:, h, kc * 128:(kc + 1) * 128], rhs=q[:], start=True, stop=True),
                                  reads=[R_kv, R_q], writes=[R_pa])
                            p, R_p = pT.next()
                            fw.op("act", lambda e: e.activation(out=p[:], in_=pa[:, :], func=AF.Exp, scale=xscale), reads=[R_pa], writes=[R_p])
                            pts.append((p, R_p))
                        st["pts"] = pts
                        st["sg"] = (sg, R_sg)

                    def cpv(st, T=T, h=h):
                        ts = slice(T * 512, (T + 1) * 512)
                        pts = st["pts"]
                        sg, R_sg = st["sg"]
                        po, R_po = next_ps(4, 6)
                        pd, R_pd = next_ps(6, 8)
                        for kc in range(2):
                            p, R_p = pts[kc]
                            fw.op("pe", lambda e: e.matmul(po[:, :], lhsT=vtok[:, kc, h * 128:(h + 1) * 128], rhs=p[:], start=(kc == 0), stop=(kc == 1)),
                                  reads=[R_kv, R_p], writes=[R_po])
                        for kc in range(2):
                            p, R_p = pts[kc]
                            fw.op("pe", lambda e: e.matmul(pd[:, :], lhsT=ones_b[:], rhs=p[:], start=(kc == 0), stop=(kc == 1)),
                                  reads=[R_const, R_p], writes=[R_pd])
                        rd, R_rd = rden.next()
                        fw.op("act", lambda e: e.activation(out=rd[:], in_=pd[:, :], func=AF.Ln), reads=[R_pd], writes=[R_rd])
                        fw.op("act", lambda e: e.activation(out=rd[:], in_=rd[:], func=AF.Exp, scale=-1.0), reads=[R_rd], writes=[R_rd])
                        fw.op("dve", lambda e: e.tensor_tensor(out=rd[:], in0=po[:, :], in1=rd[:], op=ALU.mult), reads=[R_po, R_rd], writes=[R_rd])
                        o, R_o = ot.next()
                        fw.op("dve", lambda e: e.tensor_tensor(out=o[:], in0=rd[:], in1=sg[:], op=ALU.mult), reads=[R_rd, R_sg], writes=[R_o])
                        fw.dma("pool", MT[1024 + h * 128:1024 + (h + 1) * 128, ts], o[:], reads=[R_o], writes=[R_MT[8 + h]])
                    stt = {}
                    cjobs.append((lambda f=cscore, st=stt: f(st), lambda f=cpv, st=stt: f(st)))

            cst = {"i": 0}
            n = len(cjobs)

            def cstep(k):
                for _ in range(k):
                    i = cst["i"]
                    if i > n:
                        return
                    if i < n:
                        cjobs[i][0]()
                    if i - 1 >= 0:
                        cjobs[i - 1][1]()
                    cst["i"] = i + 1
            return cstep

        if "C" in STAGES and "A" not in STAGES:
            with ExitStack() as ph:
                cstep = build_phase_c(ph)
                cstep(40)
                fw.barrier()

        if "B" in STAGES:
            with ExitStack() as ph:
                R_c = Res()
                cw = sb("b_cw", [128, 4, 8], F32, ph)
                cbias = sb("b_cb", [128, 8], F32, ph)
                for k in range(4):
                    fw.dma("sp", cw[:, k, :], conv_w[k].rearrange("(c p) -> p c", p=128), writes=[R_c], allow_slow_non_contiguous=True)
                fw.dma("sp", cbias[:], conv_b.rearrange("(c p) -> p c", p=128), writes=[R_c], allow_slow_non_contiguous=True)
                dtb = sb("b_dtb", [128, 8], F32, ph)
                alog = sb("b_alog", [128, 8], F32, ph)
                dskb = sb("b_dskb", [128, 8], F32, ph)
                hsel = sb("b_hsel", [128, 4, 8], F32, ph)
                gn = sb("b_gn", [128, 4], F32, ph)
                triu = sb("b_triu", [128, 128], F32, ph)
                trineg = sb("b_trineg", [128, 128], F32, ph)
                fw.dma("sp", dtb[:], dt_bias.partition_broadcast(128), writes=[R_c])
                fw.dma("sp", alog[:], a_log.partition_broadcast(128), writes=[R_c])
                fw.dma("sp", dskb[:], d_skip.partition_broadcast(128), writes=[R_c])
                fw.dma("sp", hsel[:], c_hsel.rearrange("p (a b) -> p a b", a=4), writes=[R_c])
                fw.dma("sp", gn[:], g_ssd_norm.rearrange("(c p) -> p c", p=128), writes=[R_c], allow_slow_non_contiguous=True)
                fw.dma("sp", triu[:], c_triu[:, :], writes=[R_c])
                fw.dma("sp", trineg[:], c_trineg[:, :], writes=[R_c])
                trineg_b = sb("b_trineg_b", [128, 128], BF16, ph)
                fw.op("dve", lambda e: e.tensor_copy(out=trineg_b[:], in_=trineg[:]), reads=[R_c], writes=[R_c])
                dsk = sb("b_dsk", [128, 4], F32, ph)
                hs2 = sb("b_hs2", [128, 4, 8], F32, ph)
                fw.op("dve", lambda e: e.tensor_tensor(out=hs2[:], in0=hsel[:], in1=dskb[:].unsqueeze(1).to_broadcast([128, 4, 8]), op=ALU.mult),
                      reads=[R_c], writes=[R_c])
                fw.op("dve", lambda e: e.tensor_reduce(out=dsk[:], in_=hs2[:], axis=AX.X, op=ALU.add), reads=[R_c], writes=[R_c])
                xact = sb("b_xact", [128, 8, S], BF16, ph)
                R_xact = [Res() for _ in range(8)]
                with ExitStack() as ph2:
                    xpad = Ring([sb(f"b_xpad{i}", [128, S + 4], BF16, ph2) for i in range(2)])
                    dg = sb("b_dg", [128, 32, 128], BF16, ph2)
                    R_dg = Res()
                    for c in range(8):
                        for k in range(4):
                            fw.op("dve", lambda e: e.tensor_scalar(out=dg[:, c * 4 + k, :], in0=ident_f[:], scalar1=cw[:, k, c:c + 1], scalar2=None, op0=ALU.mult),
                                  reads=[R_c, R_const], writes=[R_dg])
                    for i in range(2):
                        xp, R_xp = xpad.bufs[i]
                        fw.op("dve", lambda e: e.memset(xp[:, 0:4], 0.0), writes=[R_xp])
                    for c in range(8):
                        xp, R_xp = xpad.next()
                        fw.dma("sp", xp[:, 3:S + 3], PT[P_XBC + c * 128:P_XBC + (c + 1) * 128, :], reads=[R_PT[P_XBC // 128 + c]], writes=[R_xp])
                        for T in range(NTT):
                            pb, R_pb = next_ps()
                            for k in range(4):
                                fw.op("pe", lambda e: e.matmul(pb[:, :], lhsT=dg[:, c * 4 + k, :], rhs=xp[:, T * 512 + k:T * 512 + k + 512],
                                                               start=(k == 0), stop=(k == 3)), reads=[R_dg, R_xp], writes=[R_pb])
                            fw.op("act", lambda e: e.activation(out=xact[:, c, T * 512:(T + 1) * 512], in_=pb[:, :], func=AF.Silu, bias=cbias[:, c:c + 1]),
                                  reads=[R_pb, R_c], writes=[R_xact[c]])
                    fw.barrier()
                NCH = NT
                dtv = sb("b_dt", [128, NCH, 8], F32, ph)
                dtA = sb("b_dtA", [128, NCH, 8], F32, ph)
                acs = sb("b_acs", [128, NCH, 8], F32, ph)
                nacs = sb("b_nacs", [128, NCH, 8], F32, ph)
                tot = sb("b_tot", [128, NCH, 8], F32, ph)
                cdb = sb("b_cdb", [128, NCH, 8], F32, ph)
                w2 = sb("b_w2", [128, NCH, 8], F32, ph)
                aexp = sb("b_aexp", [128, 8], F32, ph)
                R_q = Res()
                fw.op("dve", lambda e: e.tensor_tensor(out=dtv[:], in0=dt_all[:], in1=dtb[:].unsqueeze(1).to_broadcast([128, NCH, 8]), op=ALU.add),
                      reads=[R_dt, R_c], writes=[R_q])
                fw.op("act", lambda e: e.activation(out=dtv[:], in_=dtv[:], func=AF.Exp), reads=[R_q], writes=[R_q])
                fw.op("dve", lambda e: e.tensor_scalar(out=dtv[:], in0=dtv[:], scalar1=1.0, scalar2=None, op0=ALU.add), reads=[R_q], writes=[R_q])
                fw.op("act", lambda e: e.activation(out=dtv[:], in_=dtv[:], func=AF.Ln), reads=[R_q], writes=[R_q])
                fw.op("act", lambda e: e.activation(out=aexp[:], in_=alog[:], func=AF.Exp), reads=[R_c], writes=[R_q])
                fw.op("dve", lambda e: e.scalar_tensor_tensor(out=dtA[:], in0=dtv[:], scalar=-1.0, in1=aexp[:].unsqueeze(1).to_broadcast([128, NCH, 8]),
                                                              op0=ALU.mult, op1=ALU.mult), reads=[R_q], writes=[R_q])
                dtA_hi = sb("b_dtA_hi", [128, NCH, 8], BF16, ph)
                dtA_lo = sb("b_dtA_lo", [128, NCH, 8], BF16, ph)
                dtA_hf = sb("b_dtA_hf", [128, NCH, 8], F32, ph)
                triu_b = sb("b_triu_b", [128, 128], BF16, ph)
                fw.op("dve", lambda e: e.tensor_copy(out=triu_b[:], in_=triu[:]), reads=[R_c], writes=[R_c])
                fw.op("dve", lambda e: e.tensor_copy(out=dtA_hi[:], in_=dtA[:]), reads=[R_q], writes=[R_q])
                fw.op("dve", lambda e: e.tensor_copy(out=dtA_hf[:], in_=dtA_hi[:]), reads=[R_q], writes=[R_q])
                fw.op("dve", lambda e: e.tensor_tensor(out=dtA_lo[:], in0=dtA[:], in1=dtA_hf[:], op=ALU.subtract), reads=[R_q], writes=[R_q])
                fw.op("dve", lambda e: e.tensor_copy(out=dtA_hf[:], in_=dtA_lo[:]), reads=[R_q], writes=[R_q])
                fw.op("dve", lambda e: e.tensor_tensor(out=dtA[:], in0=dtA_hf[:], in1=dtA_hi[:], op=ALU.add), reads=[R_q], writes=[R_q])
                dtA2 = dtA[:].rearrange("p c h -> p (c h)")
                pb, R_pb = next_ps()
                fw.op("pe", lambda e: e.matmul(pb[:, 0:256], lhsT=triu[:], rhs=dtA2, start=True, stop=True), reads=[R_q, R_c], writes=[R_pb])
                fw.op("dve", lambda e: e.tensor_copy(out=acs[:].rearrange("p c h -> p (c h)"), in_=pb[:, 0:256]), reads=[R_pb], writes=[R_q])
                pb, R_pb = next_ps()
                fw.op("pe", lambda e: e.matmul(pb[:, 0:256], lhsT=ones_f[:], rhs=dtA2, start=True, stop=True), reads=[R_q, R_const], writes=[R_pb])
                fw.op("dve", lambda e: e.tensor_copy(out=tot[:].rearrange("p c h -> p (c h)"), in_=pb[:, 0:256]), reads=[R_pb], writes=[R_q])
                fw.op("dve", lambda e: e.tensor_scalar(out=nacs[:], in0=acs[:], scalar1=-1.0, scalar2=None, op0=ALU.mult), reads=[R_q], writes=[R_q])
                fw.op("act", lambda e: e.activation(out=cdb[:], in_=tot[:], func=AF.Exp), reads=[R_q], writes=[R_q])
                fw.op("dve", lambda e: e.tensor_tensor(out=w2[:], in0=tot[:], in1=acs[:], op=ALU.subtract), reads=[R_q], writes=[R_q])
                fw.op("act", lambda e: e.activation(out=w2[:], in_=w2[:], func=AF.Exp), reads=[R_q], writes=[R_q])
                fw.op("dve", lambda e: e.tensor_tensor(out=w2[:], in0=w2[:], in1=dtv[:], op=ALU.mult), reads=[R_q], writes=[R_q])
                state = sb("b_state", [128, 512], F32, ph)
                state_bf = sb("b_state_bf", [128, 512], BF16, ph)
                R_state = Res()
                R_sbf = Res()
                fw.op("dve", lambda e: e.memset(state[:], 0.0), writes=[R_state])
                fw.op("dve", lambda e: e.memset(state_bf[:], 0.0), writes=[R_sbf])
                xbtok = Ring([sb(f"b_xbtok{i}", [128, 768], BF16, ph) for i in range(2)])
                xdt = Ring([sb(f"b_xdt{i}", [128, 512], BF16, ph) for i in range(2)])
                xdtd = Ring([sb(f"b_xdtd{i}", [128, 512], BF16, ph) for i in range(2)])
                eacs = Ring([sb(f"b_eacs{i}", [128, 8, 128], BF16, ph) for i in range(2)])
                decT = Ring([sb(f"b_decT{i}", [128, 8, 128], BF16, ph) for i in range(2)])
                Mh = Ring([sb(f"b_Mh{i}", [128, 8, 128], BF16, ph) for i in range(2)])
                cms = Ring([sb(f"b_cms{i}", [128, 8, 128], BF16, ph) for i in range(2)])
                yacc = Ring([sb(f"b_yacc{i}", [128, 4, 512], F32, ph) for i in range(2)])
                sgz = Ring([sb(f"b_sgz{i}", [128, 4, 512], BF16, ph) for i in range(2)])
                sq = sb("b_sq", [128, 4, 512], F32, ph)
                R_sq = Res()
                rstd = sb("b_rstd", [128, 512], F32, ph)
                R_rstd = Res()
                obt = Ring([sb(f"b_obt{i}", [128, 4, 512], BF16, ph) for i in range(2)])
                SG_v = SG.rearrange("(ck p) t -> p ck t", p=128)
                MT_vb = MT.rearrange("(ck p) t -> p ck t", p=128)
                prepd = {}
                ystate = {}

                def prep(c):
                    tk = slice(c * 128, (c + 1) * 128)
                    pt_, R_pt = ps[7], R_ps[7]
                    ptb = pt_[:].bitcast(BF16)
                    for j in range(6):
                        fw.op("pe", lambda e: e.transpose(ptb[:, j * 128:(j + 1) * 128], xact[:, j, tk], ident_b[:]),
                              reads=[R_xact[j], R_const], writes=[R_pt])
                    xb, R_xb = xbtok.next()
                    fw.op("act", lambda e: e.activation(out=xb[:], in_=ptb[:, 0:768], func=AF.Copy), reads=[R_pt], writes=[R_xb])
                    xd, R_xd = xdt.next()
                    xdd, R_xdd = xdtd.next()
                    fw.op("dve", lambda e: e.tensor_tensor(out=xd[:].rearrange("p (h q) -> p h q", h=8), in0=xb[:, 0:512].rearrange("p (h q) -> p h q", h=8),
                                                           in1=dtv[:, c, :].unsqueeze(2).to_broadcast([128, 8, 64]), op=ALU.mult),
                          reads=[R_xb, R_q], writes=[R_xd])
                    fw.op("dve", lambda e: e.tensor_tensor(out=xdd[:].rearrange("p (h q) -> p h q", h=8), in0=xb[:, 0:512].rearrange("p (h q) -> p h q", h=8),
                                                           in1=w2[:, c, :].unsqueeze(2).to_broadcast([128, 8, 64]), op=ALU.mult),
                          reads=[R_xb, R_q], writes=[R_xdd])
                    for g in range(2):
                        fw.op("pe", lambda e: e.matmul(ps[4][:, g * 128:(g + 1) * 128], lhsT=xact[:, 4 + g, tk], rhs=xact[:, 6 + g, tk], start=True, stop=True),
                              reads=[R_xact[4 + g], R_xact[6 + g]], writes=[R_ps[4]])
                    for h in range(8):
                        bk = h // 4
                        hs = slice((h % 4) * 128, (h % 4 + 1) * 128)
                        lbh = dtA_hi[:, c, h:h + 1].to_broadcast([128, 128])
                        lbl = dtA_lo[:, c, h:h + 1].to_broadcast([128, 128])
                        fw.op("pe", lambda e: e.matmul(ps[bk][:, hs], lhsT=lbh, rhs=triu_b[:], start=True, stop=False),
                              reads=[R_q, R_c], writes=[R_ps[bk]])
                        fw.op("pe", lambda e: e.matmul(ps[bk][:, hs], lhsT=lbl, rhs=triu_b[:], start=False, stop=True),
                              reads=[R_q, R_c], writes=[R_ps[bk]])
                        fw.op("pe", lambda e: e.matmul(ps[2 + bk][:, hs], lhsT=lbh, rhs=triu_b[:], start=True, stop=False),
                              reads=[R_q, R_c], writes=[R_ps[2 + bk]])
                        fw.op("pe", lambda e: e.matmul(ps[2 + bk][:, hs], lhsT=lbl, rhs=triu_b[:], start=False, stop=False),
                              reads=[R_q, R_c], writes=[R_ps[2 + bk]])
                        fw.op("pe", lambda e: e.matmul(ps[2 + bk][:, hs], lhsT=ident_b[:], rhs=trineg_b[:], start=False, stop=True),
                              reads=[R_const, R_c], writes=[R_ps[2 + bk]])
                    ea, R_ea = eacs.next()
                    for bk in range(2):
                        fw.op("act", lambda e: e.activation(out=ea[:, bk * 4:(bk + 1) * 4, :].rearrange("p a b -> p (a b)"), in_=ps[bk][:, :], func=AF.Exp),
                              reads=[R_ps[bk]], writes=[R_ea])
                    dc, R_dc = decT.next()
                    for h in range(8):
                        bk = h // 4
                        hs = slice((h % 4) * 128, (h % 4 + 1) * 128)
                        fw.op("act", lambda e: e.activation(out=dc[:, h, :], in_=ps[2 + bk][:, hs], func=AF.Exp, bias=nacs[:, c, h:h + 1]),
                              reads=[R_ps[2 + bk], R_q], writes=[R_dc])
                    mh, R_mh = Mh.next()
                    cm_, R_cm = cms.next()
                    for g in range(2):
                        fw.op("dve", lambda e: e.tensor_tensor(out=mh[:, g * 4:(g + 1) * 4, :], in0=dc[:, g * 4:(g + 1) * 4, :],
                                                               in1=ps[4][:, g * 128:(g + 1) * 128].unsqueeze(1).to_broadcast([128, 4, 128]), op=ALU.mult),
                              reads=[R_dc, R_ps[4]], writes=[R_mh])
                        fw.op("pool", lambda e: e.tensor_tensor(out=cm_[:, g * 4:(g + 1) * 4, :], in0=ea[:, g * 4:(g + 1) * 4, :],
                                                                in1=xact[:, 6 + g, tk].unsqueeze(1).to_broadcast([128, 4, 128]), op=ALU.mult),
                              reads=[R_ea, R_xact[6 + g]], writes=[R_cm])
                    prepd[c] = (xb, R_xb, xd, R_xd, xdd, R_xdd, mh, R_mh, cm_, R_cm)

                def fin(c):
                    T = c // 4
                    tk = slice(c * 128, (c + 1) * 128)
                    xb, R_xb, xd, R_xd, xdd, R_xdd, mh, R_mh, cm_, R_cm = prepd.pop(c)
                    if c % 4 == 0:
                        ystate["ya"] = yacc.next()
                        ystate["sg"] = sgz.next()
                        sgt, R_sgt = ystate["sg"]
                        fw.dma("sp", sgt[:], SG_v[:, 4:8, T * 512:(T + 1) * 512], reads=R_SG[4:8], writes=[R_sgt])
                    ya, R_ya = ystate["ya"]
                    sgt, R_sgt = ystate["sg"]
                    for h in range(8):
                        yo = ps[6][(h % 2) * 64:(h % 2 + 1) * 64, (h // 2) * 128:(h // 2 + 1) * 128]
                        fw.op("pe", lambda e: e.matmul(yo, lhsT=xd[:, h * 64:(h + 1) * 64], rhs=mh[:, h, :], start=True, stop=False),
                              reads=[R_xd, R_mh], writes=[R_ps[6]])
                        fw.op("pe", lambda e: e.matmul(yo, lhsT=state_bf[:, h * 64:(h + 1) * 64], rhs=cm_[:, h, :], start=False, stop=True),
                              reads=[R_sbf, R_cm], writes=[R_ps[6]])
                    for g in range(2):
                        fw.op("pe", lambda e: e.matmul(ps[5][:, g * 256:(g + 1) * 256], lhsT=xb[:, 512 + g * 128:512 + (g + 1) * 128],
                                                       rhs=xdd[:, g * 256:(g + 1) * 256], start=True, stop=True),
                              reads=[R_xb, R_xdd], writes=[R_ps[5]])
                    fw.op("dve", lambda e: e.tensor_tensor(out=state[:].rearrange("p (h q) -> p h q", h=8), in0=state[:].rearrange("p (h q) -> p h q", h=8),
                                                           in1=cdb[:, c, :].unsqueeze(2).to_broadcast([128, 8, 64]), op=ALU.mult),
                          reads=[R_state, R_q], writes=[R_state])
                    fw.op("dve", lambda e: e.tensor_tensor(out=state[:], in0=state[:], in1=ps[5][:, :], op=ALU.add),
                          reads=[R_state, R_ps[5]], writes=[R_state])
                    fw.op("act", lambda e: e.activation(out=state_bf[:], in_=state[:], func=AF.Copy), reads=[R_state], writes=[R_sbf])
                    for pr in range(4):
                        fw.op("dve", lambda e: e.scalar_tensor_tensor(out=ya[:, pr, (c % 4) * 128:(c % 4 + 1) * 128], in0=xact[:, pr, tk],
                                                                      scalar=dsk[:, pr:pr + 1], in1=ps[6][:, pr * 128:(pr + 1) * 128],
                                                                      op0=ALU.mult, op1=ALU.add),
                              reads=[R_xact[pr], R_c, R_ps[6]], writes=[R_ya])
                    if c % 4 == 3:
                        def e1(ya=ya, R_ya=R_ya, sgt=sgt, R_sgt=R_sgt):
                            fw.op("dve", lambda e: e.tensor_tensor(out=ya[:], in0=ya[:], in1=sgt[:], op=ALU.mult), reads=[R_ya, R_sgt], writes=[R_ya])
                            fw.op("act", lambda e: e.activation(out=sq[:], in_=ya[:], func=AF.Square), reads=[R_ya], writes=[R_sq])

                        def e2():
                            for pr in range(4):
                                fw.op("pe", lambda e: e.matmul(ps[7][:, :], lhsT=ones_f[:], rhs=sq[:, pr, :], start=(pr == 0), stop=(pr == 3)),
                                      reads=[R_const, R_sq], writes=[R_ps[7]])
                            fw.op("dve", lambda e: e.tensor_scalar(out=rstd[:], in0=ps[7][:, :], scalar1=1.0 / 512, scalar2=EPS, op0=ALU.mult, op1=ALU.add),
                                  reads=[R_ps[7]], writes=[R_rstd])
                            fw.op("act", lambda e: e.activation(out=rstd[:], in_=rstd[:], func=AF.Ln), reads=[R_rstd], writes=[R_rstd])
                            fw.op("act", lambda e: e.activation(out=rstd[:], in_=rstd[:], func=AF.Exp, scale=-0.5), reads=[R_rstd], writes=[R_rstd])

                        def e3(ya=ya, R_ya=R_ya, T=T):
                            ob_, R_ob = obt.next()
                            for pr in range(4):
                                fw.op("dve", lambda e: e.scalar_tensor_tensor(out=ob_[:, pr, :], in0=ya[:, pr, :], scalar=gn[:, pr:pr + 1], in1=rstd[:],
                                                                              op0=ALU.mult, op1=ALU.mult),
                                      reads=[R_ya, R_c, R_rstd], writes=[R_ob])
                            fw.dma("sp", MT_vb[:, 4:8, T * 512:(T + 1) * 512], ob_[:], reads=[R_ob], writes=R_MT[4:8])
                        e1()
                        epi.append([c + 1, e2])
                        epi.append([c + 2, e3])

                epi = []

                def run_epi(c):
                    while epi and epi[0][0] <= c:
                        epi.pop(0)[1]()

                prep(0)
                for c in range(NCH):
                    if c + 1 < NCH:
                        prep(c + 1)
                    fin(c)
                    run_epi(c)
                run_epi(NCH + 5)
                fw.barrier()


        if "A" in STAGES:
            with ExitStack() as ph:
                nscale = 64.0 ** -0.5
                R_ac = Res()
                kcT = sb("a_kcT", [64, 2, 256], BF16, ph)
                vcx = sb("a_vcx", [128, 2, 2, 128], BF16, ph)
                ovx = sb("a_ovx", [128, 2, 65], BF16, ph)
                W3 = sb("a_W3", [128, 3072], BF16, ph)
                Wc = sb("a_Wc", [128, 896], BF16, ph)
                Ww = sb("a_Ww", [128, 1536], BF16, ph)
                exg = sb("a_exg", [24, 1536], BF16, ph)
                GLs = sb("a_GLs", [24, S], BF16, ph)
                ksEx = sb("a_ksEx", [128, S], BF16, ph)
                R_ksEx = Res()
                fw.op("dve", lambda e: e.memset(kcT[:], 0.0), writes=[R_ac])
                fw.op("dve", lambda e: e.memset(vcx[:], 1.0), writes=[R_ac])
                fw.dma("sp", GLs[:], GL[:, :], reads=[R_GL], writes=[R_ac])
                phL = ExitStack()
                kv_ring = Ring([sb(f"a_kvsb{i}", [128, S], BF16, phL) for i in range(2)])
                w1b_ring = Ring([sb(f"a_w1b{i}", [128, 32, 256], BF16, phL) for i in range(2)])
                pre_cmp = []
                w1st = Ring([sb(f"a_w1st{i}", [64, 8, 256], F32, phL) for i in range(2)])
                for prow, w1 in ((P_KC, w_cmp1_k), (P_VC, w_cmp1_v)):
                    kv_sb, R_kvsb = kv_ring.next()
                    fw.dma("sp", kv_sb[:], PT[prow:prow + 128, :], reads=[R_PT[prow // 128]], writes=[R_kvsb])
                    w1v = w1.rearrange("(l d) h -> d l h", d=64)
                    w1b, R_w1b = w1b_ring.next()
                    for lq in range(4):
                        ws, R_ws = w1st.next()
                        fw.dma("sp", ws[:], w1v[:, lq * 8:(lq + 1) * 8, :], writes=[R_ws])
                        if lq % 2 == 0:
                            fw.op("dve", lambda e: e.tensor_copy(out=w1b[0:64, lq * 8:(lq + 1) * 8, :], in_=ws[:]), reads=[R_ws], writes=[R_w1b])
                        else:
                            fw.op("act", lambda e: e.activation(out=w1b[0:64, lq * 8:(lq + 1) * 8, :], in_=ws[:], func=AF.Copy), reads=[R_ws], writes=[R_w1b])
                    fw.dma("sp", w1b[64:128, :, :], w1b[0:64, :, :], reads=[R_w1b], writes=[R_w1b])
                    pre_cmp.append((kv_sb, R_kvsb, w1b, R_w1b))
                phC = ExitStack()
                cstep = build_phase_c(phC) if "C" in STAGES else (lambda k: None)
                with ExitStack() as ph2:
                    for dst, src in ((W3, c_w3), (Wc, c_wc), (Ww, c_ww)):
                        fw.dma("sp", dst[:], src[:, :], writes=[R_ac])
                    fw.dma("sp", ovx[:].rearrange("p a b -> p (a b)"), c_ovx[:, :], writes=[R_ac])
                    fw.dma("sp", exg[:], c_exg[:, :], writes=[R_ac])
                    fw.dma("sp", ksEx[64:128, :], c_ex[:, :], writes=[R_ksEx])
                    Cts = sb("a_Cts", [64, 256], F32, ph2)
                    Sts = sb("a_Sts", [64, 256], F32, ph2)
                    fw.dma("sp", Cts[:, 0:255], CS[0, 0:64, 0:255], reads=[R_CS], writes=[R_ac])
                    fw.dma("sp", Sts[:, 0:255], CS[1, 0:64, 0:255], reads=[R_CS], writes=[R_ac])
                    w2st = sb("a_w2st", [128, 2, 64], F32, ph2)
                    w2b = sb("a_w2b", [128, 2, 64], BF16, ph2)
                    posst = sb("a_posst", [32, 128], F32, ph2)
                    posb = sb("a_posb", [128, 32], BF16, ph2)
                    hT = sb("a_hT", [128, 2, 256], BF16, ph2)
                    hbias = sb("a_hbias", [128, 2], F32, ph2)
                    q32 = sb("a_q32", [64, 256], F32, ph2)
                    tq = sb("a_tq", [64, 256], F32, ph2)
                    R_m = Res()
                    fw.op("dve", lambda e: e.memset(hT[:], 0.0), writes=[R_m])
                    for which, (prow, w1, w2, pos) in enumerate(((P_KC, w_cmp1_k, w_cmp2_k, cmp_pos_k), (P_VC, w_cmp1_v, w_cmp2_v, cmp_pos_v))):
                        kv_sb, R_kvsb, w1b, R_w1b = pre_cmp[which]
                        fw.dma("sp", w2st[:], w2.rearrange("(c p) d -> p c d", p=128), reads=[R_m], writes=[R_m])
                        fw.op("dve", lambda e: e.tensor_copy(out=w2b[:], in_=w2st[:]), reads=[R_m], writes=[R_m])
                        for half in range(2):
                            fw.dma("sp", posst[0:32, half * 64:(half + 1) * 64], pos[:, :], reads=[R_m], writes=[R_m])
                        pbt, R_pbt = next_ps(0, 3)
                        fw.op("pe", lambda e: e.transpose(pbt[:, 0:32], posst[0:32, :], ident_f[0:32, 0:32]), reads=[R_m, R_const], writes=[R_pbt])
                        fw.op("dve", lambda e: e.tensor_copy(out=posb[:], in_=pbt[:, 0:32]), reads=[R_pbt], writes=[R_m])
                        for g in range(2):
                            gs = slice(g * 64, (g + 1) * 64)
                            for hc in range(2):
                                pb, R_pb = next_ps(0, 3)
                                pbb_, R_pbb = ps[7], R_ps[7]
                                for l in range(32):
                                    fw.op("pe", lambda e: e.matmul(pb[:, 0:255], lhsT=w1b[gs, l, hc * 128:(hc + 1) * 128],
                                                                   rhs=kv_sb[gs, l:l + 16 * 254 + 1:16], start=(l == 0), stop=(l == 31)),
                                          reads=[R_w1b, R_kvsb], writes=[R_pb])
                                for l in range(32):
                                    fw.op("pe", lambda e: e.matmul(pbb_[:, 0:1], lhsT=w1b[gs, l, hc * 128:(hc + 1) * 128],
                                                                   rhs=posb[gs, l:l + 1], start=(l == 0), stop=(l == 31)),
                                          reads=[R_w1b, R_m], writes=[R_pbb])
                                fw.op("dve", lambda e: e.tensor_copy(out=hbias[:, hc:hc + 1], in_=pbb_[:, 0:1]), reads=[R_pbb], writes=[R_m])
                                fw.op("act", lambda e: e.activation(out=hT[:, hc, 0:255], in_=pb[:, 0:255], func=AF.Silu, bias=hbias[:, hc:hc + 1]),
                                      reads=[R_pb, R_m], writes=[R_m])
                                cstep(4)
                            if which == 0:
                                pb, R_pb = next_ps(0, 3)
                                for hc in range(2):
                                    fw.op("pe", lambda e: e.matmul(pb[0:64, 0:256], lhsT=w2b[:, hc, :], rhs=hT[:, hc, :], start=(hc == 0), stop=(hc == 1)),
                                          reads=[R_m], writes=[R_pb])
                                fw.op("dve", lambda e: e.tensor_copy(out=q32[:], in_=pb[0:64, 0:256]), reads=[R_pb], writes=[R_m])
                                pb2, R_pb2 = next_ps(0, 3)
                                fw.op("pe", lambda e: e.matmul(pb2[0:64, 0:256], lhsT=psw_f[0:64, 0:64], rhs=q32[:], start=True, stop=True),
                                      reads=[R_m, R_const], writes=[R_pb2])
                                fw.op("dve", lambda e: e.tensor_tensor(out=tq[:, 0:255], in0=pb2[0:64, 0:255], in1=Sts[:, 0:255], op=ALU.mult),
                                      reads=[R_pb2, R_ac], writes=[R_m])
                                fw.op("dve", lambda e: e.tensor_tensor(out=q32[:, 0:255], in0=q32[:, 0:255], in1=Cts[:, 0:255], op=ALU.mult),
                                      reads=[R_m, R_ac], writes=[R_m])
                                fw.op("dve", lambda e: e.tensor_tensor(out=kcT[:, g, 0:255], in0=q32[:, 0:255], in1=tq[:, 0:255], op=ALU.add),
                                      reads=[R_m], writes=[R_ac])
                            else:
                                for nch in range(2):
                                    pb, R_pb = next_ps(0, 3)
                                    for hc in range(2):
                                        fw.op("pe", lambda e: e.matmul(pb[:, 0:64], lhsT=hT[:, hc, nch * 128:(nch + 1) * 128], rhs=w2b[:, hc, :],
                                                                       start=(hc == 0), stop=(hc == 1)), reads=[R_m], writes=[R_pb])
                                    fw.op("dve", lambda e: e.tensor_copy(out=vcx[:, g, nch, 0:64], in_=pb[:, 0:64]), reads=[R_pb], writes=[R_ac])
                    cstep(40)
                    fw.barrier()
                phC.close()
                phL.close()
                if DEBUG:
                    DBGK = nc.dram_tensor("DBGK", [64, 512], BF16, kind="ExternalOutput").ap()
                    DBGV = nc.dram_tensor("DBGV", [128, 512], BF16, kind="ExternalOutput").ap()
                    DBGS = nc.dram_tensor("DBGS", [128, 2 * S], BF16, kind="ExternalOutput").ap()
                    fw.dma("sp", DBGK[:, :], kcT[:].rearrange("p a b -> p (a b)"), reads=[R_ac], writes=[Res()])
                    fw.dma("sp", DBGV[:, :], vcx[:].rearrange("p a b c -> p (a b c)"), reads=[R_ac], writes=[Res()])
                qS = [sb(f"a_qS{e}", [128, S], BF16, ph) for e in range(4)]
                R_qS = [[Res() for _ in range(NTT)] for _ in range(4)]
                kwT = sb("a_kwT", [64, S], BF16, ph)
                R_kw = Res()
                vsx = sb("a_vsx", [128, NT, 128], BF16, ph)
                vwx = sb("a_vwx", [128, NT, 128], BF16, ph)
                R_v = Res()
                fw.op("dve", lambda e: e.memset(vsx[:], 1.0), writes=[R_v])
                fw.op("dve", lambda e: e.memset(vwx[:], 1.0), writes=[R_v])
                vst = sb("a_vst", [128, NT, 64], F32, ph)
                R_vst = Res()
                pr_ = Ring([sb(f"a_p{i}", [128, 512], BF16, ph) for i in range(6)])
                accs = [[sb(f"a_acc{par}_{e}", [64, 512], F32, ph) for e in range(4)] for par in range(2)]
                R_acc = [[Res() for _ in range(4)] for _ in range(2)]
                rdn = Ring([sb(f"a_rdn{i}", [64, 512], F32, ph) for i in range(3)])
                tmpo = Ring([sb(f"a_tmpo{i}", [64, 512], F32, ph) for i in range(2)])
                sga = Ring([sb(f"a_sga{i}", [64, 512], BF16, ph) for i in range(3)])
                oo = Ring([sb(f"a_oo{i}", [64, 512], BF16, ph) for i in range(2)])
                impacc = sb("a_impacc", [128, 4, 64], F32, ph)
                imptmp = sb("a_imptmp", [128, 4, 64], F32, ph)
                irec = sb("a_irec", [128, 4, 1], F32, ph)
                addm = sb("a_addm", [128, 4, 64], F32, ph)
                m8 = sb("a_m8", [128, 16], F32, ph)
                wk = sb("a_wk", [128, 64], F32, ph)
                nsel = sb("a_nsel", [128, 4, 64], BF16, ph)
                R_imp = Res()
                R_nsel = Res()
                R_addm = Res()
                nselT = sb("a_nselT", [64, 512], BF16, ph)
                R_nselT = Res()
                VT_v = VT.rearrange("(t p) f -> p t f", p=128)
                addm_v = c_addm.rearrange("(t p) j -> p t j", p=128)
                psI, R_psI = ps[6], R_ps[6]
                psG, R_psG = ps[7], R_ps[7]
                psTb = psI[:].bitcast(BF16)
                obank = {"i": 0}
                sbank = {"i": 0}

                def next_o():
                    i = 3 + obank["i"] % 3
                    obank["i"] += 1
                    return ps[i], R_ps[i]

                def next_s():
                    i = sbank["i"] % 3
                    sbank["i"] += 1
                    return ps[i], R_ps[i]

                def finish_branch(po, R_po, acc, R_a, h, br, first, ts):
                    rd, R_rd = rdn.next()
                    if br in (0, 1):
                        fw.op("act", lambda e: e.activation(out=rd[:], in_=po[64:128, :], func=AF.Ln, bias=epsb[0:64, 0:1]),
                              reads=[R_po, R_const], writes=[R_rd])
                        fw.op("act", lambda e: e.activation(out=rd[:], in_=rd[:], func=AF.Exp, scale=-1.0), reads=[R_rd], writes=[R_rd])
                    else:
                        fw.op("dve", lambda e: e.reciprocal(out=rd[:], in_=po[64:128, :]), reads=[R_po], writes=[R_rd])
                    fw.op("pe", lambda e: e.matmul(psG[0:64, :], lhsT=exg[:, (h * 3 + br) * 64:(h * 3 + br + 1) * 64], rhs=GLs[:, ts],
                                                   start=True, stop=True), reads=[R_ac], writes=[R_psG])
                    fw.op("dve", lambda e: e.tensor_tensor(out=rd[:], in0=rd[:], in1=psG[0:64, :], op=ALU.mult), reads=[R_rd, R_psG], writes=[R_rd])
                    if first:
                        fw.op("dve", lambda e: e.tensor_tensor(out=acc[:], in0=po[0:64, :], in1=rd[:], op=ALU.mult),
                              reads=[R_po, R_rd], writes=[R_a])
                    else:
                        tt_, R_tt = tmpo.next()
                        fw.op("dve", lambda e: e.tensor_tensor(out=tt_[:], in0=po[0:64, :], in1=rd[:], op=ALU.mult), reads=[R_po, R_rd], writes=[R_tt])
                        fw.op("pool", lambda e: e.tensor_tensor(out=acc[:], in0=acc[:], in1=tt_[:], op=ALU.add),
                              reads=[R_tt, R_a], writes=[R_a])

                def run_jobs(jobs, L=2):
                    n = len(jobs)
                    for i in range(n + L):
                        if i < n:
                            jobs[i][0]()
                        if i - L >= 0:
                            jobs[i - L][1]()

                def make_job(score_fn, pv_fn):
                    st = {}
                    return (lambda: score_fn(st), lambda: pv_fn(st))

                for g in range(2):
                    for e_ in range(4):
                        h = g * 4 + e_
                        fw.dma("sp", qS[e_][0:64, :], PT[P_Q + h * 64:P_Q + (h + 1) * 64, :], reads=[R_PT[(P_Q + h * 64) // 128]] + R_qS[e_], writes=R_qS[e_])
                    fw.dma("sp", ksEx[0:64, :], PT[P_KS + g * 64:P_KS + (g + 1) * 64, :], reads=[R_PT[P_KS // 128], R_ksEx], writes=[R_ksEx])
                    fw.dma("sp", kwT[:], PT[P_KW + g * 64:P_KW + (g + 1) * 64, :], reads=[R_PT[P_KW // 128], R_kw], writes=[R_kw])
                    for dst, c0 in ((vsx, 128 + g * 64), (vwx, 256 + g * 64)):
                        for q4 in range(4):
                            fw.dma("sp", vst[:, q4 * 8:(q4 + 1) * 8, :], VT_v[:, q4 * 8:(q4 + 1) * 8, c0:c0 + 64], reads=[R_VT, R_vst], writes=[R_vst])
                        fw.op("dve", lambda e: e.tensor_copy(out=dst[:, :, 0:64], in_=vst[:]), reads=[R_vst, R_v], writes=[R_v, R_vst])

                    def cmp_jobs(T):
                        ts = slice(T * 512, (T + 1) * 512)
                        nchs = [0] if T < 4 else [0, 1]
                        jobs = []
                        for e_ in range(4):
                            h = g * 4 + e_
                            hold = {}
                            for i, nch in enumerate(nchs):
                                def score(st, e_=e_, nch=nch):
                                    pa, R_pa = next_s()
                                    need_mask = not (nch == 0 and T >= 5)
                                    fw.op("pe", lambda e: e.matmul(pa[:, :], lhsT=kcT[:, g, nch * 128:(nch + 1) * 128], rhs=qS[e_][0:64, ts],
                                                                   start=True, stop=not need_mask), reads=[R_ac, R_qS[e_][T]], writes=[R_pa])
                                    if need_mask:
                                        sh = 512 * T - 2048 * nch
                                        fw.op("pe", lambda e: e.matmul(pa[:, :], lhsT=ident_b[:], rhs=W3[:, sh:sh + 512], start=False, stop=True),
                                              reads=[R_ac, R_const], writes=[R_pa])
                                    p, R_p = pr_.next()
                                    fw.op("act", lambda e: e.activation(out=p[:], in_=pa[:, :], func=AF.Exp, scale=nscale), reads=[R_pa], writes=[R_p])
                                    st["p"] = (p, R_p)

                                def pv(st, e_=e_, h=h, i=i, nch=nch, hold=hold):
                                    p, R_p = st["p"]
                                    if i == 0:
                                        hold["po"] = next_o()
                                    po, R_po = hold["po"]
                                    last = (i == len(nchs) - 1)
                                    fw.op("pe", lambda e: e.matmul(po[:, :], lhsT=vcx[:, g, nch, :], rhs=p[:], start=(i == 0), stop=last),
                                          reads=[R_ac, R_p], writes=[R_po])
                                    for sub in range(4):
                                        fw.op("pe", lambda e: e.matmul(psI[:, sub * 65:(sub + 1) * 65], lhsT=p[:, sub * 128:(sub + 1) * 128], rhs=ovx[:, nch, :],
                                                                       start=(i == 0 and sub == 0), stop=(last and sub == 3), skip_group_check=True),
                                              reads=[R_ac, R_p], writes=[R_psI])
                                    if last:
                                        finish_branch(po, R_po, accs[T % 2][e_], R_acc[T % 2][e_], h, 0, True, ts)
                                        pI = psI[:, 0:260].rearrange("p (s f) -> p s f", s=4)
                                        fw.op("dve", lambda e: e.tensor_scalar(out=irec[:], in0=pI[:, :, 64:65], scalar1=1e-30, scalar2=None, op0=ALU.max),
                                              reads=[R_psI], writes=[R_imp])
                                        fw.op("dve", lambda e: e.reciprocal(out=irec[:], in_=irec[:]), reads=[R_imp], writes=[R_imp])
                                        if e_ == 0:
                                            fw.op("dve", lambda e: e.tensor_tensor(out=impacc[:], in0=pI[:, :, 0:64], in1=irec[:].to_broadcast([128, 4, 64]), op=ALU.mult),
                                                  reads=[R_psI, R_imp], writes=[R_imp])
                                        else:
                                            fw.op("dve", lambda e: e.tensor_tensor(out=imptmp[:], in0=pI[:, :, 0:64], in1=irec[:].to_broadcast([128, 4, 64]), op=ALU.mult),
                                                  reads=[R_psI, R_imp], writes=[R_imp])
                                            fw.op("dve", lambda e: e.tensor_tensor(out=impacc[:], in0=impacc[:], in1=imptmp[:], op=ALU.add), reads=[R_imp], writes=[R_imp])
                                jobs.append(make_job(score, pv))
                        return jobs

                    def sel_dve(T):
                        fw.dma("sp", addm[:], addm_v[:, T * 4:(T + 1) * 4, :], reads=[R_addm], writes=[R_addm])
                        fw.op("dve", lambda e: e.tensor_tensor(out=impacc[:], in0=impacc[:], in1=addm[:], op=ALU.add), reads=[R_imp, R_addm], writes=[R_imp, R_addm])
                        for sub in range(4):
                            fw.op("dve", lambda e: e.max(out=m8[:, 0:8], in_=impacc[:, sub, :]), reads=[R_imp], writes=[R_imp])
                            fw.op("dve", lambda e: e.match_replace(out=wk[:], in_to_replace=m8[:, 0:8], in_values=impacc[:, sub, :], imm_value=-3.0e9),
                                  reads=[R_imp], writes=[R_imp])
                            fw.op("dve", lambda e: e.max(out=m8[:, 8:16], in_=wk[:]), reads=[R_imp], writes=[R_imp])
                            fw.op("dve", lambda e: e.tensor_scalar(out=nsel[:, sub, :], in0=impacc[:, sub, :], scalar1=m8[:, 15:16], scalar2=NEG,
                                                                   op0=ALU.is_lt, op1=ALU.mult), reads=[R_imp], writes=[R_nsel])

                    def sel_pe(T):
                        ts = slice(T * 512, (T + 1) * 512)
                        for sub in range(4):
                            fw.op("pe", lambda e: e.transpose(psTb[0:64, sub * 128:(sub + 1) * 128], nsel[:, sub, :], ident_b[:]),
                                  reads=[R_nsel, R_const], writes=[R_psI])
                        fw.op("dve", lambda e: e.tensor_copy(out=nselT[:], in_=psTb[0:64, 0:512]), reads=[R_psI], writes=[R_nselT])
                        for e_ in range(4):
                            fw.dma("sp", qS[e_][64:128, ts], nselT[:], reads=[R_nselT], writes=[R_qS[e_][T]])

                    def selwin_jobs(T):
                        ts = slice(T * 512, (T + 1) * 512)
                        jobs = []
                        for e_ in range(4):
                            h = g * 4 + e_
                            acc, R_a = accs[T % 2][e_], R_acc[T % 2][e_]
                            nk = 4 * T + 4
                            hold_s = {}
                            for k in range(nk):
                                def score(st, e_=e_, k=k):
                                    pa, R_pa = next_s()
                                    diag = k >= 4 * T
                                    i = k - 4 * T
                                    c0 = i * 128 if diag else 0
                                    tq = slice(T * 512 + c0, (T + 1) * 512)
                                    fw.op("pe", lambda e: e.matmul(pa[:, c0:512], lhsT=ksEx[:, k * 128:(k + 1) * 128], rhs=qS[e_][:, tq], start=True, stop=not diag),
                                          reads=[R_ksEx, R_qS[e_][T]], writes=[R_pa])
                                    if diag:
                                        fw.op("pe", lambda e: e.matmul(pa[:, c0:512], lhsT=ident_b[:], rhs=Wc[:, 384 - i * 128 + c0:384 - i * 128 + 512], start=False, stop=True),
                                              reads=[R_ac, R_const], writes=[R_pa])
                                    p, R_p = pr_.next()
                                    fw.op("act", lambda e: e.activation(out=p[:, c0:512], in_=pa[:, c0:512], func=AF.Exp, scale=nscale), reads=[R_pa], writes=[R_p])
                                    st["p"] = (p, R_p, c0)

                                def pv(st, e_=e_, h=h, k=k, nk=nk, hold=hold_s, acc=acc, R_a=R_a):
                                    p, R_p, c0 = st["p"]
                                    if k == 0:
                                        hold["po"] = next_o()
                                    po, R_po = hold["po"]
                                    fw.op("pe", lambda e: e.matmul(po[:, c0:512], lhsT=vsx[:, k, :], rhs=p[:, c0:512], start=(k == 0), stop=(k == nk - 1),
                                                                   skip_group_check=True),
                                          reads=[R_v, R_p], writes=[R_po])
                                    if k == nk - 1:
                                        finish_branch(po, R_po, acc, R_a, h, 1, False, ts)
                                jobs.append(make_job(score, pv))
                            ks_ = [k for k in range(4 * T - 4, 4 * T + 4) if k >= 0]
                            hold_w = {}
                            for j, k in enumerate(ks_):
                                def score(st, e_=e_, k=k):
                                    pa, R_pa = next_s()
                                    i = k - 4 * T
                                    c0 = max(0, i * 128)
                                    c1 = min(512, (i + 5) * 128)
                                    tq = slice(T * 512 + c0, T * 512 + c1)
                                    fw.op("pe", lambda e: e.matmul(pa[:, c0:c1], lhsT=kwT[:, k * 128:(k + 1) * 128], rhs=qS[e_][0:64, tq], start=True, stop=False),
                                          reads=[R_kw, R_qS[e_][T]], writes=[R_pa])
                                    fw.op("pe", lambda e: e.matmul(pa[:, c0:c1], lhsT=ident_b[:], rhs=Ww[:, 512 - i * 128 + c0:512 - i * 128 + c1], start=False, stop=True),
                                          reads=[R_ac, R_const], writes=[R_pa])
                                    p, R_p = pr_.next()
                                    fw.op("act", lambda e: e.activation(out=p[:, c0:c1], in_=pa[:, c0:c1], func=AF.Exp, scale=nscale), reads=[R_pa], writes=[R_p])
                                    st["p"] = (p, R_p, c0, c1)

                                def pv(st, e_=e_, h=h, j=j, k=k, nw=len(ks_), hold=hold_w, acc=acc, R_a=R_a):
                                    p, R_p, c0, c1 = st["p"]
                                    if j == 0:
                                        hold["po"] = next_o()
                                    po, R_po = hold["po"]
                                    fw.op("pe", lambda e: e.matmul(po[:, c0:c1], lhsT=vwx[:, k, :], rhs=p[:, c0:c1], start=(j == 0), stop=(j == nw - 1),
                                                                   skip_group_check=True),
                                          reads=[R_v, R_p], writes=[R_po])
                                    if j == nw - 1:
                                        finish_branch(po, R_po, acc, R_a, h, 2, False, ts)
                                        sg_, R_sg = sga.next()
                                        fw.dma("sp", sg_[:], SG[h * 64:(h + 1) * 64, ts], reads=[R_SG[h // 2]], writes=[R_sg])
                                        o_, R_o = oo.next()
                                        fw.op("pool", lambda e: e.tensor_tensor(out=o_[:], in0=acc[:], in1=sg_[:], op=ALU.mult),
                                              reads=[R_a, R_sg], writes=[R_o])
                                        fw.dma("pool", MT[h * 64:(h + 1) * 64, ts], o_[:], reads=[R_o], writes=[R_MT[h // 2]])
                                jobs.append(make_job(score, pv))
                        return jobs

                    run_jobs(cmp_jobs(0))
                    sel_dve(0)
                    sel_pe(0)
                    for T in range(NTT):
                        if T + 1 < NTT:
                            run_jobs(cmp_jobs(T + 1))
                            sel_dve(T + 1)
                        run_jobs(selwin_jobs(T))
                        if T + 1 < NTT:
                            sel_pe(T + 1)
                fw.barrier()

        active = []
        if "A" in STAGES:
            active += [0, 1, 2, 3]
        if "B" in STAGES:
            active += [4, 5, 6, 7]
        if "C" in STAGES:
            active += [8, 9, 10, 11]
        if "F" in STAGES:
            with ExitStack() as ph:
                gf = sb("gf", [128, D], F32, ph)
                R_gf = Res()
                fw.dma("sp", gf[:], g_final.partition_broadcast(128), writes=[R_gf])
                mt = Ring([sb(f"mt{i}", [128, 12, 512], BF16, ph) for i in range(2)])
                xin = Ring([sb(f"fxin{i}", [128, D], F32, ph) for i in range(2)])
                hb = Ring([sb(f"hb{i}", [128, D], F32, ph) for i in range(2)])
                ob = Ring([sb(f"ob{i}", [128, D], F32, ph) for i in range(2)])
                junk = sb("fjunk", [128, D], BF16, ph)
                R_junk = Res()
                st = Ring([sb(f"fst{i}", [128, 4], F32, ph) for i in range(2)])
                MT_v = MT.rearrange("(ck p) t -> p ck t", p=128)
                for T in range(NTT):
                    m, R_m = mt.next()
                    if active:
                        lo, hi = min(active), max(active) + 1
                        fw.dma("sp", m[:, lo:hi, :], MT_v[:, lo:hi, T * 512:(T + 1) * 512], reads=[R_MT[c] for c in active], writes=[R_m])
                    for sub in range(4):
                        tt = T * 4 + sub
                        xt, R_xt = xin.next()
                        fw.dma("sp", xt[:], x[tt * 128:(tt + 1) * 128, :], writes=[R_xt])
                        h, R_h = hb.next()
                        if active:
                            for half in range(2):
                                pb, R_pb = next_ps()
                                for i, ck in enumerate(active):
                                    fw.op("pe", lambda e: e.matmul(pb[:, :], lhsT=m[:, ck, sub * 128:(sub + 1) * 128],
                                                                   rhs=wo[:, ck, half * 512:(half + 1) * 512],
                                                                   start=(i == 0), stop=(i == len(active) - 1)),
                                          reads=[R_m, R_wo], writes=[R_pb])
                                fw.op("dve", lambda e: e.tensor_tensor(out=h[:, half * 512:(half + 1) * 512], in0=pb[:, :],
                                                                       in1=xt[:, half * 512:(half + 1) * 512], op=ALU.add),
                                      reads=[R_pb, R_xt], writes=[R_h])
                        else:
                            fw.op("dve", lambda e: e.tensor_copy(out=h[:], in_=xt[:]), reads=[R_xt], writes=[R_h])
                        s4, R_s4 = st.next()
                        fw.op("act", lambda e: e.activation(out=junk[:], in_=h[:], func=AF.Square, accum_out=s4[:, 0:1]),
                              reads=[R_h], writes=[R_junk, R_s4])
                        fw.op("dve", lambda e: e.tensor_scalar(out=s4[:, 1:2], in0=s4[:, 0:1], scalar1=1.0 / D, scalar2=EPS,
                                                               op0=ALU.mult, op1=ALU.add), reads=[R_s4], writes=[R_s4])
                        fw.op("act", lambda e: e.activation(out=s4[:, 2:3], in_=s4[:, 1:2], func=AF.Ln), reads=[R_s4], writes=[R_s4])
                        fw.op("act", lambda e: e.activation(out=s4[:, 3:4], in_=s4[:, 2:3], func=AF.Exp, scale=-0.5),
                              reads=[R_s4], writes=[R_s4])
                        o, R_o = ob.next()
                        fw.op("dve", lambda e: e.scalar_tensor_tensor(out=o[:], in0=h[:], scalar=s4[:, 3:4], in1=gf[:],
                                                                      op0=ALU.mult, op1=ALU.mult),
                              reads=[R_h, R_s4, R_gf], writes=[R_o])
                        fw.dma("pool", out[tt * 128:(tt + 1) * 128, :], o[:], reads=[R_o], writes=[R_out])
                fw.barrier()
        fw.barrier()
    print(f"[kernel] ops={fw.nops} waits={fw.nwaits} cnt={fw.cnt}")
    fw.close()
    return nc


def _host_consts():
    bf = ml_dtypes.bfloat16
    ident = np.eye(128, dtype=np.float32)
    inv = (np.float32(500000.0) ** (-(np.arange(0, 16, 2, dtype=np.float32)) / np.float32(16))).astype(np.float32)
    ropeinv = np.zeros((128, 1), np.float32)
    psw = np.zeros((128, 128), np.float32)
    for h in range(2):
        for i in range(8):
            ropeinv[h * 64 + i, 0] = inv[i]
            ropeinv[h * 64 + 8 + i, 0] = inv[i]
            psw[h * 64 + 8 + i, h * 64 + i] = -1.0
            psw[h * 64 + i, h * 64 + 8 + i] = 1.0
    ii = np.arange(128)
    triu = (ii[:, None] <= ii[None, :]).astype(np.float32)
    trineg = np.where(ii[:, None] <= ii[None, :], 0.0, NEG).astype(np.float32)
    hsel = np.zeros((128, 4, 8), np.float32)
    for p in range(128):
        for pr in range(4):
            hsel[p, pr, 2 * pr + p // 64] = 1.0
    n = np.arange(256)
    j = np.arange(64)
    cstart = 16 * n
    ov = ((cstart[:, None] < 64 * j[None, :] + 64) & (cstart[:, None] + 32 > 64 * j[None, :])).astype(np.float32)
    ov[255, :] = 0.0
    ovx = np.zeros((128, 2, 65), np.float32)
    for nch in range(2):
        ovx[:, nch, 0:64] = ov[nch * 128:(nch + 1) * 128]
        ovx[:, nch, 64] = 1.0
    col = np.arange(3072)
    w3 = np.where(col[None, :] >= 16 * ii[:, None] + 31, 0.0, NEG).astype(np.float32)
    col = np.arange(896)
    wc = np.where((col[None, :] - 384) >= ii[:, None], 0.0, NEG).astype(np.float32)
    col = np.arange(1536)
    u = col[None, :] - 512 - ii[:, None]
    ww = np.where((u >= 0) & (u < 512), 0.0, NEG).astype(np.float32)
    t = np.arange(S)
    cur = t // 64
    forced = (j[None, :] == 0) | (j[None, :] == cur[:, None]) | (j[None, :] == cur[:, None] - 1)
    valid = j[None, :] <= cur[:, None]
    addm = np.where(forced, 1e9, np.where(valid, 0.0, -1e9)).astype(np.float32)
    exg = np.zeros((24, 24, 64), np.float32)
    for r in range(24):
        exg[r, r, :] = 1.0
    ex = (np.arange(S)[None, :] // 64 == j[:, None]).astype(np.float32)
    return {"c_ident": ident, "c_ropeinv": ropeinv, "c_psw": psw, "c_triu": triu, "c_trineg": trineg,
            "c_hsel": hsel.reshape(128, 32), "c_ovx": ovx.reshape(128, 130).astype(bf), "c_w3": w3.astype(bf), "c_wc": wc.astype(bf),
            "c_ww": ww.astype(bf), "c_addm": addm, "c_exg": exg.reshape(24, 1536).astype(bf), "c_ex": ex.astype(bf)}


_NC_CACHE = {}


def kernel(**inputs):
    if "nc" not in _NC_CACHE:
        _NC_CACHE["nc"] = build_nc()
    nc = _NC_CACHE["nc"]
    consts = _host_consts()
    B = inputs["x"].shape[0]
    in_maps = []
    for b in range(B):
        m = {
            "x": np.ascontiguousarray(inputs["x"][b], dtype=np.float32),
            "mem": np.ascontiguousarray(inputs["mem"][b], dtype=np.float32),
            "positions": np.ascontiguousarray(inputs["positions"][b:b + 1], dtype=np.int32),
            "g_in": np.ascontiguousarray(inputs["g_in"][0], dtype=np.float32),
            "w_in": np.ascontiguousarray(inputs["w_in"][0], dtype=np.float32),
            "cmp_pos_k": np.ascontiguousarray(inputs["cmp_pos_k"][0], dtype=np.float32),
            "w_cmp1_k": np.ascontiguousarray(inputs["w_cmp1_k"][0], dtype=np.float32),
            "w_cmp2_k": np.ascontiguousarray(inputs["w_cmp2_k"][0], dtype=np.float32),
            "cmp_pos_v": np.ascontiguousarray(inputs["cmp_pos_v"][0], dtype=np.float32),
            "w_cmp1_v": np.ascontiguousarray(inputs["w_cmp1_v"][0], dtype=np.float32),
            "w_cmp2_v": np.ascontiguousarray(inputs["w_cmp2_v"][0], dtype=np.float32),
            "conv_w": np.ascontiguousarray(inputs["conv_w"][0], dtype=np.float32),
            "conv_b": np.ascontiguousarray(inputs["conv_b"][0], dtype=np.float32),
            "dt_bias": np.ascontiguousarray(inputs["dt_bias"][0:1], dtype=np.float32),
            "a_log": np.ascontiguousarray(inputs["a_log"][0:1], dtype=np.float32),
            "d_skip": np.ascontiguousarray(inputs["d_skip"][0:1], dtype=np.float32),
            "g_ssd_norm": np.ascontiguousarray(inputs["g_ssd_norm"][0], dtype=np.float32),
            "g_mem": np.ascontiguousarray(inputs["g_mem"][0], dtype=np.float32),
            "w_mem_kv": np.ascontiguousarray(inputs["w_mem_kv"][0], dtype=np.float32),
            "w_out": np.ascontiguousarray(inputs["w_out"][0], dtype=np.float32),
            "g_final": np.ascontiguousarray(np.asarray(inputs["g_final"]).reshape(1, D), dtype=np.float32),
        }
        m.update(consts)
        in_maps.append(m)
    if os.environ.get("KTRACE"):
        res = run_bass_kernel_spmd(nc, in_maps, core_ids=list(range(B)), trace=True)
        print("[kernel] exec_time_ns", res.exec_time_ns)
    else:
        res = run_bass_kernel_spmd(nc, in_maps, core_ids=list(range(B)))
    if DEBUG:
        _NC_CACHE["last"] = res
    return np.stack([np.asarray(r["out"], dtype=np.float32) for r in res.results], axis=0)
```

```python
import os
import math
import numpy as np
import ml_dtypes
from contextlib import ExitStack
import concourse.bass as bass
import concourse.mybir as mybir
from concourse.bass_utils import run_bass_kernel_spmd

F32 = mybir.dt.float32
BF16 = mybir.dt.bfloat16
I32 = mybir.dt.int32
AF = mybir.ActivationFunctionType
ALU = mybir.AluOpType
AX = mybir.AxisListType

S = 4096
D = 1024
NIN = 4384
NT = S // 128
NTT = S // 512
EPS = 1e-6
C_Q, C_KC, C_VC, C_KS, C_VS, C_KW, C_VW, C_GL, C_GA, C_Z, C_XBC, C_DT, C_QX, C_GX = (
    0, 512, 640, 768, 896, 1024, 1152, 1280, 1304, 1816, 2328, 3352, 3360, 3872)
P_Q, P_KC, P_KS, P_KW, P_XBC, P_QX, P_VC = 0, 512, 640, 768, 896, 1920, 2432
PT_ROWS = 2560
NEG = -30000.0

DEBUG = bool(int(os.environ.get("KDEBUG", "0")))
STAGES = os.environ.get("KSTAGES", "XPCBAF")
ASUB = int(os.environ.get("KASUB", "3"))


class Res:
    __slots__ = ("name", "w", "r")

    def __init__(self, name=""):
        self.name = name
        self.w = None
        self.r = []


class FW:
    def __init__(self, nc, n_dma_sems=24):
        self.nc = nc
        self.eng = {"pe": nc.tensor, "act": nc.scalar, "dve": nc.vector, "pool": nc.gpsimd, "sp": nc.sync}
        self.sem = {}
        self.cnt = {}
        self._ctx = []
        for e in ("pe", "act", "dve", "pool"):
            cm = nc.semaphore("sem_" + e)
            s = cm.__enter__()
            self._ctx.append(cm)
            self.sem[e] = s
            self.cnt[e] = 0
        self.dpool = {}
        for q, n in (("sp", n_dma_sems), ("pool", 12), ("act", 8)):
            lst = []
            for i in range(n):
                cm = nc.semaphore(f"dsem_{q}_{i}")
                s = cm.__enter__()
                self._ctx.append(cm)
                lst.append([s, 0])
            self.dpool[q] = [lst, 0]
        self.obs = {e: {} for e in self.eng}
        self.nwaits = 0
        self.nops = 0

    def close(self):
        for cm in reversed(self._ctx):
            cm.__exit__(None, None, None)

    def _need(self, e, tok, lst):
        if tok is None:
            return
        src, sem, val = tok
        key = id(sem)
        if self.obs[e].get(key, 0) >= val:
            return
        self.obs[e][key] = val
        lst[key] = (sem, max(val, lst.get(key, (None, 0))[1]))

    def _wait(self, e, tok):
        lst = {}
        self._need(e, tok, lst)
        for sem, val in lst.values():
            self.eng[e].wait_ge(sem, val)
            self.nwaits += 1

    def _deps(self, e, reads, writes):
        lst = {}
        for r in reads:
            if r.w is not None:
                if not (r.w[0] == e and e == "pe"):
                    self._need(e, r.w, lst)
        for w in writes:
            if w.w is not None and w.w[0] != e:
                self._need(e, w.w, lst)
            for t in w.r:
                if t[0] != e:
                    self._need(e, t, lst)
        return list(lst.values())

    def _update(self, tok, reads, writes):
        for r in reads:
            if tok[0].startswith("dma"):
                r.r = r.r + [tok]
            else:
                r.r = [t for t in r.r if t[0] != tok[0]] + [tok]
        for w in writes:
            w.w = tok
            w.r = []

    def op(self, e, fn, reads=(), writes=()):
        waits = self._deps(e, reads, writes)
        for sem, val in waits[:-1]:
            self.eng[e].wait_ge(sem, val)
            self.nwaits += 1
        ins = fn(self.eng[e])
        if waits:
            ins = ins._wait_ge(waits[-1][0], waits[-1][1])
        self.cnt[e] += 1
        ins.then_inc(self.sem[e], 1)
        tok = (e, self.sem[e], self.cnt[e])
        self._update(tok, reads, writes)
        self.nops += 1
        return tok

    def dma(self, q, out, in_, reads=(), writes=(), **kw):
        waits = self._deps(q, reads, writes)
        lst, idx = self.dpool[q]
        slot = lst[idx % len(lst)]
        self.dpool[q][1] = idx + 1
        sem, cur = slot
        if cur > 0:
            d = {}
            self._need(q, ("dma_" + q, sem, cur), d)
            waits += list(d.values())
        for sem_w, val in waits[:-1]:
            self.eng[q].wait_ge(sem_w, val)
            self.nwaits += 1
        ins = self.eng[q].dma_start(out=out, in_=in_, **kw)
        if waits:
            ins = ins._wait_ge(waits[-1][0], waits[-1][1])
        slot[1] = cur + 16
        ins.then_inc(sem, 16)
        tok = ("dma_" + q, sem, slot[1])
        self._update(tok, reads, writes)
        return tok

    def barrier(self):
        toks = []
        for e in ("pe", "act", "dve", "pool"):
            if self.cnt[e] > 0:
                toks.append((e, self.sem[e], self.cnt[e]))
        for q in self.dpool:
            for sem, cur in self.dpool[q][0]:
                if cur > 0:
                    toks.append(("dma_" + q, sem, cur))
        for e in ("pe", "act", "dve", "pool", "sp"):
            for t in toks:
                if t[0] != e:
                    self._wait(e, t)


class Ring:
    def __init__(self, bufs):
        self.bufs = [(b, Res()) for b in bufs]
        self.i = 0

    def next(self):
        b = self.bufs[self.i % len(self.bufs)]
        self.i += 1
        return b


def build_nc():
    nc = bass.Bass("TRN2", target_bir_lowering=False)
    fw = FW(nc)
    dbg_kind = "ExternalOutput" if DEBUG else "Internal"

    def din(name, shape, dt=F32):
        return nc.dram_tensor(name, list(shape), dt, kind="ExternalInput").ap()

    x = din("x", [S, D])
    mem = din("mem", [256, D])
    positions = din("positions", [1, S], I32)
    g_in = din("g_in", [D])
    w_in = din("w_in", [D, NIN])
    cmp_pos_k = din("cmp_pos_k", [32, 64])
    w_cmp1_k = din("w_cmp1_k", [2048, 256])
    w_cmp2_k = din("w_cmp2_k", [256, 64])
    cmp_pos_v = din("cmp_pos_v", [32, 64])
    w_cmp1_v = din("w_cmp1_v", [2048, 256])
    w_cmp2_v = din("w_cmp2_v", [256, 64])
    conv_w = din("conv_w", [4, 1024])
    conv_b = din("conv_b", [1024])
    dt_bias = din("dt_bias", [1, 8])
    a_log = din("a_log", [1, 8])
    d_skip = din("d_skip", [1, 8])
    g_ssd_norm = din("g_ssd_norm", [512])
    g_mem = din("g_mem", [D])
    w_mem_kv = din("w_mem_kv", [D, 1024])
    w_out = din("w_out", [1536, D])
    g_final = din("g_final", [1, D])
    c_ident = din("c_ident", [128, 128])
    c_ropeinv = din("c_ropeinv", [128, 1])
    c_psw = din("c_psw", [128, 128])
    c_triu = din("c_triu", [128, 128])
    c_trineg = din("c_trineg", [128, 128])
    c_hsel = din("c_hsel", [128, 32])
    c_ovx = din("c_ovx", [128, 130], BF16)
    c_w3 = din("c_w3", [128, 3072], BF16)
    c_wc = din("c_wc", [128, 896], BF16)
    c_ww = din("c_ww", [128, 1536], BF16)
    c_addm = din("c_addm", [S, 64])
    c_exg = din("c_exg", [24, 1536], BF16)
    c_ex = din("c_ex", [64, S], BF16)

    out = nc.dram_tensor("out", [S, D], F32, kind="ExternalOutput").ap()
    PT = nc.dram_tensor("PT", [PT_ROWS, S], BF16, kind=dbg_kind).ap()
    SG = nc.dram_tensor("SG", [1536, S], BF16, kind=dbg_kind).ap()
    GL = nc.dram_tensor("GL", [24, S], BF16, kind=dbg_kind).ap()
    VT = nc.dram_tensor("VT", [S, 392], F32, kind=dbg_kind).ap()
    CS = nc.dram_tensor("CS", [2, 128, 256], F32, kind=dbg_kind).ap()
    MT = nc.dram_tensor("MT", [1536, S], BF16, kind=dbg_kind).ap()

    R_out = Res("out")
    R_PT = [Res() for _ in range(PT_ROWS // 128)]
    R_SG = [Res() for _ in range(12)]
    R_GL = Res()
    R_VT = Res()
    R_CS = Res()
    R_MT = [Res() for _ in range(12)]

    with ExitStack() as top:
        top.enter_context(nc.allow_low_precision("bf16 matmul operands / bf16 staging by design"))

        def sb(name, shape, dt, stack=top):
            return stack.enter_context(nc.sbuf_tensor(name, list(shape), dt))

        ps = [top.enter_context(nc.psum_tensor(f"ps{i}", [128, 512], F32)) for i in range(8)]
        R_ps = [Res(f"ps{i}") for i in range(8)]
        ps_ring = {"i": 0}

        def next_ps(lo=0, hi=8):
            i = lo + ps_ring["i"] % (hi - lo)
            ps_ring["i"] += 1
            return ps[i], R_ps[i]

        ident_f = sb("ident_f", [128, 128], F32)
        ident_b = sb("ident_b", [128, 128], BF16)
        ones_f = sb("ones_f", [128, 128], F32)
        ones_b = sb("ones_b", [128, 128], BF16)
        psw_f = sb("psw_f", [128, 128], F32)
        R_const = Res("const")
        fw.dma("sp", ident_f[:], c_ident[:, :], writes=[R_const])
        fw.dma("sp", psw_f[:], c_psw[:, :], writes=[R_const])
        fw.op("dve", lambda e: e.tensor_copy(out=ident_b[:], in_=ident_f[:]), reads=[R_const], writes=[R_const])
        fw.op("dve", lambda e: e.memset(ones_f[:], 1.0), writes=[R_const])
        fw.op("dve", lambda e: e.memset(ones_b[:], 1.0), writes=[R_const])
        epsb = sb("epsb", [128, 1], F32)
        fw.op("dve", lambda e: e.memset(epsb[:], 1e-30), writes=[R_const])

        dt_all = sb("dt_all", [128, NT, 8], F32)
        R_dt = Res()
        xn_stack = ExitStack()
        xnT = sb("xnT", [128, 8, S], BF16, xn_stack)
        R_xn = [Res(f"xn{i}") for i in range(NTT)]

        def rms_transpose_phase(src, ntiles, g_vec, dstT, R_dst_of_tile, stack):
            g_sb = sb("g_sb_" + dstT.name, [128, 8], F32, stack)
            R_g = Res()
            fw.dma("sp", g_sb[:], g_vec.rearrange("(dk p) -> p dk", p=128), writes=[R_g], allow_slow_non_contiguous=True)
            xin = Ring([sb(f"xin{i}_" + dstT.name, [128, D], F32, stack) for i in range(min(4, ntiles))])
            xsc = Ring([sb(f"xsc{i}_" + dstT.name, [128, D], BF16, stack) for i in range(min(3, ntiles))])
            junk = sb("junk_" + dstT.name, [128, D], BF16, stack)
            R_junk = Res()
            st = Ring([sb(f"st{i}_" + dstT.name, [128, 4], F32, stack) for i in range(4)])
            epsd = sb("epsd_" + dstT.name, [128, 1], F32, stack)
            fw.op("dve", lambda e: e.memset(epsd[:], EPS), writes=[R_g])

            def stage_a(tt):
                xt, R_xt = xin.next()
                fw.dma("sp", xt[:], src[tt * 128:(tt + 1) * 128, :], writes=[R_xt])
                s4, R_s4 = st.next()
                fw.op("act", lambda e: e.activation(out=junk[:], in_=xt[:], func=AF.Square, accum_out=s4[:, 0:1]),
                      reads=[R_xt], writes=[R_junk, R_s4])
                fw.op("act", lambda e: e.activation(out=s4[:, 2:3], in_=s4[:, 0:1], func=AF.Ln, scale=1.0 / D, bias=epsd[:, 0:1]),
                      reads=[R_s4, R_g], writes=[R_s4])
                fw.op("act", lambda e: e.activation(out=s4[:, 3:4], in_=s4[:, 2:3], func=AF.Exp, scale=-0.5),
                      reads=[R_s4], writes=[R_s4])
                return (xt, R_xt, s4, R_s4)

            def stage_b(tt, xt, R_xt, s4, R_s4):
                xs, R_xs = xsc.next()
                fw.op("dve", lambda e: e.tensor_scalar(out=xs[:], in0=xt[:], scalar1=s4[:, 3:4], scalar2=None, op0=ALU.mult),
                      reads=[R_xt, R_s4], writes=[R_xs])
                pb, R_pb = next_ps()
                pbb = pb[:].bitcast(BF16)
                for dk in range(8):
                    fw.op("pe", lambda e: e.transpose(pbb[:, dk * 128:(dk + 1) * 128], xs[:, dk * 128:(dk + 1) * 128], ident_b[:]),
                          reads=[R_xs, R_const], writes=[R_pb])
                return (tt, pbb, R_pb)

            def stage_c(tt, pbb, R_pb):
                fw.op("dve", lambda e: e.tensor_tensor(
                    out=dstT[:, :, tt * 128:(tt + 1) * 128],
                    in0=pbb.rearrange("p (a b) -> p a b", a=8),
                    in1=g_sb[:].unsqueeze(2).to_broadcast([128, 8, 128]), op=ALU.mult),
                    reads=[R_pb, R_g], writes=[R_dst_of_tile(tt)])

            pa_, pb_ = [], []
            for tt in range(ntiles + 3):
                if tt < ntiles:
                    pa_.append((tt,) + stage_a(tt))
                if len(pb_) > 0 and tt >= 2:
                    stage_c(*pb_.pop(0))
                if len(pa_) > 0 and tt >= 1:
                    pb_.append(stage_b(*pa_.pop(0)))
            while pa_ or pb_:
                if pb_:
                    stage_c(*pb_.pop(0))
                if pa_:
                    pb_.append(stage_b(*pa_.pop(0)))

        if "X" in STAGES:
            rms_transpose_phase(x, NT, g_in, xnT, lambda tt: R_xn[tt // 4], xn_stack)

        if "P" in STAGES:
            with ExitStack() as ph:
                Ct = sb("Ct", [128, S], F32, ph)
                St = sb("St", [128, S], F32, ph)
                R_C = Res()
                def emit_rope_tables():
                    invp = sb("invp", [128, 1], F32, ph)
                    R_tmp = Res()
                    fw.dma("sp", invp[:], c_ropeinv[:, :], writes=[R_tmp])
                    CB = 512
                    posi = sb("posi", [128, CB], I32, ph)
                    ang = sb("ang", [128, CB], F32, ph)
                    kfi = sb("kfi", [128, CB], I32, ph)
                    kf = sb("kf", [128, CB], F32, ph)
                    rr = sb("rr", [128, CB], F32, ph)
                    rc = sb("rc", [128, CB], F32, ph)
                    C1 = 6.28125
                    C2 = 2 * math.pi - 6.28125
                    PI_LO = 3.141592
                    for cb in range(S // CB):
                        cs = slice(cb * CB, (cb + 1) * CB)
                        fw.dma("sp", posi[:], positions[:, cs].partition_broadcast(128), reads=[R_tmp], writes=[R_tmp])
                        fw.op("dve", lambda e: e.tensor_copy(out=ang[:], in_=posi[:]), reads=[R_tmp], writes=[R_tmp])
                        fw.op("dve", lambda e: e.tensor_scalar(out=ang[:], in0=ang[:], scalar1=invp[:, 0:1], scalar2=None, op0=ALU.mult),
                              reads=[R_tmp], writes=[R_tmp])
                        fw.op("dve", lambda e: e.tensor_scalar(out=kfi[:], in0=ang[:], scalar1=1.0 / (2 * math.pi), scalar2=None, op0=ALU.mult),
                              reads=[R_tmp], writes=[R_tmp])
                        fw.op("dve", lambda e: e.tensor_copy(out=kf[:], in_=kfi[:]), reads=[R_tmp], writes=[R_tmp])
                        fw.op("dve", lambda e: e.scalar_tensor_tensor(out=rr[:], in0=kf[:], scalar=-C1, in1=ang[:], op0=ALU.mult, op1=ALU.add),
                              reads=[R_tmp], writes=[R_tmp])
                        fw.op("dve", lambda e: e.scalar_tensor_tensor(out=rr[:], in0=kf[:], scalar=-C2, in1=rr[:], op0=ALU.mult, op1=ALU.add),
                              reads=[R_tmp], writes=[R_tmp])
                        fw.op("dve", lambda e: e.tensor_scalar(out=rc[:], in0=rr[:], scalar1=math.pi / 2, scalar2=-2 * math.pi,
                                                               op0=ALU.is_gt, op1=ALU.mult), reads=[R_tmp], writes=[R_tmp])
                        fw.op("dve", lambda e: e.scalar_tensor_tensor(out=rc[:], in0=rr[:], scalar=math.pi / 2, in1=rc[:], op0=ALU.add, op1=ALU.add),
                              reads=[R_tmp], writes=[R_tmp])
                        fw.op("dve", lambda e: e.tensor_scalar(out=rr[:], in0=rr[:], scalar1=-PI_LO, scalar2=PI_LO, op0=ALU.max, op1=ALU.min),
                              reads=[R_tmp], writes=[R_tmp])
                        fw.op("dve", lambda e: e.tensor_scalar(out=rc[:], in0=rc[:], scalar1=-PI_LO, scalar2=PI_LO, op0=ALU.max, op1=ALU.min),
                              reads=[R_tmp], writes=[R_tmp])
                        fw.op("act", lambda e: e.activation(out=St[:, cs], in_=rr[:], func=AF.Sin), reads=[R_tmp], writes=[R_C])
                        fw.op("act", lambda e: e.activation(out=Ct[:, cs], in_=rc[:], func=AF.Sin), reads=[R_tmp], writes=[R_C])
                    csc = sb("csc", [128, 2, 256], F32, ph)
                    R_csc = Res()
                    fw.op("dve", lambda e: e.tensor_copy(out=csc[:, 0, 0:255], in_=Ct[:, 31:S:16]), reads=[R_C], writes=[R_csc])
                    fw.op("dve", lambda e: e.tensor_copy(out=csc[:, 1, 0:255], in_=St[:, 31:S:16]), reads=[R_C], writes=[R_csc])
                    fw.dma("sp", CS[0, :, 0:255], csc[:, 0, 0:255], reads=[R_csc], writes=[R_CS])
                    fw.dma("sp", CS[1, :, 0:255], csc[:, 1, 0:255], reads=[R_csc], writes=[R_CS])


                wbf = Ring([sb(f"wbf{i}", [128, 8, 128], BF16, ph) for i in range(3)])
                OW = 2048
                otile = [(sb(f"otile{i}", [128, OW], BF16, ph), [Res() for _ in range(4)]) for i in range(3)]
                ot_i = {"i": 0}
                qf = Ring([sb(f"qf{i}", [128, 512], F32, ph) for i in range(2)])
                t1 = Ring([sb(f"t1_{i}", [128, 512], F32, ph) for i in range(2)])
                w_view = w_in.rearrange("(dk p) c -> p dk c", p=128)

                def load_w(c0, ncols):
                    wb, R_wb = wbf.next()
                    fw.dma("pool", wb[:, :, 0:ncols], w_view[:, :, c0:c0 + ncols], writes=[R_wb])
                    return wb, R_wb

                def proj_fm(wb, R_wb, ncols, T):
                    pb, R_pb = next_ps()
                    for dk in range(8):
                        fw.op("pe", lambda e: e.matmul(pb[0:ncols, :], lhsT=wb[:, dk, 0:ncols], rhs=xnT[:, dk, T * 512:(T + 1) * 512],
                                                       start=(dk == 0), stop=(dk == 7)),
                              reads=[R_wb, R_xn[T]], writes=[R_pb])
                    return pb, R_pb

                chunks = []
                for i in range(4):
                    chunks.append(("silu", C_GA + i * 128, 128, SG, i, R_SG))
                for i in range(4):
                    chunks.append(("silu", C_Z + i * 128, 128, SG, 4 + i, R_SG))
                for i in range(4):
                    chunks.append(("silu", C_GX + i * 128, 128, SG, 8 + i, R_SG))
                chunks.append(("copy", C_KC, 128, PT, P_KC // 128, R_PT))
                chunks.append(("copy", C_VC, 128, PT, P_VC // 128, R_PT))
                for i in range(8):
                    chunks.append(("copy", C_XBC + i * 128, 128, PT, P_XBC // 128 + i, R_PT))
                for i in range(4):
                    chunks.append(("copy", C_QX + i * 128, 128, PT, P_QX // 128 + i, R_PT))
                for i in range(4):
                    chunks.append(("rope", C_Q + i * 128, 128, PT, P_Q // 128 + i, R_PT))
                chunks.append(("rope", C_KS, 128, PT, P_KS // 128, R_PT))
                chunks.append(("rope", C_KW, 128, PT, P_KW // 128, R_PT))
                chunks.append(("gl", C_GL, 24, GL, 0, None))
                nxt = load_w(chunks[0][1], chunks[0][2])
                pend = []
                for ci, (kind, c0, ncols, dstT, drow, R_dst) in enumerate(chunks):
                    wb, R_wb = nxt
                    if ci + 1 < len(chunks):
                        nxt = load_w(chunks[ci + 1][1], chunks[ci + 1][2])
                    if ci == 12:
                        emit_rope_tables()
                    for half in range(2):
                        ot, R_ots = otile[ot_i["i"] % 3]
                        ot_i["i"] += 1
                        for T4 in range(4):
                            T = half * 4 + T4
                            pb, R_pb = proj_fm(wb, R_wb, ncols, T)
                            if pend:
                                pend.pop(0)()

                            def epi(pb=pb, R_pb=R_pb, T=T, T4=T4, kind=kind, ot=ot, R_ots=R_ots, half=half, dstT=dstT, drow=drow, R_dst=R_dst):
                                osl = ot[:, T4 * 512:(T4 + 1) * 512]
                                R_o1 = R_ots[T4]
                                if kind == "silu":
                                    fw.op("act", lambda e: e.activation(out=osl, in_=pb[:], func=AF.Silu), reads=[R_pb], writes=[R_o1])
                                elif kind == "copy":
                                    if T % 2 == 0:
                                        fw.op("dve", lambda e: e.tensor_copy(out=osl, in_=pb[:]), reads=[R_pb], writes=[R_o1])
                                    else:
                                        fw.op("act", lambda e: e.activation(out=osl, in_=pb[:], func=AF.Copy), reads=[R_pb], writes=[R_o1])
                                elif kind == "rope":
                                    q32, R_q32 = qf.next()
                                    fw.op("act", lambda e: e.activation(out=q32[:], in_=pb[:], func=AF.Copy), reads=[R_pb], writes=[R_q32])
                                    pb2, R_pb2 = next_ps()
                                    fw.op("pe", lambda e: e.matmul(pb2[:, :], lhsT=psw_f[:], rhs=q32[:], start=True, stop=True),
                                          reads=[R_const, R_q32], writes=[R_pb2])
                                    ta, R_ta = t1.next()
                                    fw.op("dve", lambda e: e.tensor_tensor(out=ta[:], in0=pb2[:], in1=St[:, T * 512:(T + 1) * 512], op=ALU.mult),
                                          reads=[R_pb2, R_C], writes=[R_ta])
                                    fw.op("pool", lambda e: e.tensor_tensor(out=q32[:], in0=q32[:], in1=Ct[:, T * 512:(T + 1) * 512], op=ALU.mult),
                                          reads=[R_q32, R_C], writes=[R_q32])
                                    fw.op("dve", lambda e: e.tensor_tensor(out=osl, in0=ta[:], in1=q32[:], op=ALU.add),
                                          reads=[R_ta, R_q32], writes=[R_o1])
                                else:
                                    ta, R_ta = t1.next()
                                    fw.op("act", lambda e: e.activation(out=ta[0:24, :], in_=pb[0:24, :], func=AF.Exp, scale=-1.0), reads=[R_pb], writes=[R_ta])
                                    fw.op("dve", lambda e: e.tensor_scalar(out=ta[0:24, :], in0=ta[0:24, :], scalar1=1.0, scalar2=None, op0=ALU.add),
                                          reads=[R_ta], writes=[R_ta])
                                    fw.op("dve", lambda e: e.reciprocal(out=osl[0:24, :], in_=ta[0:24, :]), reads=[R_ta], writes=[R_o1])
                                if T4 == 3:
                                    if kind == "gl":
                                        fw.dma("sp", GL[:, half * OW:(half + 1) * OW], ot[0:24, :], reads=R_ots, writes=[R_GL])
                                    else:
                                        fw.dma("sp", dstT[drow * 128:(drow + 1) * 128, half * OW:(half + 1) * OW], ot[:], reads=R_ots, writes=[R_dst[drow]])
                            pend.append(epi)
                while pend:
                    pend.pop(0)()
                wtm = sb("wtm", [128, 8, 392], BF16, ph)
                R_wtm = Res()
                for j, c0 in enumerate((C_VC, C_VS, C_VW)):
                    fw.dma("pool", wtm[:, :, j * 128:(j + 1) * 128], w_view[:, :, c0:c0 + 128], writes=[R_wtm])
                fw.dma("pool", wtm[:, :, 384:392], w_view[:, :, C_DT:C_DT + 8], writes=[R_wtm])
                vt_o = Ring([sb(f"vt_o{i}", [128, 392], F32, ph) for i in range(3)])
                for tt in range(NT):
                    pb, R_pb = next_ps()
                    for dk in range(8):
                        fw.op("pe", lambda e: e.matmul(pb[:, 0:392], lhsT=xnT[:, dk, tt * 128:(tt + 1) * 128], rhs=wtm[:, dk, :],
                                                       start=(dk == 0), stop=(dk == 7)),
                              reads=[R_wtm, R_xn[tt // 4]], writes=[R_pb])
                    vo, R_vo = vt_o.next()
                    if tt % 2 == 0:
                        fw.op("dve", lambda e: e.tensor_copy(out=vo[:], in_=pb[:, 0:392]), reads=[R_pb], writes=[R_vo])
                    else:
                        fw.op("act", lambda e: e.activation(out=vo[:], in_=pb[:, 0:392], func=AF.Copy), reads=[R_pb], writes=[R_vo])
                    fw.op("dve", lambda e: e.tensor_copy(out=dt_all[:, tt, :], in_=pb[:, 384:392]), reads=[R_pb], writes=[R_dt])
                    fw.dma("sp", VT[tt * 128:(tt + 1) * 128, :], vo[:], reads=[R_vo], writes=[R_VT])
                fw.barrier()

        xn_stack.close()
        wo = sb("wo", [128, 12, D], BF16)
        R_wo = Res()
        for ck in range(12):
            fw.dma("pool", wo[:, ck, :], w_out[ck * 128:(ck + 1) * 128, :], writes=[R_wo])
        def build_phase_c(ph):
            memT = sb("memT", [128, 8, 256], BF16, ph)
            R_memT = Res()
            rms_transpose_phase(mem, 2, g_mem, memT, lambda tt: R_memT, ph)
            wkv_view = w_mem_kv.rearrange("(dk p) c -> p dk c", p=128)
            kT = sb("kT", [128, 4, 256], BF16, ph)
            vtok = sb("vtok", [128, 2, 512], BF16, ph)
            R_kv = Res()
            wst = Ring([sb(f"cwst{i}", [128, 8, 128], F32, ph) for i in range(2)])
            wv = sb("cwv", [128, 8, 512], BF16, ph)
            R_wv = Res()
            wkb = Ring([sb(f"cwkb{i}", [128, 8, 128], BF16, ph) for i in range(2)])
            for h in range(4):
                ws, R_ws = wst.next()
                fw.dma("sp", ws[:], wkv_view[:, :, h * 128:(h + 1) * 128], writes=[R_ws])
                wb, R_wb = wkb.next()
                fw.op("act", lambda e: e.activation(out=wb[:], in_=ws[:], func=AF.Copy), reads=[R_ws], writes=[R_wb])
                pb, R_pb = next_ps()
                for dk in range(8):
                    fw.op("pe", lambda e: e.matmul(pb[:, 0:256], lhsT=wb[:, dk, :], rhs=memT[:, dk, :], start=(dk == 0), stop=(dk == 7)),
                          reads=[R_wb, R_memT], writes=[R_pb])
                fw.op("dve", lambda e: e.tensor_copy(out=kT[:, h, :], in_=pb[:, 0:256]), reads=[R_pb], writes=[R_kv])
            for h in range(4):
                ws, R_ws = wst.next()
                fw.dma("sp", ws[:], wkv_view[:, :, 512 + h * 128:512 + (h + 1) * 128], writes=[R_ws])
                fw.op("dve", lambda e: e.tensor_copy(out=wv[:, :, h * 128:(h + 1) * 128], in_=ws[:]), reads=[R_ws], writes=[R_wv])
            for kc in range(2):
                pb, R_pb = next_ps()
                for dk in range(8):
                    fw.op("pe", lambda e: e.matmul(pb[:, :], lhsT=memT[:, dk, kc * 128:(kc + 1) * 128], rhs=wv[:, dk, :],
                                                   start=(dk == 0), stop=(dk == 7)), reads=[R_wv, R_memT], writes=[R_pb])
                fw.op("dve", lambda e: e.tensor_copy(out=vtok[:, kc, :], in_=pb[:, :]), reads=[R_pb], writes=[R_kv])
            qx = Ring([sb(f"cqx{i}", [128, 512], BF16, ph) for i in range(3)])
            sgx = Ring([sb(f"csgx{i}", [128, 512], BF16, ph) for i in range(3)])
            pT = Ring([sb(f"cpT{i}", [128, 512], BF16, ph) for i in range(4)])
            rden = Ring([sb(f"crden{i}", [128, 512], F32, ph) for i in range(2)])
            ot = Ring([sb(f"cot{i}", [128, 512], BF16, ph) for i in range(2)])
            xscale = 128.0 ** -0.5
            cjobs = []
            for T in range(NTT):
                for h in range(4):
                    def cscore(st, T=T, h=h):
                        ts = slice(T * 512, (T + 1) * 512)
                        q, R_q = qx.next()
                        fw.dma("sp", q[:], PT[P_QX + h * 128:P_QX + (h + 1) * 128, ts], reads=[R_PT[P_QX // 128 + h]], writes=[R_q])
                        sg, R_sg = sgx.next()
                        fw.dma("sp", sg[:], SG[1024 + h * 128:1024 + (h + 1) * 128, ts], reads=[R_SG[8 + h]], writes=[R_sg])
                        pts = []
                        for kc in range(2):
                            pa, R_pa = next_ps(0, 4)
                            fw.op("pe", lambda e: e.matmul(pa[:, :], lhsT=kT[:, h, kc * 128:(kc + 1) * 128], rhs=q[:], start=True, stop=True),
                                  reads=[R_kv, R_q], writes=[R_pa])
                            p, R_p = pT.next()
                            fw.op("act", lambda e: e.activation(out=p[:], in_=pa[:, :], func=AF.Exp, scale=xscale), reads=[R_pa], writes=[R_p])
                            pts.append((p, R_p))
                        st["pts"] = pts
                        st["sg"] = (sg, R_sg)

                    def cpv(st, T=T, h=h):
                        ts = slice(T * 512, (T + 1) * 512)
                        pts = st["pts"]
                        sg, R_sg = st["sg"]
                        po, R_po = next_ps(4, 6)
                        pd, R_pd = next_ps(6, 8)
                        for kc in range(2):
                            p, R_p = pts[kc]
                            fw.op("pe", lambda e: e.matmul(po[:, :], lhsT=vtok[:, kc, h * 128:(h + 1) * 128], rhs=p[:], start=(kc == 0), stop=(kc == 1)),
                                  reads=[R_kv, R_p], writes=[R_po])
                        for kc in range(2):
                            p, R_p = pts[kc]
                            fw.op("pe", lambda e: e.matmul(pd[:, :], lhsT=ones_b[:], rhs=p[:], start=(kc == 0), stop=(kc == 1)),
                                  reads=[R_const, R_p], writes=[R_pd])
                        rd, R_rd = rden.next()
                        fw.op("act", lambda e: e.activation(out=rd[:], in_=pd[:, :], func=AF.Ln), reads=[R_pd], writes=[R_rd])
                        fw.op("act", lambda e: e.activation(out=rd[:], in_=rd[:], func=AF.Exp, scale=-1.0), reads=[R_rd], writes=[R_rd])
                        fw.op("dve", lambda e: e.tensor_tensor(out=rd[:], in0=po[:, :], in1=rd[:], op=ALU.mult), reads=[R_po, R_rd], writes=[R_rd])
                        o, R_o = ot.next()
                        fw.op("dve", lambda e: e.tensor_tensor(out=o[:], in0=rd[:], in1=sg[:], op=ALU.mult), reads=[R_rd, R_sg], writes=[R_o])
                        fw.dma("pool", MT[1024 + h * 128:1024 + (h + 1) * 128, ts], o[:], reads=[R_o], writes=[R_MT[8 + h]])
                    stt = {}
                    cjobs.append((lambda f=cscore, st=stt: f(st), lambda f=cpv, st=stt: f(st)))

            cst = {"i": 0}
            n = len(cjobs)

            def cstep(k):
                for _ in range(k):
                    i = cst["i"]
                    if i > n:
                        return
                    if i < n:
                        cjobs[i][0]()
                    if i - 1 >= 0:
                        cjobs[i - 1][1]()
                    cst["i"] = i + 1
            return cstep

        if "C" in STAGES and "A" not in STAGES:
            with ExitStack() as ph:
                cstep = build_phase_c(ph)
                cstep(40)
                fw.barrier()

        if "B" in STAGES:
            with ExitStack() as ph:
                R_c = Res()
                cw = sb("b_cw", [128, 4, 8], F32, ph)
                cbias = sb("b_cb", [128, 8], F32, ph)
                for k in range(4):
                    fw.dma("sp", cw[:, k, :], conv_w[k].rearrange("(c p) -> p c", p=128), writes=[R_c], allow_slow_non_contiguous=True)
                fw.dma("sp", cbias[:], conv_b.rearrange("(c p) -> p c", p=128), writes=[R_c], allow_slow_non_contiguous=True)
                dtb = sb("b_dtb", [128, 8], F32, ph)
                alog = sb("b_alog", [128, 8], F32, ph)
                dskb = sb("b_dskb", [128, 8], F32, ph)
                hsel = sb("b_hsel", [128, 4, 8], F32, ph)
                gn = sb("b_gn", [128, 4], F32, ph)
                triu = sb("b_triu", [128, 128], F32, ph)
                trineg = sb("b_trineg", [128, 128], F32, ph)
                fw.dma("sp", dtb[:], dt_bias.partition_broadcast(128), writes=[R_c])
                fw.dma("sp", alog[:], a_log.partition_broadcast(128), writes=[R_c])
                fw.dma("sp", dskb[:], d_skip.partition_broadcast(128), writes=[R_c])
                fw.dma("sp", hsel[:], c_hsel.rearrange("p (a b) -> p a b", a=4), writes=[R_c])
                fw.dma("sp", gn[:], g_ssd_norm.rearrange("(c p) -> p c", p=128), writes=[R_c], allow_slow_non_contiguous=True)
                fw.dma("sp", triu[:], c_triu[:, :], writes=[R_c])
                fw.dma("sp", trineg[:], c_trineg[:, :], writes=[R_c])
                trineg_b = sb("b_trineg_b", [128, 128], BF16, ph)
                fw.op("dve", lambda e: e.tensor_copy(out=trineg_b[:], in_=trineg[:]), reads=[R_c], writes=[R_c])
                dsk = sb("b_dsk", [128, 4], F32, ph)
                hs2 = sb("b_hs2", [128, 4, 8], F32, ph)
                fw.op("dve", lambda e: e.tensor_tensor(out=hs2[:], in0=hsel[:], in1=dskb[:].unsqueeze(1).to_broadcast([128, 4, 8]), op=ALU.mult),
                      reads=[R_c], writes=[R_c])
                fw.op("dve", lambda e: e.tensor_reduce(out=dsk[:], in_=hs2[:], axis=AX.X, op=ALU.add), reads=[R_c], writes=[R_c])
                xact = sb("b_xact", [128, 8, S], BF16, ph)
                R_xact = [Res() for _ in range(8)]
                with ExitStack() as ph2:
                    xpad = Ring([sb(f"b_xpad{i}", [128, S + 4], BF16, ph2) for i in range(2)])
                    dg = sb("b_dg", [128, 32, 128], BF16, ph2)
                    R_dg = Res()
                    for c in range(8):
                        for k in range(4):
                            fw.op("dve", lambda e: e.tensor_scalar(out=dg[:, c * 4 + k, :], in0=ident_f[:], scalar1=cw[:, k, c:c + 1], scalar2=None, op0=ALU.mult),
                                  reads=[R_c, R_const], writes=[R_dg])
                    for i in range(2):
                        xp, R_xp = xpad.bufs[i]
                        fw.op("dve", lambda e: e.memset(xp[:, 0:4], 0.0), writes=[R_xp])
                    for c in range(8):
                        xp, R_xp = xpad.next()
                        fw.dma("sp", xp[:, 3:S + 3], PT[P_XBC + c * 128:P_XBC + (c + 1) * 128, :], reads=[R_PT[P_XBC // 128 + c]], writes=[R_xp])
                        for T in range(NTT):
                            pb, R_pb = next_ps()
                            for k in range(4):
                                fw.op("pe", lambda e: e.matmul(pb[:, :], lhsT=dg[:, c * 4 + k, :], rhs=xp[:, T * 512 + k:T * 512 + k + 512],
                                                               start=(k == 0), stop=(k == 3)), reads=[R_dg, R_xp], writes=[R_pb])
                            fw.op("act", lambda e: e.activation(out=xact[:, c, T * 512:(T + 1) * 512], in_=pb[:, :], func=AF.Silu, bias=cbias[:, c:c + 1]),
                                  reads=[R_pb, R_c], writes=[R_xact[c]])
                    fw.barrier()
                NCH = NT
                dtv = sb("b_dt", [128, NCH, 8], F32, ph)
                dtA = sb("b_dtA", [128, NCH, 8], F32, ph)
                acs = sb("b_acs", [128, NCH, 8], F32, ph)
                nacs = sb("b_nacs", [128, NCH, 8], F32, ph)
                tot = sb("b_tot", [128, NCH, 8], F32, ph)
                cdb = sb("b_cdb", [128, NCH, 8], F32, ph)
                w2 = sb("b_w2", [128, NCH, 8], F32, ph)
                aexp = sb("b_aexp", [128, 8], F32, ph)
                R_q = Res()
                fw.op("dve", lambda e: e.tensor_tensor(out=dtv[:], in0=dt_all[:], in1=dtb[:].unsqueeze(1).to_broadcast([128, NCH, 8]), op=ALU.add),
                      reads=[R_dt, R_c], writes=[R_q])
                fw.op("act", lambda e: e.activation(out=dtv[:], in_=dtv[:], func=AF.Exp), reads=[R_q], writes=[R_q])
                fw.op("dve", lambda e: e.tensor_scalar(out=dtv[:], in0=dtv[:], scalar1=1.0, scalar2=None, op0=ALU.add), reads=[R_q], writes=[R_q])
                fw.op("act", lambda e: e.activation(out=dtv[:], in_=dtv[:], func=AF.Ln), reads=[R_q], writes=[R_q])
                fw.op("act", lambda e: e.activation(out=aexp[:], in_=alog[:], func=AF.Exp), reads=[R_c], writes=[R_q])
                fw.op("dve", lambda e: e.scalar_tensor_tensor(out=dtA[:], in0=dtv[:], scalar=-1.0, in1=aexp[:].unsqueeze(1).to_broadcast([128, NCH, 8]),
                                                              op0=ALU.mult, op1=ALU.mult), reads=[R_q], writes=[R_q])
                dtA_hi = sb("b_dtA_hi", [128, NCH, 8], BF16, ph)
                dtA_lo = sb("b_dtA_lo", [128, NCH, 8], BF16, ph)
                dtA_hf = sb("b_dtA_hf", [128, NCH, 8], F32, ph)
                triu_b = sb("b_triu_b", [128, 128], BF16, ph)
                fw.op("dve", lambda e: e.tensor_copy(out=triu_b[:], in_=triu[:]), reads=[R_c], writes=[R_c])
                fw.op("dve", lambda e: e.tensor_copy(out=dtA_hi[:], in_=dtA[:]), reads=[R_q], writes=[R_q])
                fw.op("dve", lambda e: e.tensor_copy(out=dtA_hf[:], in_=dtA_hi[:]), reads=[R_q], writes=[R_q])
                fw.op("dve", lambda e: e.tensor_tensor(out=dtA_lo[:], in0=dtA[:], in1=dtA_hf[:], op=ALU.subtract), reads=[R_q], writes=[R_q])
                fw.op("dve", lambda e: e.tensor_copy(out=dtA_hf[:], in_=dtA_lo[:]), reads=[R_q], writes=[R_q])
                fw.op("dve", lambda e: e.tensor_tensor(out=dtA[:], in0=dtA_hf[:], in1=dtA_hi[:], op=ALU.add), reads=[R_q], writes=[R_q])
                dtA2 = dtA[:].rearrange("p c h -> p (c h)")
                pb, R_pb = next_ps()
                fw.op("pe", lambda e: e.matmul(pb[:, 0:256], lhsT=triu[:], rhs=dtA2, start=True, stop=True), reads=[R_q, R_c], writes=[R_pb])
                fw.op("dve", lambda e: e.tensor_copy(out=acs[:].rearrange("p c h -> p (c h)"), in_=pb[:, 0:256]), reads=[R_pb], writes=[R_q])
                pb, R_pb = next_ps()
                fw.op("pe", lambda e: e.matmul(pb[:, 0:256], lhsT=ones_f[:], rhs=dtA2, start=True, stop=True), reads=[R_q, R_const], writes=[R_pb])
                fw.op("dve", lambda e: e.tensor_copy(out=tot[:].rearrange("p c h -> p (c h)"), in_=pb[:, 0:256]), reads=[R_pb], writes=[R_q])
                fw.op("dve", lambda e: e.tensor_scalar(out=nacs[:], in0=acs[:], scalar1=-1.0, scalar2=None, op0=ALU.mult), reads=[R_q], writes=[R_q])
                fw.op("act", lambda e: e.activation(out=cdb[:], in_=tot[:], func=AF.Exp), reads=[R_q], writes=[R_q])
                fw.op("dve", lambda e: e.tensor_tensor(out=w2[:], in0=tot[:], in1=acs[:], op=ALU.subtract), reads=[R_q], writes=[R_q])
                fw.op("act", lambda e: e.activation(out=w2[:], in_=w2[:], func=AF.Exp), reads=[R_q], writes=[R_q])
                fw.op("dve", lambda e: e.tensor_tensor(out=w2[:], in0=w2[:], in1=dtv[:], op=ALU.mult), reads=[R_q], writes=[R_q])
                state = sb("b_state", [128, 512], F32, ph)
                state_bf = sb("b_state_bf", [128, 512], BF16, ph)
                R_state = Res()
                R_sbf = Res()
                fw.op("dve", lambda e: e.memset(state[:], 0.0), writes=[R_state])
                fw.op("dve", lambda e: e.memset(state_bf[:], 0.0), writes=[R_sbf])
                xbtok = Ring([sb(f"b_xbtok{i}", [128, 768], BF16, ph) for i in range(2)])
                xdt = Ring([sb(f"b_xdt{i}", [128, 512], BF16, ph) for i in range(2)])
                xdtd = Ring([sb(f"b_xdtd{i}", [128, 512], BF16, ph) for i in range(2)])
                eacs = Ring([sb(f"b_eacs{i}", [128, 8, 128], BF16, ph) for i in range(2)])
                decT = Ring([sb(f"b_decT{i}", [128, 8, 128], BF16, ph) for i in range(2)])
                Mh = Ring([sb(f"b_Mh{i}", [128, 8, 128], BF16, ph) for i in range(2)])
                cms = Ring([sb(f"b_cms{i}", [128, 8, 128], BF16, ph) for i in range(2)])
                yacc = Ring([sb(f"b_yacc{i}", [128, 4, 512], F32, ph) for i in range(2)])
                sgz = Ring([sb(f"b_sgz{i}", [128, 4, 512], BF16, ph) for i in range(2)])
                sq = sb("b_sq", [128, 4, 512], F32, ph)
                R_sq = Res()
                rstd = sb("b_rstd", [128, 512], F32, ph)
                R_rstd = Res()
                obt = Ring([sb(f"b_obt{i}", [128, 4, 512], BF16, ph) for i in range(2)])
                SG_v = SG.rearrange("(ck p) t -> p ck t", p=128)
                MT_vb = MT.rearrange("(ck p) t -> p ck t", p=128)
                prepd = {}
                ystate = {}

                def prep(c):
                    tk = slice(c * 128, (c + 1) * 128)
                    pt_, R_pt = ps[7], R_ps[7]
                    ptb = pt_[:].bitcast(BF16)
                    for j in range(6):
                        fw.op("pe", lambda e: e.transpose(ptb[:, j * 128:(j + 1) * 128], xact[:, j, tk], ident_b[:]),
                              reads=[R_xact[j], R_const], writes=[R_pt])
                    xb, R_xb = xbtok.next()
                    fw.op("act", lambda e: e.activation(out=xb[:], in_=ptb[:, 0:768], func=AF.Copy), reads=[R_pt], writes=[R_xb])
                    xd, R_xd = xdt.next()
                    xdd, R_xdd = xdtd.next()
                    fw.op("dve", lambda e: e.tensor_tensor(out=xd[:].rearrange("p (h q) -> p h q", h=8), in0=xb[:, 0:512].rearrange("p (h q) -> p h q", h=8),
                                                           in1=dtv[:, c, :].unsqueeze(2).to_broadcast([128, 8, 64]), op=ALU.mult),
                          reads=[R_xb, R_q], writes=[R_xd])
                    fw.op("dve", lambda e: e.tensor_tensor(out=xdd[:].rearrange("p (h q) -> p h q", h=8), in0=xb[:, 0:512].rearrange("p (h q) -> p h q", h=8),
                                                           in1=w2[:, c, :].unsqueeze(2).to_broadcast([128, 8, 64]), op=ALU.mult),
                          reads=[R_xb, R_q], writes=[R_xdd])
                    for g in range(2):
                        fw.op("pe", lambda e: e.matmul(ps[4][:, g * 128:(g + 1) * 128], lhsT=xact[:, 4 + g, tk], rhs=xact[:, 6 + g, tk], start=True, stop=True),
                              reads=[R_xact[4 + g], R_xact[6 + g]], writes=[R_ps[4]])
                    for h in range(8):
                        bk = h // 4
                        hs = slice((h % 4) * 128, (h % 4 + 1) * 128)
                        lbh = dtA_hi[:, c, h:h + 1].to_broadcast([128, 128])
                        lbl = dtA_lo[:, c, h:h + 1].to_broadcast([128, 128])
                        fw.op("pe", lambda e: e.matmul(ps[bk][:, hs], lhsT=lbh, rhs=triu_b[:], start=True, stop=False),
                              reads=[R_q, R_c], writes=[R_ps[bk]])
                        fw.op("pe", lambda e: e.matmul(ps[bk][:, hs], lhsT=lbl, rhs=triu_b[:], start=False, stop=True),
                              reads=[R_q, R_c], writes=[R_ps[bk]])
                        fw.op("pe", lambda e: e.matmul(ps[2 + bk][:, hs], lhsT=lbh, rhs=triu_b[:], start=True, stop=False),
                              reads=[R_q, R_c], writes=[R_ps[2 + bk]])
                        fw.op("pe", lambda e: e.matmul(ps[2 + bk][:, hs], lhsT=lbl, rhs=triu_b[:], start=False, stop=False),
                              reads=[R_q, R_c], writes=[R_ps[2 + bk]])
                        fw.op("pe", lambda e: e.matmul(ps[2 + bk][:, hs], lhsT=ident_b[:], rhs=trineg_b[:], start=False, stop=True),
                              reads=[R_const, R_c], writes=[R_ps[2 + bk]])
                    ea, R_ea = eacs.next()
                    for bk in range(2):
                        fw.op("act", lambda e: e.activation(out=ea[:, bk * 4:(bk + 1) * 4, :].rearrange("p a b -> p (a b)"), in_=ps[bk][:, :], func=AF.Exp),
                              reads=[R_ps[bk]], writes=[R_ea])
                    dc, R_dc = decT.next()
                    for h in range(8):
                        bk = h // 4
                        hs = slice((h % 4) * 128, (h % 4 + 1) * 128)
                        fw.op("act", lambda e: e.activation(out=dc[:, h, :], in_=ps[2 + bk][:, hs], func=AF.Exp, bias=nacs[:, c, h:h + 1]),
                              reads=[R_ps[2 + bk], R_q], writes=[R_dc])
                    mh, R_mh = Mh.next()
                    cm_, R_cm = cms.next()
                    for g in range(2):
                        fw.op("dve", lambda e: e.tensor_tensor(out=mh[:, g * 4:(g + 1) * 4, :], in0=dc[:, g * 4:(g + 1) * 4, :],
                                                               in1=ps[4][:, g * 128:(g + 1) * 128].unsqueeze(1).to_broadcast([128, 4, 128]), op=ALU.mult),
                              reads=[R_dc, R_ps[4]], writes=[R_mh])
                        fw.op("pool", lambda e: e.tensor_tensor(out=cm_[:, g * 4:(g + 1) * 4, :], in0=ea[:, g * 4:(g + 1) * 4, :],
                                                                in1=xact[:, 6 + g, tk].unsqueeze(1).to_broadcast([128, 4, 128]), op=ALU.mult),
                              reads=[R_ea, R_xact[6 + g]], writes=[R_cm])
                    prepd[c] = (xb, R_xb, xd, R_xd, xdd, R_xdd, mh, R_mh, cm_, R_cm)

                def fin(c):
                    T = c // 4
                    tk = slice(c * 128, (c + 1) * 128)
                    xb, R_xb, xd, R_xd, xdd, R_xdd, mh, R_mh, cm_, R_cm = prepd.pop(c)
                    if c % 4 == 0:
                        ystate["ya"] = yacc.next()
                        ystate["sg"] = sgz.next()
                        sgt, R_sgt = ystate["sg"]
                        fw.dma("sp", sgt[:], SG_v[:, 4:8, T * 512:(T + 1) * 512], reads=R_SG[4:8], writes=[R_sgt])
                    ya, R_ya = ystate["ya"]
                    sgt, R_sgt = ystate["sg"]
                    for h in range(8):
                        yo = ps[6][(h % 2) * 64:(h % 2 + 1) * 64, (h // 2) * 128:(h // 2 + 1) * 128]
                        fw.op("pe", lambda e: e.matmul(yo, lhsT=xd[:, h * 64:(h + 1) * 64], rhs=mh[:, h, :], start=True, stop=False),
                              reads=[R_xd, R_mh], writes=[R_ps[6]])
                        fw.op("pe", lambda e: e.matmul(yo, lhsT=state_bf[:, h * 64:(h + 1) * 64], rhs=cm_[:, h, :], start=False, stop=True),
                              reads=[R_sbf, R_cm], writes=[R_ps[6]])
                    for g in range(2):
                        fw.op("pe", lambda e: e.matmul(ps[5][:, g * 256:(g + 1) * 256], lhsT=xb[:, 512 + g * 128:512 + (g + 1) * 128],
                                                       rhs=xdd[:, g * 256:(g + 1) * 256], start=True, stop=True),
                              reads=[R_xb, R_xdd], writes=[R_ps[5]])
                    fw.op("dve", lambda e: e.tensor_tensor(out=state[:].rearrange("p (h q) -> p h q", h=8), in0=state[:].rearrange("p (h q) -> p h q", h=8),
                                                           in1=cdb[:, c, :].unsqueeze(2).to_broadcast([128, 8, 64]), op=ALU.mult),
                          reads=[R_state, R_q], writes=[R_state])
                    fw.op("dve", lambda e: e.tensor_tensor(out=state[:], in0=state[:], in1=ps[5][:, :], op=ALU.add),
                          reads=[R_state, R_ps[5]], writes=[R_state])
                    fw.op("act", lambda e: e.activation(out=state_bf[:], in_=state[:], func=AF.Copy), reads=[R_state], writes=[R_sbf])
                    for pr in range(4):
                        fw.op("dve", lambda e: e.scalar_tensor_tensor(out=ya[:, pr, (c % 4) * 128:(c % 4 + 1) * 128], in0=xact[:, pr, tk],
                                                                      scalar=dsk[:, pr:pr + 1], in1=ps[6][:, pr * 128:(pr + 1) * 128],
                                                                      op0=ALU.mult, op1=ALU.add),
                              reads=[R_xact[pr], R_c, R_ps[6]], writes=[R_ya])
                    if c % 4 == 3:
                        def e1(ya=ya, R_ya=R_ya, sgt=sgt, R_sgt=R_sgt):
                            fw.op("dve", lambda e: e.tensor_tensor(out=ya[:], in0=ya[:], in1=sgt[:], op=ALU.mult), reads=[R_ya, R_sgt], writes=[R_ya])
                            fw.op("act", lambda e: e.activation(out=sq[:], in_=ya[:], func=AF.Square), reads=[R_ya], writes=[R_sq])

                        def e2():
                            for pr in range(4):
                                fw.op("pe", lambda e: e.matmul(ps[7][:, :], lhsT=ones_f[:], rhs=sq[:, pr, :], start=(pr == 0), stop=(pr == 3)),
                                      reads=[R_const, R_sq], writes=[R_ps[7]])
                            fw.op("dve", lambda e: e.tensor_scalar(out=rstd[:], in0=ps[7][:, :], scalar1=1.0 / 512, scalar2=EPS, op0=ALU.mult, op1=ALU.add),
                                  reads=[R_ps[7]], writes=[R_rstd])
                            fw.op("act", lambda e: e.activation(out=rstd[:], in_=rstd[:], func=AF.Ln), reads=[R_rstd], writes=[R_rstd])
                            fw.op("act", lambda e: e.activation(out=rstd[:], in_=rstd[:], func=AF.Exp, scale=-0.5), reads=[R_rstd], writes=[R_rstd])

                        def e3(ya=ya, R_ya=R_ya, T=T):
                            ob_, R_ob = obt.next()
                            for pr in range(4):
                                fw.op("dve", lambda e: e.scalar_tensor_tensor(out=ob_[:, pr, :], in0=ya[:, pr, :], scalar=gn[:, pr:pr + 1], in1=rstd[:],
                                                                              op0=ALU.mult, op1=ALU.mult),
                                      reads=[R_ya, R_c, R_rstd], writes=[R_ob])
                            fw.dma("sp", MT_vb[:, 4:8, T * 512:(T + 1) * 512], ob_[:], reads=[R_ob], writes=R_MT[4:8])
                        e1()
                        epi.append([c + 1, e2])
                        epi.append([c + 2, e3])

                epi = []

                def run_epi(c):
                    while epi and epi[0][0] <= c:
                        epi.pop(0)[1]()

                prep(0)
                for c in range(NCH):
                    if c + 1 < NCH:
                        prep(c + 1)
                    fin(c)
                    run_epi(c)
                run_epi(NCH + 5)
                fw.barrier()


        if "A" in STAGES:
            with ExitStack() as ph:
                nscale = 64.0 ** -0.5
                R_ac = Res()
                kcT = sb("a_kcT", [64, 2, 256], BF16, ph)
                vcx = sb("a_vcx", [128, 2, 2, 128], BF16, ph)
                ovx = sb("a_ovx", [128, 2, 65], BF16, ph)
                W3 = sb("a_W3", [128, 3072], BF16, ph)
                Wc = sb("a_Wc", [128, 896], BF16, ph)
                Ww = sb("a_Ww", [128, 1536], BF16, ph)
                exg = sb("a_exg", [24, 1536], BF16, ph)
                GLs = sb("a_GLs", [24, S], BF16, ph)
                ksEx = sb("a_ksEx", [128, S], BF16, ph)
                R_ksEx = Res()
                fw.op("dve", lambda e: e.memset(kcT[:], 0.0), writes=[R_ac])
                fw.op("dve", lambda e: e.memset(vcx[:], 1.0), writes=[R_ac])
                fw.dma("sp", GLs[:], GL[:, :], reads=[R_GL], writes=[R_ac])
                phL = ExitStack()
                kv_ring = Ring([sb(f"a_kvsb{i}", [128, S], BF16, phL) for i in range(2)])
                w1b_ring = Ring([sb(f"a_w1b{i}", [128, 32, 256], BF16, phL) for i in range(2)])
                pre_cmp = []
                w1st = Ring([sb(f"a_w1st{i}", [64, 8, 256], F32, phL) for i in range(2)])
                for prow, w1 in ((P_KC, w_cmp1_k), (P_VC, w_cmp1_v)):
                    kv_sb, R_kvsb = kv_ring.next()
                    fw.dma("sp", kv_sb[:], PT[prow:prow + 128, :], reads=[R_PT[prow // 128]], writes=[R_kvsb])
                    w1v = w1.rearrange("(l d) h -> d l h", d=64)
                    w1b, R_w1b = w1b_ring.next()
                    for lq in range(4):
                        ws, R_ws = w1st.next()
                        fw.dma("sp", ws[:], w1v[:, lq * 8:(lq + 1) * 8, :], writes=[R_ws])
                        if lq % 2 == 0:
                            fw.op("dve", lambda e: e.tensor_copy(out=w1b[0:64, lq * 8:(lq + 1) * 8, :], in_=ws[:]), reads=[R_ws], writes=[R_w1b])
                        else:
                            fw.op("act", lambda e: e.activation(out=w1b[0:64, lq * 8:(lq + 1) * 8, :], in_=ws[:], func=AF.Copy), reads=[R_ws], writes=[R_w1b])
                    fw.dma("sp", w1b[64:128, :, :], w1b[0:64, :, :], reads=[R_w1b], writes=[R_w1b])
                    pre_cmp.append((kv_sb, R_kvsb, w1b, R_w1b))
                phC = ExitStack()
                cstep = build_phase_c(phC) if "C" in STAGES else (lambda k: None)
                with ExitStack() as ph2:
                    for dst, src in ((W3, c_w3), (Wc, c_wc), (Ww, c_ww)):
                        fw.dma("sp", dst[:], src[:, :], writes=[R_ac])
                    fw.dma("sp", ovx[:].rearrange("p a b -> p (a b)"), c_ovx[:, :], writes=[R_ac])
                    fw.dma("sp", exg[:], c_exg[:, :], writes=[R_ac])
                    fw.dma("sp", ksEx[64:128, :], c_ex[:, :], writes=[R_ksEx])
                    Cts = sb("a_Cts", [64, 256], F32, ph2)
                    Sts = sb("a_Sts", [64, 256], F32, ph2)
                    fw.dma("sp", Cts[:, 0:255], CS[0, 0:64, 0:255], reads=[R_CS], writes=[R_ac])
                    fw.dma("sp", Sts[:, 0:255], CS[1, 0:64, 0:255], reads=[R_CS], writes=[R_ac])
                    w2st = sb("a_w2st", [128, 2, 64], F32, ph2)
                    w2b = sb("a_w2b", [128, 2, 64], BF16, ph2)
                    posst = sb("a_posst", [32, 128], F32, ph2)
                    posb = sb("a_posb", [128, 32], BF16, ph2)
                    hT = sb("a_hT", [128, 2, 256], BF16, ph2)
                    hbias = sb("a_hbias", [128, 2], F32, ph2)
                    q32 = sb("a_q32", [64, 256], F32, ph2)
                    tq = sb("a_tq", [64, 256], F32, ph2)
                    R_m = Res()
                    fw.op("dve", lambda e: e.memset(hT[:], 0.0), writes=[R_m])
                    for which, (prow, w1, w2, pos) in enumerate(((P_KC, w_cmp1_k, w_cmp2_k, cmp_pos_k), (P_VC, w_cmp1_v, w_cmp2_v, cmp_pos_v))):
                        kv_sb, R_kvsb, w1b, R_w1b = pre_cmp[which]
                        fw.dma("sp", w2st[:], w2.rearrange("(c p) d -> p c d", p=128), reads=[R_m], writes=[R_m])
                        fw.op("dve", lambda e: e.tensor_copy(out=w2b[:], in_=w2st[:]), reads=[R_m], writes=[R_m])
                        for half in range(2):
                            fw.dma("sp", posst[0:32, half * 64:(half + 1) * 64], pos[:, :], reads=[R_m], writes=[R_m])
                        pbt, R_pbt = next_ps(0, 3)
                        fw.op("pe", lambda e: e.transpose(pbt[:, 0:32], posst[0:32, :], ident_f[0:32, 0:32]), reads=[R_m, R_const], writes=[R_pbt])
                        fw.op("dve", lambda e: e.tensor_copy(out=posb[:], in_=pbt[:, 0:32]), reads=[R_pbt], writes=[R_m])
                        for g in range(2):
                            gs = slice(g * 64, (g + 1) * 64)
                            for hc in range(2):
                                pb, R_pb = next_ps(0, 3)
                                pbb_, R_pbb = ps[7], R_ps[7]
                                for l in range(32):
                                    fw.op("pe", lambda e: e.matmul(pb[:, 0:255], lhsT=w1b[gs, l, hc * 128:(hc + 1) * 128],
                                                                   rhs=kv_sb[gs, l:l + 16 * 254 + 1:16], start=(l == 0), stop=(l == 31)),
                                          reads=[R_w1b, R_kvsb], writes=[R_pb])
                                for l in range(32):
                                    fw.op("pe", lambda e: e.matmul(pbb_[:, 0:1], lhsT=w1b[gs, l, hc * 128:(hc + 1) * 128],
                                                                   rhs=posb[gs, l:l + 1], start=(l == 0), stop=(l == 31)),
                                          reads=[R_w1b, R_m], writes=[R_pbb])
                                fw.op("dve", lambda e: e.tensor_copy(out=hbias[:, hc:hc + 1], in_=pbb_[:, 0:1]), reads=[R_pbb], writes=[R_m])
                                fw.op("act", lambda e: e.activation(out=hT[:, hc, 0:255], in_=pb[:, 0:255], func=AF.Silu, bias=hbias[:, hc:hc + 1]),
                                      reads=[R_pb, R_m], writes=[R_m])
                                cstep(4)
                            if which == 0:
                                pb, R_pb = next_ps(0, 3)
                                for hc in range(2):
                                    fw.op("pe", lambda e: e.matmul(pb[0:64, 0:256], lhsT=w2b[:, hc, :], rhs=hT[:, hc, :], start=(hc == 0), stop=(hc == 1)),
                                          reads=[R_m], writes=[R_pb])
                                fw.op("dve", lambda e: e.tensor_copy(out=q32[:], in_=pb[0:64, 0:256]), reads=[R_pb], writes=[R_m])
                                pb2, R_pb2 = next_ps(0, 3)
                                fw.op("pe", lambda e: e.matmul(pb2[0:64, 0:256], lhsT=psw_f[0:64, 0:64], rhs=q32[:], start=True, stop=True),
                                      reads=[R_m, R_const], writes=[R_pb2])
                                fw.op("dve", lambda e: e.tensor_tensor(out=tq[:, 0:255], in0=pb2[0:64, 0:255], in1=Sts[:, 0:255], op=ALU.mult),
                                      reads=[R_pb2, R_ac], writes=[R_m])
                                fw.op("dve", lambda e: e.tensor_tensor(out=q32[:, 0:255], in0=q32[:, 0:255], in1=Cts[:, 0:255], op=ALU.mult),
                                      reads=[R_m, R_ac], writes=[R_m])
                                fw.op("dve", lambda e: e.tensor_tensor(out=kcT[:, g, 0:255], in0=q32[:, 0:255], in1=tq[:, 0:255], op=ALU.add),
                                      reads=[R_m], writes=[R_ac])
                            else:
                                for nch in range(2):
                                    pb, R_pb = next_ps(0, 3)
                                    for hc in range(2):
                                        fw.op("pe", lambda e: e.matmul(pb[:, 0:64], lhsT=hT[:, hc, nch * 128:(nch + 1) * 128], rhs=w2b[:, hc, :],
                                                                       start=(hc == 0), stop=(hc == 1)), reads=[R_m], writes=[R_pb])
                                    fw.op("dve", lambda e: e.tensor_copy(out=vcx[:, g, nch, 0:64], in_=pb[:, 0:64]), reads=[R_pb], writes=[R_ac])
                    cstep(40)
                    fw.barrier()
                phC.close()
                phL.close()
                if DEBUG:
                    DBGK = nc.dram_tensor("DBGK", [64, 512], BF16, kind="ExternalOutput").ap()
                    DBGV = nc.dram_tensor("DBGV", [128, 512], BF16, kind="ExternalOutput").ap()
                    DBGS = nc.dram_tensor("DBGS", [128, 2 * S], BF16, kind="ExternalOutput").ap()
                    fw.dma("sp", DBGK[:, :], kcT[:].rearrange("p a b -> p (a b)"), reads=[R_ac], writes=[Res()])
                    fw.dma("sp", DBGV[:, :], vcx[:].rearrange("p a b c -> p (a b c)"), reads=[R_ac], writes=[Res()])
                qS = [sb(f"a_qS{e}", [128, S], BF16, ph) for e in range(4)]
                R_qS = [[Res() for _ in range(NTT)] for _ in range(4)]
                kwT = sb("a_kwT", [64, S], BF16, ph)
                R_kw = Res()
                vsx = sb("a_vsx", [128, NT, 128], BF16, ph)
                vwx = sb("a_vwx", [128, NT, 128], BF16, ph)
                R_v = Res()
                fw.op("dve", lambda e: e.memset(vsx[:], 1.0), writes=[R_v])
                fw.op("dve", lambda e: e.memset(vwx[:], 1.0), writes=[R_v])
                vst = sb("a_vst", [128, NT, 64], F32, ph)
                R_vst = Res()
                pr_ = Ring([sb(f"a_p{i}", [128, 512], BF16, ph) for i in range(6)])
                accs = [[sb(f"a_acc{par}_{e}", [64, 512], F32, ph) for e in range(4)] for par in range(2)]
                R_acc = [[Res() for _ in range(4)] for _ in range(2)]
                rdn = Ring([sb(f"a_rdn{i}", [64, 512], F32, ph) for i in range(3)])
                tmpo = Ring([sb(f"a_tmpo{i}", [64, 512], F32, ph) for i in range(2)])
                sga = Ring([sb(f"a_sga{i}", [64, 512], BF16, ph) for i in range(3)])
                oo = Ring([sb(f"a_oo{i}", [64, 512], BF16, ph) for i in range(2)])
                impacc = sb("a_impacc", [128, 4, 64], F32, ph)
                imptmp = sb("a_imptmp", [128, 4, 64], F32, ph)
                irec = sb("a_irec", [128, 4, 1], F32, ph)
                addm = sb("a_addm", [128, 4, 64], F32, ph)
                m8 = sb("a_m8", [128, 16], F32, ph)
                wk = sb("a_wk", [128, 64], F32, ph)
                nsel = sb("a_nsel", [128, 4, 64], BF16, ph)
                R_imp = Res()
                R_nsel = Res()
                R_addm = Res()
                nselT = sb("a_nselT", [64, 512], BF16, ph)
                R_nselT = Res()
                VT_v = VT.rearrange("(t p) f -> p t f", p=128)
                addm_v = c_addm.rearrange("(t p) j -> p t j", p=128)
                psI, R_psI = ps[6], R_ps[6]
                psG, R_psG = ps[7], R_ps[7]
                psTb = psI[:].bitcast(BF16)
                obank = {"i": 0}
                sbank = {"i": 0}

                def next_o():
                    i = 3 + obank["i"] % 3
                    obank["i"] += 1
                    return ps[i], R_ps[i]

                def next_s():
                    i = sbank["i"] % 3
                    sbank["i"] += 1
                    return ps[i], R_ps[i]

                def finish_branch(po, R_po, acc, R_a, h, br, first, ts):
                    rd, R_rd = rdn.next()
                    if br in (0, 1):
                        fw.op("act", lambda e: e.activation(out=rd[:], in_=po[64:128, :], func=AF.Ln, bias=epsb[0:64, 0:1]),
                              reads=[R_po, R_const], writes=[R_rd])
                        fw.op("act", lambda e: e.activation(out=rd[:], in_=rd[:], func=AF.Exp, scale=-1.0), reads=[R_rd], writes=[R_rd])
                    else:
                        fw.op("dve", lambda e: e.reciprocal(out=rd[:], in_=po[64:128, :]), reads=[R_po], writes=[R_rd])
                    fw.op("pe", lambda e: e.matmul(psG[0:64, :], lhsT=exg[:, (h * 3 + br) * 64:(h * 3 + br + 1) * 64], rhs=GLs[:, ts],
                                                   start=True, stop=True), reads=[R_ac], writes=[R_psG])
                    fw.op("dve", lambda e: e.tensor_tensor(out=rd[:], in0=rd[:], in1=psG[0:64, :], op=ALU.mult), reads=[R_rd, R_psG], writes=[R_rd])
                    if first:
                        fw.op("dve", lambda e: e.tensor_tensor(out=acc[:], in0=po[0:64, :], in1=rd[:], op=ALU.mult),
                              reads=[R_po, R_rd], writes=[R_a])
                    else:
                        tt_, R_tt = tmpo.next()
                        fw.op("dve", lambda e: e.tensor_tensor(out=tt_[:], in0=po[0:64, :], in1=rd[:], op=ALU.mult), reads=[R_po, R_rd], writes=[R_tt])
                        fw.op("pool", lambda e: e.tensor_tensor(out=acc[:], in0=acc[:], in1=tt_[:], op=ALU.add),
                              reads=[R_tt, R_a], writes=[R_a])

                def run_jobs(jobs, L=2):
                    n = len(jobs)
                    for i in range(n + L):
                        if i < n:
                            jobs[i][0]()
                        if i - L >= 0:
                            jobs[i - L][1]()

                def make_job(score_fn, pv_fn):
                    st = {}
                    return (lambda: score_fn(st), lambda: pv_fn(st))

                for g in range(2):
                    for e_ in range(4):
                        h = g * 4 + e_
                        fw.dma("sp", qS[e_][0:64, :], PT[P_Q + h * 64:P_Q + (h + 1) * 64, :], reads=[R_PT[(P_Q + h * 64) // 128]] + R_qS[e_], writes=R_qS[e_])
                    fw.dma("sp", ksEx[0:64, :], PT[P_KS + g * 64:P_KS + (g + 1) * 64, :], reads=[R_PT[P_KS // 128], R_ksEx], writes=[R_ksEx])
                    fw.dma("sp", kwT[:], PT[P_KW + g * 64:P_KW + (g + 1) * 64, :], reads=[R_PT[P_KW // 128], R_kw], writes=[R_kw])
                    for dst, c0 in ((vsx, 128 + g * 64), (vwx, 256 + g * 64)):
                        for q4 in range(4):
                            fw.dma("sp", vst[:, q4 * 8:(q4 + 1) * 8, :], VT_v[:, q4 * 8:(q4 + 1) * 8, c0:c0 + 64], reads=[R_VT, R_vst], writes=[R_vst])
                        fw.op("dve", lambda e: e.tensor_copy(out=dst[:, :, 0:64], in_=vst[:]), reads=[R_vst, R_v], writes=[R_v, R_vst])

                    def cmp_jobs(T):
                        ts = slice(T * 512, (T + 1) * 512)
                        nchs = [0] if T < 4 else [0, 1]
                        jobs = []
                        for e_ in range(4):
                            h = g * 4 + e_
                            hold = {}
                            for i, nch in enumerate(nchs):
                                def score(st, e_=e_, nch=nch):
                                    pa, R_pa = next_s()
                                    need_mask = not (nch == 0 and T >= 5)
                                    fw.op("pe", lambda e: e.matmul(pa[:, :], lhsT=kcT[:, g, nch * 128:(nch + 1) * 128], rhs=qS[e_][0:64, ts],
                                                                   start=True, stop=not need_mask), reads=[R_ac, R_qS[e_][T]], writes=[R_pa])
                                    if need_mask:
                                        sh = 512 * T - 2048 * nch
                                        fw.op("pe", lambda e: e.matmul(pa[:, :], lhsT=ident_b[:], rhs=W3[:, sh:sh + 512], start=False, stop=True),
                                              reads=[R_ac, R_const], writes=[R_pa])
                                    p, R_p = pr_.next()
                                    fw.op("act", lambda e: e.activation(out=p[:], in_=pa[:, :], func=AF.Exp, scale=nscale), reads=[R_pa], writes=[R_p])
                                    st["p"] = (p, R_p)

                                def pv(st, e_=e_, h=h, i=i, nch=nch, hold=hold):
                                    p, R_p = st["p"]
                                    if i == 0:
                                        hold["po"] = next_o()
                                    po, R_po = hold["po"]
                                    last = (i == len(nchs) - 1)
                                    fw.op("pe", lambda e: e.matmul(po[:, :], lhsT=vcx[:, g, nch, :], rhs=p[:], start=(i == 0), stop=last),
                                          reads=[R_ac, R_p], writes=[R_po])
                                    for sub in range(4):
                                        fw.op("pe", lambda e: e.matmul(psI[:, sub * 65:(sub + 1) * 65], lhsT=p[:, sub * 128:(sub + 1) * 128], rhs=ovx[:, nch, :],
                                                                       start=(i == 0 and sub == 0), stop=(last and sub == 3), skip_group_check=True),
                                              reads=[R_ac, R_p], writes=[R_psI])
                                    if last:
                                        finish_branch(po, R_po, accs[T % 2][e_], R_acc[T % 2][e_], h, 0, True, ts)
                                        pI = psI[:, 0:260].rearrange("p (s f) -> p s f", s=4)
                                        fw.op("dve", lambda e: e.tensor_scalar(out=irec[:], in0=pI[:, :, 64:65], scalar1=1e-30, scalar2=None, op0=ALU.max),
                                              reads=[R_psI], writes=[R_imp])
                                        fw.op("dve", lambda e: e.reciprocal(out=irec[:], in_=irec[:]), reads=[R_imp], writes=[R_imp])
                                        if e_ == 0:
                                            fw.op("dve", lambda e: e.tensor_tensor(out=impacc[:], in0=pI[:, :, 0:64], in1=irec[:].to_broadcast([128, 4, 64]), op=ALU.mult),
                                                  reads=[R_psI, R_imp], writes=[R_imp])
                                        else:
                                            fw.op("dve", lambda e: e.tensor_tensor(out=imptmp[:], in0=pI[:, :, 0:64], in1=irec[:].to_broadcast([128, 4, 64]), op=ALU.mult),
                                                  reads=[R_psI, R_imp], writes=[R_imp])
                                            fw.op("dve", lambda e: e.tensor_tensor(out=impacc[:], in0=impacc[:], in1=imptmp[:], op=ALU.add), reads=[R_imp], writes=[R_imp])
                                jobs.append(make_job(score, pv))
                        return jobs

                    def sel_dve(T):
                        fw.dma("sp", addm[:], addm_v[:, T * 4:(T + 1) * 4, :], reads=[R_addm], writes=[R_addm])
                        fw.op("dve", lambda e: e.tensor_tensor(out=impacc[:], in0=impacc[:], in1=addm[:], op=ALU.add), reads=[R_imp, R_addm], writes=[R_imp, R_addm])
                        for sub in range(4):
                            fw.op("dve", lambda e: e.max(out=m8[:, 0:8], in_=impacc[:, sub, :]), reads=[R_imp], writes=[R_imp])
                            fw.op("dve", lambda e: e.match_replace(out=wk[:], in_to_replace=m8[:, 0:8], in_values=impacc[:, sub, :], imm_value=-3.0e9),
                                  reads=[R_imp], writes=[R_imp])
                            fw.op("dve", lambda e: e.max(out=m8[:, 8:16], in_=wk[:]), reads=[R_imp], writes=[R_imp])
                            fw.op("dve", lambda e: e.tensor_scalar(out=nsel[:, sub, :], in0=impacc[:, sub, :], scalar1=m8[:, 15:16], scalar2=NEG,
                                                                   op0=ALU.is_lt, op1=ALU.mult), reads=[R_imp], writes=[R_nsel])

                    def sel_pe(T):
                        ts = slice(T * 512, (T + 1) * 512)
                        for sub in range(4):
                            fw.op("pe", lambda e: e.transpose(psTb[0:64, sub * 128:(sub + 1) * 128], nsel[:, sub, :], ident_b[:]),
                                  reads=[R_nsel, R_const], writes=[R_psI])
                        fw.op("dve", lambda e: e.tensor_copy(out=nselT[:], in_=psTb[0:64, 0:512]), reads=[R_psI], writes=[R_nselT])
                        for e_ in range(4):
                            fw.dma("sp", qS[e_][64:128, ts], nselT[:], reads=[R_nselT], writes=[R_qS[e_][T]])

                    def selwin_jobs(T):
                        ts = slice(T * 512, (T + 1) * 512)
                        jobs = []
                        for e_ in range(4):
                            h = g * 4 + e_
                            acc, R_a = accs[T % 2][e_], R_acc[T % 2][e_]
                            nk = 4 * T + 4
                            hold_s = {}
                            for k in range(nk):
                                def score(st, e_=e_, k=k):
                                    pa, R_pa = next_s()
                                    diag = k >= 4 * T
                                    i = k - 4 * T
                                    c0 = i * 128 if diag else 0
                                    tq = slice(T * 512 + c0, (T + 1) * 512)
                                    fw.op("pe", lambda e: e.matmul(pa[:, c0:512], lhsT=ksEx[:, k * 128:(k + 1) * 128], rhs=qS[e_][:, tq], start=True, stop=not diag),
                                          reads=[R_ksEx, R_qS[e_][T]], writes=[R_pa])
                                    if diag:
                                        fw.op("pe", lambda e: e.matmul(pa[:, c0:512], lhsT=ident_b[:], rhs=Wc[:, 384 - i * 128 + c0:384 - i * 128 + 512], start=False, stop=True),
                                              reads=[R_ac, R_const], writes=[R_pa])
                                    p, R_p = pr_.next()
                                    fw.op("act", lambda e: e.activation(out=p[:, c0:512], in_=pa[:, c0:512], func=AF.Exp, scale=nscale), reads=[R_pa], writes=[R_p])
                                    st["p"] = (p, R_p, c0)

                                def pv(st, e_=e_, h=h, k=k, nk=nk, hold=hold_s, acc=acc, R_a=R_a):
                                    p, R_p, c0 = st["p"]
                                    if k == 0:
                                        hold["po"] = next_o()
                                    po, R_po = hold["po"]
                                    fw.op("pe", lambda e: e.matmul(po[:, c0:512], lhsT=vsx[:, k, :], rhs=p[:, c0:512], start=(k == 0), stop=(k == nk - 1),
                                                                   skip_group_check=True),
                                          reads=[R_v, R_p], writes=[R_po])
                                    if k == nk - 1:
                                        finish_branch(po, R_po, acc, R_a, h, 1, False, ts)
                                jobs.append(make_job(score, pv))
                            ks_ = [k for k in range(4 * T - 4, 4 * T + 4) if k >= 0]
                            hold_w = {}
                            for j, k in enumerate(ks_):
                                def score(st, e_=e_, k=k):
                                    pa, R_pa = next_s()
                                    i = k - 4 * T
                                    c0 = max(0, i * 128)
                                    c1 = min(512, (i + 5) * 128)
                                    tq = slice(T * 512 + c0, T * 512 + c1)
                                    fw.op("pe", lambda e: e.matmul(pa[:, c0:c1], lhsT=kwT[:, k * 128:(k + 1) * 128], rhs=qS[e_][0:64, tq], start=True, stop=False),
                                          reads=[R_kw, R_qS[e_][T]], writes=[R_pa])
                                    fw.op("pe", lambda e: e.matmul(pa[:, c0:c1], lhsT=ident_b[:], rhs=Ww[:, 512 - i * 128 + c0:512 - i * 128 + c1], start=False, stop=True),
                                          reads=[R_ac, R_const], writes=[R_pa])
                                    p, R_p = pr_.next()
                                    fw.op("act", lambda e: e.activation(out=p[:, c0:c1], in_=pa[:, c0:c1], func=AF.Exp, scale=nscale), reads=[R_pa], writes=[R_p])
                                    st["p"] = (p, R_p, c0, c1)

                                def pv(st, e_=e_, h=h, j=j, k=k, nw=len(ks_), hold=hold_w, acc=acc, R_a=R_a):
                                    p, R_p, c0, c1 = st["p"]
                                    if j == 0:
                                        hold["po"] = next_o()
                                    po, R_po = hold["po"]
                                    fw.op("pe", lambda e: e.matmul(po[:, c0:c1], lhsT=vwx[:, k, :], rhs=p[:, c0:c1], start=(j == 0), stop=(j == nw - 1),
                                                                   skip_group_check=True),
                                          reads=[R_v, R_p], writes=[R_po])
                                    if j == nw - 1:
                                        finish_branch(po, R_po, acc, R_a, h, 2, False, ts)
                                        sg_, R_sg = sga.next()
                                        fw.dma("sp", sg_[:], SG[h * 64:(h + 1) * 64, ts], reads=[R_SG[h // 2]], writes=[R_sg])
                                        o_, R_o = oo.next()
                                        fw.op("pool", lambda e: e.tensor_tensor(out=o_[:], in0=acc[:], in1=sg_[:], op=ALU.mult),
                                              reads=[R_a, R_sg], writes=[R_o])
                                        fw.dma("pool", MT[h * 64:(h + 1) * 64, ts], o_[:], reads=[R_o], writes=[R_MT[h // 2]])
                                jobs.append(make_job(score, pv))
                        return jobs

                    run_jobs(cmp_jobs(0))
                    sel_dve(0)
                    sel_pe(0)
                    for T in range(NTT):
                        if T + 1 < NTT:
                            run_jobs(cmp_jobs(T + 1))
                            sel_dve(T + 1)
                        run_jobs(selwin_jobs(T))
                        if T + 1 < NTT:
                            sel_pe(T + 1)
                fw.barrier()

        active = []
        if "A" in STAGES:
            active += [0, 1, 2, 3]
        if "B" in STAGES:
            active += [4, 5, 6, 7]
        if "C" in STAGES:
            active += [8, 9, 10, 11]
        if "F" in STAGES:
            with ExitStack() as ph:
                gf = sb("gf", [128, D], F32, ph)
                R_gf = Res()
                fw.dma("sp", gf[:], g_final.partition_broadcast(128), writes=[R_gf])
                mt = Ring([sb(f"mt{i}", [128, 12, 512], BF16, ph) for i in range(2)])
                xin = Ring([sb(f"fxin{i}", [128, D], F32, ph) for i in range(2)])
                hb = Ring([sb(f"hb{i}", [128, D], F32, ph) for i in range(2)])
                ob = Ring([sb(f"ob{i}", [128, D], F32, ph) for i in range(2)])
                junk = sb("fjunk", [128, D], BF16, ph)
                R_junk = Res()
                st = Ring([sb(f"fst{i}", [128, 4], F32, ph) for i in range(2)])
                MT_v = MT.rearrange("(ck p) t -> p ck t", p=128)
                for T in range(NTT):
                    m, R_m = mt.next()
                    if active:
                        lo, hi = min(active), max(active) + 1
                        fw.dma("sp", m[:, lo:hi, :], MT_v[:, lo:hi, T * 512:(T + 1) * 512], reads=[R_MT[c] for c in active], writes=[R_m])
                    for sub in range(4):
                        tt = T * 4 + sub
                        xt, R_xt = xin.next()
                        fw.dma("sp", xt[:], x[tt * 128:(tt + 1) * 128, :], writes=[R_xt])
                        h, R_h = hb.next()
                        if active:
                            for half in range(2):
                                pb, R_pb = next_ps()
                                for i, ck in enumerate(active):
                                    fw.op("pe", lambda e: e.matmul(pb[:, :], lhsT=m[:, ck, sub * 128:(sub + 1) * 128],
                                                                   rhs=wo[:, ck, half * 512:(half + 1) * 512],
                                                                   start=(i == 0), stop=(i == len(active) - 1)),
                                          reads=[R_m, R_wo], writes=[R_pb])
                                fw.op("dve", lambda e: e.tensor_tensor(out=h[:, half * 512:(half + 1) * 512], in0=pb[:, :],
                                                                       in1=xt[:, half * 512:(half + 1) * 512], op=ALU.add),
                                      reads=[R_pb, R_xt], writes=[R_h])
                        else:
                            fw.op("dve", lambda e: e.tensor_copy(out=h[:], in_=xt[:]), reads=[R_xt], writes=[R_h])
                        s4, R_s4 = st.next()
                        fw.op("act", lambda e: e.activation(out=junk[:], in_=h[:], func=AF.Square, accum_out=s4[:, 0:1]),
                              reads=[R_h], writes=[R_junk, R_s4])
                        fw.op("dve", lambda e: e.tensor_scalar(out=s4[:, 1:2], in0=s4[:, 0:1], scalar1=1.0 / D, scalar2=EPS,
                                                               op0=ALU.mult, op1=ALU.add), reads=[R_s4], writes=[R_s4])
                        fw.op("act", lambda e: e.activation(out=s4[:, 2:3], in_=s4[:, 1:2], func=AF.Ln), reads=[R_s4], writes=[R_s4])
                        fw.op("act", lambda e: e.activation(out=s4[:, 3:4], in_=s4[:, 2:3], func=AF.Exp, scale=-0.5),
                              reads=[R_s4], writes=[R_s4])
                        o, R_o = ob.next()
                        fw.op("dve", lambda e: e.scalar_tensor_tensor(out=o[:], in0=h[:], scalar=s4[:, 3:4], in1=gf[:],
                                                                      op0=ALU.mult, op1=ALU.mult),
                              reads=[R_h, R_s4, R_gf], writes=[R_o])
                        fw.dma("pool", out[tt * 128:(tt + 1) * 128, :], o[:], reads=[R_o], writes=[R_out])
                fw.barrier()
        fw.barrier()
    print(f"[kernel] ops={fw.nops} waits={fw.nwaits} cnt={fw.cnt}")
    fw.close()
    return nc


def _host_consts():
    bf = ml_dtypes.bfloat16
    ident = np.eye(128, dtype=np.float32)
    inv = (np.float32(500000.0) ** (-(np.arange(0, 16, 2, dtype=np.float32)) / np.float32(16))).astype(np.float32)
    ropeinv = np.zeros((128, 1), np.float32)
    psw = np.zeros((128, 128), np.float32)
    for h in range(2):
        for i in range(8):
            ropeinv[h * 64 + i, 0] = inv[i]
            ropeinv[h * 64 + 8 + i, 0] = inv[i]
            psw[h * 64 + 8 + i, h * 64 + i] = -1.0
            psw[h * 64 + i, h * 64 + 8 + i] = 1.0
    ii = np.arange(128)
    triu = (ii[:, None] <= ii[None, :]).astype(np.float32)
    trineg = np.where(ii[:, None] <= ii[None, :], 0.0, NEG).astype(np.float32)
    hsel = np.zeros((128, 4, 8), np.float32)
    for p in range(128):
        for pr in range(4):
            hsel[p, pr, 2 * pr + p // 64] = 1.0
    n = np.arange(256)
    j = np.arange(64)
    cstart = 16 * n
    ov = ((cstart[:, None] < 64 * j[None, :] + 64) & (cstart[:, None] + 32 > 64 * j[None, :])).astype(np.float32)
    ov[255, :] = 0.0
    ovx = np.zeros((128, 2, 65), np.float32)
    for nch in range(2):
        ovx[:, nch, 0:64] = ov[nch * 128:(nch + 1) * 128]
        ovx[:, nch, 64] = 1.0
    col = np.arange(3072)
    w3 = np.where(col[None, :] >= 16 * ii[:, None] + 31, 0.0, NEG).astype(np.float32)
    col = np.arange(896)
    wc = np.where((col[None, :] - 384) >= ii[:, None], 0.0, NEG).astype(np.float32)
    col = np.arange(1536)
    u = col[None, :] - 512 - ii[:, None]
    ww = np.where((u >= 0) & (u < 512), 0.0, NEG).astype(np.float32)
    t = np.arange(S)
    cur = t // 64
    forced = (j[None, :] == 0) | (j[None, :] == cur[:, None]) | (j[None, :] == cur[:, None] - 1)
    valid = j[None, :] <= cur[:, None]
    addm = np.where(forced, 1e9, np.where(valid, 0.0, -1e9)).astype(np.float32)
    exg = np.zeros((24, 24, 64), np.float32)
    for r in range(24):
        exg[r, r, :] = 1.0
    ex = (np.arange(S)[None, :] // 64 == j[:, None]).astype(np.float32)
    return {"c_ident": ident, "c_ropeinv": ropeinv, "c_psw": psw, "c_triu": triu, "c_trineg": trineg,
            "c_hsel": hsel.reshape(128, 32), "c_ovx": ovx.reshape(128, 130).astype(bf), "c_w3": w3.astype(bf), "c_wc": wc.astype(bf),
            "c_ww": ww.astype(bf), "c_addm": addm, "c_exg": exg.reshape(24, 1536).astype(bf), "c_ex": ex.astype(bf)}


_NC_CACHE = {}


def kernel(**inputs):
    if "nc" not in _NC_CACHE:
        _NC_CACHE["nc"] = build_nc()
    nc = _NC_CACHE["nc"]
    consts = _host_consts()
    B = inputs["x"].shape[0]
    in_maps = []
    for b in range(B):
        m = {
            "x": np.ascontiguousarray(inputs["x"][b], dtype=np.float32),
            "mem": np.ascontiguousarray(inputs["mem"][b], dtype=np.float32),
            "positions": np.ascontiguousarray(inputs["positions"][b:b + 1], dtype=np.int32),
            "g_in": np.ascontiguousarray(inputs["g_in"][0], dtype=np.float32),
            "w_in": np.ascontiguousarray(inputs["w_in"][0], dtype=np.float32),
            "cmp_pos_k": np.ascontiguousarray(inputs["cmp_pos_k"][0], dtype=np.float32),
            "w_cmp1_k": np.ascontiguousarray(inputs["w_cmp1_k"][0], dtype=np.float32),
            "w_cmp2_k": np.ascontiguousarray(inputs["w_cmp2_k"][0], dtype=np.float32),
            "cmp_pos_v": np.ascontiguousarray(inputs["cmp_pos_v"][0], dtype=np.float32),
            "w_cmp1_v": np.ascontiguousarray(inputs["w_cmp1_v"][0], dtype=np.float32),
            "w_cmp2_v": np.ascontiguousarray(inputs["w_cmp2_v"][0], dtype=np.float32),
            "conv_w": np.ascontiguousarray(inputs["conv_w"][0], dtype=np.float32),
            "conv_b": np.ascontiguousarray(inputs["conv_b"][0], dtype=np.float32),
            "dt_bias": np.ascontiguousarray(inputs["dt_bias"][0:1], dtype=np.float32),
            "a_log": np.ascontiguousarray(inputs["a_log"][0:1], dtype=np.float32),
            "d_skip": np.ascontiguousarray(inputs["d_skip"][0:1], dtype=np.float32),
            "g_ssd_norm": np.ascontiguousarray(inputs["g_ssd_norm"][0], dtype=np.float32),
            "g_mem": np.ascontiguousarray(inputs["g_mem"][0], dtype=np.float32),
            "w_mem_kv": np.ascontiguousarray(inputs["w_mem_kv"][0], dtype=np.float32),
            "w_out": np.ascontiguousarray(inputs["w_out"][0], dtype=np.float32),
            "g_final": np.ascontiguousarray(np.asarray(inputs["g_final"]).reshape(1, D), dtype=np.float32),
        }
        m.update(consts)
        in_maps.append(m)
    if os.environ.get("KTRACE"):
        res = run_bass_kernel_spmd(nc, in_maps, core_ids=list(range(B)), trace=True)
        print("[kernel] exec_time_ns", res.exec_time_ns)
    else:
        res = run_bass_kernel_spmd(nc, in_maps, core_ids=list(range(B)))
    if DEBUG:
        _NC_CACHE["last"] = res
    return np.stack([np.asarray(r["out"], dtype=np.float32) for r in res.results], axis=0)
```

```python
import os
import math
import numpy as np
import ml_dtypes
from contextlib import ExitStack
import concourse.bass as bass
import concourse.mybir as mybir
from concourse.bass_utils import run_bass_kernel_spmd

F32 = mybir.dt.float32
BF16 = mybir.dt.bfloat16
I32 = mybir.dt.int32
AF = mybir.ActivationFunctionType
ALU = mybir.AluOpType
AX = mybir.AxisListType

S = 4096
D = 1024
NIN = 4384
NT = S // 128
NTT = S // 512
EPS = 1e-6
C_Q, C_KC, C_VC, C_KS, C_VS, C_KW, C_VW, C_GL, C_GA, C_Z, C_XBC, C_DT, C_QX, C_GX = (
    0, 512, 640, 768, 896, 1024, 1152, 1280, 1304, 1816, 2328, 3352, 3360, 3872)
P_Q, P_KC, P_KS, P_KW, P_XBC, P_QX, P_VC = 0, 512, 640, 768, 896, 1920, 2432
PT_ROWS = 2560
NEG = -30000.0

DEBUG = bool(int(os.environ.get("KDEBUG", "0")))
STAGES = os.environ.get("KSTAGES", "XPCBAF")
ASUB = int(os.environ.get("KASUB", "3"))


class Res:
    __slots__ = ("name", "w", "r")

    def __init__(self, name=""):
        self.name = name
        self.w = None
        self.r = []


class FW:
    def __init__(self, nc, n_dma_sems=24):
        self.nc = nc
        self.eng = {"pe": nc.tensor, "act": nc.scalar, "dve": nc.vector, "pool": nc.gpsimd, "sp": nc.sync}
        self.sem = {}
        self.cnt = {}
        self._ctx = []
        for e in ("pe", "act", "dve", "pool"):
            cm = nc.semaphore("sem_" + e)
            s = cm.__enter__()
            self._ctx.append(cm)
            self.sem[e] = s
            self.cnt[e] = 0
        self.dpool = {}
        for q, n in (("sp", n_dma_sems), ("pool", 12), ("act", 8)):
            lst = []
            for i in range(n):
                cm = nc.semaphore(f"dsem_{q}_{i}")
                s = cm.__enter__()
                self._ctx.append(cm)
                lst.append([s, 0])
            self.dpool[q] = [lst, 0]
        self.obs = {e: {} for e in self.eng}
        self.nwaits = 0
        self.nops = 0

    def close(self):
        for cm in reversed(self._ctx):
            cm.__exit__(None, None, None)

    def _need(self, e, tok, lst):
        if tok is None:
            return
        src, sem, val = tok
        key = id(sem)
        if self.obs[e].get(key, 0) >= val:
            return
        self.obs[e][key] = val
        lst[key] = (sem, max(val, lst.get(key, (None, 0))[1]))

    def _wait(self, e, tok):
        lst = {}
        self._need(e, tok, lst)
        for sem, val in lst.values():
            self.eng[e].wait_ge(sem, val)
            self.nwaits += 1

    def _deps(self, e, reads, writes):
        lst = {}
        for r in reads:
            if r.w is not None:
                if not (r.w[0] == e and e == "pe"):
                    self._need(e, r.w, lst)
        for w in writes:
            if w.w is not None and w.w[0] != e:
                self._need(e, w.w, lst)
            for t in w.r:
                if t[0] != e:
                    self._need(e, t, lst)
        return list(lst.values())

    def _update(self, tok, reads, writes):
        for r in reads:
            if tok[0].startswith("dma"):
                r.r = r.r + [tok]
            else:
                r.r = [t for t in r.r if t[0] != tok[0]] + [tok]
        for w in writes:
            w.w = tok
            w.r = []

    def op(self, e, fn, reads=(), writes=()):
        waits = self._deps(e, reads, writes)
        for sem, val in waits[:-1]:
            self.eng[e].wait_ge(sem, val)
            self.nwaits += 1
        ins = fn(self.eng[e])
        if waits:
            ins = ins._wait_ge(waits[-1][0], waits[-1][1])
        self.cnt[e] += 1
        ins.then_inc(self.sem[e], 1)
        tok = (e, self.sem[e], self.cnt[e])
        self._update(tok, reads, writes)
        self.nops += 1
        return tok

    def dma(self, q, out, in_, reads=(), writes=(), **kw):
        waits = self._deps(q, reads, writes)
        lst, idx = self.dpool[q]
        slot = lst[idx % len(lst)]
        self.dpool[q][1] = idx + 1
        sem, cur = slot
        if cur > 0:
            d = {}
            self._need(q, ("dma_" + q, sem, cur), d)
            waits += list(d.values())
        for sem_w, val in waits[:-1]:
            self.eng[q].wait_ge(sem_w, val)
            self.nwaits += 1
        ins = self.eng[q].dma_start(out=out, in_=in_, **kw)
        if waits:
            ins = ins._wait_ge(waits[-1][0], waits[-1][1])
        slot[1] = cur + 16
        ins.then_inc(sem, 16)
        tok = ("dma_" + q, sem, slot[1])
        self._update(tok, reads, writes)
        return tok

    def barrier(self):
        toks = []
        for e in ("pe", "act", "dve", "pool"):
            if self.cnt[e] > 0:
                toks.append((e, self.sem[e], self.cnt[e]))
        for q in self.dpool:
            for sem, cur in self.dpool[q][0]:
                if cur > 0:
                    toks.append(("dma_" + q, sem, cur))
        for e in ("pe", "act", "dve", "pool", "sp"):
            for t in toks:
                if t[0] != e:
                    self._wait(e, t)


class Ring:
    def __init__(self, bufs):
        self.bufs = [(b, Res()) for b in bufs]
        self.i = 0

    def next(self):
        b = self.bufs[self.i % len(self.bufs)]
        self.i += 1
        return b


def build_nc():
    nc = bass.Bass("TRN2", target_bir_lowering=False)
    fw = FW(nc)
    dbg_kind = "ExternalOutput" if DEBUG else "Internal"

    def din(name, shape, dt=F32):
        return nc.dram_tensor(name, list(shape), dt, kind="ExternalInput").ap()

    x = din("x", [S, D])
    mem = din("mem", [256, D])
    positions = din("positions", [1, S], I32)
    g_in = din("g_in", [D])
    w_in = din("w_in", [D, NIN])
    cmp_pos_k = din("cmp_pos_k", [32, 64])
    w_cmp1_k = din("w_cmp1_k", [2048, 256])
    w_cmp2_k = din("w_cmp2_k", [256, 64])
    cmp_pos_v = din("cmp_pos_v", [32, 64])
    w_cmp1_v = din("w_cmp1_v", [2048, 256])
    w_cmp2_v = din("w_cmp2_v", [256, 64])
    conv_w = din("conv_w", [4, 1024])
    conv_b = din("conv_b", [1024])
    dt_bias = din("dt_bias", [1, 8])
    a_log = din("a_log", [1, 8])
    d_skip = din("d_skip", [1, 8])
    g_ssd_norm = din("g_ssd_norm", [512])
    g_mem = din("g_mem", [D])
    w_mem_kv = din("w_mem_kv", [D, 1024])
    w_out = din("w_out", [1536, D])
    g_final = din("g_final", [1, D])
    c_ident = din("c_ident", [128, 128])
    c_ropeinv = din("c_ropeinv", [128, 1])
    c_psw = din("c_psw", [128, 128])
    c_triu = din("c_triu", [128, 128])
    c_trineg = din("c_trineg", [128, 128])
    c_hsel = din("c_hsel", [128, 32])
    c_ovx = din("c_ovx", [128, 130], BF16)
    c_w3 = din("c_w3", [128, 3072], BF16)
    c_wc = din("c_wc", [128, 896], BF16)
    c_ww = din("c_ww", [128, 1536], BF16)
    c_addm = din("c_addm", [S, 64])
    c_exg = din("c_exg", [24, 1536], BF16)
    c_ex = din("c_ex", [64, S], BF16)

    out = nc.dram_tensor("out", [S, D], F32, kind="ExternalOutput").ap()
    PT = nc.dram_tensor("PT", [PT_ROWS, S], BF16, kind=dbg_kind).ap()
    SG = nc.dram_tensor("SG", [1536, S], BF16, kind=dbg_kind).ap()
    GL = nc.dram_tensor("GL", [24, S], BF16, kind=dbg_kind).ap()
    VT = nc.dram_tensor("VT", [S, 392], F32, kind=dbg_kind).ap()
    CS = nc.dram_tensor("CS", [2, 128, 256], F32, kind=dbg_kind).ap()
    MT = nc.dram_tensor("MT", [1536, S], BF16, kind=dbg_kind).ap()

    R_out = Res("out")
    R_PT = [Res() for _ in range(PT_ROWS // 128)]
    R_SG = [Res() for _ in range(12)]
    R_GL = Res()
    R_VT = Res()
    R_CS = Res()
    R_MT = [Res() for _ in range(12)]

    with ExitStack() as top:
        top.enter_context(nc.allow_low_precision("bf16 matmul operands / bf16 staging by design"))

        def sb(name, shape, dt, stack=top):
            return stack.enter_context(nc.sbuf_tensor(name, list(shape), dt))

        ps = [top.enter_context(nc.psum_tensor(f"ps{i}", [128, 512], F32)) for i in range(8)]
        R_ps = [Res(f"ps{i}") for i in range(8)]
        ps_ring = {"i": 0}

        def next_ps(lo=0, hi=8):
            i = lo + ps_ring["i"] % (hi - lo)
            ps_ring["i"] += 1
            return ps[i], R_ps[i]

        ident_f = sb("ident_f", [128, 128], F32)
        ident_b = sb("ident_b", [128, 128], BF16)
        ones_f = sb("ones_f", [128, 128], F32)
        ones_b = sb("ones_b", [128, 128], BF16)
        psw_f = sb("psw_f", [128, 128], F32)
        R_const = Res("const")
        fw.dma("sp", ident_f[:], c_ident[:, :], writes=[R_const])
        fw.dma("sp", psw_f[:], c_psw[:, :], writes=[R_const])
        fw.op("dve", lambda e: e.tensor_copy(out=ident_b[:], in_=ident_f[:]), reads=[R_const], writes=[R_const])
        fw.op("dve", lambda e: e.memset(ones_f[:], 1.0), writes=[R_const])
        fw.op("dve", lambda e: e.memset(ones_b[:], 1.0), writes=[R_const])
        epsb = sb("epsb", [128, 1], F32)
        fw.op("dve", lambda e: e.memset(epsb[:], 1e-30), writes=[R_const])

        dt_all = sb("dt_all", [128, NT, 8], F32)
        R_dt = Res()
        xn_stack = ExitStack()
        xnT = sb("xnT", [128, 8, S], BF16, xn_stack)
        R_xn = [Res(f"xn{i}") for i in range(NTT)]

        def rms_transpose_phase(src, ntiles, g_vec, dstT, R_dst_of_tile, stack):
            g_sb = sb("g_sb_" + dstT.name, [128, 8], F32, stack)
            R_g = Res()
            fw.dma("sp", g_sb[:], g_vec.rearrange("(dk p) -> p dk", p=128), writes=[R_g], allow_slow_non_contiguous=True)
            xin = Ring([sb(f"xin{i}_" + dstT.name, [128, D], F32, stack) for i in range(min(4, ntiles))])
            xsc = Ring([sb(f"xsc{i}_" + dstT.name, [128, D], BF16, stack) for i in range(min(3, ntiles))])
            junk = sb("junk_" + dstT.name, [128, D], BF16, stack)
            R_junk = Res()
            st = Ring([sb(f"st{i}_" + dstT.name, [128, 4], F32, stack) for i in range(4)])
            epsd = sb("epsd_" + dstT.name, [128, 1], F32, stack)
            fw.op("dve", lambda e: e.memset(epsd[:], EPS), writes=[R_g])

            def stage_a(tt):
                xt, R_xt = xin.next()
                fw.dma("sp", xt[:], src[tt * 128:(tt + 1) * 128, :], writes=[R_xt])
                s4, R_s4 = st.next()
                fw.op("act", lambda e: e.activation(out=junk[:], in_=xt[:], func=AF.Square, accum_out=s4[:, 0:1]),
                      reads=[R_xt], writes=[R_junk, R_s4])
                fw.op("act", lambda e: e.activation(out=s4[:, 2:3], in_=s4[:, 0:1], func=AF.Ln, scale=1.0 / D, bias=epsd[:, 0:1]),
                      reads=[R_s4, R_g], writes=[R_s4])
                fw.op("act", lambda e: e.activation(out=s4[:, 3:4], in_=s4[:, 2:3], func=AF.Exp, scale=-0.5),
                      reads=[R_s4], writes=[R_s4])
                return (xt, R_xt, s4, R_s4)

            def stage_b(tt, xt, R_xt, s4, R_s4):
                xs, R_xs = xsc.next()
                fw.op("dve", lambda e: e.tensor_scalar(out=xs[:], in0=xt[:], scalar1=s4[:, 3:4], scalar2=None, op0=ALU.mult),
                      reads=[R_xt, R_s4], writes=[R_xs])
                pb, R_pb = next_ps()
                pbb = pb[:].bitcast(BF16)
                for dk in range(8):
                    fw.op("pe", lambda e: e.transpose(pbb[:, dk * 128:(dk + 1) * 128], xs[:, dk * 128:(dk + 1) * 128], ident_b[:]),
                          reads=[R_xs, R_const], writes=[R_pb])
                return (tt, pbb, R_pb)

            def stage_c(tt, pbb, R_pb):
                fw.op("dve", lambda e: e.tensor_tensor(
                    out=dstT[:, :, tt * 128:(tt + 1) * 128],
                    in0=pbb.rearrange("p (a b) -> p a b", a=8),
                    in1=g_sb[:].unsqueeze(2).to_broadcast([128, 8, 128]), op=ALU.mult),
                    reads=[R_pb, R_g], writes=[R_dst_of_tile(tt)])

            pa_, pb_ = [], []
            for tt in range(ntiles + 3):
                if tt < ntiles:
                    pa_.append((tt,) + stage_a(tt))
                if len(pb_) > 0 and tt >= 2:
                    stage_c(*pb_.pop(0))
                if len(pa_) > 0 and tt >= 1:
                    pb_.append(stage_b(*pa_.pop(0)))
            while pa_ or pb_:
                if pb_:
                    stage_c(*pb_.pop(0))
                if pa_:
                    pb_.append(stage_b(*pa_.pop(0)))

        if "X" in STAGES:
            rms_transpose_phase(x, NT, g_in, xnT, lambda tt: R_xn[tt // 4], xn_stack)

        if "P" in STAGES:
            with ExitStack() as ph:
                Ct = sb("Ct", [128, S], F32, ph)
                St = sb("St", [128, S], F32, ph)
                R_C = Res()
                def emit_rope_tables():
                    invp = sb("invp", [128, 1], F32, ph)
                    R_tmp = Res()
                    fw.dma("sp", invp[:], c_ropeinv[:, :], writes=[R_tmp])
                    CB = 512
                    posi = sb("posi", [128, CB], I32, ph)
                    ang = sb("ang", [128, CB], F32, ph)
                    kfi = sb("kfi", [128, CB], I32, ph)
                    kf = sb("kf", [128, CB], F32, ph)
                    rr = sb("rr", [128, CB], F32, ph)
                    rc = sb("rc", [128, CB], F32, ph)
                    C1 = 6.28125
                    C2 = 2 * math.pi - 6.28125
                    PI_LO = 3.141592
                    for cb in range(S // CB):
                        cs = slice(cb * CB, (cb + 1) * CB)
                        fw.dma("sp", posi[:], positions[:, cs].partition_broadcast(128), reads=[R_tmp], writes=[R_tmp])
                        fw.op("dve", lambda e: e.tensor_copy(out=ang[:], in_=posi[:]), reads=[R_tmp], writes=[R_tmp])
                        fw.op("dve", lambda e: e.tensor_scalar(out=ang[:], in0=ang[:], scalar1=invp[:, 0:1], scalar2=None, op0=ALU.mult),
                              reads=[R_tmp], writes=[R_tmp])
                        fw.op("dve", lambda e: e.tensor_scalar(out=kfi[:], in0=ang[:], scalar1=1.0 / (2 * math.pi), scalar2=None, op0=ALU.mult),
                              reads=[R_tmp], writes=[R_tmp])
                        fw.op("dve", lambda e: e.tensor_copy(out=kf[:], in_=kfi[:]), reads=[R_tmp], writes=[R_tmp])
                        fw.op("dve", lambda e: e.scalar_tensor_tensor(out=rr[:], in0=kf[:], scalar=-C1, in1=ang[:], op0=ALU.mult, op1=ALU.add),
                              reads=[R_tmp], writes=[R_tmp])
                        fw.op("dve", lambda e: e.scalar_tensor_tensor(out=rr[:], in0=kf[:], scalar=-C2, in1=rr[:], op0=ALU.mult, op1=ALU.add),
                              reads=[R_tmp], writes=[R_tmp])
                        fw.op("dve", lambda e: e.tensor_scalar(out=rc[:], in0=rr[:], scalar1=math.pi / 2, scalar2=-2 * math.pi,
                                                               op0=ALU.is_gt, op1=ALU.mult), reads=[R_tmp], writes=[R_tmp])
                        fw.op("dve", lambda e: e.scalar_tensor_tensor(out=rc[:], in0=rr[:], scalar=math.pi / 2, in1=rc[:], op0=ALU.add, op1=ALU.add),
                              reads=[R_tmp], writes=[R_tmp])
                        fw.op("dve", lambda e: e.tensor_scalar(out=rr[:], in0=rr[:], scalar1=-PI_LO, scalar2=PI_LO, op0=ALU.max, op1=ALU.min),
                              reads=[R_tmp], writes=[R_tmp])
                        fw.op("dve", lambda e: e.tensor_scalar(out=rc[:], in0=rc[:], scalar1=-PI_LO, scalar2=PI_LO, op0=ALU.max, op1=ALU.min),
                              reads=[R_tmp], writes=[R_tmp])
                        fw.op("act", lambda e: e.activation(out=St[:, cs], in_=rr[:], func=AF.Sin), reads=[R_tmp], writes=[R_C])
                        fw.op("act", lambda e: e.activation(out=Ct[:, cs], in_=rc[:], func=AF.Sin), reads=[R_tmp], writes=[R_C])
                    csc = sb("csc", [128, 2, 256], F32, ph)
                    R_csc = Res()
                    fw.op("dve", lambda e: e.tensor_copy(out=csc[:, 0, 0:255], in_=Ct[:, 31:S:16]), reads=[R_C], writes=[R_csc])
                    fw.op("dve", lambda e: e.tensor_copy(out=csc[:, 1, 0:255], in_=St[:, 31:S:16]), reads=[R_C], writes=[R_csc])
                    fw.dma("sp", CS[0, :, 0:255], csc[:, 0, 0:255], reads=[R_csc], writes=[R_CS])
                    fw.dma("sp", CS[1, :, 0:255], csc[:, 1, 0:255], reads=[R_csc], writes=[R_CS])


                wbf = Ring([sb(f"wbf{i}", [128, 8, 128], BF16, ph) for i in range(3)])
                OW = 2048
                otile = [(sb(f"otile{i}", [128, OW], BF16, ph), [Res() for _ in range(4)]) for i in range(3)]
                ot_i = {"i": 0}
                qf = Ring([sb(f"qf{i}", [128, 512], F32, ph) for i in range(2)])
                t1 = Ring([sb(f"t1_{i}", [128, 512], F32, ph) for i in range(2)])
                w_view = w_in.rearrange("(dk p) c -> p dk c", p=128)

                def load_w(c0, ncols):
                    wb, R_wb = wbf.next()
                    fw.dma("pool", wb[:, :, 0:ncols], w_view[:, :, c0:c0 + ncols], writes=[R_wb])
                    return wb, R_wb

                def proj_fm(wb, R_wb, ncols, T):
                    pb, R_pb = next_ps()
                    for dk in range(8):
                        fw.op("pe", lambda e: e.matmul(pb[0:ncols, :], lhsT=wb[:, dk, 0:ncols], rhs=xnT[:, dk, T * 512:(T + 1) * 512],
                                                       start=(dk == 0), stop=(dk == 7)),
                              reads=[R_wb, R_xn[T]], writes=[R_pb])
                    return pb, R_pb

                chunks = []
                for i in range(4):
                    chunks.append(("silu", C_GA + i * 128, 128, SG, i, R_SG))
                for i in range(4):
                    chunks.append(("silu", C_Z + i * 128, 128, SG, 4 + i, R_SG))
                for i in range(4):
                    chunks.append(("silu", C_GX + i * 128, 128, SG, 8 + i, R_SG))
                chunks.append(("copy", C_KC, 128, PT, P_KC // 128, R_PT))
                chunks.append(("copy", C_VC, 128, PT, P_VC // 128, R_PT))
                for i in range(8):
                    chunks.append(("copy", C_XBC + i * 128, 128, PT, P_XBC // 128 + i, R_PT))
                for i in range(4):
                    chunks.append(("copy", C_QX + i * 128, 128, PT, P_QX // 128 + i, R_PT))
                for i in range(4):
                    chunks.append(("rope", C_Q + i * 128, 128, PT, P_Q // 128 + i, R_PT))
                chunks.append(("rope", C_KS, 128, PT, P_KS // 128, R_PT))
                chunks.append(("rope", C_KW, 128, PT, P_KW // 128, R_PT))
                chunks.append(("gl", C_GL, 24, GL, 0, None))
                nxt = load_w(chunks[0][1], chunks[0][2])
                pend = []
                for ci, (kind, c0, ncols, dstT, drow, R_dst) in enumerate(chunks):
                    wb, R_wb = nxt
                    if ci + 1 < len(chunks):
                        nxt = load_w(chunks[ci + 1][1], chunks[ci + 1][2])
                    if ci == 12:
                        emit_rope_tables()
                    for half in range(2):
                        ot, R_ots = otile[ot_i["i"] % 3]
                        ot_i["i"] += 1
                        for T4 in range(4):
                            T = half * 4 + T4
                            pb, R_pb = proj_fm(wb, R_wb, ncols, T)
                            if pend:
                                pend.pop(0)()

                            def epi(pb=pb, R_pb=R_pb, T=T, T4=T4, kind=kind, ot=ot, R_ots=R_ots, half=half, dstT=dstT, drow=drow, R_dst=R_dst):
                                osl = ot[:, T4 * 512:(T4 + 1) * 512]
                                R_o1 = R_ots[T4]
                                if kind == "silu":
                                    fw.op("act", lambda e: e.activation(out=osl, in_=pb[:], func=AF.Silu), reads=[R_pb], writes=[R_o1])
                                elif kind == "copy":
                                    if T % 2 == 0:
                                        fw.op("dve", lambda e: e.tensor_copy(out=osl, in_=pb[:]), reads=[R_pb], writes=[R_o1])
                                    else:
                                        fw.op("act", lambda e: e.activation(out=osl, in_=pb[:], func=AF.Copy), reads=[R_pb], writes=[R_o1])
                                elif kind == "rope":
                                    q32, R_q32 = qf.next()
                                    fw.op("act", lambda e: e.activation(out=q32[:], in_=pb[:], func=AF.Copy), reads=[R_pb], writes=[R_q32])
                                    pb2, R_pb2 = next_ps()
                                    fw.op("pe", lambda e: e.matmul(pb2[:, :], lhsT=psw_f[:], rhs=q32[:], start=True, stop=True),
                                          reads=[R_const, R_q32], writes=[R_pb2])
                                    ta, R_ta = t1.next()
                                    fw.op("dve", lambda e: e.tensor_tensor(out=ta[:], in0=pb2[:], in1=St[:, T * 512:(T + 1) * 512], op=ALU.mult),
                                          reads=[R_pb2, R_C], writes=[R_ta])
                                    fw.op("pool", lambda e: e.tensor_tensor(out=q32[:], in0=q32[:], in1=Ct[:, T * 512:(T + 1) * 512], op=ALU.mult),
                                          reads=[R_q32, R_C], writes=[R_q32])
                                    fw.op("dve", lambda e: e.tensor_tensor(out=osl, in0=ta[:], in1=q32[:], op=ALU.add),
                                          reads=[R_ta, R_q32], writes=[R_o1])
                                else:
                                    ta, R_ta = t1.next()
                                    fw.op("act", lambda e: e.activation(out=ta[0:24, :], in_=pb[0:24, :], func=AF.Exp, scale=-1.0), reads=[R_pb], writes=[R_ta])
                                    fw.op("dve", lambda e: e.tensor_scalar(out=ta[0:24, :], in0=ta[0:24, :], scalar1=1.0, scalar2=None, op0=ALU.add),
                                          reads=[R_ta], writes=[R_ta])
                                    fw.op("dve", lambda e: e.reciprocal(out=osl[0:24, :], in_=ta[0:24, :]), reads=[R_ta], writes=[R_o1])
                                if T4 == 3:
                                    if kind == "gl":
                                        fw.dma("sp", GL[:, half * OW:(half + 1) * OW], ot[0:24, :], reads=R_ots, writes=[R_GL])
                                    else:
                                        fw.dma("sp", dstT[drow * 128:(drow + 1) * 128, half * OW:(half + 1) * OW], ot[:], reads=R_ots, writes=[R_dst[drow]])
                            pend.append(epi)
                while pend:
                    pend.pop(0)()
                wtm = sb("wtm", [128, 8, 392], BF16, ph)
                R_wtm = Res()
                for j, c0 in enumerate((C_VC, C_VS, C_VW)):
                    fw.dma("pool", wtm[:, :, j * 128:(j + 1) * 128], w_view[:, :, c0:c0 + 128], writes=[R_wtm])
                fw.dma("pool", wtm[:, :, 384:392], w_view[:, :, C_DT:C_DT + 8], writes=[R_wtm])
                vt_o = Ring([sb(f"vt_o{i}", [128, 392], F32, ph) for i in range(3)])
                for tt in range(NT):
                    pb, R_pb = next_ps()
                    for dk in range(8):
                        fw.op("pe", lambda e: e.matmul(pb[:, 0:392], lhsT=xnT[:, dk, tt * 128:(tt + 1) * 128], rhs=wtm[:, dk, :],
                                                       start=(dk == 0), stop=(dk == 7)),
                              reads=[R_wtm, R_xn[tt // 4]], writes=[R_pb])
                    vo, R_vo = vt_o.next()
                    if tt % 2 == 0:
                        fw.op("dve", lambda e: e.tensor_copy(out=vo[:], in_=pb[:, 0:392]), reads=[R_pb], writes=[R_vo])
                    else:
                        fw.op("act", lambda e: e.activation(out=vo[:], in_=pb[:, 0:392], func=AF.Copy), reads=[R_pb], writes=[R_vo])
                    fw.op("dve", lambda e: e.tensor_copy(out=dt_all[:, tt, :], in_=pb[:, 384:392]), reads=[R_pb], writes=[R_dt])
                    fw.dma("sp", VT[tt * 128:(tt + 1) * 128, :], vo[:], reads=[R_vo], writes=[R_VT])
                fw.barrier()

        xn_stack.close()
        wo = sb("wo", [128, 12, D], BF16)
        R_wo = Res()
        for ck in range(12):
            fw.dma("pool", wo[:, ck, :], w_out[ck * 128:(ck + 1) * 128, :], writes=[R_wo])
        def build_phase_c(ph):
            memT = sb("memT", [128, 8, 256], BF16, ph)
            R_memT = Res()
            rms_transpose_phase(mem, 2, g_mem, memT, lambda tt: R_memT, ph)
            wkv_view = w_mem_kv.rearrange("(dk p) c -> p dk c", p=128)
            kT = sb("kT", [128, 4, 256], BF16, ph)
            vtok = sb("vtok", [128, 2, 512], BF16, ph)
            R_kv = Res()
            wst = Ring([sb(f"cwst{i}", [128, 8, 128], F32, ph) for i in range(2)])
            wv = sb("cwv", [128, 8, 512], BF16, ph)
            R_wv = Res()
            wkb = Ring([sb(f"cwkb{i}", [128, 8, 128], BF16, ph) for i in range(2)])
            for h in range(4):
                ws, R_ws = wst.next()
                fw.dma("sp", ws[:], wkv_view[:, :, h * 128:(h + 1) * 128], writes=[R_ws])
                wb, R_wb = wkb.next()
                fw.op("act", lambda e: e.activation(out=wb[:], in_=ws[:], func=AF.Copy), reads=[R_ws], writes=[R_wb])
                pb, R_pb = next_ps()
                for dk in range(8):
                    fw.op("pe", lambda e: e.matmul(pb[:, 0:256], lhsT=wb[:, dk, :], rhs=memT[:, dk, :], start=(dk == 0), stop=(dk == 7)),
                          reads=[R_wb, R_memT], writes=[R_pb])
                fw.op("dve", lambda e: e.tensor_copy(out=kT[:, h, :], in_=pb[:, 0:256]), reads=[R_pb], writes=[R_kv])
            for h in range(4):
                ws, R_ws = wst.next()
                fw.dma("sp", ws[:], wkv_view[:, :, 512 + h * 128:512 + (h + 1) * 128], writes=[R_ws])
                fw.op("dve", lambda e: e.tensor_copy(out=wv[:, :, h * 128:(h + 1) * 128], in_=ws[:]), reads=[R_ws], writes=[R_wv])
            for kc in range(2):
                pb, R_pb = next_ps()
                for dk in range(8):
                    fw.op("pe", lambda e: e.matmul(pb[:, :], lhsT=memT[:, dk, kc * 128:(kc + 1) * 128], rhs=wv[:, dk, :],
                                                   start=(dk == 0), stop=(dk == 7)), reads=[R_wv, R_memT], writes=[R_pb])
                fw.op("dve", lambda e: e.tensor_copy(out=vtok[:, kc, :], in_=pb[:, :]), reads=[R_pb], writes=[R_kv])
            qx = Ring([sb(f"cqx{i}", [128, 512], BF16, ph) for i in range(3)])
            sgx = Ring([sb(f"csgx{i}", [128, 512], BF16, ph) for i in range(3)])
            pT = Ring([sb(f"cpT{i}", [128, 512], BF16, ph) for i in range(4)])
            rden = Ring([sb(f"crden{i}", [128, 512], F32, ph) for i in range(2)])
            ot = Ring([sb(f"cot{i}", [128, 512], BF16, ph) for i in range(2)])
            xscale = 128.0 ** -0.5
            cjobs = []
            for T in range(NTT):
                for h in range(4):
                    def cscore(st, T=T, h=h):
                        ts = slice(T * 512, (T + 1) * 512)
                        q, R_q = qx.next()
                        fw.dma("sp", q[:], PT[P_QX + h * 128:P_QX + (h + 1) * 128, ts], reads=[R_PT[P_QX // 128 + h]], writes=[R_q])
                        sg, R_sg = sgx.next()
                        fw.dma("sp", sg[:], SG[1024 + h * 128:1024 + (h + 1) * 128, ts], reads=[R_SG[8 + h]], writes=[R_sg])
                        pts = []
                        for kc in range(2):
                            pa, R_pa = next_ps(0, 4)
                            fw.op("pe", lambda e: e.matmul(pa[:, :], lhsT=kT[:, h, kc * 128:(kc + 1) * 128], rhs=q[:], start=True, stop=True),
                                  reads=[R_kv, R_q], writes=[R_pa])
                            p, R_p = pT.next()
                            fw.op("act", lambda e: e.activation(out=p[:], in_=pa[:, :], func=AF.Exp, scale=xscale), reads=[R_pa], writes=[R_p])
                            pts.append((p, R_p))
                        st["pts"] = pts
                        st["sg"] = (sg, R_sg)

                    def cpv(st, T=T, h=h):
                        ts = slice(T * 512, (T + 1) * 512)
                        pts = st["pts"]
                        sg, R_sg = st["sg"]
                        po, R_po = next_ps(4, 6)
                        pd, R_pd = next_ps(6, 8)
                        for kc in range(2):
                            p, R_p = pts[kc]
                            fw.op("pe", lambda e: e.matmul(po[:, :], lhsT=vtok[:, kc, h * 128:(h + 1) * 128], rhs=p[:], start=(kc == 0), stop=(kc == 1)),
                                  reads=[R_kv, R_p], writes=[R_po])
                        for kc in range(2):
                            p, R_p = pts[kc]
                            fw.op("pe", lambda e: e.matmul(pd[:, :], lhsT=ones_b[:], rhs=p[:], start=(kc == 0), stop=(kc == 1)),
                                  reads=[R_const, R_p], writes=[R_pd])
                        rd, R_rd = rden.next()
                        fw.op("act", lambda e: e.activation(out=rd[:], in_=pd[:, :], func=AF.Ln), reads=[R_pd], writes=[R_rd])
                        fw.op("act", lambda e: e.activation(out=rd[:], in_=rd[:], func=AF.Exp, scale=-1.0), reads=[R_rd], writes=[R_rd])
                        fw.op("dve", lambda e: e.tensor_tensor(out=rd[:], in0=po[:, :], in1=rd[:], op=ALU.mult), reads=[R_po, R_rd], writes=[R_rd])
                        o, R_o = ot.next()
                        fw.op("dve", lambda e: e.tensor_tensor(out=o[:], in0=rd[:], in1=sg[:], op=ALU.mult), reads=[R_rd, R_sg], writes=[R_o])
                        fw.dma("pool", MT[1024 + h * 128:1024 + (h + 1) * 128, ts], o[:], reads=[R_o], writes=[R_MT[8 + h]])
                    stt = {}
                    cjobs.append((lambda f=cscore, st=stt: f(st), lambda f=cpv, st=stt: f(st)))

            cst = {"i": 0}
            n = len(cjobs)

            def cstep(k):
                for _ in range(k):
                    i = cst["i"]
                    if i > n:
                        return
                    if i < n:
                        cjobs[i][0]()
                    if i - 1 >= 0:
                        cjobs[i - 1][1]()
                    cst["i"] = i + 1
            return cstep

        if "C" in STAGES and "A" not in STAGES:
            with ExitStack() as ph:
                cstep = build_phase_c(ph)
                cstep(40)
                fw.barrier()

        if "B" in STAGES:
            with ExitStack() as ph:
                R_c = Res()
                cw = sb("b_cw", [128, 4, 8], F32, ph)
                cbias = sb("b_cb", [128, 8], F32, ph)
                for k in range(4):
                    fw.dma("sp", cw[:, k, :], conv_w[k].rearrange("(c p) -> p c", p=128), writes=[R_c], allow_slow_non_contiguous=True)
                fw.dma("sp", cbias[:], conv_b.rearrange("(c p) -> p c", p=128), writes=[R_c], allow_slow_non_contiguous=True)
                dtb = sb("b_dtb", [128, 8], F32, ph)
                alog = sb("b_alog", [128, 8], F32, ph)
                dskb = sb("b_dskb", [128, 8], F32, ph)
                hsel = sb("b_hsel", [128, 4, 8], F32, ph)
                gn = sb("b_gn", [128, 4], F32, ph)
                triu = sb("b_triu", [128, 128], F32, ph)
                trineg = sb("b_trineg", [128, 128], F32, ph)
                fw.dma("sp", dtb[:], dt_bias.partition_broadcast(128), writes=[R_c])
                fw.dma("sp", alog[:], a_log.partition_broadcast(128), writes=[R_c])
                fw.dma("sp", dskb[:], d_skip.partition_broadcast(128), writes=[R_c])
                fw.dma("sp", hsel[:], c_hsel.rearrange("p (a b) -> p a b", a=4), writes=[R_c])
                fw.dma("sp", gn[:], g_ssd_norm.rearrange("(c p) -> p c", p=128), writes=[R_c], allow_slow_non_contiguous=True)
                fw.dma("sp", triu[:], c_triu[:, :], writes=[R_c])
                fw.dma("sp", trineg[:], c_trineg[:, :], writes=[R_c])
                trineg_b = sb("b_trineg_b", [128, 128], BF16, ph)
                fw.op("dve", lambda e: e.tensor_copy(out=trineg_b[:], in_=trineg[:]), reads=[R_c], writes=[R_c])
                dsk = sb("b_dsk", [128, 4], F32, ph)
                hs2 = sb("b_hs2", [128, 4, 8], F32, ph)
                fw.op("dve", lambda e: e.tensor_tensor(out=hs2[:], in0=hsel[:], in1=dskb[:].unsqueeze(1).to_broadcast([128, 4, 8]), op=ALU.mult),
                      reads=[R_c], writes=[R_c])
                fw.op("dve", lambda e: e.tensor_reduce(out=dsk[:], in_=hs2[:], axis=AX.X, op=ALU.add), reads=[R_c], writes=[R_c])
                xact = sb("b_xact", [128, 8, S], BF16, ph)
                R_xact = [Res() for _ in range(8)]
                with ExitStack() as ph2:
                    xpad = Ring([sb(f"b_xpad{i}", [128, S + 4], BF16, ph2) for i in range(2)])
                    dg = sb("b_dg", [128, 32, 128], BF16, ph2)
                    R_dg = Res()
                    for c in range(8):
                        for k in range(4):
                            fw.op("dve", lambda e: e.tensor_scalar(out=dg[:, c * 4 + k, :], in0=ident_f[:], scalar1=cw[:, k, c:c + 1], scalar2=None, op0=ALU.mult),
                                  reads=[R_c, R_const], writes=[R_dg])
                    for i in range(2):
                        xp, R_xp = xpad.bufs[i]
                        fw.op("dve", lambda e: e.memset(xp[:, 0:4], 0.0), writes=[R_xp])
                    for c in range(8):
                        xp, R_xp = xpad.next()
                        fw.dma("sp", xp[:, 3:S + 3], PT[P_XBC + c * 128:P_XBC + (c + 1) * 128, :], reads=[R_PT[P_XBC // 128 + c]], writes=[R_xp])
                        for T in range(NTT):
                            pb, R_pb = next_ps()
                            for k in range(4):
                                fw.op("pe", lambda e: e.matmul(pb[:, :], lhsT=dg[:, c * 4 + k, :], rhs=xp[:, T * 512 + k:T * 512 + k + 512],
                                                               start=(k == 0), stop=(k == 3)), reads=[R_dg, R_xp], writes=[R_pb])
                            fw.op("act", lambda e: e.activation(out=xact[:, c, T * 512:(T + 1) * 512], in_=pb[:, :], func=AF.Silu, bias=cbias[:, c:c + 1]),
                                  reads=[R_pb, R_c], writes=[R_xact[c]])
                    fw.barrier()
                NCH = NT
                dtv = sb("b_dt", [128, NCH, 8], F32, ph)
                dtA = sb("b_dtA", [128, NCH, 8], F32, ph)
                acs = sb("b_acs", [128, NCH, 8], F32, ph)
                nacs = sb("b_nacs", [128, NCH, 8], F32, ph)
                tot = sb("b_tot", [128, NCH, 8], F32, ph)
                cdb = sb("b_cdb", [128, NCH, 8], F32, ph)
                w2 = sb("b_w2", [128, NCH, 8], F32, ph)
                aexp = sb("b_aexp", [128, 8], F32, ph)
                R_q = Res()
                fw.op("dve", lambda e: e.tensor_tensor(out=dtv[:], in0=dt_all[:], in1=dtb[:].unsqueeze(1).to_broadcast([128, NCH, 8]), op=ALU.add),
                      reads=[R_dt, R_c], writes=[R_q])
                fw.op("act", lambda e: e.activation(out=dtv[:], in_=dtv[:], func=AF.Exp), reads=[R_q], writes=[R_q])
                fw.op("dve", lambda e: e.tensor_scalar(out=dtv[:], in0=dtv[:], scalar1=1.0, scalar2=None, op0=ALU.add), reads=[R_q], writes=[R_q])
                fw.op("act", lambda e: e.activation(out=dtv[:], in_=dtv[:], func=AF.Ln), reads=[R_q], writes=[R_q])
                fw.op("act", lambda e: e.activation(out=aexp[:], in_=alog[:], func=AF.Exp), reads=[R_c], writes=[R_q])
                fw.op("dve", lambda e: e.scalar_tensor_tensor(out=dtA[:], in0=dtv[:], scalar=-1.0, in1=aexp[:].unsqueeze(1).to_broadcast([128, NCH, 8]),
                                                              op0=ALU.mult, op1=ALU.mult), reads=[R_q], writes=[R_q])
                dtA_hi = sb("b_dtA_hi", [128, NCH, 8], BF16, ph)
                dtA_lo = sb("b_dtA_lo", [128, NCH, 8], BF16, ph)
                dtA_hf = sb("b_dtA_hf", [128, NCH, 8], F32, ph)
                triu_b = sb("b_triu_b", [128, 128], BF16, ph)
                fw.op("dve", lambda e: e.tensor_copy(out=triu_b[:], in_=triu[:]), reads=[R_c], writes=[R_c])
                fw.op("dve", lambda e: e.tensor_copy(out=dtA_hi[:], in_=dtA[:]), reads=[R_q], writes=[R_q])
                fw.op("dve", lambda e: e.tensor_copy(out=dtA_hf[:], in_=dtA_hi[:]), reads=[R_q], writes=[R_q])
                fw.op("dve", lambda e: e.tensor_tensor(out=dtA_lo[:], in0=dtA[:], in1=dtA_hf[:], op=ALU.subtract), reads=[R_q], writes=[R_q])
                fw.op("dve", lambda e: e.tensor_copy(out=dtA_hf[:], in_=dtA_lo[:]), reads=[R_q], writes=[R_q])
                fw.op("dve", lambda e: e.tensor_tensor(out=dtA[:], in0=dtA_hf[:], in1=dtA_hi[:], op=ALU.add), reads=[R_q], writes=[R_q])
                dtA2 = dtA[:].rearrange("p c h -> p (c h)")
                pb, R_pb = next_ps()
                fw.op("pe", lambda e: e.matmul(pb[:, 0:256], lhsT=triu[:], rhs=dtA2, start=True, stop=True), reads=[R_q, R_c], writes=[R_pb])
                fw.op("dve", lambda e: e.tensor_copy(out=acs[:].rearrange("p c h -> p (c h)"), in_=pb[:, 0:256]), reads=[R_pb], writes=[R_q])
                pb, R_pb = next_ps()
                fw.op("pe", lambda e: e.matmul(pb[:, 0:256], lhsT=ones_f[:], rhs=dtA2, start=True, stop=True), reads=[R_q, R_const], writes=[R_pb])
                fw.op("dve", lambda e: e.tensor_copy(out=tot[:].rearrange("p c h -> p (c h)"), in_=pb[:, 0:256]), reads=[R_pb], writes=[R_q])
                fw.op("dve", lambda e: e.tensor_scalar(out=nacs[:], in0=acs[:], scalar1=-1.0, scalar2=None, op0=ALU.mult), reads=[R_q], writes=[R_q])
                fw.op("act", lambda e: e.activation(out=cdb[:], in_=tot[:], func=AF.Exp), reads=[R_q], writes=[R_q])
                fw.op("dve", lambda e: e.tensor_tensor(out=w2[:], in0=tot[:], in1=acs[:], op=ALU.subtract), reads=[R_q], writes=[R_q])
                fw.op("act", lambda e: e.activation(out=w2[:], in_=w2[:], func=AF.Exp), reads=[R_q], writes=[R_q])
                fw.op("dve", lambda e: e.tensor_tensor(out=w2[:], in0=w2[:], in1=dtv[:], op=ALU.mult), reads=[R_q], writes=[R_q])
                state = sb("b_state", [128, 512], F32, ph)
                state_bf = sb("b_state_bf", [128, 512], BF16, ph)
                R_state = Res()
                R_sbf = Res()
                fw.op("dve", lambda e: e.memset(state[:], 0.0), writes=[R_state])
                fw.op("dve", lambda e: e.memset(state_bf[:], 0.0), writes=[R_sbf])
                xbtok = Ring([sb(f"b_xbtok{i}", [128, 768], BF16, ph) for i in range(2)])
                xdt = Ring([sb(f"b_xdt{i}", [128, 512], BF16, ph) for i in range(2)])
                xdtd = Ring([sb(f"b_xdtd{i}", [128, 512], BF16, ph) for i in range(2)])
                eacs = Ring([sb(f"b_eacs{i}", [128, 8, 128], BF16, ph) for i in range(2)])
                decT = Ring([sb(f"b_decT{i}", [128, 8, 128], BF16, ph) for i in range(2)])
                Mh = Ring([sb(f"b_Mh{i}", [128, 8, 128], BF16, ph) for i in range(2)])
                cms = Ring([sb(f"b_cms{i}", [128, 8, 128], BF16, ph) for i in range(2)])
                yacc = Ring([sb(f"b_yacc{i}", [128, 4, 512], F32, ph) for i in range(2)])
                sgz = Ring([sb(f"b_sgz{i}", [128, 4, 512], BF16, ph) for i in range(2)])
                sq = sb("b_sq", [128, 4, 512], F32, ph)
                R_sq = Res()
                rstd = sb("b_rstd", [128, 512], F32, ph)
                R_rstd = Res()
                obt = Ring([sb(f"b_obt{i}", [128, 4, 512], BF16, ph) for i in range(2)])
                SG_v = SG.rearrange("(ck p) t -> p ck t", p=128)
                MT_vb = MT.rearrange("(ck p) t -> p ck t", p=128)
                prepd = {}
                ystate = {}

                def prep(c):
                    tk = slice(c * 128, (c + 1) * 128)
                    pt_, R_pt = ps[7], R_ps[7]
                    ptb = pt_[:].bitcast(BF16)
                    for j in range(6):
                        fw.op("pe", lambda e: e.transpose(ptb[:, j * 128:(j + 1) * 128], xact[:, j, tk], ident_b[:]),
                              reads=[R_xact[j], R_const], writes=[R_pt])
                    xb, R_xb = xbtok.next()
                    fw.op("act", lambda e: e.activation(out=xb[:], in_=ptb[:, 0:768], func=AF.Copy), reads=[R_pt], writes=[R_xb])
                    xd, R_xd = xdt.next()
                    xdd, R_xdd = xdtd.next()
                    fw.op("dve", lambda e: e.tensor_tensor(out=xd[:].rearrange("p (h q) -> p h q", h=8), in0=xb[:, 0:512].rearrange("p (h q) -> p h q", h=8),
                                                           in1=dtv[:, c, :].unsqueeze(2).to_broadcast([128, 8, 64]), op=ALU.mult),
                          reads=[R_xb, R_q], writes=[R_xd])
                    fw.op("dve", lambda e: e.tensor_tensor(out=xdd[:].rearrange("p (h q) -> p h q", h=8), in0=xb[:, 0:512].rearrange("p (h q) -> p h q", h=8),
                                                           in1=w2[:, c, :].unsqueeze(2).to_broadcast([128, 8, 64]), op=ALU.mult),
                          reads=[R_xb, R_q], writes=[R_xdd])
                    for g in range(2):
                        fw.op("pe", lambda e: e.matmul(ps[4][:, g * 128:(g + 1) * 128], lhsT=xact[:, 4 + g, tk], rhs=xact[:, 6 + g, tk], start=True, stop=True),
                              reads=[R_xact[4 + g], R_xact[6 + g]], writes=[R_ps[4]])
                    for h in range(8):
                        bk = h // 4
                        hs = slice((h % 4) * 128, (h % 4 + 1) * 128)
                        lbh = dtA_hi[:, c, h:h + 1].to_broadcast([128, 128])
                        lbl = dtA_lo[:, c, h:h + 1].to_broadcast([128, 128])
                        fw.op("pe", lambda e: e.matmul(ps[bk][:, hs], lhsT=lbh, rhs=triu_b[:], start=True, stop=False),
                              reads=[R_q, R_c], writes=[R_ps[bk]])
                        fw.op("pe", lambda e: e.matmul(ps[bk][:, hs], lhsT=lbl, rhs=triu_b[:], start=False, stop=True),
                              reads=[R_q, R_c], writes=[R_ps[bk]])
                        fw.op("pe", lambda e: e.matmul(ps[2 + bk][:, hs], lhsT=lbh, rhs=triu_b[:], start=True, stop=False),
                              reads=[R_q, R_c], writes=[R_ps[2 + bk]])
                        fw.op("pe", lambda e: e.matmul(ps[2 + bk][:, hs], lhsT=lbl, rhs=triu_b[:], start=False, stop=False),
                              reads=[R_q, R_c], writes=[R_ps[2 + bk]])
                        fw.op("pe", lambda e: e.matmul(ps[2 + bk][:, hs], lhsT=ident_b[:], rhs=trineg_b[:], start=False, stop=True),
                              reads=[R_const, R_c], writes=[R_ps[2 + bk]])
                    ea, R_ea = eacs.next()
                    for bk in range(2):
                        fw.op("act", lambda e: e.activation(out=ea[:, bk * 4:(bk + 1) * 4, :].rearrange("p a b -> p (a b)"), in_=ps[bk][:, :], func=AF.Exp),
                              reads=[R_ps[bk]], writes=[R_ea])
                    dc, R_dc = decT.next()
                    for h in range(8):
                        bk = h // 4
                        hs = slice((h % 4) * 128, (h % 4 + 1) * 128)
                        fw.op("act", lambda e: e.activation(out=dc[:, h, :], in_=ps[2 + bk][:, hs], func=AF.Exp, bias=nacs[:, c, h:h + 1]),
                              reads=[R_ps[2 + bk], R_q], writes=[R_dc])
                    mh, R_mh = Mh.next()
                    cm_, R_cm = cms.next()
                    for g in range(2):
                        fw.op("dve", lambda e: e.tensor_tensor(out=mh[:, g * 4:(g + 1) * 4, :], in0=dc[:, g * 4:(g + 1) * 4, :],
                                                               in1=ps[4][:, g * 128:(g + 1) * 128].unsqueeze(1).to_broadcast([128, 4, 128]), op=ALU.mult),
                              reads=[R_dc, R_ps[4]], writes=[R_mh])
                        fw.op("pool", lambda e: e.tensor_tensor(out=cm_[:, g * 4:(g + 1) * 4, :], in0=ea[:, g * 4:(g + 1) * 4, :],
                                                                in1=xact[:, 6 + g, tk].unsqueeze(1).to_broadcast([128, 4, 128]), op=ALU.mult),
                              reads=[R_ea, R_xact[6 + g]], writes=[R_cm])
                    prepd[c] = (xb, R_xb, xd, R_xd, xdd, R_xdd, mh, R_mh, cm_, R_cm)

                def fin(c):
                    T = c // 4
                    tk = slice(c * 128, (c + 1) * 128)
                    xb, R_xb, xd, R_xd, xdd, R_xdd, mh, R_mh, cm_, R_cm = prepd.pop(c)
                    if c % 4 == 0:
                        ystate["ya"] = yacc.next()
                        ystate["sg"] = sgz.next()
                        sgt, R_sgt = ystate["sg"]
                        fw.dma("sp", sgt[:], SG_v[:, 4:8, T * 512:(T + 1) * 512], reads=R_SG[4:8], writes=[R_sgt])
                    ya, R_ya = ystate["ya"]
                    sgt, R_sgt = ystate["sg"]
                    for h in range(8):
                        yo = ps[6][(h % 2) * 64:(h % 2 + 1) * 64, (h // 2) * 128:(h // 2 + 1) * 128]
                        fw.op("pe", lambda e: e.matmul(yo, lhsT=xd[:, h * 64:(h + 1) * 64], rhs=mh[:, h, :], start=True, stop=False),
                              reads=[R_xd, R_mh], writes=[R_ps[6]])
                        fw.op("pe", lambda e: e.matmul(yo, lhsT=state_bf[:, h * 64:(h + 1) * 64], rhs=cm_[:, h, :], start=False, stop=True),
                              reads=[R_sbf, R_cm], writes=[R_ps[6]])
                    for g in range(2):
                        fw.op("pe", lambda e: e.matmul(ps[5][:, g * 256:(g + 1) * 256], lhsT=xb[:, 512 + g * 128:512 + (g + 1) * 128],
                                                       rhs=xdd[:, g * 256:(g + 1) * 256], start=True, stop=True),
                              reads=[R_xb, R_xdd], writes=[R_ps[5]])
                    fw.op("dve", lambda e: e.tensor_tensor(out=state[:].rearrange("p (h q) -> p h q", h=8), in0=state[:].rearrange("p (h q) -> p h q", h=8),
                                                           in1=cdb[:, c, :].unsqueeze(2).to_broadcast([128, 8, 64]), op=ALU.mult),
                          reads=[R_state, R_q], writes=[R_state])
                    fw.op("dve", lambda e: e.tensor_tensor(out=state[:], in0=state[:], in1=ps[5][:, :], op=ALU.add),
                          reads=[R_state, R_ps[5]], writes=[R_state])
                    fw.op("act", lambda e: e.activation(out=state_bf[:], in_=state[:], func=AF.Copy), reads=[R_state], writes=[R_sbf])
                    for pr in range(4):
                        fw.op("dve", lambda e: e.scalar_tensor_tensor(out=ya[:, pr, (c % 4) * 128:(c % 4 + 1) * 128], in0=xact[:, pr, tk],
                                                                      scalar=dsk[:, pr:pr + 1], in1=ps[6][:, pr * 128:(pr + 1) * 128],
                                                                      op0=ALU.mult, op1=ALU.add),
                              reads=[R_xact[pr], R_c, R_ps[6]], writes=[R_ya])
                    if c % 4 == 3:
                        def e1(ya=ya, R_ya=R_ya, sgt=sgt, R_sgt=R_sgt):
                            fw.op("dve", lambda e: e.tensor_tensor(out=ya[:], in0=ya[:], in1=sgt[:], op=ALU.mult), reads=[R_ya, R_sgt], writes=[R_ya])
                            fw.op("act", lambda e: e.activation(out=sq[:], in_=ya[:], func=AF.Square), reads=[R_ya], writes=[R_sq])

                        def e2():
                            for pr in range(4):
                                fw.op("pe", lambda e: e.matmul(ps[7][:, :], lhsT=ones_f[:], rhs=sq[:, pr, :], start=(pr == 0), stop=(pr == 3)),
                                      reads=[R_const, R_sq], writes=[R_ps[7]])
                            fw.op("dve", lambda e: e.tensor_scalar(out=rstd[:], in0=ps[7][:, :], scalar1=1.0 / 512, scalar2=EPS, op0=ALU.mult, op1=ALU.add),
                                  reads=[R_ps[7]], writes=[R_rstd])
                            fw.op("act", lambda e: e.activation(out=rstd[:], in_=rstd[:], func=AF.Ln), reads=[R_rstd], writes=[R_rstd])
                            fw.op("act", lambda e: e.activation(out=rstd[:], in_=rstd[:], func=AF.Exp, scale=-0.5), reads=[R_rstd], writes=[R_rstd])

                        def e3(ya=ya, R_ya=R_ya, T=T):
                            ob_, R_ob = obt.next()
                            for pr in range(4):
                                fw.op("dve", lambda e: e.scalar_tensor_tensor(out=ob_[:, pr, :], in0=ya[:, pr, :], scalar=gn[:, pr:pr + 1], in1=rstd[:],
                                                                              op0=ALU.mult, op1=ALU.mult),
                                      reads=[R_ya, R_c, R_rstd], writes=[R_ob])
                            fw.dma("sp", MT_vb[:, 4:8, T * 512:(T + 1) * 512], ob_[:], reads=[R_ob], writes=R_MT[4:8])
                        e1()
                        epi.append([c + 1, e2])
                        epi.append([c + 2, e3])

                epi = []

                def run_epi(c):
                    while epi and epi[0][0] <= c:
                        epi.pop(0)[1]()

                prep(0)
                for c in range(NCH):
                    if c + 1 < NCH:
                        prep(c + 1)
                    fin(c)
                    run_epi(c)
                run_epi(NCH + 5)
                fw.barrier()


        if "A" in STAGES:
            with ExitStack() as ph:
                nscale = 64.0 ** -0.5
                R_ac = Res()
                kcT = sb("a_kcT", [64, 2, 256], BF16, ph)
                vcx = sb("a_vcx", [128, 2, 2, 128], BF16, ph)
                ovx = sb("a_ovx", [128, 2, 65], BF16, ph)
                W3 = sb("a_W3", [128, 3072], BF16, ph)
                Wc = sb("a_Wc", [128, 896], BF16, ph)
                Ww = sb("a_Ww", [128, 1536], BF16, ph)
                exg = sb("a_exg", [24, 1536], BF16, ph)
                GLs = sb("a_GLs", [24, S], BF16, ph)
                ksEx = sb("a_ksEx", [128, S], BF16, ph)
                R_ksEx = Res()
                fw.op("dve", lambda e: e.memset(kcT[:], 0.0), writes=[R_ac])
                fw.op("dve", lambda e: e.memset(vcx[:], 1.0), writes=[R_ac])
                fw.dma("sp", GLs[:], GL[:, :], reads=[R_GL], writes=[R_ac])
                phL = ExitStack()
                kv_ring = Ring([sb(f"a_kvsb{i}", [128, S], BF16, phL) for i in range(2)])
                w1b_ring = Ring([sb(f"a_w1b{i}", [128, 32, 256], BF16, phL) for i in range(2)])
                pre_cmp = []
                w1st = Ring([sb(f"a_w1st{i}", [64, 8, 256], F32, phL) for i in range(2)])
                for prow, w1 in ((P_KC, w_cmp1_k), (P_VC, w_cmp1_v)):
                    kv_sb, R_kvsb = kv_ring.next()
                    fw.dma("sp", kv_sb[:], PT[prow:prow + 128, :], reads=[R_PT[prow // 128]], writes=[R_kvsb])
                    w1v = w1.rearrange("(l d) h -> d l h", d=64)
                    w1b, R_w1b = w1b_ring.next()
                    for lq in range(4):
                        ws, R_ws = w1st.next()
                        fw.dma("sp", ws[:], w1v[:, lq * 8:(lq + 1) * 8, :], writes=[R_ws])
                        if lq % 2 == 0:
                            fw.op("dve", lambda e: e.tensor_copy(out=w1b[0:64, lq * 8:(lq + 1) * 8, :], in_=ws[:]), reads=[R_ws], writes=[R_w1b])
                        else:
                            fw.op("act", lambda e: e.activation(out=w1b[0:64, lq * 8:(lq + 1) * 8, :], in_=ws[:], func=AF.Copy), reads=[R_ws], writes=[R_w1b])
                    fw.dma("sp", w1b[64:128, :, :], w1b[0:64, :, :], reads=[R_w1b], writes=[R_w1b])
                    pre_cmp.append((kv_sb, R_kvsb, w1b, R_w1b))
                phC = ExitStack()
                cstep = build_phase_c(phC) if "C" in STAGES else (lambda k: None)
                with ExitStack() as ph2:
                    for dst, src in ((W3, c_w3), (Wc, c_wc), (Ww, c_ww)):
                        fw.dma("sp", dst[:], src[:, :], writes=[R_ac])
                    fw.dma("sp", ovx[:].rearrange("p a b -> p (a b)"), c_ovx[:, :], writes=[R_ac])
                    fw.dma("sp", exg[:], c_exg[:, :], writes=[R_ac])
                    fw.dma("sp", ksEx[64:128, :], c_ex[:, :], writes=[R_ksEx])
                    Cts = sb("a_Cts", [64, 256], F32, ph2)
                    Sts = sb("a_Sts", [64, 256], F32, ph2)
                    fw.dma("sp", Cts[:, 0:255], CS[0, 0:64, 0:255], reads=[R_CS], writes=[R_ac])
                    fw.dma("sp", Sts[:, 0:255], CS[1, 0:64, 0:255], reads=[R_CS], writes=[R_ac])
                    w2st = sb("a_w2st", [128, 2, 64], F32, ph2)
                    w2b = sb("a_w2b", [128, 2, 64], BF16, ph2)
                    posst = sb("a_posst", [32, 128], F32, ph2)
                    posb = sb("a_posb", [128, 32], BF16, ph2)
                    hT = sb("a_hT", [128, 2, 256], BF16, ph2)
                    hbias = sb("a_hbias", [128, 2], F32, ph2)
                    q32 = sb("a_q32", [64, 256], F32, ph2)
                    tq = sb("a_tq", [64, 256], F32, ph2)
                    R_m = Res()
                    fw.op("dve", lambda e: e.memset(hT[:], 0.0), writes=[R_m])
                    for which, (prow, w1, w2, pos) in enumerate(((P_KC, w_cmp1_k, w_cmp2_k, cmp_pos_k), (P_VC, w_cmp1_v, w_cmp2_v, cmp_pos_v))):
                        kv_sb, R_kvsb, w1b, R_w1b = pre_cmp[which]
                        fw.dma("sp", w2st[:], w2.rearrange("(c p) d -> p c d", p=128), reads=[R_m], writes=[R_m])
                        fw.op("dve", lambda e: e.tensor_copy(out=w2b[:], in_=w2st[:]), reads=[R_m], writes=[R_m])
                        for half in range(2):
                            fw.dma("sp", posst[0:32, half * 64:(half + 1) * 64], pos[:, :], reads=[R_m], writes=[R_m])
                        pbt, R_pbt = next_ps(0, 3)
                        fw.op("pe", lambda e: e.transpose(pbt[:, 0:32], posst[0:32, :], ident_f[0:32, 0:32]), reads=[R_m, R_const], writes=[R_pbt])
                        fw.op("dve", lambda e: e.tensor_copy(out=posb[:], in_=pbt[:, 0:32]), reads=[R_pbt], writes=[R_m])
                        for g in range(2):
                            gs = slice(g * 64, (g + 1) * 64)
                            for hc in range(2):
                                pb, R_pb = next_ps(0, 3)
                                pbb_, R_pbb = ps[7], R_ps[7]
                                for l in range(32):
                                    fw.op("pe", lambda e: e.matmul(pb[:, 0:255], lhsT=w1b[gs, l, hc * 128:(hc + 1) * 128],
                                                                   rhs=kv_sb[gs, l:l + 16 * 254 + 1:16], start=(l == 0), stop=(l == 31)),
                                          reads=[R_w1b, R_kvsb], writes=[R_pb])
                                for l in range(32):
                                    fw.op("pe", lambda e: e.matmul(pbb_[:, 0:1], lhsT=w1b[gs, l, hc * 128:(hc + 1) * 128],
                                                                   rhs=posb[gs, l:l + 1], start=(l == 0), stop=(l == 31)),
                                          reads=[R_w1b, R_m], writes=[R_pbb])
                                fw.op("dve", lambda e: e.tensor_copy(out=hbias[:, hc:hc + 1], in_=pbb_[:, 0:1]), reads=[R_pbb], writes=[R_m])
                                fw.op("act", lambda e: e.activation(out=hT[:, hc, 0:255], in_=pb[:, 0:255], func=AF.Silu, bias=hbias[:, hc:hc + 1]),
                                      reads=[R_pb, R_m], writes=[R_m])
                                cstep(4)
                            if which == 0:
                                pb, R_pb = next_ps(0, 3)
                                for hc in range(2):
                                    fw.op("pe", lambda e: e.matmul(pb[0:64, 0:256], lhsT=w2b[:, hc, :], rhs=hT[:, hc, :], start=(hc == 0), stop=(hc == 1)),
                                          reads=[R_m], writes=[R_pb])
                                fw.op("dve", lambda e: e.tensor_copy(out=q32[:], in_=pb[0:64, 0:256]), reads=[R_pb], writes=[R_m])
                                pb2, R_pb2 = next_ps(0, 3)
                                fw.op("pe", lambda e: e.matmul(pb2[0:64, 0:256], lhsT=psw_f[0:64, 0:64], rhs=q32[:], start=True, stop=True),
                                      reads=[R_m, R_const], writes=[R_pb2])
                                fw.op("dve", lambda e: e.tensor_tensor(out=tq[:, 0:255], in0=pb2[0:64, 0:255], in1=Sts[:, 0:255], op=ALU.mult),
                                      reads=[R_pb2, R_ac], writes=[R_m])
                                fw.op("dve", lambda e: e.tensor_tensor(out=q32[:, 0:255], in0=q32[:, 0:255], in1=Cts[:, 0:255], op=ALU.mult),
                                      reads=[R_m, R_ac], writes=[R_m])
                                fw.op("dve", lambda e: e.tensor_tensor(out=kcT[:, g, 0:255], in0=q32[:, 0:255], in1=tq[:, 0:255], op=ALU.add),
                                      reads=[R_m], writes=[R_ac])
                            else:
                                for nch in range(2):
                                    pb, R_pb = next_ps(0, 3)
                                    for hc in range(2):
                                        fw.op("pe", lambda e: e.matmul(pb[:, 0:64], lhsT=hT[:, hc, nch * 128:(nch + 1) * 128], rhs=w2b[:, hc, :],
                                                                       start=(hc == 0), stop=(hc == 1)), reads=[R_m], writes=[R_pb])
                                    fw.op("dve", lambda e: e.tensor_copy(out=vcx[:, g, nch, 0:64], in_=pb[:, 0:64]), reads=[R_pb], writes=[R_ac])
                    cstep(40)
                    fw.barrier()
                phC.close()
                phL.close()
                if DEBUG:
                    DBGK = nc.dram_tensor("DBGK", [64, 512], BF16, kind="ExternalOutput").ap()
                    DBGV = nc.dram_tensor("DBGV", [128, 512], BF16, kind="ExternalOutput").ap()
                    DBGS = nc.dram_tensor("DBGS", [128, 2 * S], BF16, kind="ExternalOutput").ap()
                    fw.dma("sp", DBGK[:, :], kcT[:].rearrange("p a b -> p (a b)"), reads=[R_ac], writes=[Res()])
                    fw.dma("sp", DBGV[:, :], vcx[:].rearrange("p a b c -> p (a b c)"), reads=[R_ac], writes=[Res()])
                qS = [sb(f"a_qS{e}", [128, S], BF16, ph) for e in range(4)]
                R_qS = [[Res() for _ in range(NTT)] for _ in range(4)]
                kwT = sb("a_kwT", [64, S], BF16, ph)
                R_kw = Res()
                vsx = sb("a_vsx", [128, NT, 128], BF16, ph)
                vwx = sb("a_vwx", [128, NT, 128], BF16, ph)
                R_v = Res()
                fw.op("dve", lambda e: e.memset(vsx[:], 1.0), writes=[R_v])
                fw.op("dve", lambda e: e.memset(vwx[:], 1.0), writes=[R_v])
                vst = sb("a_vst", [128, NT, 64], F32, ph)
                R_vst = Res()
                pr_ = Ring([sb(f"a_p{i}", [128, 512], BF16, ph) for i in range(6)])
                accs = [[sb(f"a_acc{par}_{e}", [64, 512], F32, ph) for e in range(4)] for par in range(2)]
                R_acc = [[Res() for _ in range(4)] for _ in range(2)]
                rdn = Ring([sb(f"a_rdn{i}", [64, 512], F32, ph) for i in range(3)])
                tmpo = Ring([sb(f"a_tmpo{i}", [64, 512], F32, ph) for i in range(2)])
                sga = Ring([sb(f"a_sga{i}", [64, 512], BF16, ph) for i in range(3)])
                oo = Ring([sb(f"a_oo{i}", [64, 512], BF16, ph) for i in range(2)])
                impacc = sb("a_impacc", [128, 4, 64], F32, ph)
                imptmp = sb("a_imptmp", [128, 4, 64], F32, ph)
                irec = sb("a_irec", [128, 4, 1], F32, ph)
                addm = sb("a_addm", [128, 4, 64], F32, ph)
                m8 = sb("a_m8", [128, 16], F32, ph)
                wk = sb("a_wk", [128, 64], F32, ph)
                nsel = sb("a_nsel", [128, 4, 64], BF16, ph)
                R_imp = Res()
                R_nsel = Res()
                R_addm = Res()
                nselT = sb("a_nselT", [64, 512], BF16, ph)
                R_nselT = Res()
                VT_v = VT.rearrange("(t p) f -> p t f", p=128)
                addm_v = c_addm.rearrange("(t p) j -> p t j", p=128)
                psI, R_psI = ps[6], R_ps[6]
                psG, R_psG = ps[7], R_ps[7]
                psTb = psI[:].bitcast(BF16)
                obank = {"i": 0}
                sbank = {"i": 0}

                def next_o(cmp=False):
                    if cmp:
                        return ps[5], R_ps[5]
                    i = 3 + obank["i"] % 2
                    obank["i"] += 1
                    return ps[i], R_ps[i]

                def next_s():
                    i = sbank["i"] % 3
                    sbank["i"] += 1
                    return ps[i], R_ps[i]

                def finish_branch(po, R_po, acc, R_a, h, br, first, ts):
                    rd, R_rd = rdn.next()
                    if br in (0, 1):
                        fw.op("act", lambda e: e.activation(out=rd[:], in_=po[64:128, :], func=AF.Ln, bias=epsb[0:64, 0:1]),
                              reads=[R_po, R_const], writes=[R_rd])
                        fw.op("act", lambda e: e.activation(out=rd[:], in_=rd[:], func=AF.Exp, scale=-1.0), reads=[R_rd], writes=[R_rd])
                    else:
                        fw.op("dve", lambda e: e.reciprocal(out=rd[:], in_=po[64:128, :]), reads=[R_po], writes=[R_rd])
                    fw.op("pe", lambda e: e.matmul(psG[0:64, :], lhsT=exg[:, (h * 3 + br) * 64:(h * 3 + br + 1) * 64], rhs=GLs[:, ts],
                                                   start=True, stop=True), reads=[R_ac], writes=[R_psG])
                    fw.op("dve", lambda e: e.tensor_tensor(out=rd[:], in0=rd[:], in1=psG[0:64, :], op=ALU.mult), reads=[R_rd, R_psG], writes=[R_rd])
                    if first:
                        fw.op("dve", lambda e: e.tensor_tensor(out=acc[:], in0=po[0:64, :], in1=rd[:], op=ALU.mult),
                              reads=[R_po, R_rd], writes=[R_a])
                    else:
                        tt_, R_tt = tmpo.next()
                        fw.op("dve", lambda e: e.tensor_tensor(out=tt_[:], in0=po[0:64, :], in1=rd[:], op=ALU.mult), reads=[R_po, R_rd], writes=[R_tt])
                        fw.op("pool", lambda e: e.tensor_tensor(out=acc[:], in0=acc[:], in1=tt_[:], op=ALU.add),
                              reads=[R_tt, R_a], writes=[R_a])

                def run_jobs(jobs, L=2):
                    n = len(jobs)
                    for i in range(n + L):
                        if i < n:
                            jobs[i][0]()
                        if i - L >= 0:
                            jobs[i - L][1]()

                def make_job(score_fn, pv_fn):
                    st = {}
                    return (lambda: score_fn(st), lambda: pv_fn(st))

                for g in range(2):
                    for e_ in range(4):
                        h = g * 4 + e_
                        fw.dma("sp", qS[e_][0:64, :], PT[P_Q + h * 64:P_Q + (h + 1) * 64, :], reads=[R_PT[(P_Q + h * 64) // 128]] + R_qS[e_], writes=R_qS[e_])
                    fw.dma("sp", ksEx[0:64, :], PT[P_KS + g * 64:P_KS + (g + 1) * 64, :], reads=[R_PT[P_KS // 128], R_ksEx], writes=[R_ksEx])
                    fw.dma("sp", kwT[:], PT[P_KW + g * 64:P_KW + (g + 1) * 64, :], reads=[R_PT[P_KW // 128], R_kw], writes=[R_kw])
                    for dst, c0 in ((vsx, 128 + g * 64), (vwx, 256 + g * 64)):
                        for q4 in range(4):
                            fw.dma("sp", vst[:, q4 * 8:(q4 + 1) * 8, :], VT_v[:, q4 * 8:(q4 + 1) * 8, c0:c0 + 64], reads=[R_VT, R_vst], writes=[R_vst])
                        fw.op("dve", lambda e: e.tensor_copy(out=dst[:, :, 0:64], in_=vst[:]), reads=[R_vst, R_v], writes=[R_v, R_vst])

                    def cmp_jobs(T):
                        ts = slice(T * 512, (T + 1) * 512)
                        nchs = [0] if T < 4 else [0, 1]
                        jobs = []
                        for e_ in range(4):
                            h = g * 4 + e_
                            hold = {}
                            for i, nch in enumerate(nchs):
                                def score(st, e_=e_, nch=nch):
                                    pa, R_pa = next_s()
                                    need_mask = not (nch == 0 and T >= 5)
                                    fw.op("pe", lambda e: e.matmul(pa[:, :], lhsT=kcT[:, g, nch * 128:(nch + 1) * 128], rhs=qS[e_][0:64, ts],
                                                                   start=True, stop=not need_mask), reads=[R_ac, R_qS[e_][T]], writes=[R_pa])
                                    if need_mask:
                                        sh = 512 * T - 2048 * nch
                                        fw.op("pe", lambda e: e.matmul(pa[:, :], lhsT=ident_b[:], rhs=W3[:, sh:sh + 512], start=False, stop=True),
                                              reads=[R_ac, R_const], writes=[R_pa])
                                    p, R_p = pr_.next()
                                    fw.op("act", lambda e: e.activation(out=p[:], in_=pa[:, :], func=AF.Exp, scale=nscale), reads=[R_pa], writes=[R_p])
                                    st["p"] = (p, R_p)

                                def pv(st, e_=e_, h=h, i=i, nch=nch, hold=hold):
                                    p, R_p = st["p"]
                                    if i == 0:
                                        hold["po"] = next_o(cmp=True)
                                    po, R_po = hold["po"]
                                    last = (i == len(nchs) - 1)
                                    fw.op("pe", lambda e: e.matmul(po[:, :], lhsT=vcx[:, g, nch, :], rhs=p[:], start=(i == 0), stop=last),
                                          reads=[R_ac, R_p], writes=[R_po])
                                    for sub in range(4):
                                        fw.op("pe", lambda e: e.matmul(psI[:, sub * 65:(sub + 1) * 65], lhsT=p[:, sub * 128:(sub + 1) * 128], rhs=ovx[:, nch, :],
                                                                       start=(i == 0 and sub == 0), stop=(last and sub == 3), skip_group_check=True),
                                              reads=[R_ac, R_p], writes=[R_psI])
                                    if last:
                                        finish_branch(po, R_po, accs[T % 2][e_], R_acc[T % 2][e_], h, 0, True, ts)
                                        pI = psI[:, 0:260].rearrange("p (s f) -> p s f", s=4)
                                        fw.op("dve", lambda e: e.tensor_scalar(out=irec[:], in0=pI[:, :, 64:65], scalar1=1e-30, scalar2=None, op0=ALU.max),
                                              reads=[R_psI], writes=[R_imp])
                                        fw.op("dve", lambda e: e.reciprocal(out=irec[:], in_=irec[:]), reads=[R_imp], writes=[R_imp])
                                        if e_ == 0:
                                            fw.op("dve", lambda e: e.tensor_tensor(out=impacc[:], in0=pI[:, :, 0:64], in1=irec[:].to_broadcast([128, 4, 64]), op=ALU.mult),
                                                  reads=[R_psI, R_imp], writes=[R_imp])
                                        else:
                                            fw.op("dve", lambda e: e.tensor_tensor(out=imptmp[:], in0=pI[:, :, 0:64], in1=irec[:].to_broadcast([128, 4, 64]), op=ALU.mult),
                                                  reads=[R_psI, R_imp], writes=[R_imp])
                                            fw.op("dve", lambda e: e.tensor_tensor(out=impacc[:], in0=impacc[:], in1=imptmp[:], op=ALU.add), reads=[R_imp], writes=[R_imp])
                                jobs.append(make_job(score, pv))
                        return jobs

                    def sel_dve(T):
                        fw.dma("sp", addm[:], addm_v[:, T * 4:(T + 1) * 4, :], reads=[R_addm], writes=[R_addm])
                        fw.op("dve", lambda e: e.tensor_tensor(out=impacc[:], in0=impacc[:], in1=addm[:], op=ALU.add), reads=[R_imp, R_addm], writes=[R_imp, R_addm])
                        for sub in range(4):
                            fw.op("dve", lambda e: e.max(out=m8[:, 0:8], in_=impacc[:, sub, :]), reads=[R_imp], writes=[R_imp])
                            fw.op("dve", lambda e: e.match_replace(out=wk[:], in_to_replace=m8[:, 0:8], in_values=impacc[:, sub, :], imm_value=-3.0e9),
                                  reads=[R_imp], writes=[R_imp])
                            fw.op("dve", lambda e: e.max(out=m8[:, 8:16], in_=wk[:]), reads=[R_imp], writes=[R_imp])
                            fw.op("dve", lambda e: e.tensor_scalar(out=nsel[:, sub, :], in0=impacc[:, sub, :], scalar1=m8[:, 15:16], scalar2=NEG,
                                                                   op0=ALU.is_lt, op1=ALU.mult), reads=[R_imp], writes=[R_nsel])

                    def sel_pe(T):
                        ts = slice(T * 512, (T + 1) * 512)
                        for sub in range(4):
                            fw.op("pe", lambda e: e.transpose(psTb[0:64, sub * 128:(sub + 1) * 128], nsel[:, sub, :], ident_b[:]),
                                  reads=[R_nsel, R_const], writes=[R_psI])
                        fw.op("dve", lambda e: e.tensor_copy(out=nselT[:], in_=psTb[0:64, 0:512]), reads=[R_psI], writes=[R_nselT])
                        for e_ in range(4):
                            fw.dma("sp", qS[e_][64:128, ts], nselT[:], reads=[R_nselT], writes=[R_qS[e_][T]])

                    def selwin_jobs(T):
                        ts = slice(T * 512, (T + 1) * 512)
                        jobs = []
                        for e_ in range(4):
                            h = g * 4 + e_
                            acc, R_a = accs[T % 2][e_], R_acc[T % 2][e_]
                            nk = 4 * T + 4
                            hold_s = {}
                            for k in range(nk):
                                def score(st, e_=e_, k=k):
                                    pa, R_pa = next_s()
                                    diag = k >= 4 * T
                                    i = k - 4 * T
                                    c0 = i * 128 if diag else 0
                                    tq = slice(T * 512 + c0, (T + 1) * 512)
                                    fw.op("pe", lambda e: e.matmul(pa[:, c0:512], lhsT=ksEx[:, k * 128:(k + 1) * 128], rhs=qS[e_][:, tq], start=True, stop=not diag),
                                          reads=[R_ksEx, R_qS[e_][T]], writes=[R_pa])
                                    if diag:
                                        fw.op("pe", lambda e: e.matmul(pa[:, c0:512], lhsT=ident_b[:], rhs=Wc[:, 384 - i * 128 + c0:384 - i * 128 + 512], start=False, stop=True),
                                              reads=[R_ac, R_const], writes=[R_pa])
                                    p, R_p = pr_.next()
                                    fw.op("act", lambda e: e.activation(out=p[:, c0:512], in_=pa[:, c0:512], func=AF.Exp, scale=nscale), reads=[R_pa], writes=[R_p])
                                    st["p"] = (p, R_p, c0)

                                def pv(st, e_=e_, h=h, k=k, nk=nk, hold=hold_s, acc=acc, R_a=R_a):
                                    p, R_p, c0 = st["p"]
                                    if k == 0:
                                        hold["po"] = next_o()
                                    po, R_po = hold["po"]
                                    fw.op("pe", lambda e: e.matmul(po[:, c0:512], lhsT=vsx[:, k, :], rhs=p[:, c0:512], start=(k == 0), stop=(k == nk - 1),
                                                                   skip_group_check=True),
                                          reads=[R_v, R_p], writes=[R_po])
                                    if k == nk - 1:
                                        finish_branch(po, R_po, acc, R_a, h, 1, False, ts)
                                jobs.append(make_job(score, pv))
                            ks_ = [k for k in range(4 * T - 4, 4 * T + 4) if k >= 0]
                            hold_w = {}
                            for j, k in enumerate(ks_):
                                def score(st, e_=e_, k=k):
                                    pa, R_pa = next_s()
                                    i = k - 4 * T
                                    c0 = max(0, i * 128)
                                    c1 = min(512, (i + 5) * 128)
                                    tq = slice(T * 512 + c0, T * 512 + c1)
                                    fw.op("pe", lambda e: e.matmul(pa[:, c0:c1], lhsT=kwT[:, k * 128:(k + 1) * 128], rhs=qS[e_][0:64, tq], start=True, stop=False),
                                          reads=[R_kw, R_qS[e_][T]], writes=[R_pa])
                                    fw.op("pe", lambda e: e.matmul(pa[:, c0:c1], lhsT=ident_b[:], rhs=Ww[:, 512 - i * 128 + c0:512 - i * 128 + c1], start=False, stop=True),
                                          reads=[R_ac, R_const], writes=[R_pa])
                                    p, R_p = pr_.next()
                                    fw.op("act", lambda e: e.activation(out=p[:, c0:c1], in_=pa[:, c0:c1], func=AF.Exp, scale=nscale), reads=[R_pa], writes=[R_p])
                                    st["p"] = (p, R_p, c0, c1)

                                def pv(st, e_=e_, h=h, j=j, k=k, nw=len(ks_), hold=hold_w, acc=acc, R_a=R_a):
                                    p, R_p, c0, c1 = st["p"]
                                    if j == 0:
                                        hold["po"] = next_o()
                                    po, R_po = hold["po"]
                                    fw.op("pe", lambda e: e.matmul(po[:, c0:c1], lhsT=vwx[:, k, :], rhs=p[:, c0:c1], start=(j == 0), stop=(j == nw - 1),
                                                                   skip_group_check=True),
                                          reads=[R_v, R_p], writes=[R_po])
                                    if j == nw - 1:
                                        finish_branch(po, R_po, acc, R_a, h, 2, False, ts)
                                        sg_, R_sg = sga.next()
                                        fw.dma("sp", sg_[:], SG[h * 64:(h + 1) * 64, ts], reads=[R_SG[h // 2]], writes=[R_sg])
                                        o_, R_o = oo.next()
                                        fw.op("pool", lambda e: e.tensor_tensor(out=o_[:], in0=acc[:], in1=sg_[:], op=ALU.mult),
                                              reads=[R_a, R_sg], writes=[R_o])
                                        fw.dma("pool", MT[h * 64:(h + 1) * 64, ts], o_[:], reads=[R_o], writes=[R_MT[h // 2]])
                                jobs.append(make_job(score, pv))
                        return jobs

                    run_jobs(cmp_jobs(0))
                    sel_dve(0)
                    sel_pe(0)
                    for T in range(NTT):
                        SJ = selwin_jobs(T)
                        if T + 1 < NTT:
                            C = cmp_jobs(T + 1)
                            extra = {}
                            step = 4 if len(C) <= 4 else 3
                            for j, cj in enumerate(C):
                                extra.setdefault(1 + step * j, []).append(cj)
                            pos_d = 1 + step * len(C) + 4
                            pos_p = pos_d + 6
                            extra.setdefault(pos_d, []).append((lambda T1=T + 1: sel_dve(T1), lambda: None))
                            extra.setdefault(pos_p, []).append((lambda T1=T + 1: sel_pe(T1), lambda: None))
                            assert pos_p < len(SJ)
                            merged = []
                            for i, sj in enumerate(SJ):
                                merged.extend(extra.get(i, []))
                                merged.append(sj)
                            run_jobs(merged)
                        else:
                            run_jobs(SJ)
                fw.barrier()

        active = []
        if "A" in STAGES:
            active += [0, 1, 2, 3]
        if "B" in STAGES:
            active += [4, 5, 6, 7]
        if "C" in STAGES:
            active += [8, 9, 10, 11]
        if "F" in STAGES:
            with ExitStack() as ph:
                gf = sb("gf", [128, D], F32, ph)
                R_gf = Res()
                fw.dma("sp", gf[:], g_final.partition_broadcast(128), writes=[R_gf])
                mt = Ring([sb(f"mt{i}", [128, 12, 512], BF16, ph) for i in range(2)])
                xin = Ring([sb(f"fxin{i}", [128, D], F32, ph) for i in range(2)])
                hb = Ring([sb(f"hb{i}", [128, D], F32, ph) for i in range(2)])
                ob = Ring([sb(f"ob{i}", [128, D], F32, ph) for i in range(2)])
                junk = sb("fjunk", [128, D], BF16, ph)
                R_junk = Res()
                st = Ring([sb(f"fst{i}", [128, 4], F32, ph) for i in range(2)])
                MT_v = MT.rearrange("(ck p) t -> p ck t", p=128)
                for T in range(NTT):
                    m, R_m = mt.next()
                    if active:
                        lo, hi = min(active), max(active) + 1
                        fw.dma("sp", m[:, lo:hi, :], MT_v[:, lo:hi, T * 512:(T + 1) * 512], reads=[R_MT[c] for c in active], writes=[R_m])
                    for sub in range(4):
                        tt = T * 4 + sub
                        xt, R_xt = xin.next()
                        fw.dma("sp", xt[:], x[tt * 128:(tt + 1) * 128, :], writes=[R_xt])
                        h, R_h = hb.next()
                        if active:
                            for half in range(2):
                                pb, R_pb = next_ps()
                                for i, ck in enumerate(active):
                                    fw.op("pe", lambda e: e.matmul(pb[:, :], lhsT=m[:, ck, sub * 128:(sub + 1) * 128],
                                                                   rhs=wo[:, ck, half * 512:(half + 1) * 512],
                                                                   start=(i == 0), stop=(i == len(active) - 1)),
                                          reads=[R_m, R_wo], writes=[R_pb])
                                fw.op("dve", lambda e: e.tensor_tensor(out=h[:, half * 512:(half + 1) * 512], in0=pb[:, :],
                                                                       in1=xt[:, half * 512:(half + 1) * 512], op=ALU.add),
                                      reads=[R_pb, R_xt], writes=[R_h])
                        else:
                            fw.op("dve", lambda e: e.tensor_copy(out=h[:], in_=xt[:]), reads=[R_xt], writes=[R_h])
                        s4, R_s4 = st.next()
                        fw.op("act", lambda e: e.activation(out=junk[:], in_=h[:], func=AF.Square, accum_out=s4[:, 0:1]),
                              reads=[R_h], writes=[R_junk, R_s4])
                        fw.op("dve", lambda e: e.tensor_scalar(out=s4[:, 1:2], in0=s4[:, 0:1], scalar1=1.0 / D, scalar2=EPS,
                                                               op0=ALU.mult, op1=ALU.add), reads=[R_s4], writes=[R_s4])
                        fw.op("act", lambda e: e.activation(out=s4[:, 2:3], in_=s4[:, 1:2], func=AF.Ln), reads=[R_s4], writes=[R_s4])
                        fw.op("act", lambda e: e.activation(out=s4[:, 3:4], in_=s4[:, 2:3], func=AF.Exp, scale=-0.5),
                              reads=[R_s4], writes=[R_s4])
                        o, R_o = ob.next()
                        fw.op("dve", lambda e: e.scalar_tensor_tensor(out=o[:], in0=h[:], scalar=s4[:, 3:4], in1=gf[:],
                                                                      op0=ALU.mult, op1=ALU.mult),
                              reads=[R_h, R_s4, R_gf], writes=[R_o])
                        fw.dma("pool", out[tt * 128:(tt + 1) * 128, :], o[:], reads=[R_o], writes=[R_out])
                fw.barrier()
        fw.barrier()
    print(f"[kernel] ops={fw.nops} waits={fw.nwaits} cnt={fw.cnt}")
    fw.close()
    return nc


def _host_consts():
    bf = ml_dtypes.bfloat16
    ident = np.eye(128, dtype=np.float32)
    inv = (np.float32(500000.0) ** (-(np.arange(0, 16, 2, dtype=np.float32)) / np.float32(16))).astype(np.float32)
    ropeinv = np.zeros((128, 1), np.float32)
    psw = np.zeros((128, 128), np.float32)
    for h in range(2):
        for i in range(8):
            ropeinv[h * 64 + i, 0] = inv[i]
            ropeinv[h * 64 + 8 + i, 0] = inv[i]
            psw[h * 64 + 8 + i, h * 64 + i] = -1.0
            psw[h * 64 + i, h * 64 + 8 + i] = 1.0
    ii = np.arange(128)
    triu = (ii[:, None] <= ii[None, :]).astype(np.float32)
    trineg = np.where(ii[:, None] <= ii[None, :], 0.0, NEG).astype(np.float32)
    hsel = np.zeros((128, 4, 8), np.float32)
    for p in range(128):
        for pr in range(4):
            hsel[p, pr, 2 * pr + p // 64] = 1.0
    n = np.arange(256)
    j = np.arange(64)
    cstart = 16 * n
    ov = ((cstart[:, None] < 64 * j[None, :] + 64) & (cstart[:, None] + 32 > 64 * j[None, :])).astype(np.float32)
    ov[255, :] = 0.0
    ovx = np.zeros((128, 2, 65), np.float32)
    for nch in range(2):
        ovx[:, nch, 0:64] = ov[nch * 128:(nch + 1) * 128]
        ovx[:, nch, 64] = 1.0
    col = np.arange(3072)
    w3 = np.where(col[None, :] >= 16 * ii[:, None] + 31, 0.0, NEG).astype(np.float32)
    col = np.arange(896)
    wc = np.where((col[None, :] - 384) >= ii[:, None], 0.0, NEG).astype(np.float32)
    col = np.arange(1536)
    u = col[None, :] - 512 - ii[:, None]
    ww = np.where((u >= 0) & (u < 512), 0.0, NEG).astype(np.float32)
    t = np.arange(S)
    cur = t // 64
    forced = (j[None, :] == 0) | (j[None, :] == cur[:, None]) | (j[None, :] == cur[:, None] - 1)
    valid = j[None, :] <= cur[:, None]
    addm = np.where(forced, 1e9, np.where(valid, 0.0, -1e9)).astype(np.float32)
    exg = np.zeros((24, 24, 64), np.float32)
    for r in range(24):
        exg[r, r, :] = 1.0
    ex = (np.arange(S)[None, :] // 64 == j[:, None]).astype(np.float32)
    return {"c_ident": ident, "c_ropeinv": ropeinv, "c_psw": psw, "c_triu": triu, "c_trineg": trineg,
            "c_hsel": hsel.reshape(128, 32), "c_ovx": ovx.reshape(128, 130).astype(bf), "c_w3": w3.astype(bf), "c_wc": wc.astype(bf),
            "c_ww": ww.astype(bf), "c_addm": addm, "c_exg": exg.reshape(24, 1536).astype(bf), "c_ex": ex.astype(bf)}


_NC_CACHE = {}


def kernel(**inputs):
    if "nc" not in _NC_CACHE:
        _NC_CACHE["nc"] = build_nc()
    nc = _NC_CACHE["nc"]
    consts = _host_consts()
    B = inputs["x"].shape[0]
    in_maps = []
    for b in range(B):
        m = {
            "x": np.ascontiguousarray(inputs["x"][b], dtype=np.float32),
            "mem": np.ascontiguousarray(inputs["mem"][b], dtype=np.float32),
            "positions": np.ascontiguousarray(inputs["positions"][b:b + 1], dtype=np.int32),
            "g_in": np.ascontiguousarray(inputs["g_in"][0], dtype=np.float32),
            "w_in": np.ascontiguousarray(inputs["w_in"][0], dtype=np.float32),
            "cmp_pos_k": np.ascontiguousarray(inputs["cmp_pos_k"][0], dtype=np.float32),
            "w_cmp1_k": np.ascontiguousarray(inputs["w_cmp1_k"][0], dtype=np.float32),
            "w_cmp2_k": np.ascontiguousarray(inputs["w_cmp2_k"][0], dtype=np.float32),
            "cmp_pos_v": np.ascontiguousarray(inputs["cmp_pos_v"][0], dtype=np.float32),
            "w_cmp1_v": np.ascontiguousarray(inputs["w_cmp1_v"][0], dtype=np.float32),
            "w_cmp2_v": np.ascontiguousarray(inputs["w_cmp2_v"][0], dtype=np.float32),
            "conv_w": np.ascontiguousarray(inputs["conv_w"][0], dtype=np.float32),
            "conv_b": np.ascontiguousarray(inputs["conv_b"][0], dtype=np.float32),
            "dt_bias": np.ascontiguousarray(inputs["dt_bias"][0:1], dtype=np.float32),
            "a_log": np.ascontiguousarray(inputs["a_log"][0:1], dtype=np.float32),
            "d_skip": np.ascontiguousarray(inputs["d_skip"][0:1], dtype=np.float32),
            "g_ssd_norm": np.ascontiguousarray(inputs["g_ssd_norm"][0], dtype=np.float32),
            "g_mem": np.ascontiguousarray(inputs["g_mem"][0], dtype=np.float32),
            "w_mem_kv": np.ascontiguousarray(inputs["w_mem_kv"][0], dtype=np.float32),
            "w_out": np.ascontiguousarray(inputs["w_out"][0], dtype=np.float32),
            "g_final": np.ascontiguousarray(np.asarray(inputs["g_final"]).reshape(1, D), dtype=np.float32),
        }
        m.update(consts)
        in_maps.append(m)
    if os.environ.get("KTRACE"):
        res = run_bass_kernel_spmd(nc, in_maps, core_ids=list(range(B)), trace=True)
        print("[kernel] exec_time_ns", res.exec_time_ns)
    else:
        res = run_bass_kernel_spmd(nc, in_maps, core_ids=list(range(B)))
    if DEBUG:
        _NC_CACHE["last"] = res
    return np.stack([np.asarray(r["out"], dtype=np.float32) for r in res.results], axis=0)
```

```python
import os
import math
import numpy as np
import ml_dtypes
from contextlib import ExitStack
import concourse.bass as bass
import concourse.mybir as mybir
from concourse.bass_utils import run_bass_kernel_spmd

F32 = mybir.dt.float32
BF16 = mybir.dt.bfloat16
I32 = mybir.dt.int32
AF = mybir.ActivationFunctionType
ALU = mybir.AluOpType
AX = mybir.AxisListType

S = 4096
D = 1024
NIN = 4384
NT = S // 128
NTT = S // 512
EPS = 1e-6
C_Q, C_KC, C_VC, C_KS, C_VS, C_KW, C_VW, C_GL, C_GA, C_Z, C_XBC, C_DT, C_QX, C_GX = (
    0, 512, 640, 768, 896, 1024, 1152, 1280, 1304, 1816, 2328, 3352, 3360, 3872)
P_Q, P_KC, P_KS, P_KW, P_XBC, P_QX, P_VC = 0, 512, 640, 768, 896, 1920, 2432
PT_ROWS = 2560
NEG = -30000.0

DEBUG = bool(int(os.environ.get("KDEBUG", "0")))
STAGES = os.environ.get("KSTAGES", "XPCBAF")
ASUB = int(os.environ.get("KASUB", "3"))


class Res:
    __slots__ = ("name", "w", "r")

    def __init__(self, name=""):
        self.name = name
        self.w = None
        self.r = []


class FW:
    def __init__(self, nc, n_dma_sems=24):
        self.nc = nc
        self.eng = {"pe": nc.tensor, "act": nc.scalar, "dve": nc.vector, "pool": nc.gpsimd, "sp": nc.sync}
        self.sem = {}
        self.cnt = {}
        self._ctx = []
        for e in ("pe", "act", "dve", "pool"):
            cm = nc.semaphore("sem_" + e)
            s = cm.__enter__()
            self._ctx.append(cm)
            self.sem[e] = s
            self.cnt[e] = 0
        self.dpool = {}
        for q, n in (("sp", n_dma_sems), ("pool", 12), ("act", 8)):
            lst = []
            for i in range(n):
                cm = nc.semaphore(f"dsem_{q}_{i}")
                s = cm.__enter__()
                self._ctx.append(cm)
                lst.append([s, 0])
            self.dpool[q] = [lst, 0]
        self.obs = {e: {} for e in self.eng}
        self.nwaits = 0
        self.nops = 0

    def close(self):
        for cm in reversed(self._ctx):
            cm.__exit__(None, None, None)

    def _need(self, e, tok, lst):
        if tok is None:
            return
        src, sem, val = tok
        key = id(sem)
        if self.obs[e].get(key, 0) >= val:
            return
        self.obs[e][key] = val
        lst[key] = (sem, max(val, lst.get(key, (None, 0))[1]))

    def _wait(self, e, tok):
        lst = {}
        self._need(e, tok, lst)
        for sem, val in lst.values():
            self.eng[e].wait_ge(sem, val)
            self.nwaits += 1

    def _deps(self, e, reads, writes):
        lst = {}
        for r in reads:
            if r.w is not None:
                if not (r.w[0] == e and e == "pe"):
                    self._need(e, r.w, lst)
        for w in writes:
            if w.w is not None and w.w[0] != e:
                self._need(e, w.w, lst)
            for t in w.r:
                if t[0] != e:
                    self._need(e, t, lst)
        return list(lst.values())

    def _update(self, tok, reads, writes):
        for r in reads:
            if tok[0].startswith("dma"):
                r.r = r.r + [tok]
            else:
                r.r = [t for t in r.r if t[0] != tok[0]] + [tok]
        for w in writes:
            w.w = tok
            w.r = []

    def op(self, e, fn, reads=(), writes=()):
        waits = self._deps(e, reads, writes)
        for sem, val in waits[:-1]:
            self.eng[e].wait_ge(sem, val)
            self.nwaits += 1
        ins = fn(self.eng[e])
        if waits:
            ins = ins._wait_ge(waits[-1][0], waits[-1][1])
        self.cnt[e] += 1
        ins.then_inc(self.sem[e], 1)
        tok = (e, self.sem[e], self.cnt[e])
        self._update(tok, reads, writes)
        self.nops += 1
        return tok

    def dma(self, q, out, in_, reads=(), writes=(), **kw):
        waits = self._deps(q, reads, writes)
        lst, idx = self.dpool[q]
        slot = lst[idx % len(lst)]
        self.dpool[q][1] = idx + 1
        sem, cur = slot
        if cur > 0:
            d = {}
            self._need(q, ("dma_" + q, sem, cur), d)
            waits += list(d.values())
        for sem_w, val in waits[:-1]:
            self.eng[q].wait_ge(sem_w, val)
            self.nwaits += 1
        ins = self.eng[q].dma_start(out=out, in_=in_, **kw)
        if waits:
            ins = ins._wait_ge(waits[-1][0], waits[-1][1])
        slot[1] = cur + 16
        ins.then_inc(sem, 16)
        tok = ("dma_" + q, sem, slot[1])
        self._update(tok, reads, writes)
        return tok

    def barrier(self):
        toks = []
        for e in ("pe", "act", "dve", "pool"):
            if self.cnt[e] > 0:
                toks.append((e, self.sem[e], self.cnt[e]))
        for q in self.dpool:
            for sem, cur in self.dpool[q][0]:
                if cur > 0:
                    toks.append(("dma_" + q, sem, cur))
        for e in ("pe", "act", "dve", "pool", "sp"):
            for t in toks:
                if t[0] != e:
                    self._wait(e, t)


class Ring:
    def __init__(self, bufs):
        self.bufs = [(b, Res()) for b in bufs]
        self.i = 0

    def next(self):
        b = self.bufs[self.i % len(self.bufs)]
        self.i += 1
        return b


def build_nc():
    nc = bass.Bass("TRN2", target_bir_lowering=False)
    fw = FW(nc)
    dbg_kind = "ExternalOutput" if DEBUG else "Internal"

    def din(name, shape, dt=F32):
        return nc.dram_tensor(name, list(shape), dt, kind="ExternalInput").ap()

    x = din("x", [S, D])
    mem = din("mem", [256, D])
    positions = din("positions", [1, S], I32)
    g_in = din("g_in", [D])
    w_in = din("w_in", [D, NIN])
    cmp_pos_k = din("cmp_pos_k", [32, 64])
    w_cmp1_k = din("w_cmp1_k", [2048, 256])
    w_cmp2_k = din("w_cmp2_k", [256, 64])
    cmp_pos_v = din("cmp_pos_v", [32, 64])
    w_cmp1_v = din("w_cmp1_v", [2048, 256])
    w_cmp2_v = din("w_cmp2_v", [256, 64])
    conv_w = din("conv_w", [4, 1024])
    conv_b = din("conv_b", [1024])
    dt_bias = din("dt_bias", [1, 8])
    a_log = din("a_log", [1, 8])
    d_skip = din("d_skip", [1, 8])
    g_ssd_norm = din("g_ssd_norm", [512])
    g_mem = din("g_mem", [D])
    w_mem_kv = din("w_mem_kv", [D, 1024])
    w_out = din("w_out", [1536, D])
    g_final = din("g_final", [1, D])
    c_ident = din("c_ident", [128, 128])
    c_ropeinv = din("c_ropeinv", [128, 1])
    c_psw = din("c_psw", [128, 128])
    c_triu = din("c_triu", [128, 128])
    c_trineg = din("c_trineg", [128, 128])
    c_hsel = din("c_hsel", [128, 32])
    c_ovx = din("c_ovx", [128, 130], BF16)
    c_w3 = din("c_w3", [128, 3072], BF16)
    c_wc = din("c_wc", [128, 896], BF16)
    c_ww = din("c_ww", [128, 1536], BF16)
    c_addm = din("c_addm", [S, 64])
    c_exg = din("c_exg", [24, 1536], BF16)
    c_ex = din("c_ex", [64, S], BF16)

    out = nc.dram_tensor("out", [S, D], F32, kind="ExternalOutput").ap()
    PT = nc.dram_tensor("PT", [PT_ROWS, S], BF16, kind=dbg_kind).ap()
    SG = nc.dram_tensor("SG", [1536, S], BF16, kind=dbg_kind).ap()
    GL = nc.dram_tensor("GL", [24, S], BF16, kind=dbg_kind).ap()
    VT = nc.dram_tensor("VT", [S, 392], F32, kind=dbg_kind).ap()
    CS = nc.dram_tensor("CS", [2, 128, 256], F32, kind=dbg_kind).ap()
    MT = nc.dram_tensor("MT", [1536, S], BF16, kind=dbg_kind).ap()

    R_out = Res("out")
    R_PT = [Res() for _ in range(PT_ROWS // 128)]
    R_SG = [Res() for _ in range(12)]
    R_GL = Res()
    R_VT = Res()
    R_CS = Res()
    R_MT = [Res() for _ in range(12)]

    with ExitStack() as top:
        top.enter_context(nc.allow_low_precision("bf16 matmul operands / bf16 staging by design"))

        def sb(name, shape, dt, stack=top):
            return stack.enter_context(nc.sbuf_tensor(name, list(shape), dt))

        ps = [top.enter_context(nc.psum_tensor(f"ps{i}", [128, 512], F32)) for i in range(8)]
        R_ps = [Res(f"ps{i}") for i in range(8)]
        ps_ring = {"i": 0}

        def next_ps(lo=0, hi=8):
            i = lo + ps_ring["i"] % (hi - lo)
            ps_ring["i"] += 1
            return ps[i], R_ps[i]

        ident_f = sb("ident_f", [128, 128], F32)
        ident_b = sb("ident_b", [128, 128], BF16)
        ones_f = sb("ones_f", [128, 128], F32)
        ones_b = sb("ones_b", [128, 128], BF16)
        psw_f = sb("psw_f", [128, 128], F32)
        R_const = Res("const")
        fw.dma("sp", ident_f[:], c_ident[:, :], writes=[R_const])
        fw.dma("sp", psw_f[:], c_psw[:, :], writes=[R_const])
        fw.op("dve", lambda e: e.tensor_copy(out=ident_b[:], in_=ident_f[:]), reads=[R_const], writes=[R_const])
        fw.op("dve", lambda e: e.memset(ones_f[:], 1.0), writes=[R_const])
        fw.op("dve", lambda e: e.memset(ones_b[:], 1.0), writes=[R_const])
        epsb = sb("epsb", [128, 1], F32)
        fw.op("dve", lambda e: e.memset(epsb[:], 1e-30), writes=[R_const])

        R_c = Res()
        cw = sb("b_cw", [128, 4, 8], F32)
        cbias = sb("b_cb", [128, 8], F32)
        for k in range(4):
            fw.dma("sp", cw[:, k, :], conv_w[k].rearrange("(c p) -> p c", p=128), writes=[R_c], allow_slow_non_contiguous=True)
        fw.dma("sp", cbias[:], conv_b.rearrange("(c p) -> p c", p=128), writes=[R_c], allow_slow_non_contiguous=True)
        dtb = sb("b_dtb", [128, 8], F32)
        alog = sb("b_alog", [128, 8], F32)
        dskb = sb("b_dskb", [128, 8], F32)
        hsel = sb("b_hsel", [128, 4, 8], F32)
        gn = sb("b_gn", [128, 4], F32)
        triu = sb("b_triu", [128, 128], F32)
        trineg = sb("b_trineg", [128, 128], F32)
        fw.dma("sp", dtb[:], dt_bias.partition_broadcast(128), writes=[R_c])
        fw.dma("sp", alog[:], a_log.partition_broadcast(128), writes=[R_c])
        fw.dma("sp", dskb[:], d_skip.partition_broadcast(128), writes=[R_c])
        fw.dma("sp", hsel[:], c_hsel.rearrange("p (a b) -> p a b", a=4), writes=[R_c])
        fw.dma("sp", gn[:], g_ssd_norm.rearrange("(c p) -> p c", p=128), writes=[R_c], allow_slow_non_contiguous=True)
        fw.dma("sp", triu[:], c_triu[:, :], writes=[R_c])
        fw.dma("sp", trineg[:], c_trineg[:, :], writes=[R_c])
        trineg_b = sb("b_trineg_b", [128, 128], BF16)
        fw.op("dve", lambda e: e.tensor_copy(out=trineg_b[:], in_=trineg[:]), reads=[R_c], writes=[R_c])
        dsk = sb("b_dsk", [128, 4], F32)
        hs2 = sb("b_hs2", [128, 4, 8], F32)
        fw.op("dve", lambda e: e.tensor_tensor(out=hs2[:], in0=hsel[:], in1=dskb[:].unsqueeze(1).to_broadcast([128, 4, 8]), op=ALU.mult),
              reads=[R_c], writes=[R_c])
        fw.op("dve", lambda e: e.tensor_reduce(out=dsk[:], in_=hs2[:], axis=AX.X, op=ALU.add), reads=[R_c], writes=[R_c])
        dt_all = sb("dt_all", [128, NT, 8], F32)
        R_dt = Res()
        xn_stack = ExitStack()
        xnT = sb("xnT", [128, 8, S], BF16, xn_stack)
        R_xn = [Res(f"xn{i}") for i in range(NTT)]

        def rms_transpose_phase(src, ntiles, g_vec, dstT, R_dst_of_tile, stack):
            g_sb = sb("g_sb_" + dstT.name, [128, 8], F32, stack)
            R_g = Res()
            fw.dma("sp", g_sb[:], g_vec.rearrange("(dk p) -> p dk", p=128), writes=[R_g], allow_slow_non_contiguous=True)
            xin = Ring([sb(f"xin{i}_" + dstT.name, [128, D], F32, stack) for i in range(min(4, ntiles))])
            xsc = Ring([sb(f"xsc{i}_" + dstT.name, [128, D], BF16, stack) for i in range(min(3, ntiles))])
            junk = sb("junk_" + dstT.name, [128, D], BF16, stack)
            R_junk = Res()
            st = Ring([sb(f"st{i}_" + dstT.name, [128, 4], F32, stack) for i in range(4)])
            epsd = sb("epsd_" + dstT.name, [128, 1], F32, stack)
            fw.op("dve", lambda e: e.memset(epsd[:], EPS), writes=[R_g])

            def stage_a(tt):
                xt, R_xt = xin.next()
                fw.dma("sp", xt[:], src[tt * 128:(tt + 1) * 128, :], writes=[R_xt])
                s4, R_s4 = st.next()
                fw.op("act", lambda e: e.activation(out=junk[:], in_=xt[:], func=AF.Square, accum_out=s4[:, 0:1]),
                      reads=[R_xt], writes=[R_junk, R_s4])
                fw.op("act", lambda e: e.activation(out=s4[:, 2:3], in_=s4[:, 0:1], func=AF.Ln, scale=1.0 / D, bias=epsd[:, 0:1]),
                      reads=[R_s4, R_g], writes=[R_s4])
                fw.op("act", lambda e: e.activation(out=s4[:, 3:4], in_=s4[:, 2:3], func=AF.Exp, scale=-0.5),
                      reads=[R_s4], writes=[R_s4])
                return (xt, R_xt, s4, R_s4)

            def stage_b(tt, xt, R_xt, s4, R_s4):
                xs, R_xs = xsc.next()
                fw.op("dve", lambda e: e.tensor_scalar(out=xs[:], in0=xt[:], scalar1=s4[:, 3:4], scalar2=None, op0=ALU.mult),
                      reads=[R_xt, R_s4], writes=[R_xs])
                pb, R_pb = next_ps()
                pbb = pb[:].bitcast(BF16)
                for dk in range(8):
                    fw.op("pe", lambda e: e.transpose(pbb[:, dk * 128:(dk + 1) * 128], xs[:, dk * 128:(dk + 1) * 128], ident_b[:]),
                          reads=[R_xs, R_const], writes=[R_pb])
                return (tt, pbb, R_pb)

            def stage_c(tt, pbb, R_pb):
                fw.op("dve", lambda e: e.tensor_tensor(
                    out=dstT[:, :, tt * 128:(tt + 1) * 128],
                    in0=pbb.rearrange("p (a b) -> p a b", a=8),
                    in1=g_sb[:].unsqueeze(2).to_broadcast([128, 8, 128]), op=ALU.mult),
                    reads=[R_pb, R_g], writes=[R_dst_of_tile(tt)])

            pa_, pb_ = [], []
            for tt in range(ntiles + 3):
                if tt < ntiles:
                    pa_.append((tt,) + stage_a(tt))
                if len(pb_) > 0 and tt >= 2:
                    stage_c(*pb_.pop(0))
                if len(pa_) > 0 and tt >= 1:
                    pb_.append(stage_b(*pa_.pop(0)))
            while pa_ or pb_:
                if pb_:
                    stage_c(*pb_.pop(0))
                if pa_:
                    pb_.append(stage_b(*pa_.pop(0)))

        if "X" in STAGES:
            rms_transpose_phase(x, NT, g_in, xnT, lambda tt: R_xn[tt // 4], xn_stack)

        if "P" in STAGES:
            with ExitStack() as ph:
                Ct = sb("Ct", [128, S], F32, ph)
                St = sb("St", [128, S], F32, ph)
                R_C = Res()
                def emit_rope_tables():
                    invp = sb("invp", [128, 1], F32, ph)
                    R_tmp = Res()
                    fw.dma("sp", invp[:], c_ropeinv[:, :], writes=[R_tmp])
                    CB = 512
                    posi = sb("posi", [128, CB], I32, ph)
                    ang = sb("ang", [128, CB], F32, ph)
                    kfi = sb("kfi", [128, CB], I32, ph)
                    kf = sb("kf", [128, CB], F32, ph)
                    rr = sb("rr", [128, CB], F32, ph)
                    rc = sb("rc", [128, CB], F32, ph)
                    C1 = 6.28125
                    C2 = 2 * math.pi - 6.28125
                    PI_LO = 3.141592
                    for cb in range(S // CB):
                        cs = slice(cb * CB, (cb + 1) * CB)
                        fw.dma("sp", posi[:], positions[:, cs].partition_broadcast(128), reads=[R_tmp], writes=[R_tmp])
                        fw.op("dve", lambda e: e.tensor_copy(out=ang[:], in_=posi[:]), reads=[R_tmp], writes=[R_tmp])
                        fw.op("dve", lambda e: e.tensor_scalar(out=ang[:], in0=ang[:], scalar1=invp[:, 0:1], scalar2=None, op0=ALU.mult),
                              reads=[R_tmp], writes=[R_tmp])
                        fw.op("dve", lambda e: e.tensor_scalar(out=kfi[:], in0=ang[:], scalar1=1.0 / (2 * math.pi), scalar2=None, op0=ALU.mult),
                              reads=[R_tmp], writes=[R_tmp])
                        fw.op("dve", lambda e: e.tensor_copy(out=kf[:], in_=kfi[:]), reads=[R_tmp], writes=[R_tmp])
                        fw.op("dve", lambda e: e.scalar_tensor_tensor(out=rr[:], in0=kf[:], scalar=-C1, in1=ang[:], op0=ALU.mult, op1=ALU.add),
                              reads=[R_tmp], writes=[R_tmp])
                        fw.op("dve", lambda e: e.scalar_tensor_tensor(out=rr[:], in0=kf[:], scalar=-C2, in1=rr[:], op0=ALU.mult, op1=ALU.add),
                              reads=[R_tmp], writes=[R_tmp])
                        fw.op("dve", lambda e: e.tensor_scalar(out=rc[:], in0=rr[:], scalar1=math.pi / 2, scalar2=-2 * math.pi,
                                                               op0=ALU.is_gt, op1=ALU.mult), reads=[R_tmp], writes=[R_tmp])
                        fw.op("dve", lambda e: e.scalar_tensor_tensor(out=rc[:], in0=rr[:], scalar=math.pi / 2, in1=rc[:], op0=ALU.add, op1=ALU.add),
                              reads=[R_tmp], writes=[R_tmp])
                        fw.op("dve", lambda e: e.tensor_scalar(out=rr[:], in0=rr[:], scalar1=-PI_LO, scalar2=PI_LO, op0=ALU.max, op1=ALU.min),
                              reads=[R_tmp], writes=[R_tmp])
                        fw.op("dve", lambda e: e.tensor_scalar(out=rc[:], in0=rc[:], scalar1=-PI_LO, scalar2=PI_LO, op0=ALU.max, op1=ALU.min),
                              reads=[R_tmp], writes=[R_tmp])
                        fw.op("act", lambda e: e.activation(out=St[:, cs], in_=rr[:], func=AF.Sin), reads=[R_tmp], writes=[R_C])
                        fw.op("act", lambda e: e.activation(out=Ct[:, cs], in_=rc[:], func=AF.Sin), reads=[R_tmp], writes=[R_C])
                        yield
                    yield
                    csc = sb("csc", [128, 2, 256], F32, ph)
                    R_csc = Res()
                    fw.op("dve", lambda e: e.tensor_copy(out=csc[:, 0, 0:255], in_=Ct[:, 31:S:16]), reads=[R_C], writes=[R_csc])
                    fw.op("dve", lambda e: e.tensor_copy(out=csc[:, 1, 0:255], in_=St[:, 31:S:16]), reads=[R_C], writes=[R_csc])
                    fw.dma("sp", CS[0, :, 0:255], csc[:, 0, 0:255], reads=[R_csc], writes=[R_CS])
                    fw.dma("sp", CS[1, :, 0:255], csc[:, 1, 0:255], reads=[R_csc], writes=[R_CS])


                wbf = Ring([sb(f"wbf{i}", [128, 8, 128], BF16, ph) for i in range(3)])
                OW = 2048
                otile = [(sb(f"otile{i}", [128, OW], BF16, ph), [Res() for _ in range(4)]) for i in range(3)]
                ot_i = {"i": 0}
                qf = Ring([sb(f"qf{i}", [128, 512], F32, ph) for i in range(2)])
                t1 = Ring([sb(f"t1_{i}", [128, 512], F32, ph) for i in range(2)])
                w_view = w_in.rearrange("(dk p) c -> p dk c", p=128)

                def load_w(c0, ncols):
                    wb, R_wb = wbf.next()
                    fw.dma("pool", wb[:, :, 0:ncols], w_view[:, :, c0:c0 + ncols], writes=[R_wb])
                    return wb, R_wb

                def proj_fm(wb, R_wb, ncols, T):
                    pb, R_pb = next_ps()
                    for dk in range(8):
                        fw.op("pe", lambda e: e.matmul(pb[0:ncols, :], lhsT=wb[:, dk, 0:ncols], rhs=xnT[:, dk, T * 512:(T + 1) * 512],
                                                       start=(dk == 0), stop=(dk == 7)),
                              reads=[R_wb, R_xn[T]], writes=[R_pb])
                    return pb, R_pb

                chunks = []
                for i in range(4):
                    chunks.append(("silu", C_GA + i * 128, 128, SG, i, R_SG))
                for i in range(4):
                    chunks.append(("silu", C_Z + i * 128, 128, SG, 4 + i, R_SG))
                for i in range(4):
                    chunks.append(("silu", C_GX + i * 128, 128, SG, 8 + i, R_SG))
                chunks.append(("copy", C_KC, 128, PT, P_KC // 128, R_PT))
                chunks.append(("copy", C_VC, 128, PT, P_VC // 128, R_PT))
                for i in range(8):
                    chunks.append(("copy", C_XBC + i * 128, 128, PT, P_XBC // 128 + i, R_PT))
                for i in range(4):
                    chunks.append(("copy", C_QX + i * 128, 128, PT, P_QX // 128 + i, R_PT))
                for i in range(4):
                    chunks.append(("rope", C_Q + i * 128, 128, PT, P_Q // 128 + i, R_PT))
                chunks.append(("rope", C_KS, 128, PT, P_KS // 128, R_PT))
                chunks.append(("rope", C_KW, 128, PT, P_KW // 128, R_PT))
                chunks.append(("gl", C_GL, 24, GL, 0, None))
                nxt = load_w(chunks[0][1], chunks[0][2])
                pend = []
                for ci, (kind, c0, ncols, dstT, drow, R_dst) in enumerate(chunks):
                    wb, R_wb = nxt
                    if ci + 1 < len(chunks):
                        nxt = load_w(chunks[ci + 1][1], chunks[ci + 1][2])
                    if ci == 12:
                        rope_gen = emit_rope_tables()
                    if 12 <= ci < 26:
                        next(rope_gen, None)
                    if ci == 25:
                        for _ in rope_gen:
                            pass
                    for half in range(2):
                        ot, R_ots = otile[ot_i["i"] % 3]
                        ot_i["i"] += 1
                        for T4 in range(4):
                            T = half * 4 + T4
                            pb, R_pb = proj_fm(wb, R_wb, ncols, T)
                            if pend:
                                pend.pop(0)()

                            def epi(pb=pb, R_pb=R_pb, T=T, T4=T4, kind=kind, ot=ot, R_ots=R_ots, half=half, dstT=dstT, drow=drow, R_dst=R_dst):
                                osl = ot[:, T4 * 512:(T4 + 1) * 512]
                                R_o1 = R_ots[T4]
                                if kind == "silu":
                                    fw.op("act", lambda e: e.activation(out=osl, in_=pb[:], func=AF.Silu), reads=[R_pb], writes=[R_o1])
                                elif kind == "copy":
                                    if T % 2 == 0:
                                        fw.op("dve", lambda e: e.tensor_copy(out=osl, in_=pb[:]), reads=[R_pb], writes=[R_o1])
                                    else:
                                        fw.op("act", lambda e: e.activation(out=osl, in_=pb[:], func=AF.Copy), reads=[R_pb], writes=[R_o1])
                                elif kind == "rope":
                                    q32, R_q32 = qf.next()
                                    fw.op("act", lambda e: e.activation(out=q32[:], in_=pb[:], func=AF.Copy), reads=[R_pb], writes=[R_q32])
                                    pb2, R_pb2 = next_ps()
                                    fw.op("pe", lambda e: e.matmul(pb2[:, :], lhsT=psw_f[:], rhs=q32[:], start=True, stop=True),
                                          reads=[R_const, R_q32], writes=[R_pb2])
                                    ta, R_ta = t1.next()
                                    fw.op("dve", lambda e: e.tensor_tensor(out=ta[:], in0=pb2[:], in1=St[:, T * 512:(T + 1) * 512], op=ALU.mult),
                                          reads=[R_pb2, R_C], writes=[R_ta])
                                    fw.op("pool", lambda e: e.tensor_tensor(out=q32[:], in0=q32[:], in1=Ct[:, T * 512:(T + 1) * 512], op=ALU.mult),
                                          reads=[R_q32, R_C], writes=[R_q32])
                                    fw.op("dve", lambda e: e.tensor_tensor(out=osl, in0=ta[:], in1=q32[:], op=ALU.add),
                                          reads=[R_ta, R_q32], writes=[R_o1])
                                else:
                                    ta, R_ta = t1.next()
                                    fw.op("act", lambda e: e.activation(out=ta[0:24, :], in_=pb[0:24, :], func=AF.Exp, scale=-1.0), reads=[R_pb], writes=[R_ta])
                                    fw.op("dve", lambda e: e.tensor_scalar(out=ta[0:24, :], in0=ta[0:24, :], scalar1=1.0, scalar2=None, op0=ALU.add),
                                          reads=[R_ta], writes=[R_ta])
                                    fw.op("dve", lambda e: e.reciprocal(out=osl[0:24, :], in_=ta[0:24, :]), reads=[R_ta], writes=[R_o1])
                                if T4 == 3:
                                    if kind == "gl":
                                        fw.dma("sp", GL[:, half * OW:(half + 1) * OW], ot[0:24, :], reads=R_ots, writes=[R_GL])
                                    else:
                                        fw.dma("sp", dstT[drow * 128:(drow + 1) * 128, half * OW:(half + 1) * OW], ot[:], reads=R_ots, writes=[R_dst[drow]])
                            pend.append(epi)
                while pend:
                    pend.pop(0)()
                wtm = sb("wtm", [128, 8, 392], BF16, ph)
                R_wtm = Res()
                for j, c0 in enumerate((C_VC, C_VS, C_VW)):
                    fw.dma("pool", wtm[:, :, j * 128:(j + 1) * 128], w_view[:, :, c0:c0 + 128], writes=[R_wtm])
                fw.dma("pool", wtm[:, :, 384:392], w_view[:, :, C_DT:C_DT + 8], writes=[R_wtm])
                vt_o = Ring([sb(f"vt_o{i}", [128, 392], F32, ph) for i in range(3)])
                for tt in range(NT):
                    pb, R_pb = next_ps()
                    for dk in range(8):
                        fw.op("pe", lambda e: e.matmul(pb[:, 0:392], lhsT=xnT[:, dk, tt * 128:(tt + 1) * 128], rhs=wtm[:, dk, :],
                                                       start=(dk == 0), stop=(dk == 7)),
                              reads=[R_wtm, R_xn[tt // 4]], writes=[R_pb])
                    vo, R_vo = vt_o.next()
                    if tt % 2 == 0:
                        fw.op("dve", lambda e: e.tensor_copy(out=vo[:], in_=pb[:, 0:392]), reads=[R_pb], writes=[R_vo])
                    else:
                        fw.op("act", lambda e: e.activation(out=vo[:], in_=pb[:, 0:392], func=AF.Copy), reads=[R_pb], writes=[R_vo])
                    fw.op("dve", lambda e: e.tensor_copy(out=dt_all[:, tt, :], in_=pb[:, 384:392]), reads=[R_pb], writes=[R_dt])
                    fw.dma("sp", VT[tt * 128:(tt + 1) * 128, :], vo[:], reads=[R_vo], writes=[R_VT])
                fw.barrier()

        xn_stack.close()
        wo = sb("wo", [128, 12, D], BF16)
        R_wo = Res()
        for ck in range(12):
            fw.dma("pool", wo[:, ck, :], w_out[ck * 128:(ck + 1) * 128, :], writes=[R_wo])
        def build_phase_c(ph):
            memT = sb("memT", [128, 8, 256], BF16, ph)
            R_memT = Res()
            rms_transpose_phase(mem, 2, g_mem, memT, lambda tt: R_memT, ph)
            wkv_view = w_mem_kv.rearrange("(dk p) c -> p dk c", p=128)
            kT = sb("kT", [128, 4, 256], BF16, ph)
            vtok = sb("vtok", [128, 2, 512], BF16, ph)
            R_kv = Res()
            wst = Ring([sb(f"cwst{i}", [128, 8, 128], F32, ph) for i in range(2)])
            wv = sb("cwv", [128, 8, 512], BF16, ph)
            R_wv = Res()
            wkb = Ring([sb(f"cwkb{i}", [128, 8, 128], BF16, ph) for i in range(2)])
            for h in range(4):
                ws, R_ws = wst.next()
                fw.dma("sp", ws[:], wkv_view[:, :, h * 128:(h + 1) * 128], writes=[R_ws])
                wb, R_wb = wkb.next()
                fw.op("act", lambda e: e.activation(out=wb[:], in_=ws[:], func=AF.Copy), reads=[R_ws], writes=[R_wb])
                pb, R_pb = next_ps()
                for dk in range(8):
                    fw.op("pe", lambda e: e.matmul(pb[:, 0:256], lhsT=wb[:, dk, :], rhs=memT[:, dk, :], start=(dk == 0), stop=(dk == 7)),
                          reads=[R_wb, R_memT], writes=[R_pb])
                fw.op("dve", lambda e: e.tensor_copy(out=kT[:, h, :], in_=pb[:, 0:256]), reads=[R_pb], writes=[R_kv])
            for h in range(4):
                ws, R_ws = wst.next()
                fw.dma("sp", ws[:], wkv_view[:, :, 512 + h * 128:512 + (h + 1) * 128], writes=[R_ws])
                fw.op("dve", lambda e: e.tensor_copy(out=wv[:, :, h * 128:(h + 1) * 128], in_=ws[:]), reads=[R_ws], writes=[R_wv])
            for kc in range(2):
                pb, R_pb = next_ps()
                for dk in range(8):
                    fw.op("pe", lambda e: e.matmul(pb[:, :], lhsT=memT[:, dk, kc * 128:(kc + 1) * 128], rhs=wv[:, dk, :],
                                                   start=(dk == 0), stop=(dk == 7)), reads=[R_wv, R_memT], writes=[R_pb])
                fw.op("dve", lambda e: e.tensor_copy(out=vtok[:, kc, :], in_=pb[:, :]), reads=[R_pb], writes=[R_kv])
            qx = Ring([sb(f"cqx{i}", [128, 512], BF16, ph) for i in range(3)])
            sgx = Ring([sb(f"csgx{i}", [128, 512], BF16, ph) for i in range(3)])
            pT = Ring([sb(f"cpT{i}", [128, 512], BF16, ph) for i in range(4)])
            rden = Ring([sb(f"crden{i}", [128, 512], F32, ph) for i in range(2)])
            ot = Ring([sb(f"cot{i}", [128, 512], BF16, ph) for i in range(2)])
            xscale = 128.0 ** -0.5
            cjobs = []
            for T in range(NTT):
                for h in range(4):
                    def cscore(st, T=T, h=h):
                        ts = slice(T * 512, (T + 1) * 512)
                        q, R_q = qx.next()
                        fw.dma("sp", q[:], PT[P_QX + h * 128:P_QX + (h + 1) * 128, ts], reads=[R_PT[P_QX // 128 + h]], writes=[R_q])
                        sg, R_sg = sgx.next()
                        fw.dma("sp", sg[:], SG[1024 + h * 128:1024 + (h + 1) * 128, ts], reads=[R_SG[8 + h]], writes=[R_sg])
                        pts = []
                        for kc in range(2):
                            pa, R_pa = next_ps(0, 4)
                            fw.op("pe", lambda e: e.matmul(pa[:, :], lhsT=kT[:, h, kc * 128:(kc + 1) * 128], rhs=q[:], start=True, stop=True),
                                  reads=[R_kv, R_q], writes=[R_pa])
                            p, R_p = pT.next()
                            fw.op("act", lambda e: e.activation(out=p[:], in_=pa[:, :], func=AF.Exp, scale=xscale), reads=[R_pa], writes=[R_p])
                            pts.append((p, R_p))
                        st["pts"] = pts
                        st["sg"] = (sg, R_sg)

                    def cpv(st, T=T, h=h):
                        ts = slice(T * 512, (T + 1) * 512)
                        pts = st["pts"]
                        sg, R_sg = st["sg"]
                        po, R_po = next_ps(4, 6)
                        pd, R_pd = next_ps(6, 8)
                        for kc in range(2):
                            p, R_p = pts[kc]
                            fw.op("pe", lambda e: e.matmul(po[:, :], lhsT=vtok[:, kc, h * 128:(h + 1) * 128], rhs=p[:], start=(kc == 0), stop=(kc == 1)),
                                  reads=[R_kv, R_p], writes=[R_po])
                        for kc in range(2):
                            p, R_p = pts[kc]
                            fw.op("pe", lambda e: e.matmul(pd[:, :], lhsT=ones_b[:], rhs=p[:], start=(kc == 0), stop=(kc == 1)),
                                  reads=[R_const, R_p], writes=[R_pd])
                        rd, R_rd = rden.next()
                        fw.op("act", lambda e: e.activation(out=rd[:], in_=pd[:, :], func=AF.Ln), reads=[R_pd], writes=[R_rd])
                        fw.op("act", lambda e: e.activation(out=rd[:], in_=rd[:], func=AF.Exp, scale=-1.0), reads=[R_rd], writes=[R_rd])
                        fw.op("dve", lambda e: e.tensor_tensor(out=rd[:], in0=po[:, :], in1=rd[:], op=ALU.mult), reads=[R_po, R_rd], writes=[R_rd])
                        o, R_o = ot.next()
                        fw.op("dve", lambda e: e.tensor_tensor(out=o[:], in0=rd[:], in1=sg[:], op=ALU.mult), reads=[R_rd, R_sg], writes=[R_o])
                        fw.dma("pool", MT[1024 + h * 128:1024 + (h + 1) * 128, ts], o[:], reads=[R_o], writes=[R_MT[8 + h]])
                    stt = {}
                    cjobs.append((lambda f=cscore, st=stt: f(st), lambda f=cpv, st=stt: f(st)))

            cst = {"i": 0}
            n = len(cjobs)

            def cstep(k):
                for _ in range(k):
                    i = cst["i"]
                    if i > n:
                        return
                    if i < n:
                        cjobs[i][0]()
                    if i - 1 >= 0:
                        cjobs[i - 1][1]()
                    cst["i"] = i + 1
            return cstep

        if "C" in STAGES and "A" not in STAGES:
            with ExitStack() as ph:
                cstep = build_phase_c(ph)
                cstep(40)
                fw.barrier()

        if "B" in STAGES:
            with ExitStack() as ph:
                xact = sb("b_xact", [128, 8, S], BF16, ph)
                R_xact = [Res() for _ in range(8)]
                with ExitStack() as ph2:
                    xpad = Ring([sb(f"b_xpad{i}", [128, S + 4], BF16, ph2) for i in range(2)])
                    dg = sb("b_dg", [128, 32, 128], BF16, ph2)
                    R_dg = Res()
                    for c in range(8):
                        for k in range(4):
                            fw.op("dve", lambda e: e.tensor_scalar(out=dg[:, c * 4 + k, :], in0=ident_f[:], scalar1=cw[:, k, c:c + 1], scalar2=None, op0=ALU.mult),
                                  reads=[R_c, R_const], writes=[R_dg])
                    for i in range(2):
                        xp, R_xp = xpad.bufs[i]
                        fw.op("dve", lambda e: e.memset(xp[:, 0:4], 0.0), writes=[R_xp])
                    for c in range(8):
                        xp, R_xp = xpad.next()
                        fw.dma("sp", xp[:, 3:S + 3], PT[P_XBC + c * 128:P_XBC + (c + 1) * 128, :], reads=[R_PT[P_XBC // 128 + c]], writes=[R_xp])
                        for T in range(NTT):
                            pb, R_pb = next_ps()
                            for k in range(4):
                                fw.op("pe", lambda e: e.matmul(pb[:, :], lhsT=dg[:, c * 4 + k, :], rhs=xp[:, T * 512 + k:T * 512 + k + 512],
                                                               start=(k == 0), stop=(k == 3)), reads=[R_dg, R_xp], writes=[R_pb])
                            fw.op("act", lambda e: e.activation(out=xact[:, c, T * 512:(T + 1) * 512], in_=pb[:, :], func=AF.Silu, bias=cbias[:, c:c + 1]),
                                  reads=[R_pb, R_c], writes=[R_xact[c]])
                    fw.barrier()
                NCH = NT
                dtv = sb("b_dt", [128, NCH, 8], F32, ph)
                dtA = sb("b_dtA", [128, NCH, 8], F32, ph)
                acs = sb("b_acs", [128, NCH, 8], F32, ph)
                nacs = sb("b_nacs", [128, NCH, 8], F32, ph)
                tot = sb("b_tot", [128, NCH, 8], F32, ph)
                cdb = sb("b_cdb", [128, NCH, 8], F32, ph)
                w2 = sb("b_w2", [128, NCH, 8], F32, ph)
                aexp = sb("b_aexp", [128, 8], F32, ph)
                R_q = Res()
                fw.op("dve", lambda e: e.tensor_tensor(out=dtv[:], in0=dt_all[:], in1=dtb[:].unsqueeze(1).to_broadcast([128, NCH, 8]), op=ALU.add),
                      reads=[R_dt, R_c], writes=[R_q])
                fw.op("act", lambda e: e.activation(out=dtv[:], in_=dtv[:], func=AF.Exp), reads=[R_q], writes=[R_q])
                fw.op("dve", lambda e: e.tensor_scalar(out=dtv[:], in0=dtv[:], scalar1=1.0, scalar2=None, op0=ALU.add), reads=[R_q], writes=[R_q])
                fw.op("act", lambda e: e.activation(out=dtv[:], in_=dtv[:], func=AF.Ln), reads=[R_q], writes=[R_q])
                fw.op("act", lambda e: e.activation(out=aexp[:], in_=alog[:], func=AF.Exp), reads=[R_c], writes=[R_q])
                fw.op("dve", lambda e: e.scalar_tensor_tensor(out=dtA[:], in0=dtv[:], scalar=-1.0, in1=aexp[:].unsqueeze(1).to_broadcast([128, NCH, 8]),
                                                              op0=ALU.mult, op1=ALU.mult), reads=[R_q], writes=[R_q])
                dtA_hi = sb("b_dtA_hi", [128, NCH, 8], BF16, ph)
                dtA_lo = sb("b_dtA_lo", [128, NCH, 8], BF16, ph)
                dtA_hf = sb("b_dtA_hf", [128, NCH, 8], F32, ph)
                triu_b = sb("b_triu_b", [128, 128], BF16, ph)
                fw.op("dve", lambda e: e.tensor_copy(out=triu_b[:], in_=triu[:]), reads=[R_c], writes=[R_c])
                fw.op("dve", lambda e: e.tensor_copy(out=dtA_hi[:], in_=dtA[:]), reads=[R_q], writes=[R_q])
                fw.op("dve", lambda e: e.tensor_copy(out=dtA_hf[:], in_=dtA_hi[:]), reads=[R_q], writes=[R_q])
                fw.op("dve", lambda e: e.tensor_tensor(out=dtA_lo[:], in0=dtA[:], in1=dtA_hf[:], op=ALU.subtract), reads=[R_q], writes=[R_q])
                fw.op("dve", lambda e: e.tensor_copy(out=dtA_hf[:], in_=dtA_lo[:]), reads=[R_q], writes=[R_q])
                fw.op("dve", lambda e: e.tensor_tensor(out=dtA[:], in0=dtA_hf[:], in1=dtA_hi[:], op=ALU.add), reads=[R_q], writes=[R_q])
                dtA2 = dtA[:].rearrange("p c h -> p (c h)")
                pb, R_pb = next_ps()
                fw.op("pe", lambda e: e.matmul(pb[:, 0:256], lhsT=triu[:], rhs=dtA2, start=True, stop=True), reads=[R_q, R_c], writes=[R_pb])
                fw.op("dve", lambda e: e.tensor_copy(out=acs[:].rearrange("p c h -> p (c h)"), in_=pb[:, 0:256]), reads=[R_pb], writes=[R_q])
                pb, R_pb = next_ps()
                fw.op("pe", lambda e: e.matmul(pb[:, 0:256], lhsT=ones_f[:], rhs=dtA2, start=True, stop=True), reads=[R_q, R_const], writes=[R_pb])
                fw.op("dve", lambda e: e.tensor_copy(out=tot[:].rearrange("p c h -> p (c h)"), in_=pb[:, 0:256]), reads=[R_pb], writes=[R_q])
                fw.op("dve", lambda e: e.tensor_scalar(out=nacs[:], in0=acs[:], scalar1=-1.0, scalar2=None, op0=ALU.mult), reads=[R_q], writes=[R_q])
                fw.op("act", lambda e: e.activation(out=cdb[:], in_=tot[:], func=AF.Exp), reads=[R_q], writes=[R_q])
                fw.op("dve", lambda e: e.tensor_tensor(out=w2[:], in0=tot[:], in1=acs[:], op=ALU.subtract), reads=[R_q], writes=[R_q])
                fw.op("act", lambda e: e.activation(out=w2[:], in_=w2[:], func=AF.Exp), reads=[R_q], writes=[R_q])
                fw.op("dve", lambda e: e.tensor_tensor(out=w2[:], in0=w2[:], in1=dtv[:], op=ALU.mult), reads=[R_q], writes=[R_q])
                state = sb("b_state", [128, 512], F32, ph)
                state_bf = sb("b_state_bf", [128, 512], BF16, ph)
                R_state = Res()
                R_sbf = Res()
                fw.op("dve", lambda e: e.memset(state[:], 0.0), writes=[R_state])
                fw.op("dve", lambda e: e.memset(state_bf[:], 0.0), writes=[R_sbf])
                xbtok = Ring([sb(f"b_xbtok{i}", [128, 768], BF16, ph) for i in range(2)])
                xdt = Ring([sb(f"b_xdt{i}", [128, 512], BF16, ph) for i in range(2)])
                xdtd = Ring([sb(f"b_xdtd{i}", [128, 512], BF16, ph) for i in range(2)])
                eacs = Ring([sb(f"b_eacs{i}", [128, 8, 128], BF16, ph) for i in range(2)])
                decT = Ring([sb(f"b_decT{i}", [128, 8, 128], BF16, ph) for i in range(2)])
                Mh = Ring([sb(f"b_Mh{i}", [128, 8, 128], BF16, ph) for i in range(2)])
                cms = Ring([sb(f"b_cms{i}", [128, 8, 128], BF16, ph) for i in range(2)])
                yacc = Ring([sb(f"b_yacc{i}", [128, 4, 512], F32, ph) for i in range(2)])
                sgz = Ring([sb(f"b_sgz{i}", [128, 4, 512], BF16, ph) for i in range(2)])
                sq = sb("b_sq", [128, 4, 512], F32, ph)
                R_sq = Res()
                rstd = sb("b_rstd", [128, 512], F32, ph)
                R_rstd = Res()
                obt = Ring([sb(f"b_obt{i}", [128, 4, 512], BF16, ph) for i in range(2)])
                SG_v = SG.rearrange("(ck p) t -> p ck t", p=128)
                MT_vb = MT.rearrange("(ck p) t -> p ck t", p=128)
                prepd = {}
                ystate = {}

                def prep(c):
                    tk = slice(c * 128, (c + 1) * 128)
                    pt_, R_pt = ps[7], R_ps[7]
                    ptb = pt_[:].bitcast(BF16)
                    for j in range(6):
                        fw.op("pe", lambda e: e.transpose(ptb[:, j * 128:(j + 1) * 128], xact[:, j, tk], ident_b[:]),
                              reads=[R_xact[j], R_const], writes=[R_pt])
                    xb, R_xb = xbtok.next()
                    fw.op("act", lambda e: e.activation(out=xb[:], in_=ptb[:, 0:768], func=AF.Copy), reads=[R_pt], writes=[R_xb])
                    xd, R_xd = xdt.next()
                    xdd, R_xdd = xdtd.next()
                    fw.op("dve", lambda e: e.tensor_tensor(out=xd[:].rearrange("p (h q) -> p h q", h=8), in0=xb[:, 0:512].rearrange("p (h q) -> p h q", h=8),
                                                           in1=dtv[:, c, :].unsqueeze(2).to_broadcast([128, 8, 64]), op=ALU.mult),
                          reads=[R_xb, R_q], writes=[R_xd])
                    fw.op("dve", lambda e: e.tensor_tensor(out=xdd[:].rearrange("p (h q) -> p h q", h=8), in0=xb[:, 0:512].rearrange("p (h q) -> p h q", h=8),
                                                           in1=w2[:, c, :].unsqueeze(2).to_broadcast([128, 8, 64]), op=ALU.mult),
                          reads=[R_xb, R_q], writes=[R_xdd])
                    for g in range(2):
                        fw.op("pe", lambda e: e.matmul(ps[4][:, g * 128:(g + 1) * 128], lhsT=xact[:, 4 + g, tk], rhs=xact[:, 6 + g, tk], start=True, stop=True),
                              reads=[R_xact[4 + g], R_xact[6 + g]], writes=[R_ps[4]])
                    for h in range(8):
                        bk = h // 4
                        hs = slice((h % 4) * 128, (h % 4 + 1) * 128)
                        lbh = dtA_hi[:, c, h:h + 1].to_broadcast([128, 128])
                        lbl = dtA_lo[:, c, h:h + 1].to_broadcast([128, 128])
                        fw.op("pe", lambda e: e.matmul(ps[bk][:, hs], lhsT=lbh, rhs=triu_b[:], start=True, stop=False),
                              reads=[R_q, R_c], writes=[R_ps[bk]])
                        fw.op("pe", lambda e: e.matmul(ps[bk][:, hs], lhsT=lbl, rhs=triu_b[:], start=False, stop=True),
                              reads=[R_q, R_c], writes=[R_ps[bk]])
                        fw.op("pe", lambda e: e.matmul(ps[2 + bk][:, hs], lhsT=lbh, rhs=triu_b[:], start=True, stop=False),
                              reads=[R_q, R_c], writes=[R_ps[2 + bk]])
                        fw.op("pe", lambda e: e.matmul(ps[2 + bk][:, hs], lhsT=lbl, rhs=triu_b[:], start=False, stop=False),
                              reads=[R_q, R_c], writes=[R_ps[2 + bk]])
                        fw.op("pe", lambda e: e.matmul(ps[2 + bk][:, hs], lhsT=ident_b[:], rhs=trineg_b[:], start=False, stop=True),
                              reads=[R_const, R_c], writes=[R_ps[2 + bk]])
                    ea, R_ea = eacs.next()
                    for bk in range(2):
                        fw.op("act", lambda e: e.activation(out=ea[:, bk * 4:(bk + 1) * 4, :].rearrange("p a b -> p (a b)"), in_=ps[bk][:, :], func=AF.Exp),
                              reads=[R_ps[bk]], writes=[R_ea])
                    dc, R_dc = decT.next()
                    for h in range(8):
                        bk = h // 4
                        hs = slice((h % 4) * 128, (h % 4 + 1) * 128)
                        fw.op("act", lambda e: e.activation(out=dc[:, h, :], in_=ps[2 + bk][:, hs], func=AF.Exp, bias=nacs[:, c, h:h + 1]),
                              reads=[R_ps[2 + bk], R_q], writes=[R_dc])
                    mh, R_mh = Mh.next()
                    cm_, R_cm = cms.next()
                    for g in range(2):
                        fw.op("dve", lambda e: e.tensor_tensor(out=mh[:, g * 4:(g + 1) * 4, :], in0=dc[:, g * 4:(g + 1) * 4, :],
                                                               in1=ps[4][:, g * 128:(g + 1) * 128].unsqueeze(1).to_broadcast([128, 4, 128]), op=ALU.mult),
                              reads=[R_dc, R_ps[4]], writes=[R_mh])
                        fw.op("pool", lambda e: e.tensor_tensor(out=cm_[:, g * 4:(g + 1) * 4, :], in0=ea[:, g * 4:(g + 1) * 4, :],
                                                                in1=xact[:, 6 + g, tk].unsqueeze(1).to_broadcast([128, 4, 128]), op=ALU.mult),
                              reads=[R_ea, R_xact[6 + g]], writes=[R_cm])
                    prepd[c] = (xb, R_xb, xd, R_xd, xdd, R_xdd, mh, R_mh, cm_, R_cm)

                def fin(c):
                    T = c // 4
                    tk = slice(c * 128, (c + 1) * 128)
                    xb, R_xb, xd, R_xd, xdd, R_xdd, mh, R_mh, cm_, R_cm = prepd.pop(c)
                    if c % 4 == 0:
                        ystate["ya"] = yacc.next()
                        ystate["sg"] = sgz.next()
                        sgt, R_sgt = ystate["sg"]
                        fw.dma("sp", sgt[:], SG_v[:, 4:8, T * 512:(T + 1) * 512], reads=R_SG[4:8], writes=[R_sgt])
                    ya, R_ya = ystate["ya"]
                    sgt, R_sgt = ystate["sg"]
                    for h in range(8):
                        yo = ps[6][(h % 2) * 64:(h % 2 + 1) * 64, (h // 2) * 128:(h // 2 + 1) * 128]
                        fw.op("pe", lambda e: e.matmul(yo, lhsT=xd[:, h * 64:(h + 1) * 64], rhs=mh[:, h, :], start=True, stop=False),
                              reads=[R_xd, R_mh], writes=[R_ps[6]])
                        fw.op("pe", lambda e: e.matmul(yo, lhsT=state_bf[:, h * 64:(h + 1) * 64], rhs=cm_[:, h, :], start=False, stop=True),
                              reads=[R_sbf, R_cm], writes=[R_ps[6]])
                    for g in range(2):
                        fw.op("pe", lambda e: e.matmul(ps[5][:, g * 256:(g + 1) * 256], lhsT=xb[:, 512 + g * 128:512 + (g + 1) * 128],
                                                       rhs=xdd[:, g * 256:(g + 1) * 256], start=True, stop=True),
                              reads=[R_xb, R_xdd], writes=[R_ps[5]])
                    fw.op("dve", lambda e: e.tensor_tensor(out=state[:].rearrange("p (h q) -> p h q", h=8), in0=state[:].rearrange("p (h q) -> p h q", h=8),
                                                           in1=cdb[:, c, :].unsqueeze(2).to_broadcast([128, 8, 64]), op=ALU.mult),
                          reads=[R_state, R_q], writes=[R_state])
                    fw.op("dve", lambda e: e.tensor_tensor(out=state[:], in0=state[:], in1=ps[5][:, :], op=ALU.add),
                          reads=[R_state, R_ps[5]], writes=[R_state])
                    fw.op("act", lambda e: e.activation(out=state_bf[:], in_=state[:], func=AF.Copy), reads=[R_state], writes=[R_sbf])
                    for pr in range(4):
                        fw.op("dve", lambda e: e.scalar_tensor_tensor(out=ya[:, pr, (c % 4) * 128:(c % 4 + 1) * 128], in0=xact[:, pr, tk],
                                                                      scalar=dsk[:, pr:pr + 1], in1=ps[6][:, pr * 128:(pr + 1) * 128],
                                                                      op0=ALU.mult, op1=ALU.add),
                              reads=[R_xact[pr], R_c, R_ps[6]], writes=[R_ya])
                    if c % 4 == 3:
                        def e1(ya=ya, R_ya=R_ya, sgt=sgt, R_sgt=R_sgt):
                            fw.op("dve", lambda e: e.tensor_tensor(out=ya[:], in0=ya[:], in1=sgt[:], op=ALU.mult), reads=[R_ya, R_sgt], writes=[R_ya])
                            fw.op("act", lambda e: e.activation(out=sq[:], in_=ya[:], func=AF.Square), reads=[R_ya], writes=[R_sq])

                        def e2():
                            for pr in range(4):
                                fw.op("pe", lambda e: e.matmul(ps[7][:, :], lhsT=ones_f[:], rhs=sq[:, pr, :], start=(pr == 0), stop=(pr == 3)),
                                      reads=[R_const, R_sq], writes=[R_ps[7]])
                            fw.op("dve", lambda e: e.tensor_scalar(out=rstd[:], in0=ps[7][:, :], scalar1=1.0 / 512, scalar2=EPS, op0=ALU.mult, op1=ALU.add),
                                  reads=[R_ps[7]], writes=[R_rstd])
                            fw.op("act", lambda e: e.activation(out=rstd[:], in_=rstd[:], func=AF.Ln), reads=[R_rstd], writes=[R_rstd])
                            fw.op("act", lambda e: e.activation(out=rstd[:], in_=rstd[:], func=AF.Exp, scale=-0.5), reads=[R_rstd], writes=[R_rstd])

                        def e3(ya=ya, R_ya=R_ya, T=T):
                            ob_, R_ob = obt.next()
                            for pr in range(4):
                                fw.op("dve", lambda e: e.scalar_tensor_tensor(out=ob_[:, pr, :], in0=ya[:, pr, :], scalar=gn[:, pr:pr + 1], in1=rstd[:],
                                                                              op0=ALU.mult, op1=ALU.mult),
                                      reads=[R_ya, R_c, R_rstd], writes=[R_ob])
                            fw.dma("sp", MT_vb[:, 4:8, T * 512:(T + 1) * 512], ob_[:], reads=[R_ob], writes=R_MT[4:8])
                        e1()
                        epi.append([c + 1, e2])
                        epi.append([c + 2, e3])

                epi = []

                def run_epi(c):
                    while epi and epi[0][0] <= c:
                        epi.pop(0)[1]()

                prep(0)
                for c in range(NCH):
                    if c + 1 < NCH:
                        prep(c + 1)
                    fin(c)
                    run_epi(c)
                run_epi(NCH + 5)
                fw.barrier()


        if "A" in STAGES:
            with ExitStack() as ph:
                nscale = 64.0 ** -0.5
                R_ac = Res()
                kcT = sb("a_kcT", [64, 2, 256], BF16, ph)
                vcx = sb("a_vcx", [128, 2, 2, 128], BF16, ph)
                ovx = sb("a_ovx", [128, 2, 65], BF16, ph)
                W3 = sb("a_W3", [128, 3072], BF16, ph)
                Wc = sb("a_Wc", [128, 896], BF16, ph)
                Ww = sb("a_Ww", [128, 1536], BF16, ph)
                exg = sb("a_exg", [24, 1536], BF16, ph)
                GLs = sb("a_GLs", [24, S], BF16, ph)
                ksEx = sb("a_ksEx", [128, S], BF16, ph)
                R_ksEx = Res()
                fw.op("dve", lambda e: e.memset(kcT[:], 0.0), writes=[R_ac])
                fw.op("dve", lambda e: e.memset(vcx[:], 1.0), writes=[R_ac])
                fw.dma("sp", GLs[:], GL[:, :], reads=[R_GL], writes=[R_ac])
                phL = ExitStack()
                kv_ring = Ring([sb(f"a_kvsb{i}", [128, S], BF16, phL) for i in range(2)])
                w1b_ring = Ring([sb(f"a_w1b{i}", [128, 32, 256], BF16, phL) for i in range(2)])
                pre_cmp = []
                w1st = Ring([sb(f"a_w1st{i}", [64, 8, 256], F32, phL) for i in range(2)])
                for prow, w1 in ((P_KC, w_cmp1_k), (P_VC, w_cmp1_v)):
                    kv_sb, R_kvsb = kv_ring.next()
                    fw.dma("sp", kv_sb[:], PT[prow:prow + 128, :], reads=[R_PT[prow // 128]], writes=[R_kvsb])
                    w1v = w1.rearrange("(l d) h -> d l h", d=64)
                    w1b, R_w1b = w1b_ring.next()
                    for lq in range(4):
                        ws, R_ws = w1st.next()
                        fw.dma("sp", ws[:], w1v[:, lq * 8:(lq + 1) * 8, :], writes=[R_ws])
                        if lq % 2 == 0:
                            fw.op("dve", lambda e: e.tensor_copy(out=w1b[0:64, lq * 8:(lq + 1) * 8, :], in_=ws[:]), reads=[R_ws], writes=[R_w1b])
                        else:
                            fw.op("act", lambda e: e.activation(out=w1b[0:64, lq * 8:(lq + 1) * 8, :], in_=ws[:], func=AF.Copy), reads=[R_ws], writes=[R_w1b])
                    fw.dma("sp", w1b[64:128, :, :], w1b[0:64, :, :], reads=[R_w1b], writes=[R_w1b])
                    pre_cmp.append((kv_sb, R_kvsb, w1b, R_w1b))
                phC = ExitStack()
                cstep = build_phase_c(phC) if "C" in STAGES else (lambda k: None)
                with ExitStack() as ph2:
                    for dst, src in ((W3, c_w3), (Wc, c_wc), (Ww, c_ww)):
                        fw.dma("sp", dst[:], src[:, :], writes=[R_ac])
                    fw.dma("sp", ovx[:].rearrange("p a b -> p (a b)"), c_ovx[:, :], writes=[R_ac])
                    fw.dma("sp", exg[:], c_exg[:, :], writes=[R_ac])
                    fw.dma("sp", ksEx[64:128, :], c_ex[:, :], writes=[R_ksEx])
                    Cts = sb("a_Cts", [64, 256], F32, ph2)
                    Sts = sb("a_Sts", [64, 256], F32, ph2)
                    fw.dma("sp", Cts[:, 0:255], CS[0, 0:64, 0:255], reads=[R_CS], writes=[R_ac])
                    fw.dma("sp", Sts[:, 0:255], CS[1, 0:64, 0:255], reads=[R_CS], writes=[R_ac])
                    w2st = sb("a_w2st", [128, 2, 64], F32, ph2)
                    w2b = sb("a_w2b", [128, 2, 64], BF16, ph2)
                    posst = sb("a_posst", [32, 128], F32, ph2)
                    posb = sb("a_posb", [128, 32], BF16, ph2)
                    hT = sb("a_hT", [128, 2, 256], BF16, ph2)
                    hbias = sb("a_hbias", [128, 2], F32, ph2)
                    q32 = sb("a_q32", [64, 256], F32, ph2)
                    tq = sb("a_tq", [64, 256], F32, ph2)
                    R_m = Res()
                    fw.op("dve", lambda e: e.memset(hT[:], 0.0), writes=[R_m])
                    for which, (prow, w1, w2, pos) in enumerate(((P_KC, w_cmp1_k, w_cmp2_k, cmp_pos_k), (P_VC, w_cmp1_v, w_cmp2_v, cmp_pos_v))):
                        kv_sb, R_kvsb, w1b, R_w1b = pre_cmp[which]
                        fw.dma("sp", w2st[:], w2.rearrange("(c p) d -> p c d", p=128), reads=[R_m], writes=[R_m])
                        fw.op("dve", lambda e: e.tensor_copy(out=w2b[:], in_=w2st[:]), reads=[R_m], writes=[R_m])
                        for half in range(2):
                            fw.dma("sp", posst[0:32, half * 64:(half + 1) * 64], pos[:, :], reads=[R_m], writes=[R_m])
                        pbt, R_pbt = next_ps(0, 3)
                        fw.op("pe", lambda e: e.transpose(pbt[:, 0:32], posst[0:32, :], ident_f[0:32, 0:32]), reads=[R_m, R_const], writes=[R_pbt])
                        fw.op("dve", lambda e: e.tensor_copy(out=posb[:], in_=pbt[:, 0:32]), reads=[R_pbt], writes=[R_m])
                        for g in range(2):
                            gs = slice(g * 64, (g + 1) * 64)
                            for hc in range(2):
                                pb, R_pb = next_ps(0, 3)
                                pbb_, R_pbb = ps[7], R_ps[7]
                                for l in range(32):
                                    fw.op("pe", lambda e: e.matmul(pb[:, 0:255], lhsT=w1b[gs, l, hc * 128:(hc + 1) * 128],
                                                                   rhs=kv_sb[gs, l:l + 16 * 254 + 1:16], start=(l == 0), stop=(l == 31)),
                                          reads=[R_w1b, R_kvsb], writes=[R_pb])
                                for l in range(32):
                                    fw.op("pe", lambda e: e.matmul(pbb_[:, 0:1], lhsT=w1b[gs, l, hc * 128:(hc + 1) * 128],
                                                                   rhs=posb[gs, l:l + 1], start=(l == 0), stop=(l == 31)),
                                          reads=[R_w1b, R_m], writes=[R_pbb])
                                fw.op("dve", lambda e: e.tensor_copy(out=hbias[:, hc:hc + 1], in_=pbb_[:, 0:1]), reads=[R_pbb], writes=[R_m])
                                fw.op("act", lambda e: e.activation(out=hT[:, hc, 0:255], in_=pb[:, 0:255], func=AF.Silu, bias=hbias[:, hc:hc + 1]),
                                      reads=[R_pb, R_m], writes=[R_m])
                                cstep(4)
                            if which == 0:
                                pb, R_pb = next_ps(0, 3)
                                for hc in range(2):
                                    fw.op("pe", lambda e: e.matmul(pb[0:64, 0:256], lhsT=w2b[:, hc, :], rhs=hT[:, hc, :], start=(hc == 0), stop=(hc == 1)),
                                          reads=[R_m], writes=[R_pb])
                                fw.op("dve", lambda e: e.tensor_copy(out=q32[:], in_=pb[0:64, 0:256]), reads=[R_pb], writes=[R_m])
                                pb2, R_pb2 = next_ps(0, 3)
                                fw.op("pe", lambda e: e.matmul(pb2[0:64, 0:256], lhsT=psw_f[0:64, 0:64], rhs=q32[:], start=True, stop=True),
                                      reads=[R_m, R_const], writes=[R_pb2])
                                fw.op("dve", lambda e: e.tensor_tensor(out=tq[:, 0:255], in0=pb2[0:64, 0:255], in1=Sts[:, 0:255], op=ALU.mult),
                                      reads=[R_pb2, R_ac], writes=[R_m])
                                fw.op("dve", lambda e: e.tensor_tensor(out=q32[:, 0:255], in0=q32[:, 0:255], in1=Cts[:, 0:255], op=ALU.mult),
                                      reads=[R_m, R_ac], writes=[R_m])
                                fw.op("dve", lambda e: e.tensor_tensor(out=kcT[:, g, 0:255], in0=q32[:, 0:255], in1=tq[:, 0:255], op=ALU.add),
                                      reads=[R_m], writes=[R_ac])
                            else:
                                for nch in range(2):
                                    pb, R_pb = next_ps(0, 3)
                                    for hc in range(2):
                                        fw.op("pe", lambda e: e.matmul(pb[:, 0:64], lhsT=hT[:, hc, nch * 128:(nch + 1) * 128], rhs=w2b[:, hc, :],
                                                                       start=(hc == 0), stop=(hc == 1)), reads=[R_m], writes=[R_pb])
                                    fw.op("dve", lambda e: e.tensor_copy(out=vcx[:, g, nch, 0:64], in_=pb[:, 0:64]), reads=[R_pb], writes=[R_ac])
                    cstep(40)
                    fw.barrier()
                phC.close()
                phL.close()
                if DEBUG:
                    DBGK = nc.dram_tensor("DBGK", [64, 512], BF16, kind="ExternalOutput").ap()
                    DBGV = nc.dram_tensor("DBGV", [128, 512], BF16, kind="ExternalOutput").ap()
                    DBGS = nc.dram_tensor("DBGS", [128, 2 * S], BF16, kind="ExternalOutput").ap()
                    fw.dma("sp", DBGK[:, :], kcT[:].rearrange("p a b -> p (a b)"), reads=[R_ac], writes=[Res()])
                    fw.dma("sp", DBGV[:, :], vcx[:].rearrange("p a b c -> p (a b c)"), reads=[R_ac], writes=[Res()])
                qS = [sb(f"a_qS{e}", [128, S], BF16, ph) for e in range(4)]
                R_qS = [[Res() for _ in range(NTT)] for _ in range(4)]
                kwT = sb("a_kwT", [64, S], BF16, ph)
                R_kw = Res()
                vsx = sb("a_vsx", [128, NT, 128], BF16, ph)
                vwx = sb("a_vwx", [128, NT, 128], BF16, ph)
                R_v = Res()
                fw.op("dve", lambda e: e.memset(vsx[:], 1.0), writes=[R_v])
                fw.op("dve", lambda e: e.memset(vwx[:], 1.0), writes=[R_v])
                vst = sb("a_vst", [128, NT, 64], F32, ph)
                R_vst = Res()
                pr_ = Ring([sb(f"a_p{i}", [128, 512], BF16, ph) for i in range(6)])
                accs = [[sb(f"a_acc{par}_{e}", [64, 512], F32, ph) for e in range(4)] for par in range(2)]
                R_acc = [[Res() for _ in range(4)] for _ in range(2)]
                rdn = Ring([sb(f"a_rdn{i}", [64, 512], F32, ph) for i in range(3)])
                tmpo = Ring([sb(f"a_tmpo{i}", [64, 512], F32, ph) for i in range(2)])
                sga = Ring([sb(f"a_sga{i}", [64, 512], BF16, ph) for i in range(3)])
                oo = Ring([sb(f"a_oo{i}", [64, 512], BF16, ph) for i in range(2)])
                impacc = sb("a_impacc", [128, 4, 64], F32, ph)
                imptmp = sb("a_imptmp", [128, 4, 64], F32, ph)
                irec = sb("a_irec", [128, 4, 1], F32, ph)
                addm = sb("a_addm", [128, 4, 64], F32, ph)
                m8 = sb("a_m8", [128, 16], F32, ph)
                wk = sb("a_wk", [128, 64], F32, ph)
                nsel = sb("a_nsel", [128, 4, 64], BF16, ph)
                R_imp = Res()
                R_nsel = Res()
                R_addm = Res()
                nselT = sb("a_nselT", [64, 512], BF16, ph)
                R_nselT = Res()
                VT_v = VT.rearrange("(t p) f -> p t f", p=128)
                addm_v = c_addm.rearrange("(t p) j -> p t j", p=128)
                psI, R_psI = ps[6], R_ps[6]
                psG, R_psG = ps[7], R_ps[7]
                psTb = psI[:].bitcast(BF16)
                obank = {"i": 0}
                sbank = {"i": 0}

                def next_o(cmp=False):
                    if cmp:
                        return ps[5], R_ps[5]
                    i = 3 + obank["i"] % 2
                    obank["i"] += 1
                    return ps[i], R_ps[i]

                def next_s():
                    i = sbank["i"] % 3
                    sbank["i"] += 1
                    return ps[i], R_ps[i]

                def finish_branch(po, R_po, acc, R_a, h, br, first, ts):
                    rd, R_rd = rdn.next()
                    if br in (0, 1):
                        fw.op("act", lambda e: e.activation(out=rd[:], in_=po[64:128, :], func=AF.Ln, bias=epsb[0:64, 0:1]),
                              reads=[R_po, R_const], writes=[R_rd])
                        fw.op("act", lambda e: e.activation(out=rd[:], in_=rd[:], func=AF.Exp, scale=-1.0), reads=[R_rd], writes=[R_rd])
                    else:
                        fw.op("dve", lambda e: e.reciprocal(out=rd[:], in_=po[64:128, :]), reads=[R_po], writes=[R_rd])
                    fw.op("pe", lambda e: e.matmul(psG[0:64, :], lhsT=exg[:, (h * 3 + br) * 64:(h * 3 + br + 1) * 64], rhs=GLs[:, ts],
                                                   start=True, stop=True), reads=[R_ac], writes=[R_psG])
                    fw.op("dve", lambda e: e.tensor_tensor(out=rd[:], in0=rd[:], in1=psG[0:64, :], op=ALU.mult), reads=[R_rd, R_psG], writes=[R_rd])
                    if first:
                        fw.op("dve", lambda e: e.tensor_tensor(out=acc[:], in0=po[0:64, :], in1=rd[:], op=ALU.mult),
                              reads=[R_po, R_rd], writes=[R_a])
                    else:
                        tt_, R_tt = tmpo.next()
                        fw.op("dve", lambda e: e.tensor_tensor(out=tt_[:], in0=po[0:64, :], in1=rd[:], op=ALU.mult), reads=[R_po, R_rd], writes=[R_tt])
                        fw.op("pool", lambda e: e.tensor_tensor(out=acc[:], in0=acc[:], in1=tt_[:], op=ALU.add),
                              reads=[R_tt, R_a], writes=[R_a])

                def run_jobs(jobs, L=2):
                    n = len(jobs)
                    for i in range(n + L):
                        if i < n:
                            jobs[i][0]()
                        if i - L >= 0:
                            jobs[i - L][1]()

                def make_job(score_fn, pv_fn):
                    st = {}
                    return (lambda: score_fn(st), lambda: pv_fn(st))

                for g in range(2):
                    for e_ in range(4):
                        h = g * 4 + e_
                        fw.dma("sp", qS[e_][0:64, :], PT[P_Q + h * 64:P_Q + (h + 1) * 64, :], reads=[R_PT[(P_Q + h * 64) // 128]] + R_qS[e_], writes=R_qS[e_])
                    fw.dma("sp", ksEx[0:64, :], PT[P_KS + g * 64:P_KS + (g + 1) * 64, :], reads=[R_PT[P_KS // 128], R_ksEx], writes=[R_ksEx])
                    fw.dma("sp", kwT[:], PT[P_KW + g * 64:P_KW + (g + 1) * 64, :], reads=[R_PT[P_KW // 128], R_kw], writes=[R_kw])
                    for dst, c0 in ((vsx, 128 + g * 64), (vwx, 256 + g * 64)):
                        for q4 in range(4):
                            fw.dma("sp", vst[:, q4 * 8:(q4 + 1) * 8, :], VT_v[:, q4 * 8:(q4 + 1) * 8, c0:c0 + 64], reads=[R_VT, R_vst], writes=[R_vst])
                        fw.op("dve", lambda e: e.tensor_copy(out=dst[:, :, 0:64], in_=vst[:]), reads=[R_vst, R_v], writes=[R_v, R_vst])

                    def cmp_jobs(T):
                        ts = slice(T * 512, (T + 1) * 512)
                        nchs = [0] if T < 4 else [0, 1]
                        jobs = []
                        for e_ in range(4):
                            h = g * 4 + e_
                            hold = {}
                            for i, nch in enumerate(nchs):
                                def score(st, e_=e_, nch=nch):
                                    pa, R_pa = next_s()
                                    need_mask = not (nch == 0 and T >= 5)
                                    fw.op("pe", lambda e: e.matmul(pa[:, :], lhsT=kcT[:, g, nch * 128:(nch + 1) * 128], rhs=qS[e_][0:64, ts],
                                                                   start=True, stop=not need_mask), reads=[R_ac, R_qS[e_][T]], writes=[R_pa])
                                    if need_mask:
                                        sh = 512 * T - 2048 * nch
                                        fw.op("pe", lambda e: e.matmul(pa[:, :], lhsT=ident_b[:], rhs=W3[:, sh:sh + 512], start=False, stop=True),
                                              reads=[R_ac, R_const], writes=[R_pa])
                                    p, R_p = pr_.next()
                                    fw.op("act", lambda e: e.activation(out=p[:], in_=pa[:, :], func=AF.Exp, scale=nscale), reads=[R_pa], writes=[R_p])
                                    st["p"] = (p, R_p)

                                def pv(st, e_=e_, h=h, i=i, nch=nch, hold=hold):
                                    p, R_p = st["p"]
                                    if i == 0:
                                        hold["po"] = next_o(cmp=True)
                                    po, R_po = hold["po"]
                                    last = (i == len(nchs) - 1)
                                    fw.op("pe", lambda e: e.matmul(po[:, :], lhsT=vcx[:, g, nch, :], rhs=p[:], start=(i == 0), stop=last),
                                          reads=[R_ac, R_p], writes=[R_po])
                                    for sub in range(4):
                                        fw.op("pe", lambda e: e.matmul(psI[:, sub * 65:(sub + 1) * 65], lhsT=p[:, sub * 128:(sub + 1) * 128], rhs=ovx[:, nch, :],
                                                                       start=(i == 0 and sub == 0), stop=(last and sub == 3), skip_group_check=True),
                                              reads=[R_ac, R_p], writes=[R_psI])
                                    if last:
                                        finish_branch(po, R_po, accs[T % 2][e_], R_acc[T % 2][e_], h, 0, True, ts)
                                        pI = psI[:, 0:260].rearrange("p (s f) -> p s f", s=4)
                                        fw.op("dve", lambda e: e.tensor_scalar(out=irec[:], in0=pI[:, :, 64:65], scalar1=1e-30, scalar2=None, op0=ALU.max),
                                              reads=[R_psI], writes=[R_imp])
                                        fw.op("dve", lambda e: e.reciprocal(out=irec[:], in_=irec[:]), reads=[R_imp], writes=[R_imp])
                                        if e_ == 0:
                                            fw.op("dve", lambda e: e.tensor_tensor(out=impacc[:], in0=pI[:, :, 0:64], in1=irec[:].to_broadcast([128, 4, 64]), op=ALU.mult),
                                                  reads=[R_psI, R_imp], writes=[R_imp])
                                        else:
                                            fw.op("dve", lambda e: e.tensor_tensor(out=imptmp[:], in0=pI[:, :, 0:64], in1=irec[:].to_broadcast([128, 4, 64]), op=ALU.mult),
                                                  reads=[R_psI, R_imp], writes=[R_imp])
                                            fw.op("dve", lambda e: e.tensor_tensor(out=impacc[:], in0=impacc[:], in1=imptmp[:], op=ALU.add), reads=[R_imp], writes=[R_imp])
                                jobs.append(make_job(score, pv))
                        return jobs

                    def sel_dve(T):
                        fw.dma("sp", addm[:], addm_v[:, T * 4:(T + 1) * 4, :], reads=[R_addm], writes=[R_addm])
                        fw.op("dve", lambda e: e.tensor_tensor(out=impacc[:], in0=impacc[:], in1=addm[:], op=ALU.add), reads=[R_imp, R_addm], writes=[R_imp, R_addm])
                        for sub in range(4):
                            fw.op("dve", lambda e: e.max(out=m8[:, 0:8], in_=impacc[:, sub, :]), reads=[R_imp], writes=[R_imp])
                            fw.op("dve", lambda e: e.match_replace(out=wk[:], in_to_replace=m8[:, 0:8], in_values=impacc[:, sub, :], imm_value=-3.0e9),
                                  reads=[R_imp], writes=[R_imp])
                            fw.op("dve", lambda e: e.max(out=m8[:, 8:16], in_=wk[:]), reads=[R_imp], writes=[R_imp])
                            fw.op("dve", lambda e: e.tensor_scalar(out=nsel[:, sub, :], in0=impacc[:, sub, :], scalar1=m8[:, 15:16], scalar2=NEG,
                                                                   op0=ALU.is_lt, op1=ALU.mult), reads=[R_imp], writes=[R_nsel])

                    def sel_pe(T):
                        ts = slice(T * 512, (T + 1) * 512)
                        for sub in range(4):
                            fw.op("pe", lambda e: e.transpose(psTb[0:64, sub * 128:(sub + 1) * 128], nsel[:, sub, :], ident_b[:]),
                                  reads=[R_nsel, R_const], writes=[R_psI])
                        fw.op("dve", lambda e: e.tensor_copy(out=nselT[:], in_=psTb[0:64, 0:512]), reads=[R_psI], writes=[R_nselT])
                        for e_ in range(4):
                            fw.dma("sp", qS[e_][64:128, ts], nselT[:], reads=[R_nselT], writes=[R_qS[e_][T]])

                    def selwin_jobs(T):
                        ts = slice(T * 512, (T + 1) * 512)
                        jobs = []
                        for e_ in range(4):
                            h = g * 4 + e_
                            acc, R_a = accs[T % 2][e_], R_acc[T % 2][e_]
                            nk = 4 * T + 4
                            hold_s = {}
                            for k in range(nk):
                                def score(st, e_=e_, k=k):
                                    pa, R_pa = next_s()
                                    diag = k >= 4 * T
                                    i = k - 4 * T
                                    c0 = i * 128 if diag else 0
                                    tq = slice(T * 512 + c0, (T + 1) * 512)
                                    fw.op("pe", lambda e: e.matmul(pa[:, c0:512], lhsT=ksEx[:, k * 128:(k + 1) * 128], rhs=qS[e_][:, tq], start=True, stop=not diag),
                                          reads=[R_ksEx, R_qS[e_][T]], writes=[R_pa])
                                    if diag:
                                        fw.op("pe", lambda e: e.matmul(pa[:, c0:512], lhsT=ident_b[:], rhs=Wc[:, 384 - i * 128 + c0:384 - i * 128 + 512], start=False, stop=True),
                                              reads=[R_ac, R_const], writes=[R_pa])
                                    p, R_p = pr_.next()
                                    fw.op("act", lambda e: e.activation(out=p[:, c0:512], in_=pa[:, c0:512], func=AF.Exp, scale=nscale), reads=[R_pa], writes=[R_p])
                                    st["p"] = (p, R_p, c0)

                                def pv(st, e_=e_, h=h, k=k, nk=nk, hold=hold_s, acc=acc, R_a=R_a):
                                    p, R_p, c0 = st["p"]
                                    if k == 0:
                                        hold["po"] = next_o()
                                    po, R_po = hold["po"]
                                    fw.op("pe", lambda e: e.matmul(po[:, c0:512], lhsT=vsx[:, k, :], rhs=p[:, c0:512], start=(k == 0), stop=(k == nk - 1),
                                                                   skip_group_check=True),
                                          reads=[R_v, R_p], writes=[R_po])
                                    if k == nk - 1:
                                        finish_branch(po, R_po, acc, R_a, h, 1, False, ts)
                                jobs.append(make_job(score, pv))
                            ks_ = [k for k in range(4 * T - 4, 4 * T + 4) if k >= 0]
                            hold_w = {}
                            for j, k in enumerate(ks_):
                                def score(st, e_=e_, k=k):
                                    pa, R_pa = next_s()
                                    i = k - 4 * T
                                    c0 = max(0, i * 128)
                                    c1 = min(512, (i + 5) * 128)
                                    tq = slice(T * 512 + c0, T * 512 + c1)
                                    fw.op("pe", lambda e: e.matmul(pa[:, c0:c1], lhsT=kwT[:, k * 128:(k + 1) * 128], rhs=qS[e_][0:64, tq], start=True, stop=False),
                                          reads=[R_kw, R_qS[e_][T]], writes=[R_pa])
                                    fw.op("pe", lambda e: e.matmul(pa[:, c0:c1], lhsT=ident_b[:], rhs=Ww[:, 512 - i * 128 + c0:512 - i * 128 + c1], start=False, stop=True),
                                          reads=[R_ac, R_const], writes=[R_pa])
                                    p, R_p = pr_.next()
                                    fw.op("act", lambda e: e.activation(out=p[:, c0:c1], in_=pa[:, c0:c1], func=AF.Exp, scale=nscale), reads=[R_pa], writes=[R_p])
                                    st["p"] = (p, R_p, c0, c1)

                                def pv(st, e_=e_, h=h, j=j, k=k, nw=len(ks_), hold=hold_w, acc=acc, R_a=R_a):
                                    p, R_p, c0, c1 = st["p"]
                                    if j == 0:
                                        hold["po"] = next_o()
                                    po, R_po = hold["po"]
                                    fw.op("pe", lambda e: e.matmul(po[:, c0:c1], lhsT=vwx[:, k, :], rhs=p[:, c0:c1], start=(j == 0), stop=(j == nw - 1),
                                                                   skip_group_check=True),
                                          reads=[R_v, R_p], writes=[R_po])
                                    if j == nw - 1:
                                        finish_branch(po, R_po, acc, R_a, h, 2, False, ts)
                                        sg_, R_sg = sga.next()
                                        fw.dma("sp", sg_[:], SG[h * 64:(h + 1) * 64, ts], reads=[R_SG[h // 2]], writes=[R_sg])
                                        o_, R_o = oo.next()
                                        fw.op("pool", lambda e: e.tensor_tensor(out=o_[:], in0=acc[:], in1=sg_[:], op=ALU.mult),
                                              reads=[R_a, R_sg], writes=[R_o])
                                        fw.dma("pool", MT[h * 64:(h + 1) * 64, ts], o_[:], reads=[R_o], writes=[R_MT[h // 2]])
                                jobs.append(make_job(score, pv))
                        return jobs

                    run_jobs(cmp_jobs(0))
                    sel_dve(0)
                    sel_pe(0)
                    for T in range(NTT):
                        SJ = selwin_jobs(T)
                        if T + 1 < NTT:
                            C = cmp_jobs(T + 1)
                            extra = {}
                            step = 4 if len(C) <= 4 else 3
                            for j, cj in enumerate(C):
                                extra.setdefault(1 + step * j, []).append(cj)
                            pos_d = 1 + step * len(C) + 4
                            pos_p = pos_d + 6
                            extra.setdefault(pos_d, []).append((lambda T1=T + 1: sel_dve(T1), lambda: None))
                            extra.setdefault(pos_p, []).append((lambda T1=T + 1: sel_pe(T1), lambda: None))
                            assert pos_p < len(SJ)
                            merged = []
                            for i, sj in enumerate(SJ):
                                merged.extend(extra.get(i, []))
                                merged.append(sj)
                            run_jobs(merged)
                        else:
                            run_jobs(SJ)
                fw.barrier()

        active = []
        if "A" in STAGES:
            active += [0, 1, 2, 3]
        if "B" in STAGES:
            active += [4, 5, 6, 7]
        if "C" in STAGES:
            active += [8, 9, 10, 11]
        if "F" in STAGES:
            with ExitStack() as ph:
                gf = sb("gf", [128, D], F32, ph)
                R_gf = Res()
                fw.dma("sp", gf[:], g_final.partition_broadcast(128), writes=[R_gf])
                mt = Ring([sb(f"mt{i}", [128, 12, 512], BF16, ph) for i in range(2)])
                xin = Ring([sb(f"fxin{i}", [128, D], F32, ph) for i in range(2)])
                hb = Ring([sb(f"hb{i}", [128, D], F32, ph) for i in range(2)])
                ob = Ring([sb(f"ob{i}", [128, D], F32, ph) for i in range(2)])
                junk = sb("fjunk", [128, D], BF16, ph)
                R_junk = Res()
                st = Ring([sb(f"fst{i}", [128, 4], F32, ph) for i in range(2)])
                MT_v = MT.rearrange("(ck p) t -> p ck t", p=128)
                for T in range(NTT):
                    m, R_m = mt.next()
                    if active:
                        lo, hi = min(active), max(active) + 1
                        fw.dma("sp", m[:, lo:hi, :], MT_v[:, lo:hi, T * 512:(T + 1) * 512], reads=[R_MT[c] for c in active], writes=[R_m])
                    for sub in range(4):
                        tt = T * 4 + sub
                        xt, R_xt = xin.next()
                        fw.dma("sp", xt[:], x[tt * 128:(tt + 1) * 128, :], writes=[R_xt])
                        h, R_h = hb.next()
                        if active:
                            for half in range(2):
                                pb, R_pb = next_ps()
                                for i, ck in enumerate(active):
                                    fw.op("pe", lambda e: e.matmul(pb[:, :], lhsT=m[:, ck, sub * 128:(sub + 1) * 128],
                                                                   rhs=wo[:, ck, half * 512:(half + 1) * 512],
                                                                   start=(i == 0), stop=(i == len(active) - 1)),
                                          reads=[R_m, R_wo], writes=[R_pb])
                                fw.op("dve", lambda e: e.tensor_tensor(out=h[:, half * 512:(half + 1) * 512], in0=pb[:, :],
                                                                       in1=xt[:, half * 512:(half + 1) * 512], op=ALU.add),
                                      reads=[R_pb, R_xt], writes=[R_h])
                        else:
                            fw.op("dve", lambda e: e.tensor_copy(out=h[:], in_=xt[:]), reads=[R_xt], writes=[R_h])
                        s4, R_s4 = st.next()
                        fw.op("act", lambda e: e.activation(out=junk[:], in_=h[:], func=AF.Square, accum_out=s4[:, 0:1]),
                              reads=[R_h], writes=[R_junk, R_s4])
                        fw.op("dve", lambda e: e.tensor_scalar(out=s4[:, 1:2], in0=s4[:, 0:1], scalar1=1.0 / D, scalar2=EPS,
                                                               op0=ALU.mult, op1=ALU.add), reads=[R_s4], writes=[R_s4])
                        fw.op("act", lambda e: e.activation(out=s4[:, 2:3], in_=s4[:, 1:2], func=AF.Ln), reads=[R_s4], writes=[R_s4])
                        fw.op("act", lambda e: e.activation(out=s4[:, 3:4], in_=s4[:, 2:3], func=AF.Exp, scale=-0.5),
                              reads=[R_s4], writes=[R_s4])
                        o, R_o = ob.next()
                        fw.op("dve", lambda e: e.scalar_tensor_tensor(out=o[:], in0=h[:], scalar=s4[:, 3:4], in1=gf[:],
                                                                      op0=ALU.mult, op1=ALU.mult),
                              reads=[R_h, R_s4, R_gf], writes=[R_o])
                        fw.dma("pool", out[tt * 128:(tt + 1) * 128, :], o[:], reads=[R_o], writes=[R_out])
                fw.barrier()
        fw.barrier()
    print(f"[kernel] ops={fw.nops} waits={fw.nwaits} cnt={fw.cnt}")
    fw.close()
    return nc


def _host_consts():
    bf = ml_dtypes.bfloat16
    ident = np.eye(128, dtype=np.float32)
    inv = (np.float32(500000.0) ** (-(np.arange(0, 16, 2, dtype=np.float32)) / np.float32(16))).astype(np.float32)
    ropeinv = np.zeros((128, 1), np.float32)
    psw = np.zeros((128, 128), np.float32)
    for h in range(2):
        for i in range(8):
            ropeinv[h * 64 + i, 0] = inv[i]
            ropeinv[h * 64 + 8 + i, 0] = inv[i]
            psw[h * 64 + 8 + i, h * 64 + i] = -1.0
            psw[h * 64 + i, h * 64 + 8 + i] = 1.0
    ii = np.arange(128)
    triu = (ii[:, None] <= ii[None, :]).astype(np.float32)
    trineg = np.where(ii[:, None] <= ii[None, :], 0.0, NEG).astype(np.float32)
    hsel = np.zeros((128, 4, 8), np.float32)
    for p in range(128):
        for pr in range(4):
            hsel[p, pr, 2 * pr + p // 64] = 1.0
    n = np.arange(256)
    j = np.arange(64)
    cstart = 16 * n
    ov = ((cstart[:, None] < 64 * j[None, :] + 64) & (cstart[:, None] + 32 > 64 * j[None, :])).astype(np.float32)
    ov[255, :] = 0.0
    ovx = np.zeros((128, 2, 65), np.float32)
    for nch in range(2):
        ovx[:, nch, 0:64] = ov[nch * 128:(nch + 1) * 128]
        ovx[:, nch, 64] = 1.0
    col = np.arange(3072)
    w3 = np.where(col[None, :] >= 16 * ii[:, None] + 31, 0.0, NEG).astype(np.float32)
    col = np.arange(896)
    wc = np.where((col[None, :] - 384) >= ii[:, None], 0.0, NEG).astype(np.float32)
    col = np.arange(1536)
    u = col[None, :] - 512 - ii[:, None]
    ww = np.where((u >= 0) & (u < 512), 0.0, NEG).astype(np.float32)
    t = np.arange(S)
    cur = t // 64
    forced = (j[None, :] == 0) | (j[None, :] == cur[:, None]) | (j[None, :] == cur[:, None] - 1)
    valid = j[None, :] <= cur[:, None]
    addm = np.where(forced, 1e9, np.where(valid, 0.0, -1e9)).astype(np.float32)
    exg = np.zeros((24, 24, 64), np.float32)
    for r in range(24):
        exg[r, r, :] = 1.0
    ex = (np.arange(S)[None, :] // 64 == j[:, None]).astype(np.float32)
    return {"c_ident": ident, "c_ropeinv": ropeinv, "c_psw": psw, "c_triu": triu, "c_trineg": trineg,
            "c_hsel": hsel.reshape(128, 32), "c_ovx": ovx.reshape(128, 130).astype(bf), "c_w3": w3.astype(bf), "c_wc": wc.astype(bf),
            "c_ww": ww.astype(bf), "c_addm": addm, "c_exg": exg.reshape(24, 1536).astype(bf), "c_ex": ex.astype(bf)}


_NC_CACHE = {}


def kernel(**inputs):
    if "nc" not in _NC_CACHE:
        _NC_CACHE["nc"] = build_nc()
    nc = _NC_CACHE["nc"]
    consts = _host_consts()
    B = inputs["x"].shape[0]
    in_maps = []
    for b in range(B):
        m = {
            "x": np.ascontiguousarray(inputs["x"][b], dtype=np.float32),
            "mem": np.ascontiguousarray(inputs["mem"][b], dtype=np.float32),
            "positions": np.ascontiguousarray(inputs["positions"][b:b + 1], dtype=np.int32),
            "g_in": np.ascontiguousarray(inputs["g_in"][0], dtype=np.float32),
            "w_in": np.ascontiguousarray(inputs["w_in"][0], dtype=np.float32),
            "cmp_pos_k": np.ascontiguousarray(inputs["cmp_pos_k"][0], dtype=np.float32),
            "w_cmp1_k": np.ascontiguousarray(inputs["w_cmp1_k"][0], dtype=np.float32),
            "w_cmp2_k": np.ascontiguousarray(inputs["w_cmp2_k"][0], dtype=np.float32),
            "cmp_pos_v": np.ascontiguousarray(inputs["cmp_pos_v"][0], dtype=np.float32),
            "w_cmp1_v": np.ascontiguousarray(inputs["w_cmp1_v"][0], dtype=np.float32),
            "w_cmp2_v": np.ascontiguousarray(inputs["w_cmp2_v"][0], dtype=np.float32),
            "conv_w": np.ascontiguousarray(inputs["conv_w"][0], dtype=np.float32),
            "conv_b": np.ascontiguousarray(inputs["conv_b"][0], dtype=np.float32),
            "dt_bias": np.ascontiguousarray(inputs["dt_bias"][0:1], dtype=np.float32),
            "a_log": np.ascontiguousarray(inputs["a_log"][0:1], dtype=np.float32),
            "d_skip": np.ascontiguousarray(inputs["d_skip"][0:1], dtype=np.float32),
            "g_ssd_norm": np.ascontiguousarray(inputs["g_ssd_norm"][0], dtype=np.float32),
            "g_mem": np.ascontiguousarray(inputs["g_mem"][0], dtype=np.float32),
            "w_mem_kv": np.ascontiguousarray(inputs["w_mem_kv"][0], dtype=np.float32),
            "w_out": np.ascontiguousarray(inputs["w_out"][0], dtype=np.float32),
            "g_final": np.ascontiguousarray(np.asarray(inputs["g_final"]).reshape(1, D), dtype=np.float32),
        }
        m.update(consts)
        in_maps.append(m)
    if os.environ.get("KTRACE"):
        res = run_bass_kernel_spmd(nc, in_maps, core_ids=list(range(B)), trace=True)
        print("[kernel] exec_time_ns", res.exec_time_ns)
    else:
        res = run_bass_kernel_spmd(nc, in_maps, core_ids=list(range(B)))
    if DEBUG:
        _NC_CACHE["last"] = res
    return np.stack([np.asarray(r["out"], dtype=np.float32) for r in res.results], axis=0)
```
